# Optimizing a Trainium2 kernel written in Bass

```python
import jax, jax.numpy as jnp
from jax import lax
import numpy as np

D_MODEL = 1024
BATCH = 16
SEQ = 2048
DEPTH = 2

HEAD_DIM = 64
EPS = 1e-6
LRU_WIDTH = D_MODEL // 2
LRU_BLOCKS = 8
LRU_BLOCK = LRU_WIDTH // LRU_BLOCKS
LRU_CONV = 4
LRU_C = 8.0
SB_HEADS = 8
SB_WIDTH = SB_HEADS * HEAD_DIM
Q_BLOCK = 128
N_EVEN_SPLITS = 5
SWA_HEADS = D_MODEL // HEAD_DIM
SWA_KV_HEADS = 4
SWA_GROUP = SWA_HEADS // SWA_KV_HEADS
WINDOW = 128
D_FF = 2816
FFN_CONV = 3
N_EVEN = (DEPTH + 1) // 2
N_ODD = DEPTH // 2

kernel_name = "hybrid_rglru_stickbreak_swasink_convffn"


def rms_norm(x, g):
    xf = x.astype(jnp.float32)
    y = xf * lax.rsqrt(jnp.mean(xf * xf, axis=-1, keepdims=True) + EPS)
    return (y * g.astype(jnp.float32)).astype(x.dtype)


def causal_dwconv(x, w, b):
    k_width, ch = w.shape
    y = lax.conv_general_dilated(
        x, w[:, None, :].astype(x.dtype), window_strides=(1,),
        padding=[(k_width - 1, 0)], dimension_numbers=("NWC", "WIO", "NWC"),
        feature_group_count=ch)
    return y + b.astype(x.dtype)


def adaln_params(c, w, b):
    m = jax.nn.silu(c) @ w + b
    return [t[:, None, :] for t in jnp.split(m, 6, axis=-1)]


def rg_lru(x, wa, ba, wx, bx, lam):
    bsz, seq, width = x.shape
    xb = x.reshape(bsz, seq, LRU_BLOCKS, LRU_BLOCK)
    r = jax.nn.sigmoid((jnp.einsum("bsnk,nkj->bsnj", xb, wa).reshape(bsz, seq, width) + ba).astype(jnp.float32))
    i = jax.nn.sigmoid((jnp.einsum("bsnk,nkj->bsnj", xb, wx).reshape(bsz, seq, width) + bx).astype(jnp.float32))
    log_a = -LRU_C * r * jax.nn.softplus(-lam.astype(jnp.float32))
    a = jnp.exp(log_a)
    u = jnp.sqrt(-jnp.expm1(2.0 * log_a)) * (i * x.astype(jnp.float32))

    def combine(e1, e2):
        a1, b1 = e1
        a2, b2 = e2
        return a1 * a2, a2 * b1 + b2

    _, h = lax.associative_scan(combine, (a, u), axis=1)
    return h.astype(x.dtype)


def stick_breaking(q, k, v):
    bsz, seq, heads, dh = q.shape
    nb = seq // Q_BLOCK
    scale = 1.0 / np.sqrt(dh)
    qb = q.reshape(bsz, nb, Q_BLOCK, heads, dh).transpose(1, 0, 3, 2, 4)
    kh = k.transpose(0, 2, 1, 3)
    vh = v.transpose(0, 2, 1, 3)
    kpos = jnp.arange(seq)

    def block(args):
        qi, blk = args
        z = jnp.einsum("bhqd,bhkd->bhqk", qi, kh).astype(jnp.float32) * scale
        qpos = blk * Q_BLOCK + jnp.arange(Q_BLOCK)
        mask = kpos[None, :] < qpos[:, None]
        sp = jnp.where(mask, jax.nn.softplus(z), 0.0)
        rev = lax.cumsum(sp, axis=3, reverse=True)
        w = jnp.exp(jnp.where(mask, z - rev, -jnp.inf))
        return jnp.einsum("bhqk,bhkd->bhqd", w.astype(v.dtype), vh)

    out = lax.map(block, (qb, jnp.arange(nb)))
    return out.transpose(1, 0, 3, 2, 4).reshape(bsz, seq, heads * dh)


def swa_sinks(q, k, v, sinks):
    bsz, seq, _, dh = q.shape
    nb = seq // WINDOW
    scale = 1.0 / np.sqrt(dh)
    qb = q.reshape(bsz, nb, WINDOW, SWA_KV_HEADS, SWA_GROUP, dh)

    def banded(t):
        tb = t.reshape(bsz, nb, WINDOW, SWA_KV_HEADS, dh)
        prev = jnp.concatenate([jnp.zeros_like(tb[:, :1]), tb[:, :-1]], axis=1)
        return jnp.concatenate([prev, tb], axis=2)

    kk, vv = banded(k), banded(v)
    s = jnp.einsum("bnqkgd,bnjkd->bnkgqj", qb, kk).astype(jnp.float32) * scale
    i = jnp.arange(WINDOW)[:, None]
    j = jnp.arange(2 * WINDOW)[None, :]
    band = (j > i) & (j <= i + WINDOW)
    blk = jnp.arange(nb)[:, None, None]
    mask = band[None] & ((j >= WINDOW)[None] | (blk > 0))
    s = jnp.where(mask[None, :, None, None], s, -jnp.inf)
    sink = sinks.astype(jnp.float32).reshape(SWA_KV_HEADS, SWA_GROUP)[None, None, :, :, None, None]
    m = jnp.maximum(jnp.max(s, axis=-1, keepdims=True), sink)
    p = jnp.exp(s - m)
    p = p / (jnp.sum(p, axis=-1, keepdims=True) + jnp.exp(sink - m))
    out = jnp.einsum("bnkgqj,bnjkd->bnqkgd", p.astype(v.dtype), vv)
    return out.reshape(bsz, seq, SWA_HEADS * dh)


def conv_ffn(h, w_gate, w_up, conv_w, conv_b, w_down):
    a = causal_dwconv(h @ w_gate, conv_w, conv_b)
    return (jax.nn.silu(a) * (h @ w_up)) @ w_down


def setup_inputs(seed: int = 0) -> dict:
    key = jax.random.key(seed)
    ks = iter(jax.random.split(key, 40))

    def nrm(shape, scale):
        return jax.random.normal(next(ks), shape, jnp.float32) * scale

    d = D_MODEL
    u = jax.random.uniform(next(ks), (N_EVEN, LRU_WIDTH), jnp.float32, minval=0.9, maxval=0.999)
    a0 = u ** (1.0 / LRU_C)
    lam = jnp.log(a0) - jnp.log1p(-a0)
    return {
        "x": nrm((BATCH, SEQ, d), 1.0),
        "c": nrm((BATCH, d), 1.0),
        "ada_w": nrm((DEPTH, d, 6 * d), 0.5 * d ** -0.5),
        "ada_b": nrm((DEPTH, 6 * d), 0.02),
        "norm_mix_g": 1.0 + nrm((DEPTH, d), 0.02),
        "norm_ffn_g": 1.0 + nrm((DEPTH, d), 0.02),
        "ev_w_in": nrm((N_EVEN, d, N_EVEN_SPLITS * LRU_WIDTH), d ** -0.5),
        "ev_conv_w": nrm((N_EVEN, LRU_CONV, LRU_WIDTH), LRU_CONV ** -0.5),
        "ev_conv_b": nrm((N_EVEN, LRU_WIDTH), 0.02),
        "ev_wa": nrm((N_EVEN, LRU_BLOCKS, LRU_BLOCK, LRU_BLOCK), LRU_BLOCK ** -0.5),
        "ev_ba": nrm((N_EVEN, LRU_WIDTH), 0.02),
        "ev_wx": nrm((N_EVEN, LRU_BLOCKS, LRU_BLOCK, LRU_BLOCK), LRU_BLOCK ** -0.5),
        "ev_bx": nrm((N_EVEN, LRU_WIDTH), 0.02),
        "ev_lam": lam,
        "ev_qn_g": 1.0 + nrm((N_EVEN, HEAD_DIM), 0.02),
        "ev_kn_g": 1.0 + nrm((N_EVEN, HEAD_DIM), 0.02),
        "ev_w_out": nrm((N_EVEN, LRU_WIDTH + SB_WIDTH, d), (LRU_WIDTH + SB_WIDTH) ** -0.5),
        "od_w_in": nrm((N_ODD, d, (SWA_HEADS + 2 * SWA_KV_HEADS) * HEAD_DIM), d ** -0.5),
        "od_qn_g": 1.0 + nrm((N_ODD, HEAD_DIM), 0.02),
        "od_kn_g": 1.0 + nrm((N_ODD, HEAD_DIM), 0.02),
        "od_sinks": nrm((N_ODD, SWA_HEADS), 1.0),
        "od_w_out": nrm((N_ODD, SWA_HEADS * HEAD_DIM, d), (SWA_HEADS * HEAD_DIM) ** -0.5),
        "ffn_w_gate": nrm((DEPTH, d, D_FF), d ** -0.5),
        "ffn_w_up": nrm((DEPTH, d, D_FF), d ** -0.5),
        "ffn_conv_w": nrm((DEPTH, FFN_CONV, D_FF), FFN_CONV ** -0.5),
        "ffn_conv_b": nrm((DEPTH, D_FF), 0.02),
        "ffn_w_down": nrm((DEPTH, D_FF, d), D_FF ** -0.5),
    }


def reference(x, c, ada_w, ada_b, norm_mix_g, norm_ffn_g,
              ev_w_in, ev_conv_w, ev_conv_b, ev_wa, ev_ba, ev_wx, ev_bx, ev_lam,
              ev_qn_g, ev_kn_g, ev_w_out,
              od_w_in, od_qn_g, od_kn_g, od_sinks, od_w_out,
              ffn_w_gate, ffn_w_up, ffn_conv_w, ffn_conv_b, ffn_w_down):
    bsz, seq, _ = x.shape
    for layer in range(DEPTH):
        sh1, sc1, g1, sh2, sc2, g2 = adaln_params(c, ada_w[layer], ada_b[layer])
        h = rms_norm(x, norm_mix_g[layer]) * (1.0 + sc1) + sh1
        if layer % 2 == 0:
            e = layer // 2
            xr, gr, q, k, v = jnp.split(h @ ev_w_in[e], N_EVEN_SPLITS, axis=-1)
            xr = causal_dwconv(xr, ev_conv_w[e], ev_conv_b[e])
            hr = rg_lru(xr, ev_wa[e], ev_ba[e], ev_wx[e], ev_bx[e], ev_lam[e])
            ya = hr * jax.nn.gelu(gr)
            q = rms_norm(q.reshape(bsz, seq, SB_HEADS, HEAD_DIM), ev_qn_g[e])
            k = rms_norm(k.reshape(bsz, seq, SB_HEADS, HEAD_DIM), ev_kn_g[e])
            v = v.reshape(bsz, seq, SB_HEADS, HEAD_DIM)
            yb = stick_breaking(q, k, v)
            mix = jnp.concatenate([ya, yb], axis=-1) @ ev_w_out[e]
        else:
            o = layer // 2
            p = h @ od_w_in[o]
            q_w = SWA_HEADS * HEAD_DIM
            kv_w = SWA_KV_HEADS * HEAD_DIM
            q = rms_norm(p[..., :q_w].reshape(bsz, seq, SWA_HEADS, HEAD_DIM), od_qn_g[o])
            k = rms_norm(p[..., q_w:q_w + kv_w].reshape(bsz, seq, SWA_KV_HEADS, HEAD_DIM), od_kn_g[o])
            v = p[..., q_w + kv_w:].reshape(bsz, seq, SWA_KV_HEADS, HEAD_DIM)
            mix = swa_sinks(q, k, v, od_sinks[o]) @ od_w_out[o]
        x = x + g1 * mix
        h = rms_norm(x, norm_ffn_g[layer]) * (1.0 + sc2) + sh2
        x = x + g2 * conv_ffn(h, ffn_w_gate[layer], ffn_w_up[layer], ffn_conv_w[layer],
                              ffn_conv_b[layer], ffn_w_down[layer])
    return x
```

```python
from contextlib import ExitStack

import numpy as np
import concourse.bass as bass
import concourse.mybir as mybir
from concourse.bass_utils import run_bass_kernel_spmd

F32 = mybir.dt.float32
BF16 = mybir.dt.bfloat16
AF = mybir.ActivationFunctionType
ALU = mybir.AluOpType

NCORES = 8
T = 2048
NT = 4
TS = 512
D = 1024
DFF = 2816
NFF = 22
EPS = 1e-6
FQ = [(0, 6), (6, 6), (12, 5), (17, 5)]


class Prog:
    NRING = 8

    def __init__(self, nc):
        self.nc = nc
        self.engs = {"pe": nc.tensor, "act": nc.scalar, "dve": nc.vector,
                     "pool": nc.gpsimd, "sp": nc.sync}
        self.ops = []
        self.nflushed = 0
        self.last_w = {}
        self.readers = {}
        self.last_on_eng = {}
        self.dmas_since_barrier = []
        self.barrier_deps = set()
        self.dma_hist = {}
        self.sems = {e: nc.alloc_semaphore("c_" + e) for e in self.engs}
        self.rings = {}
        self.cnt = {e: 0 for e in self.engs}
        self.done = []
        self.waited = {e: {} for e in self.engs}
        self.nwaits = 0

    limit = None

    def op(self, eng, fn, rd=(), wr=(), dma=False):
        i = len(self.ops)
        if self.limit is not None and i >= self.limit and not dma and fn is not None:
            return None
        o = dict(eng=eng, fn=fn, dma=dma)
        ops = self.ops
        d = set()
        raw = set()
        for k in rd:
            j = self.last_w.get(k)
            if j is not None:
                d.add(j)
                raw.add(j)
        for k in wr:
            j = self.last_w.get(k)
            if j is not None:
                d.add(j)
            d.update(self.readers.get(k, ()))
        keep = set()
        for j in d:
            oj = ops[j]
            if (not oj["dma"]) and (not dma) and oj["eng"] == eng and eng == "pe":
                continue
            keep.add(j)
        for j in self.barrier_deps:
            oj = ops[j]
            if (not oj["dma"]) and (not dma) and oj["eng"] == eng:
                continue
            keep.add(j)
        for k in rd:
            self.readers.setdefault(k, []).append(i)
        for k in wr:
            self.last_w[k] = i
            self.readers[k] = []
        if dma:
            hist = self.dma_hist.setdefault(eng, [])
            c = len(hist)
            if eng not in self.rings:
                self.rings[eng] = [self.nc.alloc_semaphore(f"d_{eng}{r}") for r in range(self.NRING)]
            o["ring"] = c % self.NRING
            o["rval"] = 16 * (c // self.NRING + 1)
            if c >= self.NRING:
                keep.add(hist[c - self.NRING])
            hist.append(i)
            self.dmas_since_barrier.append(i)
        else:
            self.last_on_eng[eng] = i
        o["deps"] = keep
        ops.append(o)
        self.done.append(None)
        return i

    def dma(self, eng, out, in_, rd=(), wr=()):
        return self.op(eng, lambda e: e.dma_start(out=out, in_=in_), rd, wr, dma=True)

    def barrier(self):
        self.barrier_deps = set(self.last_on_eng.values()) | set(self.dmas_since_barrier)
        self.dmas_since_barrier = []

    def flush(self):
        ops = self.ops
        n = len(ops)
        start = self.nflushed
        needs = set()
        for i in range(start, n):
            needs.update(ops[i]["deps"])
        needs.update(self.last_on_eng.values())
        needs.update(self.last_w.values())
        for r in self.readers.values():
            needs.update(r)
        needs.update(self.barrier_deps)
        for i in range(start, n):
            o = ops[i]
            e = o["eng"]
            eng = self.engs[e]
            need = {}
            for j in o["deps"]:
                s, v = self.done[j]
                k = id(s)
                if k not in need or need[k][1] < v:
                    need[k] = (s, v)
            for k, (s, v) in need.items():
                if self.waited[e].get(k, 0) >= v:
                    continue
                eng.wait_ge(s, v)
                self.nwaits += 1
                self.waited[e][k] = v
            if o["fn"] is None:
                self.done[i] = (self.sems[e], self.cnt[e])
                continue
            ins = o["fn"](eng)
            if o["dma"]:
                s = self.rings[e][o["ring"]]
                ins.then_inc(s, 16)
                self.done[i] = (s, o["rval"])
            elif i in needs:
                self.cnt[e] += 1
                ins.then_inc(self.sems[e], 1)
                self.done[i] = (self.sems[e], self.cnt[e])
            else:
                self.done[i] = (self.sems[e], self.cnt[e] + 1)
            o["fn"] = None
        self.nflushed = n


PP = {}
_off = 0
for _name, _n in [("cT", 16), ("adab", 96), ("gmix", 16), ("gffn", 16), ("lcw", 16), ("lcb", 4),
                  ("lba", 4), ("lbx", 4), ("llam", 4), ("evqg", 1), ("evkg", 1), ("odqg", 1),
                  ("odkg", 1), ("sinks", 16), ("fcw", 132), ("fcb", 44)]:
    PP[_name] = (_off, _n)
    _off += _n
NPP = _off

CO = {}
_off = 0
for _name, _n in [("ident", 128), ("ones", 128), ("bones", 128), ("tri", 128), ("ntri", 128), ("nones", 128), ("md", 2048),
                  ("maskp", 512), ("maskc", 512)]:
    CO[_name] = (_off, _n)
    _off += _n
NCON = _off


def _consts():
    c = np.zeros((128, NCON), np.float32)
    p = np.arange(128)[:, None]
    m = np.arange(128)[None, :]
    c[:, CO["ident"][0]:CO["ident"][0] + 128] = (p == m)
    c[:, CO["ones"][0]:CO["ones"][0] + 128] = 1.0
    c[:, CO["bones"][0]:CO["bones"][0] + 128] = ((p // 64) == (m // 64))
    c[:, CO["tri"][0]:CO["tri"][0] + 128] = (p >= m)
    c[:, CO["ntri"][0]:CO["ntri"][0] + 128] = -1.0 * (p >= m)
    c[:, CO["nones"][0]:CO["nones"][0] + 128] = -1.0
    t = np.arange(512)[None, :]
    for d in range(4):
        c[:, CO["md"][0] + d * 512:CO["md"][0] + (d + 1) * 512] = ((d * 128 + p) < t)
    c[:, CO["maskp"][0]:CO["maskp"][0] + 512] = np.concatenate([np.tile((p > m), (1, 2)), np.tile((p <= m), (1, 2))], 1)
    c[:, CO["maskc"][0]:CO["maskc"][0] + 512] = np.tile((p <= m), (1, 4))
    return c


def _pcol(v):
    v = np.asarray(v, np.float32)
    return np.ascontiguousarray(v.reshape(-1, 128).T)


def _host_inputs(inp, core):
    f = lambda a: np.ascontiguousarray(np.asarray(a, np.float32))
    b0 = 2 * core
    x = f(inp["x"][b0:b0 + 2])
    xT = np.ascontiguousarray(x.transpose(0, 2, 1)).reshape(2, 8, 128, T)
    pp = np.zeros((128, NPP), np.float32)

    def put(name, arr):
        o, n = PP[name]
        pp[:, o:o + n] = np.asarray(arr, np.float32).reshape(128, n)

    c = f(inp["c"][b0:b0 + 2])
    put("cT", c.reshape(2, 8, 128).transpose(2, 1, 0))
    put("adab", np.stack([_pcol(inp["ada_b"][l]) for l in range(2)], 1))
    put("gmix", np.stack([_pcol(inp["norm_mix_g"][l]) for l in range(2)], 1))
    put("gffn", np.stack([_pcol(inp["norm_ffn_g"][l]) for l in range(2)], 1))
    cw = f(inp["ev_conv_w"][0])
    put("lcw", np.stack([_pcol(cw[k]) for k in range(4)], 2))
    put("lcb", _pcol(inp["ev_conv_b"][0]))
    put("lba", _pcol(inp["ev_ba"][0]))
    put("lbx", _pcol(inp["ev_bx"][0]))
    put("llam", _pcol(inp["ev_lam"][0]))
    put("evqg", np.tile(f(inp["ev_qn_g"][0]), 2)[:, None])
    put("evkg", np.tile(f(inp["ev_kn_g"][0]), 2)[:, None])
    put("odqg", np.tile(f(inp["od_qn_g"][0]), 2)[:, None])
    put("odkg", np.tile(f(inp["od_kn_g"][0]), 2)[:, None])
    put("sinks", np.tile(f(inp["od_sinks"][0])[None, :], (128, 1)))
    fcw = f(inp["ffn_conv_w"])
    put("fcw", np.stack([np.stack([_pcol(fcw[l, k]) for k in range(3)], 2) for l in range(2)], 1))
    put("fcb", np.stack([_pcol(inp["ffn_conv_b"][l]) for l in range(2)], 1))

    def bd(w):
        w = f(w)
        o = np.zeros((128, 4, 128), np.float32)
        for j in range(4):
            o[0:64, j, 0:64] = w[2 * j]
            o[64:128, j, 64:128] = w[2 * j + 1]
        return o

    return {
        "xT": xT, "pp": pp, "consts": _consts(),
        "ada_w": f(inp["ada_w"]),
        "ev_w_in": f(inp["ev_w_in"][0]), "ev_w_out": f(inp["ev_w_out"][0]),
        "od_w_in": f(inp["od_w_in"][0]), "od_w_out": f(inp["od_w_out"][0]),
        "w_gate": f(inp["ffn_w_gate"]), "w_up": f(inp["ffn_w_up"]), "w_down": f(inp["ffn_w_down"]),
        "wabd": bd(inp["ev_wa"][0]), "wxbd": bd(inp["ev_wx"][0]),
    }


def build(nseq=2, dbg=None, stop=None):
    dbg = dbg or set()
    nc = bass.Bass("TRN2", target_bir_lowering=False)
    P = Prog(nc)

    def din(name, shape):
        return nc.dram_tensor(name, list(shape), F32, kind="ExternalInput").ap()

    xT = din("xT", [2, 8, 128, T])
    pp_d = din("pp", [128, NPP])
    con_d = din("consts", [128, NCON])
    ada_w = din("ada_w", [2, D, 6 * D])
    ev_w_in = din("ev_w_in", [D, 2560])
    ev_w_out = din("ev_w_out", [D, D])
    od_w_in = din("od_w_in", [D, 1536])
    od_w_out = din("od_w_out", [D, D])
    w_gate = din("w_gate", [2, D, DFF])
    w_up = din("w_up", [2, D, DFF])
    w_down = din("w_down", [2, DFF, D])
    wabd_d = din("wabd", [128, 4, 128])
    wxbd_d = din("wxbd", [128, 4, 128])
    out_d = nc.dram_tensor("out", [2, 8, 128, T], F32, kind="ExternalOutput").ap()
    dbg_out = {}

    def dump(name, ap_sb, shape, rd):
        if name not in dbg:
            return
        d = nc.dram_tensor("dbg_" + name, list(shape), F32, kind="ExternalOutput").ap()
        dbg_out[name] = d
        P.dma("pool", d, ap_sb, rd=rd, wr=[("dbgout", name)])

    top = ExitStack()

    uid = [0]
    sb_lo = (nc.sbuf_base + 63) // 64 * 64
    free_list = [[sb_lo, nc.sbuf_top]]
    peak = [0]

    def sb(st, name, shape, dt):
        uid[0] += 1
        nbytes = int(np.prod(shape[1:])) * (2 if dt == BF16 else 4)
        nbytes = (nbytes + 63) // 64 * 64
        for seg in free_list:
            if seg[1] - seg[0] >= nbytes:
                off = seg[0]
                seg[0] += nbytes
                break
        else:
            raise RuntimeError(f"SBUF full allocating {name} ({nbytes} B); free={free_list}")
        peak[0] = max(peak[0], off + nbytes)

        def release():
            free_list.append([off, off + nbytes])
            free_list.sort()
            merged = []
            for sg in free_list:
                if sg[0] >= sg[1]:
                    continue
                if merged and merged[-1][1] == sg[0]:
                    merged[-1][1] = sg[1]
                else:
                    merged.append(sg)
            free_list[:] = merged
        st.callback(release)
        return nc.alloc_sbuf_tensor_at(f"{name}_u{uid[0]}", list(shape), dt, offset=off)

    ps = [top.enter_context(nc.psum_tensor(f"ps{i}", [128, 512], F32)) for i in range(8)]
    X = sb(top, "X", [128, 8, T], F32)
    pp = sb(top, "pp", [128, NPP], F32)
    con = sb(top, "con", [128, NCON], BF16)
    wabd = sb(top, "wabd", [128, 4, 128], BF16)
    wxbd = sb(top, "wxbd", [128, 4, 128], BF16)
    modp = sb(top, "modp", [128, 2, 2, 6, 8], F32)
    misc = sb(top, "misc", [128, 64], F32)
    wring = [sb(top, f"wring{i}", [128, 8, 128], BF16) for i in range(6)]
    ring_n = [0]

    def cview(name):
        o, n = CO[name]
        return con[:, o:o + n]

    ident, ones_c, bones, tri = cview("ident"), cview("ones"), cview("bones"), cview("tri")
    ntri, nones = cview("ntri"), cview("nones")
    md_all = cview("md")
    maskp, maskc = cview("maskp"), cview("maskc")

    def ppv(name):
        o, n = PP[name]
        return pp[:, o:o + n]

    def wslot():
        i = ring_n[0] % len(wring)
        ring_n[0] += 1
        return wring[i], ("wring", i)

    def wload(dst, src, key):
        P.dma("pool", dst, src, wr=[key])

    def wcols(w2d, c0, n):
        return w2d[:, c0:c0 + n].rearrange("(kc p) n -> p kc n", p=128)

    def mm_group(out, pairs, rd, wr):
        def fn(e):
            ins = None
            n = len(pairs)
            for i, (l, r) in enumerate(pairs):
                ins = e.matmul(out, lhsT=l, rhs=r, start=(i == 0), stop=(i == n - 1))
            return ins
        P.op("pe", fn, rd=rd, wr=wr)

    def tsl(tt):
        return slice(tt * TS, (tt + 1) * TS)

    Xk = lambda c, tt: ("X", c, tt)
    psk = lambda b: ("ps", b)

    P.dma("sp", pp[:], pp_d, wr=["pp"])
    P.dma("pool", con[:], con_d, wr=["con"])
    P.dma("pool", wabd[:], wabd_d, wr=["wabd"])
    P.dma("pool", wxbd[:], wxbd_d, wr=["wxbd"])
    with ExitStack() as st:
        cs = sb(st, "cs", [128, 8, 2], BF16)
        wbig = [sb(st, f"wbig{i}", [128, 8, 1024], BF16) for i in range(2)]
        mod = sb(st, "mod", [128, 96, 2], F32)
        tmpa = sb(st, "tmpa", [128, 16], F32)
        o, n = PP["cT"]
        P.op("act", lambda e: e.activation(out=cs[:].rearrange("p k b -> p (k b)"), in_=pp[:, o:o + n], func=AF.Silu),
             rd=["pp"], wr=["cs"])
        for l in range(2):
            for pc in range(6):
                wb = wbig[(l * 6 + pc) % 2]
                wk = ("wbig", (l * 6 + pc) % 2)
                wload(wb[:], wcols(ada_w[l], pc * 1024, 1024), wk)
                for nn in range(8):
                    g = pc * 8 + nn
                    col = (l * 48 + g) * 2
                    mm_group(ps[7][:, col:col + 2],
                             [(wb[:, k, nn * 128:(nn + 1) * 128], cs[:, k, :]) for k in range(8)],
                             rd=[wk, "cs"], wr=[psk(7)])
        P.op("dve", lambda e: e.tensor_tensor(
            out=mod[:], in0=ps[7][:, 0:192].rearrange("p (g b) -> p g b", b=2),
            in1=ppv("adab").unsqueeze(2).broadcast_to([128, 96, 2]), op=ALU.add),
            rd=[psk(7), "pp"], wr=["mod"])
        modv = mod[:].rearrange("p (l j c) b -> p l j c b", l=2, j=6)
        for l in range(2):
            for b in range(2):
                for (dst, jsc, gname) in ((0, 1, "gmix"), (3, 4, "gffn")):
                    go, _ = PP[gname]
                    P.op("dve", lambda e, l=l, b=b, dst=dst, jsc=jsc, go=go: e.scalar_tensor_tensor(
                        out=modp[:, l, b, dst, :], in0=modv[:, l, jsc, :, b], scalar=1.0,
                        in1=pp[:, go + l * 8:go + l * 8 + 8], op0=ALU.add, op1=ALU.mult),
                        rd=["mod", "pp"], wr=["modp"])
                for (dst, j) in ((1, 0), (2, 2), (4, 3), (5, 5)):
                    P.op("dve", lambda e, l=l, b=b, dst=dst, j=j: e.tensor_copy(
                        out=modp[:, l, b, dst, :], in_=modv[:, l, j, :, b]), rd=["mod"], wr=["modp"])
        P.op("act", lambda e: e.activation(out=tmpa[:, 0:4], in_=ppv("llam"), func=AF.Exp, scale=-1.0),
             rd=["pp"], wr=["tmpa"])
        P.op("act", lambda e: e.activation(out=tmpa[:, 4:8], in_=tmpa[:, 0:4], func=AF.Ln, bias=1.0),
             rd=["tmpa"], wr=["tmpa2"])
        P.op("dve", lambda e: e.tensor_scalar(out=misc[:, 0:4], in0=tmpa[:, 4:8], scalar1=-8.0, scalar2=None,
                                              op0=ALU.mult), rd=["tmpa2"], wr=["misc"])
        P.op("dve", lambda e: e.tensor_scalar(out=misc[:, 4:5], in0=ppv("evqg"), scalar1=0.125, scalar2=None,
                                              op0=ALU.mult), rd=["pp"], wr=["misc"])
        P.op("dve", lambda e: e.tensor_scalar(out=misc[:, 5:6], in0=ppv("odqg"), scalar1=0.125, scalar2=None,
                                              op0=ALU.mult), rd=["pp"], wr=["misc"])
        P.op("act", lambda e: e.activation(out=misc[:, 8:24], in_=ppv("sinks"), func=AF.Exp),
             rd=["pp"], wr=["misc"])
        dump("modp", modp[:].rearrange("p l b j c -> p (l b j c)"), [128, 192], rd=["modp"])
        P.barrier()
        P.flush()
    cl = misc[:, 0:4]
    evq8 = misc[:, 4:5]
    odq8 = misc[:, 5:6]
    esink = misc[:, 8:24]

    def do_norm(st, h, l, b, which):
        ia, ish = (0, 1) if which == 0 else (3, 4)
        sq = [sb(st, f"sq{i}", [128, 8, TS], BF16) for i in range(2)]
        sd = [sb(st, f"sd{i}", [128, TS], F32) for i in range(2)]
        rs = [sb(st, f"rs{i}", [128, TS], F32) for i in range(2)]
        tm = [sb(st, f"tm{i}", [128, TS], F32) for i in range(3)]
        ti = 0
        for tt in range(NT):
            i2 = tt % 2
            P.op("act", lambda e, tt=tt, i2=i2: e.activation(out=sq[i2][:], in_=X[:, :, tsl(tt)], func=AF.Square),
                 rd=[Xk(c, tt) for c in range(8)], wr=[("sq", i2)])
            bank = 6 + i2
            mm_group(ps[bank][:], [(ones_c, sq[i2][:, c, :]) for c in range(8)], rd=[("sq", i2), "con"], wr=[psk(bank)])
            P.op("act", lambda e, i2=i2, bank=bank: e.activation(out=sd[i2][:], in_=ps[bank][:], func=AF.Sqrt,
                                                                  scale=1.0 / D, bias=EPS),
                 rd=[psk(bank)], wr=[("sd", i2)])
            P.op("dve", lambda e, i2=i2: e.reciprocal(out=rs[i2][:], in_=sd[i2][:]), rd=[("sd", i2)], wr=[("rs", i2)])
            for c in range(8):
                t3 = ti % 3
                ti += 1
                P.op("dve", lambda e, c=c, tt=tt, i2=i2, t3=t3: e.tensor_tensor(
                    out=tm[t3][:], in0=X[:, c, tsl(tt)], in1=rs[i2][:], op=ALU.mult),
                    rd=[Xk(c, tt), ("rs", i2)], wr=[("tm", t3)])
                P.op("act", lambda e, c=c, tt=tt, t3=t3: e.activation(
                    out=h[:, c, tsl(tt)], in_=tm[t3][:], func=AF.Identity,
                    scale=modp[:, l, b, ia, c:c + 1], bias=modp[:, l, b, ish, c:c + 1]),
                    rd=[("tm", t3), "modp"], wr=[("h", c, tt)])

    def hk_all(tt):
        return [("h", c, tt) for c in range(8)]

    def out_proj_residual(w2d, ysrc, ykeys, l, b, gidx):
        slots = {}
        for n in range(min(3, 8)):
            slots[n] = wslot()
            wload(slots[n][0][:], wcols(w2d, n * 128, 128), slots[n][1])
        for n in range(8):
            if n + 3 < 8:
                slots[n + 3] = wslot()
                wload(slots[n + 3][0][:], wcols(w2d, (n + 3) * 128, 128), slots[n + 3][1])
            ws, wk = slots[n]
            for tt in range(NT):
                bank = (n * NT + tt) % 6
                mm_group(ps[bank][:], [(ws[:, k, :], ysrc(k)[:, tsl(tt)]) for k in range(8)],
                         rd=[wk] + ykeys(tt), wr=[psk(bank)])
                P.op("dve", lambda e, n=n, tt=tt, bank=bank: e.scalar_tensor_tensor(
                    out=X[:, n, tsl(tt)], in0=ps[bank][:], scalar=modp[:, l, b, gidx, n:n + 1],
                    in1=X[:, n, tsl(tt)], op0=ALU.mult, op1=ALU.add),
                    rd=[psk(bank), Xk(n, tt), "modp"], wr=[Xk(n, tt)])

    def do_ffn(st, h, l, b):
        act = sb(st, "act", [128, 6, T], BF16)
        wd = [sb(st, f"wd{i}", [128, 6, D], BF16) for i in range(2)]
        gb = [sb(st, f"gb{i}", [128, 2 + T], F32) for i in range(2)]
        gc = sb(st, "gc", [128, T], F32)
        sg = sb(st, "sg", [128, T], BF16)
        fo, _ = PP["fcw"]
        bo, _ = PP["fcb"]
        for i in range(2):
            P.op("dve", lambda e, i=i: e.memset(gb[i][:, 0:2], 0.0), wr=[("gbpad", i)])
        wg2, wu2 = w_gate[l], w_up[l]
        slots = {}

        def load_c(c):
            sg_, sk = wslot()
            wload(sg_[:], wcols(wg2, c * 128, 128), sk)
            su_, uk = wslot()
            wload(su_[:], wcols(wu2, c * 128, 128), uk)
            slots[c] = (sg_, sk, su_, uk)

        load_c(0)
        load_c(1)
        for qi, (c0, ncq) in enumerate(FQ):
            wdb = wd[qi % 2]
            wdk = ("wd", qi % 2)
            wload(wdb[:, 0:ncq, :],
                  w_down[l][c0 * 128:(c0 + ncq) * 128, :].rearrange("(kc p) n -> p kc n", p=128), wdk)
            for ci in range(ncq):
                c = c0 + ci
                if c + 2 < NFF:
                    load_c(c + 2)
                sg_, sk, su_, uk = slots.pop(c)
                gi = c % 2
                for tt in range(NT):
                    bank = tt
                    mm_group(ps[bank][:], [(sg_[:, k, :], h[:, k, tsl(tt)]) for k in range(8)],
                             rd=[sk] + hk_all(tt), wr=[psk(bank)])
                    P.op("act", lambda e, gi=gi, tt=tt, bank=bank: e.activation(
                        out=gb[gi][:, 2 + tt * TS:2 + (tt + 1) * TS], in_=ps[bank][:], func=AF.Identity),
                        rd=[psk(bank)], wr=[("gb", gi, tt)])
                wo = fo + (l * NFF + c) * 3
                P.op("dve", lambda e, gi=gi, wo=wo, c=c: e.tensor_scalar(
                    out=gc[:], in0=gb[gi][:, 0:T], scalar1=pp[:, wo:wo + 1],
                    scalar2=pp[:, bo + l * NFF + c:bo + l * NFF + c + 1], op0=ALU.mult, op1=ALU.add),
                    rd=[("gb", gi, t_) for t_ in range(NT)] + [("gbpad", gi), "pp"], wr=["gc"])
                for k in (1, 2):
                    P.op("dve", lambda e, gi=gi, wo=wo, k=k: e.scalar_tensor_tensor(
                        out=gc[:], in0=gb[gi][:, k:k + T], scalar=pp[:, wo + k:wo + k + 1], in1=gc[:],
                        op0=ALU.mult, op1=ALU.add),
                        rd=[("gb", gi, t_) for t_ in range(NT)] + ["gc", "pp"], wr=["gc"])
                P.op("act", lambda e: e.activation(out=sg[:], in_=gc[:], func=AF.Silu), rd=["gc"], wr=["sg"])
                for tt in range(NT):
                    bank = 4 + tt % 2
                    mm_group(ps[bank][:], [(su_[:, k, :], h[:, k, tsl(tt)]) for k in range(8)],
                             rd=[uk] + hk_all(tt), wr=[psk(bank)])
                    P.op("dve", lambda e, ci=ci, tt=tt, bank=bank: e.tensor_tensor(
                        out=act[:, ci, tsl(tt)], in0=sg[:, tsl(tt)], in1=ps[bank][:], op=ALU.mult),
                        rd=["sg", psk(bank)], wr=[("act", ci, tt)])
            for n in range(8):
                for tt in range(NT):
                    bank = 6 + (n * NT + tt) % 2
                    mm_group(ps[bank][:], [(wdb[:, ci, n * 128:(n + 1) * 128], act[:, ci, tsl(tt)]) for ci in range(ncq)],
                             rd=[wdk] + [("act", ci, tt) for ci in range(ncq)], wr=[psk(bank)])
                    P.op("dve", lambda e, n=n, tt=tt, bank=bank: e.scalar_tensor_tensor(
                        out=X[:, n, tsl(tt)], in0=ps[bank][:], scalar=modp[:, l, b, 5, n:n + 1],
                        in1=X[:, n, tsl(tt)], op0=ALU.mult, op1=ALU.add),
                        rd=[psk(bank), Xk(n, tt), "modp"], wr=[Xk(n, tt)])

    def qk_norm_chunk(w2d, col0, dst, dstkey, gain_ap, h, tmp, dup64=False):
        ws, wk = wslot()
        if dup64:
            for hb in range(2):
                P.dma("pool", ws[:, :, hb * 64:(hb + 1) * 64], wcols(w2d, col0, 64), wr=[wk])
        else:
            wload(ws[:], wcols(w2d, col0, 128), wk)
        sqb, sdb, rsb = tmp
        for tt in range(NT):
            i2 = tt % 2
            bA = i2
            bB = 2 + i2
            mm_group(ps[bA][:], [(ws[:, k, :], h[:, k, tsl(tt)]) for k in range(8)], rd=[wk] + hk_all(tt), wr=[psk(bA)])
            P.op("act", lambda e, i2=i2, bA=bA: e.activation(out=sqb[i2][:], in_=ps[bA][:], func=AF.Square),
                 rd=[psk(bA)], wr=[("sqb", i2)])
            mm_group(ps[bB][:], [(bones, sqb[i2][:])], rd=[("sqb", i2), "con"], wr=[psk(bB)])
            P.op("act", lambda e, i2=i2, bB=bB: e.activation(out=sdb[i2][:], in_=ps[bB][:], func=AF.Sqrt,
                                                              scale=1.0 / 64, bias=EPS),
                 rd=[psk(bB)], wr=[("sdb", i2)])
            P.op("dve", lambda e, i2=i2: e.reciprocal(out=rsb[i2][:], in_=sdb[i2][:]), rd=[("sdb", i2)], wr=[("rsb", i2)])
            P.op("dve", lambda e, i2=i2, bA=bA, tt=tt: e.scalar_tensor_tensor(
                out=dst[:, tsl(tt)], in0=ps[bA][:], scalar=gain_ap, in1=rsb[i2][:], op0=ALU.mult, op1=ALU.mult),
                rd=[psk(bA), ("rsb", i2), "misc", "pp"], wr=[(dstkey, tt)])

    for s in range(nseq):
        b = s
        for c in range(8):
            P.dma("sp", X[:, c, :], xT[s, c], wr=[Xk(c, tt) for tt in range(NT)])
        l = 0
        with ExitStack() as st0:
            ya = sb(st0, "ya", [128, 4, T], BF16)
            with ExitStack() as st1:
                h = sb(st1, "h", [128, 8, T], BF16)
                with ExitStack() as st2:
                    do_norm(st2, h, l, b, 0)
                    if s == 0:
                        dump("h0", h[:].rearrange("p c t -> p (c t)"), [128, 8 * T],
                             rd=[("h", c, tt) for c in range(8) for tt in range(NT)])
                    P.barrier()
                    P.flush()
                if stop == "norm0":
                    break
                with ExitStack() as st2:
                    xr = sb(st2, "xr", [128, 3 + T], F32)
                    xc = sb(st2, "xc", [128, T], F32)
                    xcb = sb(st2, "xcb", [128, T], BF16)
                    ra = sb(st2, "ra", [128, T], F32)
                    ig = sb(st2, "ig", [128, T], F32)
                    s2 = sb(st2, "s2", [128, T], F32)
                    gel = sb(st2, "gel", [128, T], F32)
                    gx = sb(st2, "gx", [128, T], F32)
                    P.op("dve", lambda e: e.memset(xr[:, 0:3], 0.0), wr=["xrpad"])
                    lcw, _ = PP["lcw"]
                    lcb, _ = PP["lcb"]
                    lba, _ = PP["lba"]
                    lbx, _ = PP["lbx"]
                    import os as _os
                    for j in [int(v) for v in _os.environ.get("LRU_CHUNKS", "0,1,2,3").split(",")]:
                        wsx, wkx = wslot()
                        wload(wsx[:], wcols(ev_w_in, j * 128, 128), wkx)
                        wsg, wkg = wslot()
                        wload(wsg[:], wcols(ev_w_in, 512 + j * 128, 128), wkg)
                        for tt in range(NT):
                            bank = tt % 2
                            mm_group(ps[bank][:], [(wsx[:, k, :], h[:, k, tsl(tt)]) for k in range(8)],
                                     rd=[wkx] + hk_all(tt), wr=[psk(bank)])
                            P.op("act", lambda e, tt=tt, bank=bank: e.activation(
                                out=xr[:, 3 + tt * TS:3 + (tt + 1) * TS], in_=ps[bank][:], func=AF.Identity),
                                rd=[psk(bank)], wr=[("xr", tt)])
                        xrk = [("xr", t_) for t_ in range(NT)]
                        P.op("dve", lambda e, j=j: e.tensor_scalar(
                            out=xc[:], in0=xr[:, 0:T], scalar1=pp[:, lcw + j * 4:lcw + j * 4 + 1],
                            scalar2=pp[:, lcb + j:lcb + j + 1], op0=ALU.mult, op1=ALU.add),
                            rd=xrk + ["xrpad", "pp"], wr=["xc"])
                        for k in (1, 2, 3):
                            P.op("dve", lambda e, j=j, k=k: e.scalar_tensor_tensor(
                                out=xc[:], in0=xr[:, k:k + T], scalar=pp[:, lcw + j * 4 + k:lcw + j * 4 + k + 1],
                                in1=xc[:], op0=ALU.mult, op1=ALU.add), rd=xrk + ["xc", "pp"], wr=["xc"])
                        P.op("act", lambda e: e.activation(out=xcb[:], in_=xc[:], func=AF.Identity), rd=["xc"], wr=["xcb"])
                        for tt in range(NT):
                            bank = 2 + tt % 2
                            mm_group(ps[bank][:], [(wabd[:, j, :], xcb[:, tsl(tt)])], rd=["wabd", "xcb"], wr=[psk(bank)])
                            P.op("act", lambda e, j=j, tt=tt, bank=bank: e.activation(
                                out=ra[:, tsl(tt)], in_=ps[bank][:], func=AF.Sigmoid, bias=pp[:, lba + j:lba + j + 1]),
                                rd=[psk(bank), "pp"], wr=[("ra", tt)])
                            bank2 = 4 + tt % 2
                            mm_group(ps[bank2][:], [(wxbd[:, j, :], xcb[:, tsl(tt)])], rd=["wxbd", "xcb"], wr=[psk(bank2)])
                            P.op("act", lambda e, j=j, tt=tt, bank2=bank2: e.activation(
                                out=ig[:, tsl(tt)], in_=ps[bank2][:], func=AF.Sigmoid, bias=pp[:, lbx + j:lbx + j + 1]),
                                rd=[psk(bank2), "pp"], wr=[("ig", tt)])
                        rak = [("ra", t_) for t_ in range(NT)]
                        igk = [("ig", t_) for t_ in range(NT)]
                        P.op("act", lambda e, j=j: e.activation(out=ra[:], in_=ra[:], func=AF.Exp, scale=cl[:, j:j + 1]),
                             rd=rak + ["misc"], wr=rak)
                        P.op("act", lambda e: e.activation(out=s2[:], in_=ra[:], func=AF.Square), rd=rak, wr=["s2"])
                        P.op("act", lambda e: e.activation(out=s2[:], in_=s2[:], func=AF.Sqrt, scale=-1.0, bias=1.0),
                             rd=["s2"], wr=["s2"])
                        P.op("dve", lambda e: e.tensor_tensor(out=s2[:], in0=s2[:], in1=ig[:], op=ALU.mult),
                             rd=["s2"] + igk, wr=["s2"])
                        P.op("dve", lambda e: e.tensor_tensor(out=s2[:], in0=s2[:], in1=xc[:], op=ALU.mult),
                             rd=["s2", "xc"], wr=["s2"])
                        P.op("dve", lambda e: e.tensor_tensor_scan(out=xc[:], data0=ra[:], data1=s2[:], initial=0.0,
                                                                    op0=ALU.mult, op1=ALU.add),
                             rd=rak + ["s2", "xc"], wr=["xc"])
                        for tt in range(NT):
                            bank = 6 + tt % 2
                            mm_group(ps[bank][:], [(wsg[:, k, :], h[:, k, tsl(tt)]) for k in range(8)],
                                     rd=[wkg] + hk_all(tt), wr=[psk(bank)])
                            P.op("act", lambda e, tt=tt, bank=bank: e.activation(
                                out=gx[:, tsl(tt)], in_=ps[bank][:], func=AF.Identity),
                                rd=[psk(bank)], wr=[("gx", tt)])
                        gxk = [("gx", t_) for t_ in range(NT)]
                        P.op("act", lambda e: e.activation(out=gel[:], in_=gx[:], func=AF.Square), rd=gxk, wr=["gel"])
                        P.op("dve", lambda e: e.tensor_scalar(out=gel[:], in0=gel[:], scalar1=0.044715, scalar2=1.0,
                                                              op0=ALU.mult, op1=ALU.add), rd=["gel"], wr=["gel"])
                        P.op("dve", lambda e: e.tensor_tensor(out=gel[:], in0=gel[:], in1=gx[:], op=ALU.mult),
                             rd=["gel"] + gxk, wr=["gel"])
                        P.op("act", lambda e: e.activation(out=gel[:], in_=gel[:], func=AF.Sigmoid, scale=1.5957691216057308),
                             rd=["gel"], wr=["gel"])
                        P.op("dve", lambda e: e.tensor_tensor(out=gel[:], in0=gel[:], in1=gx[:], op=ALU.mult),
                             rd=["gel"] + gxk, wr=["gel"])
                        P.op("dve", lambda e, j=j: e.tensor_tensor(out=ya[:, j, :], in0=xc[:], in1=gel[:], op=ALU.mult),
                             rd=["xc", "gel"], wr=[("ya", j)])
                    if s == 0:
                        dump("lxc", xc[:], [128, T], rd=["xc"])
                        dump("lra", ra[:], [128, T], rd=[("ra", t_) for t_ in range(NT)])
                        dump("lig", ig[:], [128, T], rd=[("ig", t_) for t_ in range(NT)])
                        dump("ls2", s2[:], [128, T], rd=["s2"])
                        dump("ya", ya[:].rearrange("p c t -> p (c t)"), [128, 4 * T], rd=[("ya", j) for j in range(4)])
                    P.barrier()
                    P.flush()
                if stop == "lru":
                    break
                qn = sb(st0, "qn", [128, 4, T], BF16)
                kn = sb(st0, "kn", [128, 4, T], BF16)
                vt = sb(st0, "vt", [128, 16, 512], BF16)
                with ExitStack() as st2:
                    sqb = [sb(st2, f"sqb{i}", [128, TS], BF16) for i in range(2)]
                    sdb = [sb(st2, f"sdb{i}", [128, TS], F32) for i in range(2)]
                    rsb = [sb(st2, f"rsb{i}", [128, TS], F32) for i in range(2)]
                    wv = sb(st2, "wv", [128, 8, 512], BF16)
                    for j in range(4):
                        qk_norm_chunk(ev_w_in, 1024 + j * 128, qn[:, j, :], ("qn", j), evq8, h, (sqb, sdb, rsb))
                        qk_norm_chunk(ev_w_in, 1536 + j * 128, kn[:, j, :], ("kn", j), ppv("evkg"), h, (sqb, sdb, rsb))
                    wload(wv[:], wcols(ev_w_in, 2048, 512), "wv")
                    for blk in range(16):
                        bank = 4 + blk % 4
                        mm_group(ps[bank][:], [(h[:, k, blk * 128:(blk + 1) * 128], wv[:, k, :]) for k in range(8)],
                                 rd=["wv"] + hk_all(blk // 4), wr=[psk(bank)])
                        if blk % 2 == 0:
                            P.op("act", lambda e, blk=blk, bank=bank: e.activation(out=vt[:, blk, :], in_=ps[bank][:],
                                                                                   func=AF.Identity),
                                 rd=[psk(bank)], wr=[("vt", blk)])
                        else:
                            P.op("dve", lambda e, blk=blk, bank=bank: e.tensor_copy(out=vt[:, blk, :], in_=ps[bank][:]),
                                 rd=[psk(bank)], wr=[("vt", blk)])
                    if s == 0:
                        dump("qn", qn[:].rearrange("p c t -> p (c t)"), [128, 4 * T],
                             rd=[(("qn", j), t_) for j in range(4) for t_ in range(NT)])
                        dump("kn", kn[:].rearrange("p c t -> p (c t)"), [128, 4 * T],
                             rd=[(("kn", j), t_) for j in range(4) for t_ in range(NT)])
                        dump("vt", vt[:].rearrange("p c t -> p (c t)"), [128, 16 * 512], rd=[("vt", k) for k in range(16)])
                    P.barrier()
                    P.flush()
            if stop in ("norm0", "lru", "sbproj"):
                break
            yb = sb(st0, "yb", [128, 4, T], BF16)
            with ExitStack() as st2:
                eb = [sb(st2, f"eb{i}", [128, TS], F32) for i in range(2)]
                spb = [sb(st2, f"spb{i}", [128, TS], BF16) for i in range(3)]
                Rb = [sb(st2, f"Rb{i}", [128, TS], BF16) for i in range(2)]
                wb_ = [sb(st2, f"wb{i}", [128, TS], BF16) for i in range(3)]
                mo, _ = CO["md"]
                cz = cr = csp = cw_ = 0
                for j in range(4):
                    for tt in range(NT):
                        ob = 6 + (j * NT + tt) % 2
                        for hh in range(2):
                            p0 = hh * 64
                            kmax = 4 * tt + 3
                            prev_sp = None
                            rcur = None
                            for idx, kb in enumerate(range(kmax, -1, -1)):
                                dz = kb - 4 * tt
                                zb = cz % 2
                                cz += 1
                                mm_group(ps[zb][:], [(kn[p0:p0 + 64, j, kb * 128:(kb + 1) * 128], qn[p0:p0 + 64, j, tsl(tt)])],
                                         rd=[(("kn", j), kb // 4), (("qn", j), tt)], wr=[psk(zb)])
                                ei = cz % 2
                                P.op("act", lambda e, ei=ei, zb=zb: e.activation(out=eb[ei][:], in_=ps[zb][:], func=AF.Exp),
                                     rd=[psk(zb)], wr=[("eb", ei)])
                                si = csp % 3
                                csp += 1
                                P.op("act", lambda e, ei=ei, si=si: e.activation(out=spb[si][:], in_=eb[ei][:], func=AF.Ln, bias=1.0),
                                     rd=[("eb", ei)], wr=[("spb", si)])
                                if dz >= 0:
                                    P.op("dve", lambda e, si=si, dz=dz: e.tensor_tensor(
                                        out=spb[si][:], in0=spb[si][:], in1=con[:, mo + dz * 512:mo + (dz + 1) * 512], op=ALU.mult),
                                        rd=[("spb", si), "con"], wr=[("spb", si)])
                                rb = 2 + cr % 2
                                cr += 1
                                pairs = [(ntri, spb[si][:])]
                                rdk = [("spb", si), "con", (("kn", j), kb // 4), (("qn", j), tt)]
                                if idx == 1:
                                    pairs.append((nones, spb[prev_sp][:]))
                                    rdk.append(("spb", prev_sp))
                                elif idx >= 2:
                                    pairs.append((nones, Rb[rcur][:]))
                                    rdk.append(("Rb", rcur))
                                pairs.append((kn[p0:p0 + 64, j, kb * 128:(kb + 1) * 128], qn[p0:p0 + 64, j, tsl(tt)]))
                                mm_group(ps[rb][:], pairs, rd=rdk, wr=[psk(rb)])
                                if kb > 0:
                                    if idx == 1:
                                        rcur = 0
                                        P.op("dve", lambda e, si=si, pv=prev_sp: e.tensor_tensor(
                                            out=Rb[0][:], in0=spb[pv][:], in1=spb[si][:], op=ALU.add),
                                            rd=[("spb", si), ("spb", prev_sp)], wr=[("Rb", 0)])
                                    elif idx >= 2:
                                        rn = 1 - rcur
                                        P.op("dve", lambda e, si=si, rc=rcur, rn=rn: e.tensor_tensor(
                                            out=Rb[rn][:], in0=Rb[rc][:], in1=spb[si][:], op=ALU.add),
                                            rd=[("spb", si), ("Rb", rcur)], wr=[("Rb", rn)])
                                        rcur = rn
                                prev_sp = si
                                wi = cw_ % 3
                                cw_ += 1
                                P.op("act", lambda e, wi=wi, rb=rb: e.activation(out=wb_[wi][:], in_=ps[rb][:], func=AF.Exp),
                                     rd=[psk(rb)], wr=[("wb", wi)])
                                if dz >= 0:
                                    P.op("dve", lambda e, wi=wi, dz=dz: e.tensor_tensor(
                                        out=wb_[wi][:], in0=wb_[wi][:], in1=con[:, mo + dz * 512:mo + (dz + 1) * 512], op=ALU.mult),
                                        rd=[("wb", wi), "con"], wr=[("wb", wi)])

                                def pv(e, ob=ob, p0=p0, kb=kb, j=j, hh=hh, wi=wi, first=(idx == 0), last=(kb == 0)):
                                    return e.matmul(ps[ob][p0:p0 + 64, :], lhsT=vt[:, kb, (2 * j + hh) * 64:(2 * j + hh + 1) * 64],
                                                    rhs=wb_[wi][:], start=first, stop=last)
                                P.op("pe", pv, rd=[("wb", wi), ("vt", kb)], wr=[("pso", ob, hh)])
                        P.op("act", lambda e, j=j, tt=tt, ob=ob: e.activation(out=yb[:, j, tsl(tt)], in_=ps[ob][:], func=AF.Identity),
                             rd=[("pso", ob, 0), ("pso", ob, 1)], wr=[("yb", j, tt)])
                if s == 0:
                    dump("yb", yb[:].rearrange("p c t -> p (c t)"), [128, 4 * T],
                         rd=[("yb", j, t_) for j in range(4) for t_ in range(NT)])
                P.barrier()
                P.flush()
            if stop == "sb":
                break
            out_proj_residual(ev_w_out, lambda k: (ya[:, k, :] if k < 4 else yb[:, k - 4, :]),
                              lambda tt: [("ya", j) for j in range(4)] + [("yb", j, tt) for j in range(4)], l, b, 2)
            P.barrier()
            P.flush()
        if s == 0:
            dump("x0mid", X[:].rearrange("p c t -> p (c t)"), [128, 8 * T], rd=[Xk(c, tt) for c in range(8) for tt in range(NT)])
        if stop == "mix0":
            break
        with ExitStack() as st1:
            h = sb(st1, "h", [128, 8, T], BF16)
            with ExitStack() as st2:
                do_norm(st2, h, l, b, 1)
                P.barrier()
                P.flush()
            with ExitStack() as st2:
                do_ffn(st2, h, l, b)
                P.barrier()
                P.flush()
        if s == 0:
            dump("x1", X[:].rearrange("p c t -> p (c t)"), [128, 8 * T], rd=[Xk(c, tt) for c in range(8) for tt in range(NT)])
        if stop == "l0":
            break
        l = 1
        with ExitStack() as st0:
            qn = sb(st0, "qn1", [128, 8, T], BF16)
            kd = sb(st0, "kd", [128, 4, T], BF16)
            va = sb(st0, "va", [128, 16, 4, 65], BF16)
            with ExitStack() as st1:
                h = sb(st1, "h", [128, 8, T], BF16)
                with ExitStack() as st2:
                    do_norm(st2, h, l, b, 0)
                    P.barrier()
                    P.flush()
                with ExitStack() as st2:
                    sqb = [sb(st2, f"sqb{i}", [128, TS], BF16) for i in range(2)]
                    sdb = [sb(st2, f"sdb{i}", [128, TS], F32) for i in range(2)]
                    rsb = [sb(st2, f"rsb{i}", [128, TS], F32) for i in range(2)]
                    wv = sb(st2, "wv", [128, 8, 512], BF16)
                    for c in range(8):
                        qk_norm_chunk(od_w_in, c * 128, qn[:, c, :], ("qn", c), odq8, h, (sqb, sdb, rsb))
                    for g in range(4):
                        qk_norm_chunk(od_w_in, 1024 + g * 64, kd[:, g, :], ("kd", g), ppv("odkg"), h, (sqb, sdb, rsb), dup64=True)
                    P.op("dve", lambda e: e.memset(va[:, :, :, 64:65], 1.0), wr=["vaones"])
                    wload(wv[:, :, 0:256], wcols(od_w_in, 1280, 256), "wv")
                    for blk in range(16):
                        bank = 4 + blk % 4
                        mm_group(ps[bank][:, 0:256], [(h[:, k, blk * 128:(blk + 1) * 128], wv[:, k, 0:256]) for k in range(8)],
                                 rd=["wv"] + hk_all(blk // 4), wr=[psk(bank)])
                        P.op("act" if blk % 2 == 0 else "dve",
                             (lambda e, blk=blk, bank=bank: e.activation(
                                 out=va[:, blk, :, 0:64], in_=ps[bank][:, 0:256].rearrange("p (g d) -> p g d", g=4), func=AF.Identity))
                             if blk % 2 == 0 else
                             (lambda e, blk=blk, bank=bank: e.tensor_copy(
                                 out=va[:, blk, :, 0:64], in_=ps[bank][:, 0:256].rearrange("p (g d) -> p g d", g=4))),
                             rd=[psk(bank)], wr=[("va", blk)])
                    P.barrier()
                    P.flush()
            if stop == "l1proj":
                break
            yT = sb(st0, "yT", [128, 8, T], BF16)
            with ExitStack() as st2:
                pb = [sb(st2, f"pb{i}", [128, 2, TS], BF16) for i in range(2)]
                den = [sb(st2, f"den{i}", [128, 4], F32) for i in range(2)]
                ytok = [sb(st2, f"ytok{i}", [128, D], BF16) for i in range(2)]
                un = 0
                for qb in range(16):
                    yi = qb % 2
                    for g in range(4):
                        pi = un % 2
                        un += 1
                        kbs = [qb - 1, qb] if qb > 0 else [qb]
                        c0 = 0 if qb > 0 else 256
                        for hb in range(2):
                            sbank = hb + 2 * pi

                            def sc(e, sbank=sbank, g=g, kbs=kbs, qb=qb, hb=hb):
                                ins = None
                                for kb in kbs:
                                    which = 0 if kb == qb - 1 else 1
                                    ins = e.matmul(ps[sbank][:, which * 256:(which + 1) * 256],
                                                   lhsT=kd[hb * 64:(hb + 1) * 64, g, kb * 128:(kb + 1) * 128],
                                                   rhs=qn[hb * 64:(hb + 1) * 64, 2 * g:2 * g + 2, qb * 128:(qb + 1) * 128],
                                                   start=True, stop=True)
                                return ins
                            P.op("pe", sc, rd=[(("kd", g), kb // 4) for kb in kbs] + [(("qn", 2 * g), qb // 4), (("qn", 2 * g + 1), qb // 4)],
                                 wr=[psk(sbank)])
                            P.op("act", lambda e, pi=pi, hb=hb, sbank=sbank, c0=c0: e.activation(
                                out=pb[pi][:, hb, c0:512], in_=ps[sbank][:, c0:512], func=AF.Exp),
                                rd=[psk(sbank)], wr=[("pb", pi, hb)])
                            P.op("dve", lambda e, pi=pi, hb=hb, c0=c0: e.tensor_tensor(
                                out=pb[pi][:, hb, c0:512], in0=pb[pi][:, hb, c0:512], in1=maskp[:, c0:512], op=ALU.mult),
                                rd=[("pb", pi, hb), "con"], wr=[("pb", pi, hb)])
                        ybank = 4 + pi

                        def pvm(e, ybank=ybank, pi=pi, kbs=kbs, qb=qb, g=g):
                            ins = None
                            for hc in range(4):
                                hb, e_ = hc // 2, hc % 2
                                for i_, kb in enumerate(kbs):
                                    which = 0 if kb == qb - 1 else 1
                                    ins = e.matmul(ps[ybank][:, hc * 65:(hc + 1) * 65],
                                                   lhsT=pb[pi][:, hb, which * 256 + e_ * 128:which * 256 + (e_ + 1) * 128],
                                                   rhs=va[:, kb, g, :], start=(i_ == 0), stop=(i_ == len(kbs) - 1))
                            return ins
                        P.op("pe", pvm, rd=[("pb", pi, 0), ("pb", pi, 1)] + [("va", kb) for kb in kbs] + ["vaones"], wr=[psk(ybank)])
                        yv = ps[ybank][:, 0:260].rearrange("p (hb e d) -> p hb e d", hb=2, e=2)
                        P.op("dve", lambda e, pi=pi, yv=yv, g=g: e.tensor_tensor(
                            out=den[pi][:].rearrange("p (hb e) -> p hb e", hb=2),
                            in0=yv[:, :, :, 64],
                            in1=esink[:, 4 * g:4 * g + 4].rearrange("p (e hb) -> p hb e", hb=2), op=ALU.add),
                            rd=[psk(ybank), "misc"], wr=[("den", pi)])
                        P.op("dve", lambda e, pi=pi: e.reciprocal(out=den[pi][:], in_=den[pi][:]), rd=[("den", pi)], wr=[("den", pi)])
                        P.op("dve", lambda e, pi=pi, yv=yv, g=g, yi=yi: e.tensor_tensor(
                            out=ytok[yi][:, g * 256:(g + 1) * 256].rearrange("p (e hb d) -> p hb e d", e=2, hb=2),
                            in0=yv[:, :, :, 0:64],
                            in1=den[pi][:].rearrange("p (hb e) -> p hb e", hb=2).unsqueeze(3).broadcast_to([128, 2, 2, 64]),
                            op=ALU.mult),
                            rd=[psk(ybank), ("den", pi)], wr=[("ytok", yi, g)])
                    tbank = 6 + yi
                    tp = ps[tbank][:].bitcast(BF16)

                    def trn(e, tp=tp, yi=yi):
                        ins = None
                        for c in range(8):
                            ins = e.transpose(out=tp[:, c * 128:(c + 1) * 128], in_=ytok[yi][:, c * 128:(c + 1) * 128], identity=ident)
                        return ins
                    P.op("pe", trn, rd=[("ytok", yi, g) for g in range(4)] + ["con"], wr=[psk(tbank)])
                    P.op("act", lambda e, tp=tp, qb=qb: e.activation(
                        out=yT[:, :, qb * 128:(qb + 1) * 128], in_=tp.rearrange("p (c t) -> p c t", c=8), func=AF.Identity),
                        rd=[psk(tbank)], wr=[("yT", qb // 4)])
                if s == 0:
                    dump("yT", yT[:].rearrange("p c t -> p (c t)"), [128, 8 * T], rd=[("yT", t_) for t_ in range(NT)])
                P.barrier()
                P.flush()
            if stop == "swa":
                break
            out_proj_residual(od_w_out, lambda k: yT[:, k, :], lambda tt: [("yT", tt)], l, b, 2)
            P.barrier()
            P.flush()
        if s == 0:
            dump("x1mid", X[:].rearrange("p c t -> p (c t)"), [128, 8 * T], rd=[Xk(c, tt) for c in range(8) for tt in range(NT)])
        with ExitStack() as st1:
            h = sb(st1, "h", [128, 8, T], BF16)
            with ExitStack() as st2:
                do_norm(st2, h, l, b, 1)
                P.barrier()
                P.flush()
            with ExitStack() as st2:
                do_ffn(st2, h, l, b)
                P.barrier()
                P.flush()
        for c in range(8):
            P.dma("sp", out_d[s, c], X[:, c, :], rd=[Xk(c, tt) for tt in range(NT)], wr=[("out", s, c)])
        P.flush()

    P.barrier()
    P.op("sp", None)
    P.flush()
    top.close()
    return nc, P, dbg_out


_CACHE = {}


def kernel(**inputs):
    if "nc" not in _CACHE:
        _CACHE["nc"] = build()[0]
    nc = _CACHE["nc"]
    in_maps = [_host_inputs(inputs, core) for core in range(NCORES)]
    res = run_bass_kernel_spmd(nc, in_maps, core_ids=list(range(NCORES)))
    outs = []
    for core in range(NCORES):
        o = np.asarray(res.results[core]["out"], np.float32).reshape(2, D, T)
        outs.append(o.transpose(0, 2, 1))
    return np.ascontiguousarray(np.concatenate(outs, axis=0)).astype(np.float32)
```

```python
from contextlib import ExitStack

import numpy as np
import concourse.bass as bass
import concourse.mybir as mybir
from concourse.bass_utils import run_bass_kernel_spmd

F32 = mybir.dt.float32
BF16 = mybir.dt.bfloat16
AF = mybir.ActivationFunctionType
ALU = mybir.AluOpType

NCORES = 8
T = 2048
NT = 4
TS = 512
D = 1024
DFF = 2816
NFF = 22
EPS = 1e-6
FQ = [(0, 6), (6, 6), (12, 5), (17, 5)]


class Prog:
    NRING = 8

    def __init__(self, nc):
        self.nc = nc
        self.engs = {"pe": nc.tensor, "act": nc.scalar, "dve": nc.vector,
                     "pool": nc.gpsimd, "sp": nc.sync}
        self.ops = []
        self.nflushed = 0
        self.last_w = {}
        self.readers = {}
        self.last_on_eng = {}
        self.dmas_since_barrier = []
        self.barrier_deps = set()
        self.dma_hist = {}
        self.sems = {e: nc.alloc_semaphore("c_" + e) for e in self.engs}
        self.rings = {}
        self.cnt = {e: 0 for e in self.engs}
        self.done = []
        self.waited = {e: {} for e in self.engs}
        self.nwaits = 0

    limit = None

    def op(self, eng, fn, rd=(), wr=(), dma=False):
        i = len(self.ops)
        if self.limit is not None and i >= self.limit and not dma and fn is not None:
            return None
        o = dict(eng=eng, fn=fn, dma=dma)
        ops = self.ops
        d = set()
        raw = set()
        for k in rd:
            j = self.last_w.get(k)
            if j is not None:
                d.add(j)
                raw.add(j)
        for k in wr:
            j = self.last_w.get(k)
            if j is not None:
                d.add(j)
            d.update(self.readers.get(k, ()))
        keep = set()
        for j in d:
            oj = ops[j]
            if (not oj["dma"]) and (not dma) and oj["eng"] == eng and eng == "pe":
                continue
            keep.add(j)
        for j in self.barrier_deps:
            oj = ops[j]
            if (not oj["dma"]) and (not dma) and oj["eng"] == eng:
                continue
            keep.add(j)
        for k in rd:
            self.readers.setdefault(k, []).append(i)
        for k in wr:
            self.last_w[k] = i
            self.readers[k] = []
        if dma:
            hist = self.dma_hist.setdefault(eng, [])
            c = len(hist)
            if eng not in self.rings:
                self.rings[eng] = [self.nc.alloc_semaphore(f"d_{eng}{r}") for r in range(self.NRING)]
            o["ring"] = c % self.NRING
            o["rval"] = 16 * (c // self.NRING + 1)
            if c >= self.NRING:
                keep.add(hist[c - self.NRING])
            hist.append(i)
            self.dmas_since_barrier.append(i)
        else:
            self.last_on_eng[eng] = i
        o["deps"] = keep
        ops.append(o)
        self.done.append(None)
        return i

    def dma(self, eng, out, in_, rd=(), wr=()):
        return self.op(eng, lambda e: e.dma_start(out=out, in_=in_), rd, wr, dma=True)

    def barrier(self):
        self.barrier_deps = set(self.last_on_eng.values()) | set(self.dmas_since_barrier)
        self.dmas_since_barrier = []

    def flush(self):
        ops = self.ops
        n = len(ops)
        start = self.nflushed
        needs = set()
        for i in range(start, n):
            needs.update(ops[i]["deps"])
        needs.update(self.last_on_eng.values())
        needs.update(self.last_w.values())
        for r in self.readers.values():
            needs.update(r)
        needs.update(self.barrier_deps)
        for i in range(start, n):
            o = ops[i]
            e = o["eng"]
            eng = self.engs[e]
            need = {}
            for j in o["deps"]:
                s, v = self.done[j]
                k = id(s)
                if k not in need or need[k][1] < v:
                    need[k] = (s, v)
            for k, (s, v) in need.items():
                if self.waited[e].get(k, 0) >= v:
                    continue
                eng.wait_ge(s, v)
                self.nwaits += 1
                self.waited[e][k] = v
            if o["fn"] is None:
                self.done[i] = (self.sems[e], self.cnt[e])
                continue
            ins = o["fn"](eng)
            if o["dma"]:
                s = self.rings[e][o["ring"]]
                ins.then_inc(s, 16)
                self.done[i] = (s, o["rval"])
            elif i in needs:
                self.cnt[e] += 1
                ins.then_inc(self.sems[e], 1)
                self.done[i] = (self.sems[e], self.cnt[e])
            else:
                self.done[i] = (self.sems[e], self.cnt[e] + 1)
            o["fn"] = None
        self.nflushed = n


PP = {}
_off = 0
for _name, _n in [("cT", 16), ("adab", 96), ("gmix", 16), ("gffn", 16), ("lcw", 16), ("lcb", 4),
                  ("lba", 4), ("lbx", 4), ("llam", 4), ("evqg", 1), ("evkg", 1), ("odqg", 1),
                  ("odkg", 1), ("sinks", 16), ("fcw", 132), ("fcb", 44)]:
    PP[_name] = (_off, _n)
    _off += _n
NPP = _off

CO = {}
_off = 0
for _name, _n in [("ident", 128), ("ones", 128), ("bones", 128), ("tri", 128), ("ntri", 128), ("nones", 128), ("md", 2048),
                  ("maskp", 512), ("maskc", 512)]:
    CO[_name] = (_off, _n)
    _off += _n
NCON = _off


def _consts():
    c = np.zeros((128, NCON), np.float32)
    p = np.arange(128)[:, None]
    m = np.arange(128)[None, :]
    c[:, CO["ident"][0]:CO["ident"][0] + 128] = (p == m)
    c[:, CO["ones"][0]:CO["ones"][0] + 128] = 1.0
    c[:, CO["bones"][0]:CO["bones"][0] + 128] = ((p // 64) == (m // 64))
    c[:, CO["tri"][0]:CO["tri"][0] + 128] = (p >= m)
    c[:, CO["ntri"][0]:CO["ntri"][0] + 128] = -1.0 * (p >= m)
    c[:, CO["nones"][0]:CO["nones"][0] + 128] = -1.0
    t = np.arange(512)[None, :]
    for d in range(4):
        c[:, CO["md"][0] + d * 512:CO["md"][0] + (d + 1) * 512] = ((d * 128 + p) < t)
    c[:, CO["maskp"][0]:CO["maskp"][0] + 512] = np.concatenate([np.tile((p > m), (1, 2)), np.tile((p <= m), (1, 2))], 1)
    c[:, CO["maskc"][0]:CO["maskc"][0] + 512] = np.tile((p <= m), (1, 4))
    return c


def _pcol(v):
    v = np.asarray(v, np.float32)
    return np.ascontiguousarray(v.reshape(-1, 128).T)


def _host_inputs(inp, core):
    f = lambda a: np.ascontiguousarray(np.asarray(a, np.float32))
    b0 = 2 * core
    x = f(inp["x"][b0:b0 + 2])
    xT = np.ascontiguousarray(x.transpose(0, 2, 1)).reshape(2, 8, 128, T)
    pp = np.zeros((128, NPP), np.float32)

    def put(name, arr):
        o, n = PP[name]
        pp[:, o:o + n] = np.asarray(arr, np.float32).reshape(128, n)

    c = f(inp["c"][b0:b0 + 2])
    put("cT", c.reshape(2, 8, 128).transpose(2, 1, 0))
    put("adab", np.stack([_pcol(inp["ada_b"][l]) for l in range(2)], 1))
    put("gmix", np.stack([_pcol(inp["norm_mix_g"][l]) for l in range(2)], 1))
    put("gffn", np.stack([_pcol(inp["norm_ffn_g"][l]) for l in range(2)], 1))
    cw = f(inp["ev_conv_w"][0])
    put("lcw", np.stack([_pcol(cw[k]) for k in range(4)], 2))
    put("lcb", _pcol(inp["ev_conv_b"][0]))
    put("lba", _pcol(inp["ev_ba"][0]))
    put("lbx", _pcol(inp["ev_bx"][0]))
    put("llam", _pcol(inp["ev_lam"][0]))
    put("evqg", np.tile(f(inp["ev_qn_g"][0]), 2)[:, None])
    put("evkg", np.tile(f(inp["ev_kn_g"][0]), 2)[:, None])
    put("odqg", np.tile(f(inp["od_qn_g"][0]), 2)[:, None])
    put("odkg", np.tile(f(inp["od_kn_g"][0]), 2)[:, None])
    put("sinks", np.tile(f(inp["od_sinks"][0])[None, :], (128, 1)))
    fcw = f(inp["ffn_conv_w"])
    put("fcw", np.stack([np.stack([_pcol(fcw[l, k]) for k in range(3)], 2) for l in range(2)], 1))
    put("fcb", np.stack([_pcol(inp["ffn_conv_b"][l]) for l in range(2)], 1))

    def bd(w):
        w = f(w)
        o = np.zeros((128, 4, 128), np.float32)
        for j in range(4):
            o[0:64, j, 0:64] = w[2 * j]
            o[64:128, j, 64:128] = w[2 * j + 1]
        return o

    return {
        "xT": xT, "pp": pp, "consts": _consts(),
        "ada_w": f(inp["ada_w"]),
        "ev_w_in": f(inp["ev_w_in"][0]), "ev_w_out": f(inp["ev_w_out"][0]),
        "od_w_in": f(inp["od_w_in"][0]), "od_w_out": f(inp["od_w_out"][0]),
        "w_gate": f(inp["ffn_w_gate"]), "w_up": f(inp["ffn_w_up"]), "w_down": f(inp["ffn_w_down"]),
        "wabd": bd(inp["ev_wa"][0]), "wxbd": bd(inp["ev_wx"][0]),
    }


def build(nseq=2, dbg=None, stop=None):
    dbg = dbg or set()
    nc = bass.Bass("TRN2", target_bir_lowering=False)
    P = Prog(nc)

    def din(name, shape):
        return nc.dram_tensor(name, list(shape), F32, kind="ExternalInput").ap()

    xT = din("xT", [2, 8, 128, T])
    pp_d = din("pp", [128, NPP])
    con_d = din("consts", [128, NCON])
    ada_w = din("ada_w", [2, D, 6 * D])
    ev_w_in = din("ev_w_in", [D, 2560])
    ev_w_out = din("ev_w_out", [D, D])
    od_w_in = din("od_w_in", [D, 1536])
    od_w_out = din("od_w_out", [D, D])
    w_gate = din("w_gate", [2, D, DFF])
    w_up = din("w_up", [2, D, DFF])
    w_down = din("w_down", [2, DFF, D])
    wabd_d = din("wabd", [128, 4, 128])
    wxbd_d = din("wxbd", [128, 4, 128])
    out_d = nc.dram_tensor("out", [2, 8, 128, T], F32, kind="ExternalOutput").ap()
    dbg_out = {}

    def dump(name, ap_sb, shape, rd):
        if name not in dbg:
            return
        d = nc.dram_tensor("dbg_" + name, list(shape), F32, kind="ExternalOutput").ap()
        dbg_out[name] = d
        P.dma("pool", d, ap_sb, rd=rd, wr=[("dbgout", name)])

    top = ExitStack()

    uid = [0]
    sb_lo = (nc.sbuf_base + 63) // 64 * 64
    free_list = [[sb_lo, nc.sbuf_top]]
    peak = [0]

    def sb(st, name, shape, dt):
        uid[0] += 1
        nbytes = int(np.prod(shape[1:])) * (2 if dt == BF16 else 4)
        nbytes = (nbytes + 63) // 64 * 64
        for seg in free_list:
            if seg[1] - seg[0] >= nbytes:
                off = seg[0]
                seg[0] += nbytes
                break
        else:
            raise RuntimeError(f"SBUF full allocating {name} ({nbytes} B); free={free_list}")
        peak[0] = max(peak[0], off + nbytes)

        def release():
            free_list.append([off, off + nbytes])
            free_list.sort()
            merged = []
            for sg in free_list:
                if sg[0] >= sg[1]:
                    continue
                if merged and merged[-1][1] == sg[0]:
                    merged[-1][1] = sg[1]
                else:
                    merged.append(sg)
            free_list[:] = merged
        st.callback(release)
        return nc.alloc_sbuf_tensor_at(f"{name}_u{uid[0]}", list(shape), dt, offset=off)

    ps = [top.enter_context(nc.psum_tensor(f"ps{i}", [128, 512], F32)) for i in range(8)]
    X = sb(top, "X", [128, 8, T], F32)
    pp = sb(top, "pp", [128, NPP], F32)
    con = sb(top, "con", [128, NCON], BF16)
    modp = sb(top, "modp", [128, 2, 2, 6, 8], F32)
    misc = sb(top, "misc", [128, 64], F32)
    wring = [sb(top, f"wring{i}", [128, 8, 128], BF16) for i in range(8)]
    ring_n = [0]

    def cview(name):
        o, n = CO[name]
        return con[:, o:o + n]

    ident, ones_c, bones, tri = cview("ident"), cview("ones"), cview("bones"), cview("tri")
    ntri, nones = cview("ntri"), cview("nones")
    md_all = cview("md")
    maskp, maskc = cview("maskp"), cview("maskc")

    def ppv(name):
        o, n = PP[name]
        return pp[:, o:o + n]

    def wslot():
        i = ring_n[0] % len(wring)
        ring_n[0] += 1
        return wring[i], ("wring", i)

    def wload(dst, src, key):
        P.dma("pool", dst, src, wr=[key])

    def wcols(w2d, c0, n):
        return w2d[:, c0:c0 + n].rearrange("(kc p) n -> p kc n", p=128)

    def mm_group(out, pairs, rd, wr):
        def fn(e):
            ins = None
            n = len(pairs)
            for i, (l, r) in enumerate(pairs):
                ins = e.matmul(out, lhsT=l, rhs=r, start=(i == 0), stop=(i == n - 1))
            return ins
        P.op("pe", fn, rd=rd, wr=wr)

    def tsl(tt):
        return slice(tt * TS, (tt + 1) * TS)

    Xk = lambda c, tt: ("X", c, tt)
    psk = lambda b: ("ps", b)

    P.dma("sp", pp[:], pp_d, wr=["pp"])
    P.dma("pool", con[:], con_d, wr=["con"])
    with ExitStack() as st:
        cs = sb(st, "cs", [128, 8, 2], BF16)
        wbig = [sb(st, f"wbig{i}", [128, 8, 1024], BF16) for i in range(2)]
        mod = sb(st, "mod", [128, 96, 2], F32)
        tmpa = sb(st, "tmpa", [128, 16], F32)
        o, n = PP["cT"]
        P.op("act", lambda e: e.activation(out=cs[:].rearrange("p k b -> p (k b)"), in_=pp[:, o:o + n], func=AF.Silu),
             rd=["pp"], wr=["cs"])
        for l in range(2):
            for pc in range(6):
                wb = wbig[(l * 6 + pc) % 2]
                wk = ("wbig", (l * 6 + pc) % 2)
                wload(wb[:], wcols(ada_w[l], pc * 1024, 1024), wk)
                for nn in range(8):
                    g = pc * 8 + nn
                    col = (l * 48 + g) * 2
                    mm_group(ps[7][:, col:col + 2],
                             [(wb[:, k, nn * 128:(nn + 1) * 128], cs[:, k, :]) for k in range(8)],
                             rd=[wk, "cs"], wr=[psk(7)])
        P.op("dve", lambda e: e.tensor_tensor(
            out=mod[:], in0=ps[7][:, 0:192].rearrange("p (g b) -> p g b", b=2),
            in1=ppv("adab").unsqueeze(2).broadcast_to([128, 96, 2]), op=ALU.add),
            rd=[psk(7), "pp"], wr=["mod"])
        modv = mod[:].rearrange("p (l j c) b -> p l j c b", l=2, j=6)
        for l in range(2):
            for b in range(2):
                for (dst, jsc, gname) in ((0, 1, "gmix"), (3, 4, "gffn")):
                    go, _ = PP[gname]
                    P.op("dve", lambda e, l=l, b=b, dst=dst, jsc=jsc, go=go: e.scalar_tensor_tensor(
                        out=modp[:, l, b, dst, :], in0=modv[:, l, jsc, :, b], scalar=1.0,
                        in1=pp[:, go + l * 8:go + l * 8 + 8], op0=ALU.add, op1=ALU.mult),
                        rd=["mod", "pp"], wr=["modp"])
                for (dst, j) in ((1, 0), (2, 2), (4, 3), (5, 5)):
                    P.op("dve", lambda e, l=l, b=b, dst=dst, j=j: e.tensor_copy(
                        out=modp[:, l, b, dst, :], in_=modv[:, l, j, :, b]), rd=["mod"], wr=["modp"])
        P.op("act", lambda e: e.activation(out=tmpa[:, 0:4], in_=ppv("llam"), func=AF.Exp, scale=-1.0),
             rd=["pp"], wr=["tmpa"])
        P.op("act", lambda e: e.activation(out=tmpa[:, 4:8], in_=tmpa[:, 0:4], func=AF.Ln, bias=1.0),
             rd=["tmpa"], wr=["tmpa2"])
        P.op("dve", lambda e: e.tensor_scalar(out=misc[:, 0:4], in0=tmpa[:, 4:8], scalar1=-8.0, scalar2=None,
                                              op0=ALU.mult), rd=["tmpa2"], wr=["misc"])
        P.op("dve", lambda e: e.tensor_scalar(out=misc[:, 4:5], in0=ppv("evqg"), scalar1=0.125, scalar2=None,
                                              op0=ALU.mult), rd=["pp"], wr=["misc"])
        P.op("dve", lambda e: e.tensor_scalar(out=misc[:, 5:6], in0=ppv("odqg"), scalar1=0.125, scalar2=None,
                                              op0=ALU.mult), rd=["pp"], wr=["misc"])
        P.op("act", lambda e: e.activation(out=misc[:, 8:24], in_=ppv("sinks"), func=AF.Exp),
             rd=["pp"], wr=["misc"])
        dump("modp", modp[:].rearrange("p l b j c -> p (l b j c)"), [128, 192], rd=["modp"])
        P.barrier()
        P.flush()
    cl = misc[:, 0:4]
    evq8 = misc[:, 4:5]
    odq8 = misc[:, 5:6]
    esink = misc[:, 8:24]

    def do_norm(st, h, l, b, which):
        ia, ish = (0, 1) if which == 0 else (3, 4)
        sq = [sb(st, f"sq{i}", [128, 8, TS], BF16) for i in range(2)]
        sd = [sb(st, f"sd{i}", [128, TS], F32) for i in range(2)]
        rs = [sb(st, f"rs{i}", [128, TS], F32) for i in range(2)]
        tm = [sb(st, f"tm{i}", [128, TS], F32) for i in range(3)]
        ti = 0
        for tt in range(NT):
            i2 = tt % 2
            P.op("act", lambda e, tt=tt, i2=i2: e.activation(out=sq[i2][:], in_=X[:, :, tsl(tt)], func=AF.Square),
                 rd=[Xk(c, tt) for c in range(8)], wr=[("sq", i2)])
            bank = 6 + i2
            mm_group(ps[bank][:], [(ones_c, sq[i2][:, c, :]) for c in range(8)], rd=[("sq", i2), "con"], wr=[psk(bank)])
            P.op("act", lambda e, i2=i2, bank=bank: e.activation(out=sd[i2][:], in_=ps[bank][:], func=AF.Sqrt,
                                                                  scale=1.0 / D, bias=EPS),
                 rd=[psk(bank)], wr=[("sd", i2)])
            P.op("dve", lambda e, i2=i2: e.reciprocal(out=rs[i2][:], in_=sd[i2][:]), rd=[("sd", i2)], wr=[("rs", i2)])
            for c in range(8):
                t3 = ti % 3
                ti += 1
                P.op("dve", lambda e, c=c, tt=tt, i2=i2, t3=t3: e.tensor_tensor(
                    out=tm[t3][:], in0=X[:, c, tsl(tt)], in1=rs[i2][:], op=ALU.mult),
                    rd=[Xk(c, tt), ("rs", i2)], wr=[("tm", t3)])
                P.op("act", lambda e, c=c, tt=tt, t3=t3: e.activation(
                    out=h[:, c, tsl(tt)], in_=tm[t3][:], func=AF.Identity,
                    scale=modp[:, l, b, ia, c:c + 1], bias=modp[:, l, b, ish, c:c + 1]),
                    rd=[("tm", t3), "modp"], wr=[("h", c, tt)])

    def hk_all(tt):
        return [("h", c, tt) for c in range(8)]

    def out_proj_residual(w2d, ysrc, ykeys, l, b, gidx):
        slots = {}
        for n in range(min(3, 8)):
            slots[n] = wslot()
            wload(slots[n][0][:], wcols(w2d, n * 128, 128), slots[n][1])
        for n in range(8):
            if n + 3 < 8:
                slots[n + 3] = wslot()
                wload(slots[n + 3][0][:], wcols(w2d, (n + 3) * 128, 128), slots[n + 3][1])
            ws, wk = slots[n]
            for tt in range(NT):
                bank = (n * NT + tt) % 6
                mm_group(ps[bank][:], [(ws[:, k, :], ysrc(k)[:, tsl(tt)]) for k in range(8)],
                         rd=[wk] + ykeys(tt), wr=[psk(bank)])
                P.op("dve", lambda e, n=n, tt=tt, bank=bank: e.scalar_tensor_tensor(
                    out=X[:, n, tsl(tt)], in0=ps[bank][:], scalar=modp[:, l, b, gidx, n:n + 1],
                    in1=X[:, n, tsl(tt)], op0=ALU.mult, op1=ALU.add),
                    rd=[psk(bank), Xk(n, tt), "modp"], wr=[Xk(n, tt)])

    def do_ffn(st, h, l, b):
        GL = 2
        act = sb(st, "act", [128, 6, T], BF16)
        wd = [sb(st, f"wd{i}", [128, 6, D], BF16) for i in range(2)]
        gb = [sb(st, f"gb{i}", [128, 2 + T], F32) for i in range(GL + 1)]
        gc = sb(st, "gc", [128, T], F32)
        sg = sb(st, "sg", [128, T], BF16)
        fo, _ = PP["fcw"]
        bo, _ = PP["fcb"]
        for i in range(GL + 1):
            P.op("dve", lambda e, i=i: e.memset(gb[i][:, 0:2], 0.0), wr=[("gbpad", i)])
        wg2, wu2 = w_gate[l], w_up[l]
        slots = {}
        qof = {}
        for qi, (c0, ncq) in enumerate(FQ):
            for ci in range(ncq):
                qof[c0 + ci] = (qi, ci, c0, ncq)

        def load_c(c):
            sg_, sk = wslot()
            wload(sg_[:], wcols(wg2, c * 128, 128), sk)
            su_, uk = wslot()
            wload(su_[:], wcols(wu2, c * 128, 128), uk)
            slots[c] = (sg_, sk, su_, uk)

        def gate_stage(c):
            if c + 1 < NFF:
                load_c(c + 1)
            sg_, sk, su_, uk = slots[c]
            gi = c % (GL + 1)
            for tt in range(NT):
                bank = tt
                mm_group(ps[bank][:], [(sg_[:, k, :], h[:, k, tsl(tt)]) for k in range(8)],
                         rd=[sk] + hk_all(tt), wr=[psk(bank)])
                P.op("act", lambda e, gi=gi, tt=tt, bank=bank: e.activation(
                    out=gb[gi][:, 2 + tt * TS:2 + (tt + 1) * TS], in_=ps[bank][:], func=AF.Identity),
                    rd=[psk(bank)], wr=[("gb", gi, tt)])

        def rest_stage(c):
            qi, ci, c0, ncq = qof[c]
            wdb = wd[qi % 2]
            wdk = ("wd", qi % 2)
            if ci == 0:
                wload(wdb[:, 0:ncq, :],
                      w_down[l][c0 * 128:(c0 + ncq) * 128, :].rearrange("(kc p) n -> p kc n", p=128), wdk)
            sg_, sk, su_, uk = slots.pop(c)
            gi = c % (GL + 1)
            wo = fo + (l * NFF + c) * 3
            P.op("dve", lambda e, gi=gi, wo=wo, c=c: e.tensor_scalar(
                out=gc[:], in0=gb[gi][:, 0:T], scalar1=pp[:, wo:wo + 1],
                scalar2=pp[:, bo + l * NFF + c:bo + l * NFF + c + 1], op0=ALU.mult, op1=ALU.add),
                rd=[("gb", gi, t_) for t_ in range(NT)] + [("gbpad", gi), "pp"], wr=["gc"])
            for k in (1, 2):
                P.op("dve", lambda e, gi=gi, wo=wo, k=k: e.scalar_tensor_tensor(
                    out=gc[:], in0=gb[gi][:, k:k + T], scalar=pp[:, wo + k:wo + k + 1], in1=gc[:],
                    op0=ALU.mult, op1=ALU.add),
                    rd=[("gb", gi, t_) for t_ in range(NT)] + ["gc", "pp"], wr=["gc"])
            P.op("act", lambda e: e.activation(out=sg[:], in_=gc[:], func=AF.Silu), rd=["gc"], wr=["sg"])
            for tt in range(NT):
                bank = 4 + tt % 2
                mm_group(ps[bank][:], [(su_[:, k, :], h[:, k, tsl(tt)]) for k in range(8)],
                         rd=[uk] + hk_all(tt), wr=[psk(bank)])
                P.op("dve", lambda e, ci=ci, tt=tt, bank=bank: e.tensor_tensor(
                    out=act[:, ci, tsl(tt)], in0=sg[:, tsl(tt)], in1=ps[bank][:], op=ALU.mult),
                    rd=["sg", psk(bank)], wr=[("act", ci, tt)])
            if ci == ncq - 1:
                for n in range(8):
                    for tt in range(NT):
                        bank = 6 + (n * NT + tt) % 2
                        mm_group(ps[bank][:], [(wdb[:, cj, n * 128:(n + 1) * 128], act[:, cj, tsl(tt)]) for cj in range(ncq)],
                                 rd=[wdk] + [("act", cj, tt) for cj in range(ncq)], wr=[psk(bank)])
                        P.op("dve", lambda e, n=n, tt=tt, bank=bank: e.scalar_tensor_tensor(
                            out=X[:, n, tsl(tt)], in0=ps[bank][:], scalar=modp[:, l, b, 5, n:n + 1],
                            in1=X[:, n, tsl(tt)], op0=ALU.mult, op1=ALU.add),
                            rd=[psk(bank), Xk(n, tt), "modp"], wr=[Xk(n, tt)])

        load_c(0)
        for i in range(NFF + GL):
            if i < NFF:
                gate_stage(i)
            if i - GL >= 0:
                rest_stage(i - GL)

    def qk_norm_chunk(w2d, col0, dst, dstkey, gain_ap, h, tmp, dup64=False):
        ws, wk = wslot()
        if dup64:
            for hb in range(2):
                P.dma("pool", ws[:, :, hb * 64:(hb + 1) * 64], wcols(w2d, col0, 64), wr=[wk])
        else:
            wload(ws[:], wcols(w2d, col0, 128), wk)
        sqb, sdb, rsb = tmp
        for tt in range(NT):
            i2 = tt % 2
            bA = i2
            bB = 2 + i2
            mm_group(ps[bA][:], [(ws[:, k, :], h[:, k, tsl(tt)]) for k in range(8)], rd=[wk] + hk_all(tt), wr=[psk(bA)])
            P.op("act", lambda e, i2=i2, bA=bA: e.activation(out=sqb[i2][:], in_=ps[bA][:], func=AF.Square),
                 rd=[psk(bA)], wr=[("sqb", i2)])
            mm_group(ps[bB][:], [(bones, sqb[i2][:])], rd=[("sqb", i2), "con"], wr=[psk(bB)])
            P.op("act", lambda e, i2=i2, bB=bB: e.activation(out=sdb[i2][:], in_=ps[bB][:], func=AF.Sqrt,
                                                              scale=1.0 / 64, bias=EPS),
                 rd=[psk(bB)], wr=[("sdb", i2)])
            P.op("dve", lambda e, i2=i2: e.reciprocal(out=rsb[i2][:], in_=sdb[i2][:]), rd=[("sdb", i2)], wr=[("rsb", i2)])
            P.op("dve", lambda e, i2=i2, bA=bA, tt=tt: e.scalar_tensor_tensor(
                out=dst[:, tsl(tt)], in0=ps[bA][:], scalar=gain_ap, in1=rsb[i2][:], op0=ALU.mult, op1=ALU.mult),
                rd=[psk(bA), ("rsb", i2), "misc", "pp"], wr=[(dstkey, tt)])

    for s in range(nseq):
        b = s
        for c in range(8):
            P.dma("sp", X[:, c, :], xT[s, c], wr=[Xk(c, tt) for tt in range(NT)])
        l = 0
        with ExitStack() as st0:
            ya = sb(st0, "ya", [128, 4, T], BF16)
            with ExitStack() as st1:
                h = sb(st1, "h", [128, 8, T], BF16)
                with ExitStack() as st2:
                    do_norm(st2, h, l, b, 0)
                    if s == 0:
                        dump("h0", h[:].rearrange("p c t -> p (c t)"), [128, 8 * T],
                             rd=[("h", c, tt) for c in range(8) for tt in range(NT)])
                    P.barrier()
                    P.flush()
                if stop == "norm0":
                    break
                with ExitStack() as st2:
                    xr = sb(st2, "xr", [128, 3 + T], F32)
                    xc = sb(st2, "xc", [128, T], F32)
                    xcb = sb(st2, "xcb", [128, T], BF16)
                    ra = sb(st2, "ra", [128, T], F32)
                    ig = sb(st2, "ig", [128, T], F32)
                    s2 = sb(st2, "s2", [128, T], F32)
                    gel = sb(st2, "gel", [128, T], F32)
                    gx = sb(st2, "gx", [128, T], F32)
                    wabd = sb(st2, "wabd", [128, 4, 128], BF16)
                    wxbd = sb(st2, "wxbd", [128, 4, 128], BF16)
                    P.dma("pool", wabd[:], wabd_d, wr=["wabd"])
                    P.dma("pool", wxbd[:], wxbd_d, wr=["wxbd"])
                    P.op("dve", lambda e: e.memset(xr[:, 0:3], 0.0), wr=["xrpad"])
                    lcw, _ = PP["lcw"]
                    lcb, _ = PP["lcb"]
                    lba, _ = PP["lba"]
                    lbx, _ = PP["lbx"]
                    import os as _os
                    for j in [int(v) for v in _os.environ.get("LRU_CHUNKS", "0,1,2,3").split(",")]:
                        wsx, wkx = wslot()
                        wload(wsx[:], wcols(ev_w_in, j * 128, 128), wkx)
                        wsg, wkg = wslot()
                        wload(wsg[:], wcols(ev_w_in, 512 + j * 128, 128), wkg)
                        for tt in range(NT):
                            bank = tt % 2
                            mm_group(ps[bank][:], [(wsx[:, k, :], h[:, k, tsl(tt)]) for k in range(8)],
                                     rd=[wkx] + hk_all(tt), wr=[psk(bank)])
                            P.op("act", lambda e, tt=tt, bank=bank: e.activation(
                                out=xr[:, 3 + tt * TS:3 + (tt + 1) * TS], in_=ps[bank][:], func=AF.Identity),
                                rd=[psk(bank)], wr=[("xr", tt)])
                        xrk = [("xr", t_) for t_ in range(NT)]
                        P.op("dve", lambda e, j=j: e.tensor_scalar(
                            out=xc[:], in0=xr[:, 0:T], scalar1=pp[:, lcw + j * 4:lcw + j * 4 + 1],
                            scalar2=pp[:, lcb + j:lcb + j + 1], op0=ALU.mult, op1=ALU.add),
                            rd=xrk + ["xrpad", "pp"], wr=["xc"])
                        for k in (1, 2, 3):
                            P.op("dve", lambda e, j=j, k=k: e.scalar_tensor_tensor(
                                out=xc[:], in0=xr[:, k:k + T], scalar=pp[:, lcw + j * 4 + k:lcw + j * 4 + k + 1],
                                in1=xc[:], op0=ALU.mult, op1=ALU.add), rd=xrk + ["xc", "pp"], wr=["xc"])
                        P.op("act", lambda e: e.activation(out=xcb[:], in_=xc[:], func=AF.Identity), rd=["xc"], wr=["xcb"])
                        for tt in range(NT):
                            bank = 2 + tt % 2
                            mm_group(ps[bank][:], [(wabd[:, j, :], xcb[:, tsl(tt)])], rd=["wabd", "xcb"], wr=[psk(bank)])
                            P.op("act", lambda e, j=j, tt=tt, bank=bank: e.activation(
                                out=ra[:, tsl(tt)], in_=ps[bank][:], func=AF.Sigmoid, bias=pp[:, lba + j:lba + j + 1]),
                                rd=[psk(bank), "pp"], wr=[("ra", tt)])
                            bank2 = 4 + tt % 2
                            mm_group(ps[bank2][:], [(wxbd[:, j, :], xcb[:, tsl(tt)])], rd=["wxbd", "xcb"], wr=[psk(bank2)])
                            P.op("act", lambda e, j=j, tt=tt, bank2=bank2: e.activation(
                                out=ig[:, tsl(tt)], in_=ps[bank2][:], func=AF.Sigmoid, bias=pp[:, lbx + j:lbx + j + 1]),
                                rd=[psk(bank2), "pp"], wr=[("ig", tt)])
                        rak = [("ra", t_) for t_ in range(NT)]
                        igk = [("ig", t_) for t_ in range(NT)]
                        P.op("act", lambda e, j=j: e.activation(out=ra[:], in_=ra[:], func=AF.Exp, scale=cl[:, j:j + 1]),
                             rd=rak + ["misc"], wr=rak)
                        P.op("act", lambda e: e.activation(out=s2[:], in_=ra[:], func=AF.Square), rd=rak, wr=["s2"])
                        P.op("act", lambda e: e.activation(out=s2[:], in_=s2[:], func=AF.Sqrt, scale=-1.0, bias=1.0),
                             rd=["s2"], wr=["s2"])
                        P.op("dve", lambda e: e.tensor_tensor(out=s2[:], in0=s2[:], in1=ig[:], op=ALU.mult),
                             rd=["s2"] + igk, wr=["s2"])
                        P.op("dve", lambda e: e.tensor_tensor(out=s2[:], in0=s2[:], in1=xc[:], op=ALU.mult),
                             rd=["s2", "xc"], wr=["s2"])
                        P.op("dve", lambda e: e.tensor_tensor_scan(out=xc[:], data0=ra[:], data1=s2[:], initial=0.0,
                                                                    op0=ALU.mult, op1=ALU.add),
                             rd=rak + ["s2", "xc"], wr=["xc"])
                        for tt in range(NT):
                            bank = 6 + tt % 2
                            mm_group(ps[bank][:], [(wsg[:, k, :], h[:, k, tsl(tt)]) for k in range(8)],
                                     rd=[wkg] + hk_all(tt), wr=[psk(bank)])
                            P.op("act", lambda e, tt=tt, bank=bank: e.activation(
                                out=gx[:, tsl(tt)], in_=ps[bank][:], func=AF.Identity),
                                rd=[psk(bank)], wr=[("gx", tt)])
                        gxk = [("gx", t_) for t_ in range(NT)]
                        P.op("act", lambda e: e.activation(out=gel[:], in_=gx[:], func=AF.Square), rd=gxk, wr=["gel"])
                        P.op("dve", lambda e: e.tensor_scalar(out=gel[:], in0=gel[:], scalar1=0.044715, scalar2=1.0,
                                                              op0=ALU.mult, op1=ALU.add), rd=["gel"], wr=["gel"])
                        P.op("dve", lambda e: e.tensor_tensor(out=gel[:], in0=gel[:], in1=gx[:], op=ALU.mult),
                             rd=["gel"] + gxk, wr=["gel"])
                        P.op("act", lambda e: e.activation(out=gel[:], in_=gel[:], func=AF.Sigmoid, scale=1.5957691216057308),
                             rd=["gel"], wr=["gel"])
                        P.op("dve", lambda e: e.tensor_tensor(out=gel[:], in0=gel[:], in1=gx[:], op=ALU.mult),
                             rd=["gel"] + gxk, wr=["gel"])
                        P.op("dve", lambda e, j=j: e.tensor_tensor(out=ya[:, j, :], in0=xc[:], in1=gel[:], op=ALU.mult),
                             rd=["xc", "gel"], wr=[("ya", j)])
                    if s == 0:
                        dump("lxc", xc[:], [128, T], rd=["xc"])
                        dump("lra", ra[:], [128, T], rd=[("ra", t_) for t_ in range(NT)])
                        dump("lig", ig[:], [128, T], rd=[("ig", t_) for t_ in range(NT)])
                        dump("ls2", s2[:], [128, T], rd=["s2"])
                        dump("ya", ya[:].rearrange("p c t -> p (c t)"), [128, 4 * T], rd=[("ya", j) for j in range(4)])
                    P.barrier()
                    P.flush()
                if stop == "lru":
                    break
                qn = sb(st0, "qn", [128, 4, T], BF16)
                kn = sb(st0, "kn", [128, 4, T], BF16)
                vt = sb(st0, "vt", [128, 16, 512], BF16)
                with ExitStack() as st2:
                    sqb = [sb(st2, f"sqb{i}", [128, TS], BF16) for i in range(2)]
                    sdb = [sb(st2, f"sdb{i}", [128, TS], F32) for i in range(2)]
                    rsb = [sb(st2, f"rsb{i}", [128, TS], F32) for i in range(2)]
                    wv = sb(st2, "wv", [128, 8, 512], BF16)
                    for j in range(4):
                        qk_norm_chunk(ev_w_in, 1024 + j * 128, qn[:, j, :], ("qn", j), evq8, h, (sqb, sdb, rsb))
                        qk_norm_chunk(ev_w_in, 1536 + j * 128, kn[:, j, :], ("kn", j), ppv("evkg"), h, (sqb, sdb, rsb))
                    wload(wv[:], wcols(ev_w_in, 2048, 512), "wv")
                    for blk in range(16):
                        bank = 4 + blk % 4
                        mm_group(ps[bank][:], [(h[:, k, blk * 128:(blk + 1) * 128], wv[:, k, :]) for k in range(8)],
                                 rd=["wv"] + hk_all(blk // 4), wr=[psk(bank)])
                        if blk % 2 == 0:
                            P.op("act", lambda e, blk=blk, bank=bank: e.activation(out=vt[:, blk, :], in_=ps[bank][:],
                                                                                   func=AF.Identity),
                                 rd=[psk(bank)], wr=[("vt", blk)])
                        else:
                            P.op("dve", lambda e, blk=blk, bank=bank: e.tensor_copy(out=vt[:, blk, :], in_=ps[bank][:]),
                                 rd=[psk(bank)], wr=[("vt", blk)])
                    if s == 0:
                        dump("qn", qn[:].rearrange("p c t -> p (c t)"), [128, 4 * T],
                             rd=[(("qn", j), t_) for j in range(4) for t_ in range(NT)])
                        dump("kn", kn[:].rearrange("p c t -> p (c t)"), [128, 4 * T],
                             rd=[(("kn", j), t_) for j in range(4) for t_ in range(NT)])
                        dump("vt", vt[:].rearrange("p c t -> p (c t)"), [128, 16 * 512], rd=[("vt", k) for k in range(16)])
                    P.barrier()
                    P.flush()
            if stop in ("norm0", "lru", "sbproj"):
                break
            yb = sb(st0, "yb", [128, 4, T], BF16)
            with ExitStack() as st2:
                eb = [sb(st2, f"eb{i}", [128, TS], F32) for i in range(3)]
                LOOK = 2
                NSP, NRB = LOOK + 3, LOOK + 2
                spb = [sb(st2, f"spb{i}", [128, TS], BF16) for i in range(NSP)]
                Rb = [sb(st2, f"Rb{i}", [128, TS], BF16) for i in range(NRB)]
                wb_ = [sb(st2, f"wb{i}", [128, TS], BF16) for i in range(3)]
                mo, _ = CO["md"]
                tiles = []
                for j in range(4):
                    for tt in range(NT):
                        ob = 6 + (j * NT + tt) % 2
                        for hh in range(2):
                            for idx, kb in enumerate(range(4 * tt + 3, -1, -1)):
                                tiles.append(dict(j=j, tt=tt, hh=hh, kb=kb, idx=idx, ob=ob, R=None))
                cnt_ = dict(z=0, e=0, sp=0, r=0, rb=0, w=0)

                def stage_a(ti):
                    t = tiles[ti]
                    j, tt, hh, kb, idx = t["j"], t["tt"], t["hh"], t["kb"], t["idx"]
                    p0 = hh * 64
                    dz = kb - 4 * tt
                    zb = (0, 1, 5)[cnt_["z"] % 3]
                    cnt_["z"] += 1
                    mm_group(ps[zb][:], [(kn[p0:p0 + 64, j, kb * 128:(kb + 1) * 128], qn[p0:p0 + 64, j, tsl(tt)])],
                             rd=[(("kn", j), kb // 4), (("qn", j), tt)], wr=[psk(zb)])
                    ei = cnt_["e"] % 3
                    cnt_["e"] += 1
                    P.op("act", lambda e, ei=ei, zb=zb: e.activation(out=eb[ei][:], in_=ps[zb][:], func=AF.Exp),
                         rd=[psk(zb)], wr=[("eb", ei)])
                    si = cnt_["sp"] % NSP
                    cnt_["sp"] += 1
                    t["si"] = si
                    P.op("act", lambda e, ei=ei, si=si: e.activation(out=spb[si][:], in_=eb[ei][:], func=AF.Ln, bias=1.0),
                         rd=[("eb", ei)], wr=[("spb", si)])
                    if dz >= 0:
                        P.op("dve", lambda e, si=si, dz=dz: e.tensor_tensor(
                            out=spb[si][:], in0=spb[si][:], in1=con[:, mo + dz * 512:mo + (dz + 1) * 512], op=ALU.mult),
                            rd=[("spb", si), "con"], wr=[("spb", si)])
                    if kb > 0:
                        nt_ = tiles[ti + 1]
                        if idx == 0:
                            nt_["R"] = (spb[si], ("spb", si))
                        else:
                            rn = cnt_["rb"] % NRB
                            cnt_["rb"] += 1
                            rsrc, rkey = t["R"]
                            P.op("dve", lambda e, si=si, rsrc=rsrc, rn=rn: e.tensor_tensor(
                                out=Rb[rn][:], in0=rsrc[:], in1=spb[si][:], op=ALU.add),
                                rd=[("spb", si), rkey], wr=[("Rb", rn)])
                            nt_["R"] = (Rb[rn], ("Rb", rn))

                def stage_b(ti):
                    t = tiles[ti]
                    j, tt, hh, kb, idx, ob, si = t["j"], t["tt"], t["hh"], t["kb"], t["idx"], t["ob"], t["si"]
                    p0 = hh * 64
                    dz = kb - 4 * tt
                    rb = (2, 3, 4)[cnt_["r"] % 3]
                    cnt_["r"] += 1
                    pairs = [(ntri, spb[si][:])]
                    rdk = [("spb", si), "con", (("kn", j), kb // 4), (("qn", j), tt)]
                    if t["R"] is not None:
                        pairs.append((nones, t["R"][0][:]))
                        rdk.append(t["R"][1])
                    pairs.append((kn[p0:p0 + 64, j, kb * 128:(kb + 1) * 128], qn[p0:p0 + 64, j, tsl(tt)]))
                    mm_group(ps[rb][:], pairs, rd=rdk, wr=[psk(rb)])
                    wi = cnt_["w"] % 3
                    cnt_["w"] += 1
                    P.op("act", lambda e, wi=wi, rb=rb: e.activation(out=wb_[wi][:], in_=ps[rb][:], func=AF.Exp),
                         rd=[psk(rb)], wr=[("wb", wi)])
                    if dz >= 0:
                        P.op("dve", lambda e, wi=wi, dz=dz: e.tensor_tensor(
                            out=wb_[wi][:], in0=wb_[wi][:], in1=con[:, mo + dz * 512:mo + (dz + 1) * 512], op=ALU.mult),
                            rd=[("wb", wi), "con"], wr=[("wb", wi)])

                    def pv(e, ob=ob, p0=p0, kb=kb, j=j, hh=hh, wi=wi, first=(idx == 0), last=(kb == 0)):
                        return e.matmul(ps[ob][p0:p0 + 64, :], lhsT=vt[:, kb, (2 * j + hh) * 64:(2 * j + hh + 1) * 64],
                                        rhs=wb_[wi][:], start=first, stop=last)
                    P.op("pe", pv, rd=[("wb", wi), ("vt", kb)], wr=[("pso", ob, hh)])
                    if kb == 0 and hh == 1:
                        P.op("act", lambda e, j=j, tt=tt, ob=ob: e.activation(out=yb[:, j, tsl(tt)], in_=ps[ob][:], func=AF.Identity),
                             rd=[("pso", ob, 0), ("pso", ob, 1)], wr=[("yb", j, tt)])

                ntl = len(tiles)
                for ti in range(ntl + LOOK):
                    if ti < ntl:
                        stage_a(ti)
                    if ti - LOOK >= 0:
                        stage_b(ti - LOOK)
                if s == 0:
                    dump("yb", yb[:].rearrange("p c t -> p (c t)"), [128, 4 * T],
                         rd=[("yb", j, t_) for j in range(4) for t_ in range(NT)])
                P.barrier()
                P.flush()
            if stop == "sb":
                break
            out_proj_residual(ev_w_out, lambda k: (ya[:, k, :] if k < 4 else yb[:, k - 4, :]),
                              lambda tt: [("ya", j) for j in range(4)] + [("yb", j, tt) for j in range(4)], l, b, 2)
            P.barrier()
            P.flush()
        if s == 0:
            dump("x0mid", X[:].rearrange("p c t -> p (c t)"), [128, 8 * T], rd=[Xk(c, tt) for c in range(8) for tt in range(NT)])
        if stop == "mix0":
            break
        with ExitStack() as st1:
            h = sb(st1, "h", [128, 8, T], BF16)
            with ExitStack() as st2:
                do_norm(st2, h, l, b, 1)
                P.barrier()
                P.flush()
            with ExitStack() as st2:
                do_ffn(st2, h, l, b)
                P.barrier()
                P.flush()
        if s == 0:
            dump("x1", X[:].rearrange("p c t -> p (c t)"), [128, 8 * T], rd=[Xk(c, tt) for c in range(8) for tt in range(NT)])
        if stop == "l0":
            break
        l = 1
        with ExitStack() as st0:
            with ExitStack() as st1:
                h = sb(st1, "h", [128, 8, T], BF16)
                with ExitStack() as st2:
                    do_norm(st2, h, l, b, 0)
                    P.barrier()
                    P.flush()
                qn = sb(st0, "qn1", [128, 8, T], BF16)
                kd = sb(st0, "kd", [128, 4, T], BF16)
                va = sb(st0, "va", [128, 16, 4, 65], BF16)
                with ExitStack() as st2:
                    sqb = [sb(st2, f"sqb{i}", [128, TS], BF16) for i in range(2)]
                    sdb = [sb(st2, f"sdb{i}", [128, TS], F32) for i in range(2)]
                    rsb = [sb(st2, f"rsb{i}", [128, TS], F32) for i in range(2)]
                    wv = sb(st2, "wv", [128, 8, 512], BF16)
                    for c in range(8):
                        qk_norm_chunk(od_w_in, c * 128, qn[:, c, :], ("qn", c), odq8, h, (sqb, sdb, rsb))
                    for g in range(4):
                        qk_norm_chunk(od_w_in, 1024 + g * 64, kd[:, g, :], ("kd", g), ppv("odkg"), h, (sqb, sdb, rsb), dup64=True)
                    P.op("dve", lambda e: e.memset(va[:, :, :, 64:65], 1.0), wr=["vaones"])
                    wload(wv[:, :, 0:256], wcols(od_w_in, 1280, 256), "wv")
                    for blk in range(16):
                        bank = 4 + blk % 4
                        mm_group(ps[bank][:, 0:256], [(h[:, k, blk * 128:(blk + 1) * 128], wv[:, k, 0:256]) for k in range(8)],
                                 rd=["wv"] + hk_all(blk // 4), wr=[psk(bank)])
                        P.op("act" if blk % 2 == 0 else "dve",
                             (lambda e, blk=blk, bank=bank: e.activation(
                                 out=va[:, blk, :, 0:64], in_=ps[bank][:, 0:256].rearrange("p (g d) -> p g d", g=4), func=AF.Identity))
                             if blk % 2 == 0 else
                             (lambda e, blk=blk, bank=bank: e.tensor_copy(
                                 out=va[:, blk, :, 0:64], in_=ps[bank][:, 0:256].rearrange("p (g d) -> p g d", g=4))),
                             rd=[psk(bank)], wr=[("va", blk)])
                    P.barrier()
                    P.flush()
            if stop == "l1proj":
                break
            yT = sb(st0, "yT", [128, 8, T], BF16)
            with ExitStack() as st2:
                pb = [sb(st2, f"pb{i}", [128, 2, TS], BF16) for i in range(2)]
                den = [sb(st2, f"den{i}", [128, 4], F32) for i in range(2)]
                ytok = [sb(st2, f"ytok{i}", [128, D], BF16) for i in range(2)]
                units = [(qb, g) for qb in range(16) for g in range(4)]

                def swa_a(u):
                    qb, g = units[u]
                    pi = u % 2
                    kbs = [qb - 1, qb] if qb > 0 else [qb]
                    c0 = 0 if qb > 0 else 256
                    for hb in range(2):
                        sbank = hb + 2 * pi

                        def sc(e, sbank=sbank, g=g, kbs=kbs, qb=qb, hb=hb):
                            ins = None
                            for kb in kbs:
                                which = 0 if kb == qb - 1 else 1
                                ins = e.matmul(ps[sbank][:, which * 256:(which + 1) * 256],
                                               lhsT=kd[hb * 64:(hb + 1) * 64, g, kb * 128:(kb + 1) * 128],
                                               rhs=qn[hb * 64:(hb + 1) * 64, 2 * g:2 * g + 2, qb * 128:(qb + 1) * 128],
                                               start=True, stop=True)
                            return ins
                        P.op("pe", sc, rd=[(("kd", g), kb // 4) for kb in kbs] + [(("qn", 2 * g), qb // 4), (("qn", 2 * g + 1), qb // 4)],
                             wr=[psk(sbank)])
                        P.op("act", lambda e, pi=pi, hb=hb, sbank=sbank, c0=c0: e.activation(
                            out=pb[pi][:, hb, c0:512], in_=ps[sbank][:, c0:512], func=AF.Exp),
                            rd=[psk(sbank)], wr=[("pb", pi, hb)])
                        P.op("dve", lambda e, pi=pi, hb=hb, c0=c0: e.tensor_tensor(
                            out=pb[pi][:, hb, c0:512], in0=pb[pi][:, hb, c0:512], in1=maskp[:, c0:512], op=ALU.mult),
                            rd=[("pb", pi, hb), "con"], wr=[("pb", pi, hb)])

                def swa_b(u):
                    qb, g = units[u]
                    pi = u % 2
                    yi = qb % 2
                    kbs = [qb - 1, qb] if qb > 0 else [qb]
                    ybank = 4 + pi

                    def pvm(e, ybank=ybank, pi=pi, kbs=kbs, qb=qb, g=g):
                        ins = None
                        for hc in range(4):
                            hb, e_ = hc // 2, hc % 2
                            for i_, kb in enumerate(kbs):
                                which = 0 if kb == qb - 1 else 1
                                ins = e.matmul(ps[ybank][:, hc * 65:(hc + 1) * 65],
                                               lhsT=pb[pi][:, hb, which * 256 + e_ * 128:which * 256 + (e_ + 1) * 128],
                                               rhs=va[:, kb, g, :], start=(i_ == 0), stop=(i_ == len(kbs) - 1))
                        return ins
                    P.op("pe", pvm, rd=[("pb", pi, 0), ("pb", pi, 1)] + [("va", kb) for kb in kbs] + ["vaones"], wr=[psk(ybank)])
                    yv = ps[ybank][:, 0:260].rearrange("p (hb e d) -> p hb e d", hb=2, e=2)
                    P.op("dve", lambda e, pi=pi, yv=yv, g=g: e.tensor_tensor(
                        out=den[pi][:].rearrange("p (hb e) -> p hb e", hb=2),
                        in0=yv[:, :, :, 64],
                        in1=esink[:, 4 * g:4 * g + 4].rearrange("p (e hb) -> p hb e", hb=2), op=ALU.add),
                        rd=[psk(ybank), "misc"], wr=[("den", pi)])
                    P.op("dve", lambda e, pi=pi: e.reciprocal(out=den[pi][:], in_=den[pi][:]), rd=[("den", pi)], wr=[("den", pi)])
                    P.op("dve", lambda e, pi=pi, yv=yv, g=g, yi=yi: e.tensor_tensor(
                        out=ytok[yi][:, g * 256:(g + 1) * 256].rearrange("p (e hb d) -> p hb e d", e=2, hb=2),
                        in0=yv[:, :, :, 0:64],
                        in1=den[pi][:].rearrange("p (hb e) -> p hb e", hb=2).unsqueeze(3).broadcast_to([128, 2, 2, 64]),
                        op=ALU.mult),
                        rd=[psk(ybank), ("den", pi)], wr=[("ytok", yi, g)])
                    if g == 3:
                        tbank = 6 + yi
                        tp = ps[tbank][:].bitcast(BF16)

                        def trn(e, tp=tp, yi=yi):
                            ins = None
                            for c in range(8):
                                ins = e.transpose(out=tp[:, c * 128:(c + 1) * 128], in_=ytok[yi][:, c * 128:(c + 1) * 128], identity=ident)
                            return ins
                        P.op("pe", trn, rd=[("ytok", yi, g_) for g_ in range(4)] + ["con"], wr=[psk(tbank)])
                        P.op("act", lambda e, tp=tp, qb=qb: e.activation(
                            out=yT[:, :, qb * 128:(qb + 1) * 128], in_=tp.rearrange("p (c t) -> p c t", c=8), func=AF.Identity),
                            rd=[psk(tbank)], wr=[("yT", qb // 4)])

                nun = len(units)
                for u in range(nun + 1):
                    if u < nun:
                        swa_a(u)
                    if u >= 1:
                        swa_b(u - 1)
                if s == 0:
                    dump("yT", yT[:].rearrange("p c t -> p (c t)"), [128, 8 * T], rd=[("yT", t_) for t_ in range(NT)])
                P.barrier()
                P.flush()
            if stop == "swa":
                break
            out_proj_residual(od_w_out, lambda k: yT[:, k, :], lambda tt: [("yT", tt)], l, b, 2)
            P.barrier()
            P.flush()
        if s == 0:
            dump("x1mid", X[:].rearrange("p c t -> p (c t)"), [128, 8 * T], rd=[Xk(c, tt) for c in range(8) for tt in range(NT)])
        with ExitStack() as st1:
            h = sb(st1, "h", [128, 8, T], BF16)
            with ExitStack() as st2:
                do_norm(st2, h, l, b, 1)
                P.barrier()
                P.flush()
            with ExitStack() as st2:
                do_ffn(st2, h, l, b)
                P.barrier()
                P.flush()
        for c in range(8):
            P.dma("sp", out_d[s, c], X[:, c, :], rd=[Xk(c, tt) for tt in range(NT)], wr=[("out", s, c)])
        P.flush()

    P.barrier()
    P.op("sp", None)
    P.flush()
    top.close()
    return nc, P, dbg_out


_CACHE = {}


def kernel(**inputs):
    if "nc" not in _CACHE:
        _CACHE["nc"] = build()[0]
    nc = _CACHE["nc"]
    in_maps = [_host_inputs(inputs, core) for core in range(NCORES)]
    res = run_bass_kernel_spmd(nc, in_maps, core_ids=list(range(NCORES)))
    outs = []
    for core in range(NCORES):
        o = np.asarray(res.results[core]["out"], np.float32).reshape(2, D, T)
        outs.append(o.transpose(0, 2, 1))
    return np.ascontiguousarray(np.concatenate(outs, axis=0)).astype(np.float32)
```

```python
from contextlib import ExitStack

import numpy as np
import concourse.bass as bass
import concourse.mybir as mybir
from concourse.bass_utils import run_bass_kernel_spmd

F32 = mybir.dt.float32
BF16 = mybir.dt.bfloat16
AF = mybir.ActivationFunctionType
ALU = mybir.AluOpType

NCORES = 8
T = 2048
NT = 4
TS = 512
D = 1024
DFF = 2816
NFF = 22
EPS = 1e-6
FQ = [(0, 6), (6, 6), (12, 5), (17, 5)]


class Prog:
    NRING = 8

    def __init__(self, nc):
        self.nc = nc
        self.engs = {"pe": nc.tensor, "act": nc.scalar, "dve": nc.vector,
                     "pool": nc.gpsimd, "sp": nc.sync}
        self.ops = []
        self.nflushed = 0
        self.last_w = {}
        self.readers = {}
        self.last_on_eng = {}
        self.dmas_since_barrier = []
        self.barrier_deps = set()
        self.dma_hist = {}
        self.sems = {e: nc.alloc_semaphore("c_" + e) for e in self.engs}
        self.rings = {}
        self.cnt = {e: 0 for e in self.engs}
        self.done = []
        self.waited = {e: {} for e in self.engs}
        self.nwaits = 0

    limit = None

    def op(self, eng, fn, rd=(), wr=(), dma=False):
        i = len(self.ops)
        if self.limit is not None and i >= self.limit and not dma and fn is not None:
            return None
        o = dict(eng=eng, fn=fn, dma=dma)
        ops = self.ops
        d = set()
        raw = set()
        for k in rd:
            j = self.last_w.get(k)
            if j is not None:
                d.add(j)
                raw.add(j)
        for k in wr:
            j = self.last_w.get(k)
            if j is not None:
                d.add(j)
            d.update(self.readers.get(k, ()))
        keep = set()
        for j in d:
            oj = ops[j]
            if (not oj["dma"]) and (not dma) and oj["eng"] == eng and eng == "pe":
                continue
            keep.add(j)
        for j in self.barrier_deps:
            oj = ops[j]
            if (not oj["dma"]) and (not dma) and oj["eng"] == eng:
                continue
            keep.add(j)
        for k in rd:
            self.readers.setdefault(k, []).append(i)
        for k in wr:
            self.last_w[k] = i
            self.readers[k] = []
        if dma:
            hist = self.dma_hist.setdefault(eng, [])
            c = len(hist)
            if eng not in self.rings:
                self.rings[eng] = [self.nc.alloc_semaphore(f"d_{eng}{r}") for r in range(self.NRING)]
            o["ring"] = c % self.NRING
            o["rval"] = 16 * (c // self.NRING + 1)
            if c >= self.NRING:
                keep.add(hist[c - self.NRING])
            hist.append(i)
            self.dmas_since_barrier.append(i)
        else:
            self.last_on_eng[eng] = i
        o["deps"] = keep
        ops.append(o)
        self.done.append(None)
        return i

    def dma(self, eng, out, in_, rd=(), wr=()):
        return self.op(eng, lambda e: e.dma_start(out=out, in_=in_), rd, wr, dma=True)

    def barrier(self):
        self.barrier_deps = set(self.last_on_eng.values()) | set(self.dmas_since_barrier)
        self.dmas_since_barrier = []

    def flush(self):
        ops = self.ops
        n = len(ops)
        start = self.nflushed
        needs = set()
        for i in range(start, n):
            needs.update(ops[i]["deps"])
        needs.update(self.last_on_eng.values())
        needs.update(self.last_w.values())
        for r in self.readers.values():
            needs.update(r)
        needs.update(self.barrier_deps)
        for i in range(start, n):
            o = ops[i]
            e = o["eng"]
            eng = self.engs[e]
            need = {}
            for j in o["deps"]:
                s, v = self.done[j]
                k = id(s)
                if k not in need or need[k][1] < v:
                    need[k] = (s, v)
            for k, (s, v) in need.items():
                if self.waited[e].get(k, 0) >= v:
                    continue
                eng.wait_ge(s, v)
                self.nwaits += 1
                self.waited[e][k] = v
            if o["fn"] is None:
                self.done[i] = (self.sems[e], self.cnt[e])
                continue
            ins = o["fn"](eng)
            if o["dma"]:
                s = self.rings[e][o["ring"]]
                ins.then_inc(s, 16)
                self.done[i] = (s, o["rval"])
            elif i in needs:
                self.cnt[e] += 1
                ins.then_inc(self.sems[e], 1)
                self.done[i] = (self.sems[e], self.cnt[e])
            else:
                self.done[i] = (self.sems[e], self.cnt[e] + 1)
            o["fn"] = None
        self.nflushed = n


PP = {}
_off = 0
for _name, _n in [("cT", 16), ("adab", 96), ("gmix", 16), ("gffn", 16), ("lcw", 16), ("lcb", 4),
                  ("lba", 4), ("lbx", 4), ("llam", 4), ("evqg", 1), ("evkg", 1), ("odqg", 1),
                  ("odkg", 1), ("sinks", 16), ("fcw", 132), ("fcb", 44)]:
    PP[_name] = (_off, _n)
    _off += _n
NPP = _off

CO = {}
_off = 0
for _name, _n in [("ident", 128), ("ones", 128), ("bones", 128), ("tri", 128), ("ntri", 128), ("nones", 128), ("md", 2048),
                  ("maskp", 512), ("maskc", 512)]:
    CO[_name] = (_off, _n)
    _off += _n
NCON = _off


def _consts():
    c = np.zeros((128, NCON), np.float32)
    p = np.arange(128)[:, None]
    m = np.arange(128)[None, :]
    c[:, CO["ident"][0]:CO["ident"][0] + 128] = (p == m)
    c[:, CO["ones"][0]:CO["ones"][0] + 128] = 1.0
    c[:, CO["bones"][0]:CO["bones"][0] + 128] = ((p // 64) == (m // 64))
    c[:, CO["tri"][0]:CO["tri"][0] + 128] = (p >= m)
    c[:, CO["ntri"][0]:CO["ntri"][0] + 128] = -1.0 * (p >= m)
    c[:, CO["nones"][0]:CO["nones"][0] + 128] = -1.0
    t = np.arange(512)[None, :]
    for d in range(4):
        c[:, CO["md"][0] + d * 512:CO["md"][0] + (d + 1) * 512] = ((d * 128 + p) < t)
    c[:, CO["maskp"][0]:CO["maskp"][0] + 512] = np.concatenate([np.tile((p > m), (1, 2)), np.tile((p <= m), (1, 2))], 1)
    c[:, CO["maskc"][0]:CO["maskc"][0] + 512] = np.tile((p <= m), (1, 4))
    return c


def _pcol(v):
    v = np.asarray(v, np.float32)
    return np.ascontiguousarray(v.reshape(-1, 128).T)


def _host_inputs(inp, core):
    f = lambda a: np.ascontiguousarray(np.asarray(a, np.float32))
    b0 = 2 * core
    x = f(inp["x"][b0:b0 + 2])
    xT = np.ascontiguousarray(x.transpose(0, 2, 1)).reshape(2, 8, 128, T)
    pp = np.zeros((128, NPP), np.float32)

    def put(name, arr):
        o, n = PP[name]
        pp[:, o:o + n] = np.asarray(arr, np.float32).reshape(128, n)

    c = f(inp["c"][b0:b0 + 2])
    put("cT", c.reshape(2, 8, 128).transpose(2, 1, 0))
    put("adab", np.stack([_pcol(inp["ada_b"][l]) for l in range(2)], 1))
    put("gmix", np.stack([_pcol(inp["norm_mix_g"][l]) for l in range(2)], 1))
    put("gffn", np.stack([_pcol(inp["norm_ffn_g"][l]) for l in range(2)], 1))
    cw = f(inp["ev_conv_w"][0])
    put("lcw", np.stack([_pcol(cw[k]) for k in range(4)], 2))
    put("lcb", _pcol(inp["ev_conv_b"][0]))
    put("lba", _pcol(inp["ev_ba"][0]))
    put("lbx", _pcol(inp["ev_bx"][0]))
    put("llam", _pcol(inp["ev_lam"][0]))
    put("evqg", np.tile(f(inp["ev_qn_g"][0]), 2)[:, None])
    put("evkg", np.tile(f(inp["ev_kn_g"][0]), 2)[:, None])
    put("odqg", np.tile(f(inp["od_qn_g"][0]), 2)[:, None])
    put("odkg", np.tile(f(inp["od_kn_g"][0]), 2)[:, None])
    put("sinks", np.tile(f(inp["od_sinks"][0])[None, :], (128, 1)))
    fcw = f(inp["ffn_conv_w"])
    put("fcw", np.stack([np.stack([_pcol(fcw[l, k]) for k in range(3)], 2) for l in range(2)], 1))
    put("fcb", np.stack([_pcol(inp["ffn_conv_b"][l]) for l in range(2)], 1))

    def bd(w):
        w = f(w)
        o = np.zeros((128, 4, 128), np.float32)
        for j in range(4):
            o[0:64, j, 0:64] = w[2 * j]
            o[64:128, j, 64:128] = w[2 * j + 1]
        return o

    return {
        "xT": xT, "pp": pp, "consts": _consts(),
        "ada_w": f(inp["ada_w"]),
        "ev_w_in": f(inp["ev_w_in"][0]), "ev_w_out": f(inp["ev_w_out"][0]),
        "od_w_in": f(inp["od_w_in"][0]), "od_w_out": f(inp["od_w_out"][0]),
        "w_gate": f(inp["ffn_w_gate"]), "w_up": f(inp["ffn_w_up"]), "w_down": f(inp["ffn_w_down"]),
        "wabd": bd(inp["ev_wa"][0]), "wxbd": bd(inp["ev_wx"][0]),
    }


def build(nseq=2, dbg=None, stop=None):
    dbg = dbg or set()
    nc = bass.Bass("TRN2", target_bir_lowering=False)
    P = Prog(nc)

    def din(name, shape):
        return nc.dram_tensor(name, list(shape), F32, kind="ExternalInput").ap()

    xT = din("xT", [2, 8, 128, T])
    pp_d = din("pp", [128, NPP])
    con_d = din("consts", [128, NCON])
    ada_w = din("ada_w", [2, D, 6 * D])
    ev_w_in = din("ev_w_in", [D, 2560])
    ev_w_out = din("ev_w_out", [D, D])
    od_w_in = din("od_w_in", [D, 1536])
    od_w_out = din("od_w_out", [D, D])
    w_gate = din("w_gate", [2, D, DFF])
    w_up = din("w_up", [2, D, DFF])
    w_down = din("w_down", [2, DFF, D])
    wabd_d = din("wabd", [128, 4, 128])
    wxbd_d = din("wxbd", [128, 4, 128])
    out_d = nc.dram_tensor("out", [2, 8, 128, T], F32, kind="ExternalOutput").ap()
    dbg_out = {}

    def dump(name, ap_sb, shape, rd):
        if name not in dbg:
            return
        d = nc.dram_tensor("dbg_" + name, list(shape), F32, kind="ExternalOutput").ap()
        dbg_out[name] = d
        P.dma("pool", d, ap_sb, rd=rd, wr=[("dbgout", name)])

    top = ExitStack()

    uid = [0]
    sb_lo = (nc.sbuf_base + 63) // 64 * 64
    free_list = [[sb_lo, nc.sbuf_top]]
    peak = [0]

    def sb(st, name, shape, dt):
        uid[0] += 1
        nbytes = int(np.prod(shape[1:])) * (2 if dt == BF16 else 4)
        nbytes = (nbytes + 63) // 64 * 64
        for seg in free_list:
            if seg[1] - seg[0] >= nbytes:
                off = seg[0]
                seg[0] += nbytes
                break
        else:
            raise RuntimeError(f"SBUF full allocating {name} ({nbytes} B); free={free_list}")
        peak[0] = max(peak[0], off + nbytes)

        def release():
            free_list.append([off, off + nbytes])
            free_list.sort()
            merged = []
            for sg in free_list:
                if sg[0] >= sg[1]:
                    continue
                if merged and merged[-1][1] == sg[0]:
                    merged[-1][1] = sg[1]
                else:
                    merged.append(sg)
            free_list[:] = merged
        st.callback(release)
        return nc.alloc_sbuf_tensor_at(f"{name}_u{uid[0]}", list(shape), dt, offset=off)

    ps = [top.enter_context(nc.psum_tensor(f"ps{i}", [128, 512], F32)) for i in range(8)]
    X = sb(top, "X", [128, 8, T], F32)
    pp = sb(top, "pp", [128, NPP], F32)
    con = sb(top, "con", [128, NCON], BF16)
    modp = sb(top, "modp", [128, 2, 2, 6, 8], F32)
    misc = sb(top, "misc", [128, 64], F32)
    wring = [sb(top, f"wring{i}", [128, 8, 128], BF16) for i in range(8)]
    ring_n = [0]

    def cview(name):
        o, n = CO[name]
        return con[:, o:o + n]

    ident, ones_c, bones, tri = cview("ident"), cview("ones"), cview("bones"), cview("tri")
    ntri, nones = cview("ntri"), cview("nones")
    md_all = cview("md")
    maskp, maskc = cview("maskp"), cview("maskc")

    def ppv(name):
        o, n = PP[name]
        return pp[:, o:o + n]

    def wslot():
        i = ring_n[0] % len(wring)
        ring_n[0] += 1
        return wring[i], ("wring", i)

    def wload(dst, src, key):
        P.dma("pool", dst, src, wr=[key])

    def wcols(w2d, c0, n):
        return w2d[:, c0:c0 + n].rearrange("(kc p) n -> p kc n", p=128)

    def mm_group(out, pairs, rd, wr):
        def fn(e):
            ins = None
            n = len(pairs)
            for i, (l, r) in enumerate(pairs):
                ins = e.matmul(out, lhsT=l, rhs=r, start=(i == 0), stop=(i == n - 1))
            return ins
        P.op("pe", fn, rd=rd, wr=wr)

    def tsl(tt):
        return slice(tt * TS, (tt + 1) * TS)

    Xk = lambda c, tt: ("X", c, tt)
    psk = lambda b: ("ps", b)

    P.dma("sp", pp[:], pp_d, wr=["pp"])
    P.dma("pool", con[:], con_d, wr=["con"])
    with ExitStack() as st:
        cs = sb(st, "cs", [128, 8, 2], BF16)
        wbig = [sb(st, f"wbig{i}", [128, 8, 1024], BF16) for i in range(2)]
        mod = sb(st, "mod", [128, 96, 2], F32)
        tmpa = sb(st, "tmpa", [128, 16], F32)
        o, n = PP["cT"]
        P.op("act", lambda e: e.activation(out=cs[:].rearrange("p k b -> p (k b)"), in_=pp[:, o:o + n], func=AF.Silu),
             rd=["pp"], wr=["cs"])
        for l in range(2):
            for pc in range(6):
                wb = wbig[(l * 6 + pc) % 2]
                wk = ("wbig", (l * 6 + pc) % 2)
                wload(wb[:], wcols(ada_w[l], pc * 1024, 1024), wk)
                for nn in range(8):
                    g = pc * 8 + nn
                    col = (l * 48 + g) * 2
                    mm_group(ps[7][:, col:col + 2],
                             [(wb[:, k, nn * 128:(nn + 1) * 128], cs[:, k, :]) for k in range(8)],
                             rd=[wk, "cs"], wr=[psk(7)])
        P.op("dve", lambda e: e.tensor_tensor(
            out=mod[:], in0=ps[7][:, 0:192].rearrange("p (g b) -> p g b", b=2),
            in1=ppv("adab").unsqueeze(2).broadcast_to([128, 96, 2]), op=ALU.add),
            rd=[psk(7), "pp"], wr=["mod"])
        modv = mod[:].rearrange("p (l j c) b -> p l j c b", l=2, j=6)
        for l in range(2):
            for b in range(2):
                for (dst, jsc, gname) in ((0, 1, "gmix"), (3, 4, "gffn")):
                    go, _ = PP[gname]
                    P.op("dve", lambda e, l=l, b=b, dst=dst, jsc=jsc, go=go: e.scalar_tensor_tensor(
                        out=modp[:, l, b, dst, :], in0=modv[:, l, jsc, :, b], scalar=1.0,
                        in1=pp[:, go + l * 8:go + l * 8 + 8], op0=ALU.add, op1=ALU.mult),
                        rd=["mod", "pp"], wr=["modp"])
                for (dst, j) in ((1, 0), (2, 2), (4, 3), (5, 5)):
                    P.op("dve", lambda e, l=l, b=b, dst=dst, j=j: e.tensor_copy(
                        out=modp[:, l, b, dst, :], in_=modv[:, l, j, :, b]), rd=["mod"], wr=["modp"])
        P.op("act", lambda e: e.activation(out=tmpa[:, 0:4], in_=ppv("llam"), func=AF.Exp, scale=-1.0),
             rd=["pp"], wr=["tmpa"])
        P.op("act", lambda e: e.activation(out=tmpa[:, 4:8], in_=tmpa[:, 0:4], func=AF.Ln, bias=1.0),
             rd=["tmpa"], wr=["tmpa2"])
        P.op("dve", lambda e: e.tensor_scalar(out=misc[:, 0:4], in0=tmpa[:, 4:8], scalar1=-8.0, scalar2=None,
                                              op0=ALU.mult), rd=["tmpa2"], wr=["misc"])
        P.op("dve", lambda e: e.tensor_scalar(out=misc[:, 4:5], in0=ppv("evqg"), scalar1=0.125, scalar2=None,
                                              op0=ALU.mult), rd=["pp"], wr=["misc"])
        P.op("dve", lambda e: e.tensor_scalar(out=misc[:, 5:6], in0=ppv("odqg"), scalar1=0.125, scalar2=None,
                                              op0=ALU.mult), rd=["pp"], wr=["misc"])
        P.op("act", lambda e: e.activation(out=misc[:, 8:24], in_=ppv("sinks"), func=AF.Exp),
             rd=["pp"], wr=["misc"])
        dump("modp", modp[:].rearrange("p l b j c -> p (l b j c)"), [128, 192], rd=["modp"])
        P.barrier()
        P.flush()
    cl = misc[:, 0:4]
    evq8 = misc[:, 4:5]
    odq8 = misc[:, 5:6]
    esink = misc[:, 8:24]

    def do_norm(st, h, l, b, which):
        ia, ish = (0, 1) if which == 0 else (3, 4)
        sq = [sb(st, f"sq{i}", [128, 8, TS], BF16) for i in range(2)]
        sd = [sb(st, f"sd{i}", [128, TS], F32) for i in range(2)]
        rs = [sb(st, f"rs{i}", [128, TS], F32) for i in range(2)]
        tm = [sb(st, f"tm{i}", [128, TS], F32) for i in range(3)]
        ti = 0
        for tt in range(NT):
            i2 = tt % 2
            P.op("act", lambda e, tt=tt, i2=i2: e.activation(out=sq[i2][:], in_=X[:, :, tsl(tt)], func=AF.Square),
                 rd=[Xk(c, tt) for c in range(8)], wr=[("sq", i2)])
            bank = 6 + i2
            mm_group(ps[bank][:], [(ones_c, sq[i2][:, c, :]) for c in range(8)], rd=[("sq", i2), "con"], wr=[psk(bank)])
            P.op("act", lambda e, i2=i2, bank=bank: e.activation(out=sd[i2][:], in_=ps[bank][:], func=AF.Sqrt,
                                                                  scale=1.0 / D, bias=EPS),
                 rd=[psk(bank)], wr=[("sd", i2)])
            P.op("dve", lambda e, i2=i2: e.reciprocal(out=rs[i2][:], in_=sd[i2][:]), rd=[("sd", i2)], wr=[("rs", i2)])
            for c in range(8):
                t3 = ti % 3
                ti += 1
                P.op("dve", lambda e, c=c, tt=tt, i2=i2, t3=t3: e.tensor_tensor(
                    out=tm[t3][:], in0=X[:, c, tsl(tt)], in1=rs[i2][:], op=ALU.mult),
                    rd=[Xk(c, tt), ("rs", i2)], wr=[("tm", t3)])
                P.op("act", lambda e, c=c, tt=tt, t3=t3: e.activation(
                    out=h[:, c, tsl(tt)], in_=tm[t3][:], func=AF.Identity,
                    scale=modp[:, l, b, ia, c:c + 1], bias=modp[:, l, b, ish, c:c + 1]),
                    rd=[("tm", t3), "modp"], wr=[("h", c, tt)])

    def hk_all(tt):
        return [("h", c, tt) for c in range(8)]

    def out_proj_residual(w2d, ysrc, ykeys, l, b, gidx, r0=0, nk=8):
        slots = {}

        def ld(n):
            slots[n] = wslot()
            wload(slots[n][0][:, 0:nk, :],
                  w2d[r0 * 128:(r0 + nk) * 128, n * 128:(n + 1) * 128].rearrange("(kc p) n -> p kc n", p=128), slots[n][1])
        for n in range(3):
            ld(n)
        for n in range(8):
            if n + 3 < 8:
                ld(n + 3)
            ws, wk = slots[n]
            for tt in range(NT):
                bank = (n * NT + tt) % 6
                mm_group(ps[bank][:], [(ws[:, k, :], ysrc(k)[:, tsl(tt)]) for k in range(nk)],
                         rd=[wk] + ykeys(tt), wr=[psk(bank)])
                P.op("dve", lambda e, n=n, tt=tt, bank=bank: e.scalar_tensor_tensor(
                    out=X[:, n, tsl(tt)], in0=ps[bank][:], scalar=modp[:, l, b, gidx, n:n + 1],
                    in1=X[:, n, tsl(tt)], op0=ALU.mult, op1=ALU.add),
                    rd=[psk(bank), Xk(n, tt), "modp"], wr=[Xk(n, tt)])

    def do_ffn(st, h, l, b):
        GL = 2
        act = sb(st, "act", [128, 6, T], BF16)
        wd = [sb(st, f"wd{i}", [128, 6, D], BF16) for i in range(2)]
        gb = [sb(st, f"gb{i}", [128, 2 + T], F32) for i in range(GL + 1)]
        gc = sb(st, "gc", [128, T], F32)
        sg = sb(st, "sg", [128, T], BF16)
        fo, _ = PP["fcw"]
        bo, _ = PP["fcb"]
        for i in range(GL + 1):
            P.op("dve", lambda e, i=i: e.memset(gb[i][:, 0:2], 0.0), wr=[("gbpad", i)])
        wg2, wu2 = w_gate[l], w_up[l]
        slots = {}
        qof = {}
        for qi, (c0, ncq) in enumerate(FQ):
            for ci in range(ncq):
                qof[c0 + ci] = (qi, ci, c0, ncq)

        def load_c(c):
            sg_, sk = wslot()
            wload(sg_[:], wcols(wg2, c * 128, 128), sk)
            su_, uk = wslot()
            wload(su_[:], wcols(wu2, c * 128, 128), uk)
            slots[c] = (sg_, sk, su_, uk)

        def gate_stage(c):
            if c + 1 < NFF:
                load_c(c + 1)
            sg_, sk, su_, uk = slots[c]
            gi = c % (GL + 1)
            for tt in range(NT):
                bank = tt
                mm_group(ps[bank][:], [(sg_[:, k, :], h[:, k, tsl(tt)]) for k in range(8)],
                         rd=[sk] + hk_all(tt), wr=[psk(bank)])
                P.op("act", lambda e, gi=gi, tt=tt, bank=bank: e.activation(
                    out=gb[gi][:, 2 + tt * TS:2 + (tt + 1) * TS], in_=ps[bank][:], func=AF.Identity),
                    rd=[psk(bank)], wr=[("gb", gi, tt)])

        def rest_stage(c):
            qi, ci, c0, ncq = qof[c]
            wdb = wd[qi % 2]
            wdk = ("wd", qi % 2)
            if ci == 0:
                wload(wdb[:, 0:ncq, :],
                      w_down[l][c0 * 128:(c0 + ncq) * 128, :].rearrange("(kc p) n -> p kc n", p=128), wdk)
            sg_, sk, su_, uk = slots.pop(c)
            gi = c % (GL + 1)
            wo = fo + (l * NFF + c) * 3
            P.op("dve", lambda e, gi=gi, wo=wo, c=c: e.tensor_scalar(
                out=gc[:], in0=gb[gi][:, 0:T], scalar1=pp[:, wo:wo + 1],
                scalar2=pp[:, bo + l * NFF + c:bo + l * NFF + c + 1], op0=ALU.mult, op1=ALU.add),
                rd=[("gb", gi, t_) for t_ in range(NT)] + [("gbpad", gi), "pp"], wr=["gc"])
            for k in (1, 2):
                P.op("dve", lambda e, gi=gi, wo=wo, k=k: e.scalar_tensor_tensor(
                    out=gc[:], in0=gb[gi][:, k:k + T], scalar=pp[:, wo + k:wo + k + 1], in1=gc[:],
                    op0=ALU.mult, op1=ALU.add),
                    rd=[("gb", gi, t_) for t_ in range(NT)] + ["gc", "pp"], wr=["gc"])
            P.op("act", lambda e: e.activation(out=sg[:], in_=gc[:], func=AF.Silu), rd=["gc"], wr=["sg"])
            for tt in range(NT):
                bank = 4 + tt % 2
                mm_group(ps[bank][:], [(su_[:, k, :], h[:, k, tsl(tt)]) for k in range(8)],
                         rd=[uk] + hk_all(tt), wr=[psk(bank)])
                P.op("dve", lambda e, ci=ci, tt=tt, bank=bank: e.tensor_tensor(
                    out=act[:, ci, tsl(tt)], in0=sg[:, tsl(tt)], in1=ps[bank][:], op=ALU.mult),
                    rd=["sg", psk(bank)], wr=[("act", ci, tt)])
            if ci == ncq - 1:
                for n in range(8):
                    for tt in range(NT):
                        bank = 6 + (n * NT + tt) % 2
                        mm_group(ps[bank][:], [(wdb[:, cj, n * 128:(n + 1) * 128], act[:, cj, tsl(tt)]) for cj in range(ncq)],
                                 rd=[wdk] + [("act", cj, tt) for cj in range(ncq)], wr=[psk(bank)])
                        P.op("dve", lambda e, n=n, tt=tt, bank=bank: e.scalar_tensor_tensor(
                            out=X[:, n, tsl(tt)], in0=ps[bank][:], scalar=modp[:, l, b, 5, n:n + 1],
                            in1=X[:, n, tsl(tt)], op0=ALU.mult, op1=ALU.add),
                            rd=[psk(bank), Xk(n, tt), "modp"], wr=[Xk(n, tt)])

        load_c(0)
        for i in range(NFF + GL):
            if i < NFF:
                gate_stage(i)
            if i - GL >= 0:
                rest_stage(i - GL)

    qkc = [0]

    def qk_norm_chunk(w2d, col0, dst, dstkey, gain_ap, h, tmp, dup64=False, split=None):
        ws, wk = wslot()
        if dup64:
            for hb in range(2):
                P.dma("pool", ws[:, :, hb * 64:(hb + 1) * 64], wcols(w2d, col0, 64), wr=[wk])
        else:
            wload(ws[:], wcols(w2d, col0, 128), wk)
        sqb, sdb, rsb = tmp
        for tt in range(NT):
            i2 = qkc[0] % 2
            bA = qkc[0] % 4
            bB = 4 + qkc[0] % 2
            qkc[0] += 1
            mm_group(ps[bA][:], [(ws[:, k, :], h[:, k, tsl(tt)]) for k in range(8)], rd=[wk] + hk_all(tt), wr=[psk(bA)])
            P.op("act", lambda e, i2=i2, bA=bA: e.activation(out=sqb[i2][:], in_=ps[bA][:], func=AF.Square),
                 rd=[psk(bA)], wr=[("sqb", i2)])
            mm_group(ps[bB][:], [(bones, sqb[i2][:])], rd=[("sqb", i2), "con"], wr=[psk(bB)])
            P.op("act", lambda e, i2=i2, bB=bB: e.activation(out=sdb[i2][:], in_=ps[bB][:], func=AF.Sqrt,
                                                              scale=1.0 / 64, bias=EPS),
                 rd=[psk(bB)], wr=[("sdb", i2)])
            P.op("dve", lambda e, i2=i2: e.reciprocal(out=rsb[i2][:], in_=sdb[i2][:]), rd=[("sdb", i2)], wr=[("rsb", i2)])
            if split is None:
                P.op("dve", lambda e, i2=i2, bA=bA, tt=tt: e.scalar_tensor_tensor(
                    out=dst[:, tsl(tt)], in0=ps[bA][:], scalar=gain_ap, in1=rsb[i2][:], op0=ALU.mult, op1=ALU.mult),
                    rd=[psk(bA), ("rsb", i2), "misc", "pp"], wr=[(dstkey, tt)])
            else:
                for hh in range(2):
                    pr = slice(hh * 64, (hh + 1) * 64)
                    P.op("dve", lambda e, i2=i2, bA=bA, tt=tt, pr=pr, hh=hh: e.scalar_tensor_tensor(
                        out=split[hh][pr, tsl(tt)], in0=ps[bA][pr, :], scalar=gain_ap[pr, :], in1=rsb[i2][pr, :],
                        op0=ALU.mult, op1=ALU.mult),
                        rd=[psk(bA), ("rsb", i2), "misc", "pp"], wr=[(dstkey, hh, tt)])

    for s in range(nseq):
        b = s
        for c in range(8):
            P.dma("sp", X[:, c, :], xT[s, c], wr=[Xk(c, tt) for tt in range(NT)])
        l = 0
        with ExitStack() as st0:
            sty = ExitStack()
            ya = sb(sty, "ya", [128, 4, T], BF16)
            with ExitStack() as st1:
                h = sb(st1, "h", [128, 8, T], BF16)
                with ExitStack() as st2:
                    do_norm(st2, h, l, b, 0)
                    if s == 0:
                        dump("h0", h[:].rearrange("p c t -> p (c t)"), [128, 8 * T],
                             rd=[("h", c, tt) for c in range(8) for tt in range(NT)])
                    P.barrier()
                    P.flush()
                if stop == "norm0":
                    break
                with ExitStack() as st2:
                    xr = sb(st2, "xr", [128, 3 + T], F32)
                    xc = sb(st2, "xc", [128, T], F32)
                    xcb = sb(st2, "xcb", [128, T], BF16)
                    ra = sb(st2, "ra", [128, T], F32)
                    ig = sb(st2, "ig", [128, T], F32)
                    s2 = sb(st2, "s2", [128, T], F32)
                    gel = sb(st2, "gel", [128, T], F32)
                    gx = sb(st2, "gx", [128, T], F32)
                    wabd = sb(st2, "wabd", [128, 4, 128], BF16)
                    wxbd = sb(st2, "wxbd", [128, 4, 128], BF16)
                    P.dma("pool", wabd[:], wabd_d, wr=["wabd"])
                    P.dma("pool", wxbd[:], wxbd_d, wr=["wxbd"])
                    P.op("dve", lambda e: e.memset(xr[:, 0:3], 0.0), wr=["xrpad"])
                    lcw, _ = PP["lcw"]
                    lcb, _ = PP["lcb"]
                    lba, _ = PP["lba"]
                    lbx, _ = PP["lbx"]
                    import os as _os
                    for j in [int(v) for v in _os.environ.get("LRU_CHUNKS", "0,1,2,3").split(",")]:
                        wsx, wkx = wslot()
                        wload(wsx[:], wcols(ev_w_in, j * 128, 128), wkx)
                        wsg, wkg = wslot()
                        wload(wsg[:], wcols(ev_w_in, 512 + j * 128, 128), wkg)
                        for tt in range(NT):
                            bank = tt % 2
                            mm_group(ps[bank][:], [(wsx[:, k, :], h[:, k, tsl(tt)]) for k in range(8)],
                                     rd=[wkx] + hk_all(tt), wr=[psk(bank)])
                            P.op("act", lambda e, tt=tt, bank=bank: e.activation(
                                out=xr[:, 3 + tt * TS:3 + (tt + 1) * TS], in_=ps[bank][:], func=AF.Identity),
                                rd=[psk(bank)], wr=[("xr", tt)])
                        xrk = [("xr", t_) for t_ in range(NT)]
                        P.op("dve", lambda e, j=j: e.tensor_scalar(
                            out=xc[:], in0=xr[:, 0:T], scalar1=pp[:, lcw + j * 4:lcw + j * 4 + 1],
                            scalar2=pp[:, lcb + j:lcb + j + 1], op0=ALU.mult, op1=ALU.add),
                            rd=xrk + ["xrpad", "pp"], wr=["xc"])
                        for k in (1, 2, 3):
                            P.op("dve", lambda e, j=j, k=k: e.scalar_tensor_tensor(
                                out=xc[:], in0=xr[:, k:k + T], scalar=pp[:, lcw + j * 4 + k:lcw + j * 4 + k + 1],
                                in1=xc[:], op0=ALU.mult, op1=ALU.add), rd=xrk + ["xc", "pp"], wr=["xc"])
                        P.op("act", lambda e: e.activation(out=xcb[:], in_=xc[:], func=AF.Identity), rd=["xc"], wr=["xcb"])
                        for tt in range(NT):
                            bank = 2 + tt % 2
                            mm_group(ps[bank][:], [(wabd[:, j, :], xcb[:, tsl(tt)])], rd=["wabd", "xcb"], wr=[psk(bank)])
                            P.op("act", lambda e, j=j, tt=tt, bank=bank: e.activation(
                                out=ra[:, tsl(tt)], in_=ps[bank][:], func=AF.Sigmoid, bias=pp[:, lba + j:lba + j + 1]),
                                rd=[psk(bank), "pp"], wr=[("ra", tt)])
                            bank2 = 4 + tt % 2
                            mm_group(ps[bank2][:], [(wxbd[:, j, :], xcb[:, tsl(tt)])], rd=["wxbd", "xcb"], wr=[psk(bank2)])
                            P.op("act", lambda e, j=j, tt=tt, bank2=bank2: e.activation(
                                out=ig[:, tsl(tt)], in_=ps[bank2][:], func=AF.Sigmoid, bias=pp[:, lbx + j:lbx + j + 1]),
                                rd=[psk(bank2), "pp"], wr=[("ig", tt)])
                        rak = [("ra", t_) for t_ in range(NT)]
                        igk = [("ig", t_) for t_ in range(NT)]
                        P.op("act", lambda e, j=j: e.activation(out=ra[:], in_=ra[:], func=AF.Exp, scale=cl[:, j:j + 1]),
                             rd=rak + ["misc"], wr=rak)
                        P.op("act", lambda e: e.activation(out=s2[:], in_=ra[:], func=AF.Square), rd=rak, wr=["s2"])
                        P.op("act", lambda e: e.activation(out=s2[:], in_=s2[:], func=AF.Sqrt, scale=-1.0, bias=1.0),
                             rd=["s2"], wr=["s2"])
                        P.op("dve", lambda e: e.tensor_tensor(out=s2[:], in0=s2[:], in1=ig[:], op=ALU.mult),
                             rd=["s2"] + igk, wr=["s2"])
                        P.op("dve", lambda e: e.tensor_tensor(out=s2[:], in0=s2[:], in1=xc[:], op=ALU.mult),
                             rd=["s2", "xc"], wr=["s2"])
                        P.op("dve", lambda e: e.tensor_tensor_scan(out=xc[:], data0=ra[:], data1=s2[:], initial=0.0,
                                                                    op0=ALU.mult, op1=ALU.add),
                             rd=rak + ["s2", "xc"], wr=["xc"])
                        for tt in range(NT):
                            bank = 6 + tt % 2
                            mm_group(ps[bank][:], [(wsg[:, k, :], h[:, k, tsl(tt)]) for k in range(8)],
                                     rd=[wkg] + hk_all(tt), wr=[psk(bank)])
                            P.op("act", lambda e, tt=tt, bank=bank: e.activation(
                                out=gx[:, tsl(tt)], in_=ps[bank][:], func=AF.Identity),
                                rd=[psk(bank)], wr=[("gx", tt)])
                        gxk = [("gx", t_) for t_ in range(NT)]
                        P.op("act", lambda e: e.activation(out=gel[:], in_=gx[:], func=AF.Square), rd=gxk, wr=["gel"])
                        P.op("dve", lambda e: e.tensor_scalar(out=gel[:], in0=gel[:], scalar1=0.044715, scalar2=1.0,
                                                              op0=ALU.mult, op1=ALU.add), rd=["gel"], wr=["gel"])
                        P.op("dve", lambda e: e.tensor_tensor(out=gel[:], in0=gel[:], in1=gx[:], op=ALU.mult),
                             rd=["gel"] + gxk, wr=["gel"])
                        P.op("act", lambda e: e.activation(out=gel[:], in_=gel[:], func=AF.Sigmoid, scale=1.5957691216057308),
                             rd=["gel"], wr=["gel"])
                        P.op("dve", lambda e: e.tensor_tensor(out=gel[:], in0=gel[:], in1=gx[:], op=ALU.mult),
                             rd=["gel"] + gxk, wr=["gel"])
                        P.op("dve", lambda e, j=j: e.tensor_tensor(out=ya[:, j, :], in0=xc[:], in1=gel[:], op=ALU.mult),
                             rd=["xc", "gel"], wr=[("ya", j)])
                    if s == 0:
                        dump("lxc", xc[:], [128, T], rd=["xc"])
                        dump("lra", ra[:], [128, T], rd=[("ra", t_) for t_ in range(NT)])
                        dump("lig", ig[:], [128, T], rd=[("ig", t_) for t_ in range(NT)])
                        dump("ls2", s2[:], [128, T], rd=["s2"])
                        dump("ya", ya[:].rearrange("p c t -> p (c t)"), [128, 4 * T], rd=[("ya", j) for j in range(4)])
                    P.barrier()
                    P.flush()
                if stop == "lru":
                    sty.close()
                    break
                out_proj_residual(ev_w_out, lambda k: ya[:, k, :], lambda tt: [("ya", j) for j in range(4)], l, b, 2, r0=0, nk=4)
                P.barrier()
                P.flush()
                sty.close()
                qz = [sb(st0, f"qz{i}", [128, 4, T], BF16) for i in range(2)]
                kn = sb(st0, "kn", [128, 4, T], BF16)
                vt = sb(st0, "vt", [128, 16, 512], BF16)
                P.op("dve", lambda e: e.memset(qz[0][64:128, :, :], 0.0), wr=[("qzpad", 0)])
                P.op("dve", lambda e: e.memset(qz[1][0:64, :, :], 0.0), wr=[("qzpad", 1)])
                with ExitStack() as st2:
                    sqb = [sb(st2, f"sqb{i}", [128, TS], BF16) for i in range(2)]
                    sdb = [sb(st2, f"sdb{i}", [128, TS], F32) for i in range(2)]
                    rsb = [sb(st2, f"rsb{i}", [128, TS], F32) for i in range(2)]
                    for j in range(4):
                        qk_norm_chunk(ev_w_in, 1024 + j * 128, None, ("qz", j), evq8, h, (sqb, sdb, rsb),
                                      split=(qz[0][:, j, :], qz[1][:, j, :]))
                        qk_norm_chunk(ev_w_in, 1536 + j * 128, kn[:, j, :], ("kn", j), ppv("evkg"), h, (sqb, sdb, rsb))
                    wvs = []
                    for q4 in range(4):
                        ws_, wk_ = wslot()
                        wload(ws_[:], wcols(ev_w_in, 2048 + q4 * 128, 128), wk_)
                        wvs.append((ws_, wk_))
                    for blk in range(16):
                        bank = 6 + blk % 2

                        def vproj(e, blk=blk, bank=bank):
                            ins = None
                            for q4 in range(4):
                                for k in range(8):
                                    ins = e.matmul(ps[bank][:, q4 * 128:(q4 + 1) * 128], lhsT=h[:, k, blk * 128:(blk + 1) * 128],
                                                   rhs=wvs[q4][0][:, k, :], start=(k == 0), stop=(k == 7))
                            return ins
                        P.op("pe", vproj, rd=[w_[1] for w_ in wvs] + hk_all(blk // 4), wr=[psk(bank)])
                        if blk % 2 == 0:
                            P.op("act", lambda e, blk=blk, bank=bank: e.activation(out=vt[:, blk, :], in_=ps[bank][:],
                                                                                   func=AF.Identity),
                                 rd=[psk(bank)], wr=[("vt", blk)])
                        else:
                            P.op("dve", lambda e, blk=blk, bank=bank: e.tensor_copy(out=vt[:, blk, :], in_=ps[bank][:]),
                                 rd=[psk(bank)], wr=[("vt", blk)])
                    if s == 0:
                        dump("kn", kn[:].rearrange("p c t -> p (c t)"), [128, 4 * T],
                             rd=[(("kn", j), t_) for j in range(4) for t_ in range(NT)])
                        dump("vt", vt[:].rearrange("p c t -> p (c t)"), [128, 16 * 512], rd=[("vt", k) for k in range(16)])
                    P.barrier()
                    P.flush()
            if stop in ("norm0", "lru", "sbproj"):
                break
            yb = sb(st0, "yb", [128, 4, T], BF16)
            with ExitStack() as st2:
                eb = [sb(st2, f"eb{i}", [128, TS], F32) for i in range(3)]
                LOOK = 2
                NSP, NRB = LOOK + 3, LOOK + 2
                spb = [sb(st2, f"spb{i}", [128, TS], BF16) for i in range(NSP)]
                Rb = [sb(st2, f"Rb{i}", [128, TS], BF16) for i in range(NRB)]
                wb_ = [sb(st2, f"wb{i}", [128, TS], BF16) for i in range(3)]
                mo, _ = CO["md"]
                tiles = []
                for j in range(4):
                    for tt in range(NT):
                        for hh in range(2):
                            ob = 4 + ((j * NT + tt) * 2 + hh) % 4
                            for idx, kb in enumerate(range(4 * tt + 3, -1, -1)):
                                tiles.append(dict(j=j, tt=tt, hh=hh, kb=kb, idx=idx, ob=ob, R=None))
                cnt_ = dict(z=0, e=0, sp=0, r=0, rb=0, w=0)

                def stage_a(ti):
                    t = tiles[ti]
                    j, tt, hh, kb, idx = t["j"], t["tt"], t["hh"], t["kb"], t["idx"]
                    p0 = hh * 64
                    dz = kb - 4 * tt
                    zb = cnt_["z"] % 2
                    cnt_["z"] += 1
                    mm_group(ps[zb][:], [(kn[:, j, kb * 128:(kb + 1) * 128], qz[hh][:, j, tsl(tt)])],
                             rd=[(("kn", j), kb // 4), (("qz", j), hh, tt), ("qzpad", hh)], wr=[psk(zb)])
                    ei = cnt_["e"] % 3
                    cnt_["e"] += 1
                    P.op("act", lambda e, ei=ei, zb=zb: e.activation(out=eb[ei][:], in_=ps[zb][:], func=AF.Exp),
                         rd=[psk(zb)], wr=[("eb", ei)])
                    si = cnt_["sp"] % NSP
                    cnt_["sp"] += 1
                    t["si"] = si
                    P.op("act", lambda e, ei=ei, si=si: e.activation(out=spb[si][:], in_=eb[ei][:], func=AF.Ln, bias=1.0),
                         rd=[("eb", ei)], wr=[("spb", si)])
                    if dz >= 0:
                        P.op("dve", lambda e, si=si, dz=dz: e.tensor_tensor(
                            out=spb[si][:], in0=spb[si][:], in1=con[:, mo + dz * 512:mo + (dz + 1) * 512], op=ALU.mult),
                            rd=[("spb", si), "con"], wr=[("spb", si)])
                    if kb > 0:
                        nt_ = tiles[ti + 1]
                        if idx == 0:
                            nt_["R"] = (spb[si], ("spb", si))
                        else:
                            rn = cnt_["rb"] % NRB
                            cnt_["rb"] += 1
                            rsrc, rkey = t["R"]
                            P.op("dve", lambda e, si=si, rsrc=rsrc, rn=rn: e.tensor_tensor(
                                out=Rb[rn][:], in0=rsrc[:], in1=spb[si][:], op=ALU.add),
                                rd=[("spb", si), rkey], wr=[("Rb", rn)])
                            nt_["R"] = (Rb[rn], ("Rb", rn))

                def stage_b(ti):
                    t = tiles[ti]
                    j, tt, hh, kb, idx, ob, si = t["j"], t["tt"], t["hh"], t["kb"], t["idx"], t["ob"], t["si"]
                    p0 = hh * 64
                    dz = kb - 4 * tt
                    rb = 2 + cnt_["r"] % 2
                    cnt_["r"] += 1
                    pairs = [(ntri, spb[si][:])]
                    rdk = [("spb", si), "con", (("kn", j), kb // 4), (("qz", j), hh, tt), ("qzpad", hh)]
                    if t["R"] is not None:
                        pairs.append((nones, t["R"][0][:]))
                        rdk.append(t["R"][1])
                    pairs.append((kn[:, j, kb * 128:(kb + 1) * 128], qz[hh][:, j, tsl(tt)]))
                    mm_group(ps[rb][:], pairs, rd=rdk, wr=[psk(rb)])
                    wi = cnt_["w"] % 3
                    cnt_["w"] += 1
                    P.op("act", lambda e, wi=wi, rb=rb: e.activation(out=wb_[wi][:], in_=ps[rb][:], func=AF.Exp),
                         rd=[psk(rb)], wr=[("wb", wi)])
                    if dz >= 0:
                        P.op("dve", lambda e, wi=wi, dz=dz: e.tensor_tensor(
                            out=wb_[wi][:], in0=wb_[wi][:], in1=con[:, mo + dz * 512:mo + (dz + 1) * 512], op=ALU.mult),
                            rd=[("wb", wi), "con"], wr=[("wb", wi)])

                    def pv(e, ob=ob, kb=kb, j=j, wi=wi, first=(idx == 0), last=(kb == 0)):
                        return e.matmul(ps[ob][:], lhsT=vt[:, kb, j * 128:(j + 1) * 128],
                                        rhs=wb_[wi][:], start=first, stop=last)
                    P.op("pe", pv, rd=[("wb", wi), ("vt", kb)], wr=[("pso", ob)])
                    if kb == 0:
                        P.op("act", lambda e, j=j, tt=tt, ob=ob, p0=p0: e.activation(
                            out=yb[p0:p0 + 64, j, tsl(tt)], in_=ps[ob][p0:p0 + 64, :], func=AF.Identity),
                            rd=[("pso", ob)], wr=[("yb", j, tt, hh)])

                ntl = len(tiles)
                for ti in range(ntl + LOOK):
                    if ti < ntl:
                        stage_a(ti)
                    if ti - LOOK >= 0:
                        stage_b(ti - LOOK)
                if s == 0:
                    dump("yb", yb[:].rearrange("p c t -> p (c t)"), [128, 4 * T],
                         rd=[("yb", j, t_, hh) for j in range(4) for t_ in range(NT) for hh in range(2)])
                P.barrier()
                P.flush()
            if stop == "sb":
                break
            out_proj_residual(ev_w_out, lambda k: yb[:, k, :],
                              lambda tt: [("yb", j, tt, hh) for j in range(4) for hh in range(2)], l, b, 2, r0=4, nk=4)
            P.barrier()
            P.flush()
        if s == 0:
            dump("x0mid", X[:].rearrange("p c t -> p (c t)"), [128, 8 * T], rd=[Xk(c, tt) for c in range(8) for tt in range(NT)])
        if stop == "mix0":
            break
        with ExitStack() as st1:
            h = sb(st1, "h", [128, 8, T], BF16)
            with ExitStack() as st2:
                do_norm(st2, h, l, b, 1)
                P.barrier()
                P.flush()
            with ExitStack() as st2:
                do_ffn(st2, h, l, b)
                P.barrier()
                P.flush()
        if s == 0:
            dump("x1", X[:].rearrange("p c t -> p (c t)"), [128, 8 * T], rd=[Xk(c, tt) for c in range(8) for tt in range(NT)])
        if stop == "l0":
            break
        l = 1
        with ExitStack() as st0:
            with ExitStack() as st1:
                h = sb(st1, "h", [128, 8, T], BF16)
                with ExitStack() as st2:
                    do_norm(st2, h, l, b, 0)
                    P.barrier()
                    P.flush()
                qn = sb(st0, "qn1", [128, 8, T], BF16)
                kd = sb(st0, "kd", [128, 4, T], BF16)
                va = sb(st0, "va", [128, 16, 4, 65], BF16)
                with ExitStack() as st2:
                    sqb = [sb(st2, f"sqb{i}", [128, TS], BF16) for i in range(2)]
                    sdb = [sb(st2, f"sdb{i}", [128, TS], F32) for i in range(2)]
                    rsb = [sb(st2, f"rsb{i}", [128, TS], F32) for i in range(2)]
                    wv = sb(st2, "wv", [128, 8, 512], BF16)
                    for c in range(8):
                        qk_norm_chunk(od_w_in, c * 128, qn[:, c, :], ("qn", c), odq8, h, (sqb, sdb, rsb))
                    for g in range(4):
                        qk_norm_chunk(od_w_in, 1024 + g * 64, kd[:, g, :], ("kd", g), ppv("odkg"), h, (sqb, sdb, rsb), dup64=True)
                    P.op("dve", lambda e: e.memset(va[:, :, :, 64:65], 1.0), wr=["vaones"])
                    wload(wv[:, :, 0:256], wcols(od_w_in, 1280, 256), "wv")
                    for blk in range(16):
                        bank = 4 + blk % 4
                        mm_group(ps[bank][:, 0:256], [(h[:, k, blk * 128:(blk + 1) * 128], wv[:, k, 0:256]) for k in range(8)],
                                 rd=["wv"] + hk_all(blk // 4), wr=[psk(bank)])
                        P.op("act" if blk % 2 == 0 else "dve",
                             (lambda e, blk=blk, bank=bank: e.activation(
                                 out=va[:, blk, :, 0:64], in_=ps[bank][:, 0:256].rearrange("p (g d) -> p g d", g=4), func=AF.Identity))
                             if blk % 2 == 0 else
                             (lambda e, blk=blk, bank=bank: e.tensor_copy(
                                 out=va[:, blk, :, 0:64], in_=ps[bank][:, 0:256].rearrange("p (g d) -> p g d", g=4))),
                             rd=[psk(bank)], wr=[("va", blk)])
                    P.barrier()
                    P.flush()
            if stop == "l1proj":
                break
            yT = sb(st0, "yT", [128, 8, T], BF16)
            with ExitStack() as st2:
                pb = [sb(st2, f"pb{i}", [128, 2, TS], BF16) for i in range(2)]
                den = [sb(st2, f"den{i}", [128, 4], F32) for i in range(2)]
                ytok = [sb(st2, f"ytok{i}", [128, D], BF16) for i in range(2)]
                units = [(qb, g) for qb in range(16) for g in range(4)]

                def swa_a(u):
                    qb, g = units[u]
                    pi = u % 2
                    kbs = [qb - 1, qb] if qb > 0 else [qb]
                    c0 = 0 if qb > 0 else 256
                    for hb in range(2):
                        sbank = hb + 2 * pi

                        def sc(e, sbank=sbank, g=g, kbs=kbs, qb=qb, hb=hb):
                            ins = None
                            for kb in kbs:
                                which = 0 if kb == qb - 1 else 1
                                ins = e.matmul(ps[sbank][:, which * 256:(which + 1) * 256],
                                               lhsT=kd[hb * 64:(hb + 1) * 64, g, kb * 128:(kb + 1) * 128],
                                               rhs=qn[hb * 64:(hb + 1) * 64, 2 * g:2 * g + 2, qb * 128:(qb + 1) * 128],
                                               start=True, stop=True)
                            return ins
                        P.op("pe", sc, rd=[(("kd", g), kb // 4) for kb in kbs] + [(("qn", 2 * g), qb // 4), (("qn", 2 * g + 1), qb // 4)],
                             wr=[psk(sbank)])
                        P.op("act", lambda e, pi=pi, hb=hb, sbank=sbank, c0=c0: e.activation(
                            out=pb[pi][:, hb, c0:512], in_=ps[sbank][:, c0:512], func=AF.Exp),
                            rd=[psk(sbank)], wr=[("pb", pi, hb)])
                        P.op("dve", lambda e, pi=pi, hb=hb, c0=c0: e.tensor_tensor(
                            out=pb[pi][:, hb, c0:512], in0=pb[pi][:, hb, c0:512], in1=maskp[:, c0:512], op=ALU.mult),
                            rd=[("pb", pi, hb), "con"], wr=[("pb", pi, hb)])

                def swa_b(u):
                    qb, g = units[u]
                    pi = u % 2
                    yi = qb % 2
                    kbs = [qb - 1, qb] if qb > 0 else [qb]
                    ybank = 4 + pi

                    def pvm(e, ybank=ybank, pi=pi, kbs=kbs, qb=qb, g=g):
                        ins = None
                        for hc in range(4):
                            hb, e_ = hc // 2, hc % 2
                            for i_, kb in enumerate(kbs):
                                which = 0 if kb == qb - 1 else 1
                                ins = e.matmul(ps[ybank][:, hc * 65:(hc + 1) * 65],
                                               lhsT=pb[pi][:, hb, which * 256 + e_ * 128:which * 256 + (e_ + 1) * 128],
                                               rhs=va[:, kb, g, :], start=(i_ == 0), stop=(i_ == len(kbs) - 1))
                        return ins
                    P.op("pe", pvm, rd=[("pb", pi, 0), ("pb", pi, 1)] + [("va", kb) for kb in kbs] + ["vaones"], wr=[psk(ybank)])
                    yv = ps[ybank][:, 0:260].rearrange("p (hb e d) -> p hb e d", hb=2, e=2)
                    P.op("dve", lambda e, pi=pi, yv=yv, g=g: e.tensor_tensor(
                        out=den[pi][:].rearrange("p (hb e) -> p hb e", hb=2),
                        in0=yv[:, :, :, 64],
                        in1=esink[:, 4 * g:4 * g + 4].rearrange("p (e hb) -> p hb e", hb=2), op=ALU.add),
                        rd=[psk(ybank), "misc"], wr=[("den", pi)])
                    P.op("dve", lambda e, pi=pi: e.reciprocal(out=den[pi][:], in_=den[pi][:]), rd=[("den", pi)], wr=[("den", pi)])
                    P.op("dve", lambda e, pi=pi, yv=yv, g=g, yi=yi: e.tensor_tensor(
                        out=ytok[yi][:, g * 256:(g + 1) * 256].rearrange("p (e hb d) -> p hb e d", e=2, hb=2),
                        in0=yv[:, :, :, 0:64],
                        in1=den[pi][:].rearrange("p (hb e) -> p hb e", hb=2).unsqueeze(3).broadcast_to([128, 2, 2, 64]),
                        op=ALU.mult),
                        rd=[psk(ybank), ("den", pi)], wr=[("ytok", yi, g)])
                    if g == 3:
                        tbank = 6 + yi
                        tp = ps[tbank][:].bitcast(BF16)

                        def trn(e, tp=tp, yi=yi):
                            ins = None
                            for c in range(8):
                                ins = e.transpose(out=tp[:, c * 128:(c + 1) * 128], in_=ytok[yi][:, c * 128:(c + 1) * 128], identity=ident)
                            return ins
                        P.op("pe", trn, rd=[("ytok", yi, g_) for g_ in range(4)] + ["con"], wr=[psk(tbank)])
                        P.op("act", lambda e, tp=tp, qb=qb: e.activation(
                            out=yT[:, :, qb * 128:(qb + 1) * 128], in_=tp.rearrange("p (c t) -> p c t", c=8), func=AF.Identity),
                            rd=[psk(tbank)], wr=[("yT", qb // 4)])

                nun = len(units)
                for u in range(nun + 1):
                    if u < nun:
                        swa_a(u)
                    if u >= 1:
                        swa_b(u - 1)
                if s == 0:
                    dump("yT", yT[:].rearrange("p c t -> p (c t)"), [128, 8 * T], rd=[("yT", t_) for t_ in range(NT)])
                P.barrier()
                P.flush()
            if stop == "swa":
                break
            out_proj_residual(od_w_out, lambda k: yT[:, k, :], lambda tt: [("yT", tt)], l, b, 2)
            P.barrier()
            P.flush()
        if s == 0:
            dump("x1mid", X[:].rearrange("p c t -> p (c t)"), [128, 8 * T], rd=[Xk(c, tt) for c in range(8) for tt in range(NT)])
        with ExitStack() as st1:
            h = sb(st1, "h", [128, 8, T], BF16)
            with ExitStack() as st2:
                do_norm(st2, h, l, b, 1)
                P.barrier()
                P.flush()
            with ExitStack() as st2:
                do_ffn(st2, h, l, b)
                P.barrier()
                P.flush()
        for c in range(8):
            P.dma("sp", out_d[s, c], X[:, c, :], rd=[Xk(c, tt) for tt in range(NT)], wr=[("out", s, c)])
        P.flush()

    P.barrier()
    P.op("sp", None)
    P.flush()
    top.close()
    return nc, P, dbg_out


_CACHE = {}


def kernel(**inputs):
    if "nc" not in _CACHE:
        _CACHE["nc"] = build()[0]
    nc = _CACHE["nc"]
    in_maps = [_host_inputs(inputs, core) for core in range(NCORES)]
    res = run_bass_kernel_spmd(nc, in_maps, core_ids=list(range(NCORES)))
    outs = []
    for core in range(NCORES):
        o = np.asarray(res.results[core]["out"], np.float32).reshape(2, D, T)
        outs.append(o.transpose(0, 2, 1))
    return np.ascontiguousarray(np.concatenate(outs, axis=0)).astype(np.float32)
```

```python
from contextlib import ExitStack

import numpy as np
import concourse.bass as bass
import concourse.mybir as mybir
from concourse.bass_utils import run_bass_kernel_spmd

F32 = mybir.dt.float32
BF16 = mybir.dt.bfloat16
AF = mybir.ActivationFunctionType
ALU = mybir.AluOpType

NCORES = 8
T = 2048
NT = 4
TS = 512
D = 1024
DFF = 2816
NFF = 22
EPS = 1e-6
FQ = [(0, 6), (6, 6), (12, 5), (17, 5)]


class Prog:
    NRING = 8

    def __init__(self, nc):
        self.nc = nc
        self.engs = {"pe": nc.tensor, "act": nc.scalar, "dve": nc.vector,
                     "pool": nc.gpsimd, "sp": nc.sync}
        self.ops = []
        self.nflushed = 0
        self.last_w = {}
        self.readers = {}
        self.last_on_eng = {}
        self.dmas_since_barrier = []
        self.barrier_deps = set()
        self.dma_hist = {}
        self.sems = {e: nc.alloc_semaphore("c_" + e) for e in self.engs}
        self.rings = {}
        self.cnt = {e: 0 for e in self.engs}
        self.done = []
        self.waited = {e: {} for e in self.engs}
        self.nwaits = 0

    limit = None

    def op(self, eng, fn, rd=(), wr=(), dma=False):
        i = len(self.ops)
        if self.limit is not None and i >= self.limit and not dma and fn is not None:
            return None
        o = dict(eng=eng, fn=fn, dma=dma)
        ops = self.ops
        d = set()
        raw = set()
        for k in rd:
            j = self.last_w.get(k)
            if j is not None:
                d.add(j)
                raw.add(j)
        for k in wr:
            j = self.last_w.get(k)
            if j is not None:
                d.add(j)
            d.update(self.readers.get(k, ()))
        keep = set()
        for j in d:
            oj = ops[j]
            if (not oj["dma"]) and (not dma) and oj["eng"] == eng and eng == "pe":
                continue
            keep.add(j)
        for j in self.barrier_deps:
            oj = ops[j]
            if (not oj["dma"]) and (not dma) and oj["eng"] == eng:
                continue
            keep.add(j)
        for k in rd:
            self.readers.setdefault(k, []).append(i)
        for k in wr:
            self.last_w[k] = i
            self.readers[k] = []
        if dma:
            hist = self.dma_hist.setdefault(eng, [])
            c = len(hist)
            if eng not in self.rings:
                self.rings[eng] = [self.nc.alloc_semaphore(f"d_{eng}{r}") for r in range(self.NRING)]
            o["ring"] = c % self.NRING
            o["rval"] = 16 * (c // self.NRING + 1)
            if c >= self.NRING:
                keep.add(hist[c - self.NRING])
            hist.append(i)
            self.dmas_since_barrier.append(i)
        else:
            self.last_on_eng[eng] = i
        o["deps"] = keep
        ops.append(o)
        self.done.append(None)
        return i

    def dma(self, eng, out, in_, rd=(), wr=()):
        return self.op(eng, lambda e: e.dma_start(out=out, in_=in_), rd, wr, dma=True)

    def barrier(self):
        self.barrier_deps = set(self.last_on_eng.values()) | set(self.dmas_since_barrier)
        self.dmas_since_barrier = []

    def flush(self):
        ops = self.ops
        n = len(ops)
        start = self.nflushed
        needs = set()
        for i in range(start, n):
            needs.update(ops[i]["deps"])
        needs.update(self.last_on_eng.values())
        needs.update(self.last_w.values())
        for r in self.readers.values():
            needs.update(r)
        needs.update(self.barrier_deps)
        for i in range(start, n):
            o = ops[i]
            e = o["eng"]
            eng = self.engs[e]
            need = {}
            for j in o["deps"]:
                s, v = self.done[j]
                k = id(s)
                if k not in need or need[k][1] < v:
                    need[k] = (s, v)
            for k, (s, v) in need.items():
                if self.waited[e].get(k, 0) >= v:
                    continue
                eng.wait_ge(s, v)
                self.nwaits += 1
                self.waited[e][k] = v
            if o["fn"] is None:
                self.done[i] = (self.sems[e], self.cnt[e])
                continue
            ins = o["fn"](eng)
            if o["dma"]:
                s = self.rings[e][o["ring"]]
                ins.then_inc(s, 16)
                self.done[i] = (s, o["rval"])
            elif i in needs:
                self.cnt[e] += 1
                ins.then_inc(self.sems[e], 1)
                self.done[i] = (self.sems[e], self.cnt[e])
            else:
                self.done[i] = (self.sems[e], self.cnt[e] + 1)
            o["fn"] = None
        self.nflushed = n


PP = {}
_off = 0
for _name, _n in [("cT", 16), ("adab", 96), ("gmix", 16), ("gffn", 16), ("lcw", 16), ("lcb", 4),
                  ("lba", 4), ("lbx", 4), ("llam", 4), ("evqg", 1), ("evkg", 1), ("odqg", 1),
                  ("odkg", 1), ("sinks", 16), ("fcw", 132), ("fcb", 44)]:
    PP[_name] = (_off, _n)
    _off += _n
NPP = _off

CO = {}
_off = 0
for _name, _n in [("ident", 128), ("ones", 128), ("bones", 128), ("tri", 128), ("ntri", 128), ("nones", 128), ("md", 2048),
                  ("maskp", 512), ("maskc", 512)]:
    CO[_name] = (_off, _n)
    _off += _n
NCON = _off


def _consts():
    c = np.zeros((128, NCON), np.float32)
    p = np.arange(128)[:, None]
    m = np.arange(128)[None, :]
    c[:, CO["ident"][0]:CO["ident"][0] + 128] = (p == m)
    c[:, CO["ones"][0]:CO["ones"][0] + 128] = 1.0
    c[:, CO["bones"][0]:CO["bones"][0] + 128] = ((p // 64) == (m // 64))
    c[:, CO["tri"][0]:CO["tri"][0] + 128] = (p >= m)
    c[:, CO["ntri"][0]:CO["ntri"][0] + 128] = -1.0 * (p >= m)
    c[:, CO["nones"][0]:CO["nones"][0] + 128] = -1.0
    t = np.arange(512)[None, :]
    for d in range(4):
        c[:, CO["md"][0] + d * 512:CO["md"][0] + (d + 1) * 512] = ((d * 128 + p) < t)
    c[:, CO["maskp"][0]:CO["maskp"][0] + 512] = np.concatenate([np.tile((p > m), (1, 2)), np.tile((p <= m), (1, 2))], 1)
    c[:, CO["maskc"][0]:CO["maskc"][0] + 512] = np.tile((p <= m), (1, 4))
    return c


def _pcol(v):
    v = np.asarray(v, np.float32)
    return np.ascontiguousarray(v.reshape(-1, 128).T)


def _host_inputs(inp, core):
    f = lambda a: np.ascontiguousarray(np.asarray(a, np.float32))
    b0 = 2 * core
    x = f(inp["x"][b0:b0 + 2])
    xT = np.ascontiguousarray(x.transpose(0, 2, 1)).reshape(2, 8, 128, T)
    pp = np.zeros((128, NPP), np.float32)

    def put(name, arr):
        o, n = PP[name]
        pp[:, o:o + n] = np.asarray(arr, np.float32).reshape(128, n)

    c = f(inp["c"][b0:b0 + 2])
    put("cT", c.reshape(2, 8, 128).transpose(2, 1, 0))
    put("adab", np.stack([_pcol(inp["ada_b"][l]) for l in range(2)], 1))
    put("gmix", np.stack([_pcol(inp["norm_mix_g"][l]) for l in range(2)], 1))
    put("gffn", np.stack([_pcol(inp["norm_ffn_g"][l]) for l in range(2)], 1))
    cw = f(inp["ev_conv_w"][0])
    put("lcw", np.stack([_pcol(cw[k]) for k in range(4)], 2))
    put("lcb", _pcol(inp["ev_conv_b"][0]))
    put("lba", _pcol(inp["ev_ba"][0]))
    put("lbx", _pcol(inp["ev_bx"][0]))
    put("llam", _pcol(inp["ev_lam"][0]))
    put("evqg", np.tile(f(inp["ev_qn_g"][0]), 2)[:, None])
    put("evkg", np.tile(f(inp["ev_kn_g"][0]), 2)[:, None])
    put("odqg", np.tile(f(inp["od_qn_g"][0]), 2)[:, None])
    put("odkg", np.tile(f(inp["od_kn_g"][0]), 2)[:, None])
    put("sinks", np.tile(f(inp["od_sinks"][0])[None, :], (128, 1)))
    fcw = f(inp["ffn_conv_w"])
    put("fcw", np.stack([np.stack([_pcol(fcw[l, k]) for k in range(3)], 2) for l in range(2)], 1))
    put("fcb", np.stack([_pcol(inp["ffn_conv_b"][l]) for l in range(2)], 1))

    def bd(w):
        w = f(w)
        o = np.zeros((128, 4, 128), np.float32)
        for j in range(4):
            o[0:64, j, 0:64] = w[2 * j]
            o[64:128, j, 64:128] = w[2 * j + 1]
        return o

    return {
        "xT": xT, "pp": pp, "consts": _consts(),
        "ada_w": f(inp["ada_w"]),
        "ev_w_in": f(inp["ev_w_in"][0]), "ev_w_out": f(inp["ev_w_out"][0]),
        "od_w_in": f(inp["od_w_in"][0]), "od_w_out": f(inp["od_w_out"][0]),
        "w_gate": f(inp["ffn_w_gate"]), "w_up": f(inp["ffn_w_up"]), "w_down": f(inp["ffn_w_down"]),
        "wabd": bd(inp["ev_wa"][0]), "wxbd": bd(inp["ev_wx"][0]),
    }


def build(nseq=2, dbg=None, stop=None):
    dbg = dbg or set()
    nc = bass.Bass("TRN2", target_bir_lowering=False)
    P = Prog(nc)

    def din(name, shape):
        return nc.dram_tensor(name, list(shape), F32, kind="ExternalInput").ap()

    xT = din("xT", [2, 8, 128, T])
    pp_d = din("pp", [128, NPP])
    con_d = din("consts", [128, NCON])
    ada_w = din("ada_w", [2, D, 6 * D])
    ev_w_in = din("ev_w_in", [D, 2560])
    ev_w_out = din("ev_w_out", [D, D])
    od_w_in = din("od_w_in", [D, 1536])
    od_w_out = din("od_w_out", [D, D])
    w_gate = din("w_gate", [2, D, DFF])
    w_up = din("w_up", [2, D, DFF])
    w_down = din("w_down", [2, DFF, D])
    wabd_d = din("wabd", [128, 4, 128])
    wxbd_d = din("wxbd", [128, 4, 128])
    out_d = nc.dram_tensor("out", [2, 8, 128, T], F32, kind="ExternalOutput").ap()
    dbg_out = {}

    def dump(name, ap_sb, shape, rd):
        if name not in dbg:
            return
        d = nc.dram_tensor("dbg_" + name, list(shape), F32, kind="ExternalOutput").ap()
        dbg_out[name] = d
        P.dma("pool", d, ap_sb, rd=rd, wr=[("dbgout", name)])

    top = ExitStack()

    uid = [0]
    sb_lo = (nc.sbuf_base + 63) // 64 * 64
    free_list = [[sb_lo, nc.sbuf_top]]
    peak = [0]

    def sb(st, name, shape, dt):
        uid[0] += 1
        nbytes = int(np.prod(shape[1:])) * (2 if dt == BF16 else 4)
        nbytes = (nbytes + 63) // 64 * 64
        for seg in free_list:
            if seg[1] - seg[0] >= nbytes:
                off = seg[0]
                seg[0] += nbytes
                break
        else:
            raise RuntimeError(f"SBUF full allocating {name} ({nbytes} B); free={free_list}")
        peak[0] = max(peak[0], off + nbytes)

        def release():
            free_list.append([off, off + nbytes])
            free_list.sort()
            merged = []
            for sg in free_list:
                if sg[0] >= sg[1]:
                    continue
                if merged and merged[-1][1] == sg[0]:
                    merged[-1][1] = sg[1]
                else:
                    merged.append(sg)
            free_list[:] = merged
        st.callback(release)
        return nc.alloc_sbuf_tensor_at(f"{name}_u{uid[0]}", list(shape), dt, offset=off)

    ps = [top.enter_context(nc.psum_tensor(f"ps{i}", [128, 512], F32)) for i in range(8)]
    X = sb(top, "X", [128, 8, T], F32)
    pp = sb(top, "pp", [128, NPP], F32)
    con = sb(top, "con", [128, NCON], BF16)
    modp = sb(top, "modp", [128, 2, 2, 6, 8], F32)
    misc = sb(top, "misc", [128, 64], F32)
    wring = [sb(top, f"wring{i}", [128, 8, 128], BF16) for i in range(8)]
    ring_n = [0]

    def cview(name):
        o, n = CO[name]
        return con[:, o:o + n]

    ident, ones_c, bones, tri = cview("ident"), cview("ones"), cview("bones"), cview("tri")
    ntri, nones = cview("ntri"), cview("nones")
    md_all = cview("md")
    maskp, maskc = cview("maskp"), cview("maskc")

    def ppv(name):
        o, n = PP[name]
        return pp[:, o:o + n]

    def wslot():
        i = ring_n[0] % len(wring)
        ring_n[0] += 1
        return wring[i], ("wring", i)

    def wload(dst, src, key):
        P.dma("pool", dst, src, wr=[key])

    def wcols(w2d, c0, n):
        return w2d[:, c0:c0 + n].rearrange("(kc p) n -> p kc n", p=128)

    def mm_group(out, pairs, rd, wr):
        def fn(e):
            ins = None
            n = len(pairs)
            for i, (l, r) in enumerate(pairs):
                ins = e.matmul(out, lhsT=l, rhs=r, start=(i == 0), stop=(i == n - 1))
            return ins
        P.op("pe", fn, rd=rd, wr=wr)

    def tsl(tt):
        return slice(tt * TS, (tt + 1) * TS)

    Xk = lambda c, tt: ("X", c, tt)
    psk = lambda b: ("ps", b)

    P.dma("sp", pp[:], pp_d, wr=["pp"])
    P.dma("pool", con[:], con_d, wr=["con"])
    with ExitStack() as st:
        cs = sb(st, "cs", [128, 8, 2], BF16)
        wbig = [sb(st, f"wbig{i}", [128, 8, 1024], BF16) for i in range(2)]
        mod = sb(st, "mod", [128, 96, 2], F32)
        tmpa = sb(st, "tmpa", [128, 16], F32)
        o, n = PP["cT"]
        P.op("act", lambda e: e.activation(out=cs[:].rearrange("p k b -> p (k b)"), in_=pp[:, o:o + n], func=AF.Silu),
             rd=["pp"], wr=["cs"])
        for l in range(2):
            for pc in range(6):
                wb = wbig[(l * 6 + pc) % 2]
                wk = ("wbig", (l * 6 + pc) % 2)
                wload(wb[:], wcols(ada_w[l], pc * 1024, 1024), wk)
                for nn in range(8):
                    g = pc * 8 + nn
                    col = (l * 48 + g) * 2
                    mm_group(ps[7][:, col:col + 2],
                             [(wb[:, k, nn * 128:(nn + 1) * 128], cs[:, k, :]) for k in range(8)],
                             rd=[wk, "cs"], wr=[psk(7)])
        P.op("dve", lambda e: e.tensor_tensor(
            out=mod[:], in0=ps[7][:, 0:192].rearrange("p (g b) -> p g b", b=2),
            in1=ppv("adab").unsqueeze(2).broadcast_to([128, 96, 2]), op=ALU.add),
            rd=[psk(7), "pp"], wr=["mod"])
        modv = mod[:].rearrange("p (l j c) b -> p l j c b", l=2, j=6)
        for l in range(2):
            for b in range(2):
                for (dst, jsc, gname) in ((0, 1, "gmix"), (3, 4, "gffn")):
                    go, _ = PP[gname]
                    P.op("dve", lambda e, l=l, b=b, dst=dst, jsc=jsc, go=go: e.scalar_tensor_tensor(
                        out=modp[:, l, b, dst, :], in0=modv[:, l, jsc, :, b], scalar=1.0,
                        in1=pp[:, go + l * 8:go + l * 8 + 8], op0=ALU.add, op1=ALU.mult),
                        rd=["mod", "pp"], wr=["modp"])
                for (dst, j) in ((1, 0), (2, 2), (4, 3), (5, 5)):
                    P.op("dve", lambda e, l=l, b=b, dst=dst, j=j: e.tensor_copy(
                        out=modp[:, l, b, dst, :], in_=modv[:, l, j, :, b]), rd=["mod"], wr=["modp"])
        P.op("act", lambda e: e.activation(out=tmpa[:, 0:4], in_=ppv("llam"), func=AF.Exp, scale=-1.0),
             rd=["pp"], wr=["tmpa"])
        P.op("act", lambda e: e.activation(out=tmpa[:, 4:8], in_=tmpa[:, 0:4], func=AF.Ln, bias=1.0),
             rd=["tmpa"], wr=["tmpa2"])
        P.op("dve", lambda e: e.tensor_scalar(out=misc[:, 0:4], in0=tmpa[:, 4:8], scalar1=-8.0, scalar2=None,
                                              op0=ALU.mult), rd=["tmpa2"], wr=["misc"])
        P.op("dve", lambda e: e.tensor_scalar(out=misc[:, 4:5], in0=ppv("evqg"), scalar1=0.125, scalar2=None,
                                              op0=ALU.mult), rd=["pp"], wr=["misc"])
        P.op("dve", lambda e: e.tensor_scalar(out=misc[:, 5:6], in0=ppv("odqg"), scalar1=0.125, scalar2=None,
                                              op0=ALU.mult), rd=["pp"], wr=["misc"])
        P.op("act", lambda e: e.activation(out=misc[:, 8:24], in_=ppv("sinks"), func=AF.Exp),
             rd=["pp"], wr=["misc"])
        dump("modp", modp[:].rearrange("p l b j c -> p (l b j c)"), [128, 192], rd=["modp"])
        P.barrier()
        P.flush()
    cl = misc[:, 0:4]
    evq8 = misc[:, 4:5]
    odq8 = misc[:, 5:6]
    esink = misc[:, 8:24]

    def do_norm(st, h, l, b, which):
        ia, ish = (0, 1) if which == 0 else (3, 4)
        sq = [sb(st, f"sq{i}", [128, 8, TS], BF16) for i in range(2)]
        sd = [sb(st, f"sd{i}", [128, TS], F32) for i in range(2)]
        rs = [sb(st, f"rs{i}", [128, TS], F32) for i in range(2)]
        tm = [sb(st, f"tm{i}", [128, TS], F32) for i in range(3)]
        ti = 0
        for tt in range(NT):
            i2 = tt % 2
            P.op("act", lambda e, tt=tt, i2=i2: e.activation(out=sq[i2][:], in_=X[:, :, tsl(tt)], func=AF.Square),
                 rd=[Xk(c, tt) for c in range(8)], wr=[("sq", i2)])
            bank = 6 + i2
            mm_group(ps[bank][:], [(ones_c, sq[i2][:, c, :]) for c in range(8)], rd=[("sq", i2), "con"], wr=[psk(bank)])
            P.op("act", lambda e, i2=i2, bank=bank: e.activation(out=sd[i2][:], in_=ps[bank][:], func=AF.Ln,
                                                                  scale=1.0 / D, bias=EPS),
                 rd=[psk(bank)], wr=[("sd", i2)])
            P.op("act", lambda e, i2=i2: e.activation(out=rs[i2][:], in_=sd[i2][:], func=AF.Exp, scale=-0.5),
                 rd=[("sd", i2)], wr=[("rs", i2)])
            for c in range(8):
                t3 = ti % 3
                ti += 1
                P.op("dve", lambda e, c=c, tt=tt, i2=i2, t3=t3: e.tensor_tensor(
                    out=tm[t3][:], in0=X[:, c, tsl(tt)], in1=rs[i2][:], op=ALU.mult),
                    rd=[Xk(c, tt), ("rs", i2)], wr=[("tm", t3)])
                P.op("act", lambda e, c=c, tt=tt, t3=t3: e.activation(
                    out=h[:, c, tsl(tt)], in_=tm[t3][:], func=AF.Identity,
                    scale=modp[:, l, b, ia, c:c + 1], bias=modp[:, l, b, ish, c:c + 1]),
                    rd=[("tm", t3), "modp"], wr=[("h", c, tt)])

    def hk_all(tt):
        return [("h", c, tt) for c in range(8)]

    def out_proj_residual(w2d, ysrc, ykeys, l, b, gidx, r0=0, nk=8):
        slots = {}

        def ld(n):
            slots[n] = wslot()
            wload(slots[n][0][:, 0:nk, :],
                  w2d[r0 * 128:(r0 + nk) * 128, n * 128:(n + 1) * 128].rearrange("(kc p) n -> p kc n", p=128), slots[n][1])
        for n in range(3):
            ld(n)
        for n in range(8):
            if n + 3 < 8:
                ld(n + 3)
            ws, wk = slots[n]
            for tt in range(NT):
                bank = (n * NT + tt) % 6
                mm_group(ps[bank][:], [(ws[:, k, :], ysrc(k)[:, tsl(tt)]) for k in range(nk)],
                         rd=[wk] + ykeys(tt), wr=[psk(bank)])
                P.op("dve", lambda e, n=n, tt=tt, bank=bank: e.scalar_tensor_tensor(
                    out=X[:, n, tsl(tt)], in0=ps[bank][:], scalar=modp[:, l, b, gidx, n:n + 1],
                    in1=X[:, n, tsl(tt)], op0=ALU.mult, op1=ALU.add),
                    rd=[psk(bank), Xk(n, tt), "modp"], wr=[Xk(n, tt)])

    def do_ffn(st, h, l, b):
        GL = 2
        act = sb(st, "act", [128, 6, T], BF16)
        wd = [sb(st, f"wd{i}", [128, 6, D], BF16) for i in range(2)]
        gb = [sb(st, f"gb{i}", [128, 2 + T], F32) for i in range(GL + 1)]
        gc = sb(st, "gc", [128, T], F32)
        sg = sb(st, "sg", [128, T], BF16)
        fo, _ = PP["fcw"]
        bo, _ = PP["fcb"]
        for i in range(GL + 1):
            P.op("dve", lambda e, i=i: e.memset(gb[i][:, 0:2], 0.0), wr=[("gbpad", i)])
        wg2, wu2 = w_gate[l], w_up[l]
        slots = {}
        qof = {}
        for qi, (c0, ncq) in enumerate(FQ):
            for ci in range(ncq):
                qof[c0 + ci] = (qi, ci, c0, ncq)

        def load_c(c):
            sg_, sk = wslot()
            wload(sg_[:], wcols(wg2, c * 128, 128), sk)
            su_, uk = wslot()
            wload(su_[:], wcols(wu2, c * 128, 128), uk)
            slots[c] = (sg_, sk, su_, uk)

        def gate_stage(c):
            if c + 1 < NFF:
                load_c(c + 1)
            sg_, sk, su_, uk = slots[c]
            gi = c % (GL + 1)
            for tt in range(NT):
                bank = tt
                mm_group(ps[bank][:], [(sg_[:, k, :], h[:, k, tsl(tt)]) for k in range(8)],
                         rd=[sk] + hk_all(tt), wr=[psk(bank)])
                P.op("act", lambda e, gi=gi, tt=tt, bank=bank: e.activation(
                    out=gb[gi][:, 2 + tt * TS:2 + (tt + 1) * TS], in_=ps[bank][:], func=AF.Identity),
                    rd=[psk(bank)], wr=[("gb", gi, tt)])

        def rest_stage(c):
            qi, ci, c0, ncq = qof[c]
            wdb = wd[qi % 2]
            wdk = ("wd", qi % 2)
            if ci == 0:
                wload(wdb[:, 0:ncq, :],
                      w_down[l][c0 * 128:(c0 + ncq) * 128, :].rearrange("(kc p) n -> p kc n", p=128), wdk)
            sg_, sk, su_, uk = slots.pop(c)
            gi = c % (GL + 1)
            wo = fo + (l * NFF + c) * 3
            P.op("dve", lambda e, gi=gi, wo=wo, c=c: e.tensor_scalar(
                out=gc[:], in0=gb[gi][:, 0:T], scalar1=pp[:, wo:wo + 1],
                scalar2=pp[:, bo + l * NFF + c:bo + l * NFF + c + 1], op0=ALU.mult, op1=ALU.add),
                rd=[("gb", gi, t_) for t_ in range(NT)] + [("gbpad", gi), "pp"], wr=["gc"])
            for k in (1, 2):
                P.op("dve", lambda e, gi=gi, wo=wo, k=k: e.scalar_tensor_tensor(
                    out=gc[:], in0=gb[gi][:, k:k + T], scalar=pp[:, wo + k:wo + k + 1], in1=gc[:],
                    op0=ALU.mult, op1=ALU.add),
                    rd=[("gb", gi, t_) for t_ in range(NT)] + ["gc", "pp"], wr=["gc"])
            P.op("act", lambda e: e.activation(out=sg[:], in_=gc[:], func=AF.Silu), rd=["gc"], wr=["sg"])
            for tt in range(NT):
                bank = 4 + tt % 2
                mm_group(ps[bank][:], [(su_[:, k, :], h[:, k, tsl(tt)]) for k in range(8)],
                         rd=[uk] + hk_all(tt), wr=[psk(bank)])
                P.op("dve", lambda e, ci=ci, tt=tt, bank=bank: e.tensor_tensor(
                    out=act[:, ci, tsl(tt)], in0=sg[:, tsl(tt)], in1=ps[bank][:], op=ALU.mult),
                    rd=["sg", psk(bank)], wr=[("act", ci, tt)])
            if ci == ncq - 1:
                for n in range(8):
                    for tt in range(NT):
                        bank = 6 + (n * NT + tt) % 2
                        mm_group(ps[bank][:], [(wdb[:, cj, n * 128:(n + 1) * 128], act[:, cj, tsl(tt)]) for cj in range(ncq)],
                                 rd=[wdk] + [("act", cj, tt) for cj in range(ncq)], wr=[psk(bank)])
                        P.op("dve", lambda e, n=n, tt=tt, bank=bank: e.scalar_tensor_tensor(
                            out=X[:, n, tsl(tt)], in0=ps[bank][:], scalar=modp[:, l, b, 5, n:n + 1],
                            in1=X[:, n, tsl(tt)], op0=ALU.mult, op1=ALU.add),
                            rd=[psk(bank), Xk(n, tt), "modp"], wr=[Xk(n, tt)])

        load_c(0)
        for i in range(NFF + GL):
            if i < NFF:
                gate_stage(i)
            if i - GL >= 0:
                rest_stage(i - GL)

    qkc = [0]

    def qk_norm_chunk(w2d, col0, dst, dstkey, gain_ap, h, tmp, dup64=False, split=None):
        ws, wk = wslot()
        if dup64:
            for hb in range(2):
                P.dma("pool", ws[:, :, hb * 64:(hb + 1) * 64], wcols(w2d, col0, 64), wr=[wk])
        else:
            wload(ws[:], wcols(w2d, col0, 128), wk)
        sqb, sdb, rsb = tmp
        for tt in range(NT):
            i2 = qkc[0] % 2
            bA = qkc[0] % 4
            bB = 4 + qkc[0] % 2
            qkc[0] += 1
            mm_group(ps[bA][:], [(ws[:, k, :], h[:, k, tsl(tt)]) for k in range(8)], rd=[wk] + hk_all(tt), wr=[psk(bA)])
            P.op("act", lambda e, i2=i2, bA=bA: e.activation(out=sqb[i2][:], in_=ps[bA][:], func=AF.Square),
                 rd=[psk(bA)], wr=[("sqb", i2)])
            mm_group(ps[bB][:], [(bones, sqb[i2][:])], rd=[("sqb", i2), "con"], wr=[psk(bB)])
            P.op("act", lambda e, i2=i2, bB=bB: e.activation(out=sdb[i2][:], in_=ps[bB][:], func=AF.Ln,
                                                              scale=1.0 / 64, bias=EPS),
                 rd=[psk(bB)], wr=[("sdb", i2)])
            P.op("act", lambda e, i2=i2: e.activation(out=rsb[i2][:], in_=sdb[i2][:], func=AF.Exp, scale=-0.5),
                 rd=[("sdb", i2)], wr=[("rsb", i2)])
            if split is None:
                P.op("dve", lambda e, i2=i2, bA=bA, tt=tt: e.scalar_tensor_tensor(
                    out=dst[:, tsl(tt)], in0=ps[bA][:], scalar=gain_ap, in1=rsb[i2][:], op0=ALU.mult, op1=ALU.mult),
                    rd=[psk(bA), ("rsb", i2), "misc", "pp"], wr=[(dstkey, tt)])
            else:
                for hh in range(2):
                    pr = slice(hh * 64, (hh + 1) * 64)
                    P.op("dve", lambda e, i2=i2, bA=bA, tt=tt, pr=pr, hh=hh: e.scalar_tensor_tensor(
                        out=split[hh][pr, tsl(tt)], in0=ps[bA][pr, :], scalar=gain_ap[pr, :], in1=rsb[i2][pr, :],
                        op0=ALU.mult, op1=ALU.mult),
                        rd=[psk(bA), ("rsb", i2), "misc", "pp"], wr=[(dstkey, hh, tt)])

    for s in range(nseq):
        b = s
        for c in range(8):
            P.dma("sp", X[:, c, :], xT[s, c], wr=[Xk(c, tt) for tt in range(NT)])
        l = 0
        with ExitStack() as st0:
            sty = ExitStack()
            ya = sb(sty, "ya", [128, 4, T], BF16)
            with ExitStack() as st1:
                h = sb(st1, "h", [128, 8, T], BF16)
                with ExitStack() as st2:
                    do_norm(st2, h, l, b, 0)
                    if s == 0:
                        dump("h0", h[:].rearrange("p c t -> p (c t)"), [128, 8 * T],
                             rd=[("h", c, tt) for c in range(8) for tt in range(NT)])
                    P.barrier()
                    P.flush()
                if stop == "norm0":
                    break
                with ExitStack() as st2:
                    xr = sb(st2, "xr", [128, 3 + T], F32)
                    xc = sb(st2, "xc", [128, T], F32)
                    xcb = sb(st2, "xcb", [128, T], BF16)
                    ra = sb(st2, "ra", [128, T], F32)
                    ig = sb(st2, "ig", [128, T], F32)
                    s2 = sb(st2, "s2", [128, T], F32)
                    gel = sb(st2, "gel", [128, T], F32)
                    gx = sb(st2, "gx", [128, T], F32)
                    wabd = sb(st2, "wabd", [128, 4, 128], BF16)
                    wxbd = sb(st2, "wxbd", [128, 4, 128], BF16)
                    P.dma("pool", wabd[:], wabd_d, wr=["wabd"])
                    P.dma("pool", wxbd[:], wxbd_d, wr=["wxbd"])
                    P.op("dve", lambda e: e.memset(xr[:, 0:3], 0.0), wr=["xrpad"])
                    lcw, _ = PP["lcw"]
                    lcb, _ = PP["lcb"]
                    lba, _ = PP["lba"]
                    lbx, _ = PP["lbx"]
                    import os as _os
                    for j in [int(v) for v in _os.environ.get("LRU_CHUNKS", "0,1,2,3").split(",")]:
                        wsx, wkx = wslot()
                        wload(wsx[:], wcols(ev_w_in, j * 128, 128), wkx)
                        wsg, wkg = wslot()
                        wload(wsg[:], wcols(ev_w_in, 512 + j * 128, 128), wkg)
                        for tt in range(NT):
                            bank = tt % 2
                            mm_group(ps[bank][:], [(wsx[:, k, :], h[:, k, tsl(tt)]) for k in range(8)],
                                     rd=[wkx] + hk_all(tt), wr=[psk(bank)])
                            P.op("act", lambda e, tt=tt, bank=bank: e.activation(
                                out=xr[:, 3 + tt * TS:3 + (tt + 1) * TS], in_=ps[bank][:], func=AF.Identity),
                                rd=[psk(bank)], wr=[("xr", tt)])
                        xrk = [("xr", t_) for t_ in range(NT)]
                        P.op("dve", lambda e, j=j: e.tensor_scalar(
                            out=xc[:], in0=xr[:, 0:T], scalar1=pp[:, lcw + j * 4:lcw + j * 4 + 1],
                            scalar2=pp[:, lcb + j:lcb + j + 1], op0=ALU.mult, op1=ALU.add),
                            rd=xrk + ["xrpad", "pp"], wr=["xc"])
                        for k in (1, 2, 3):
                            P.op("dve", lambda e, j=j, k=k: e.scalar_tensor_tensor(
                                out=xc[:], in0=xr[:, k:k + T], scalar=pp[:, lcw + j * 4 + k:lcw + j * 4 + k + 1],
                                in1=xc[:], op0=ALU.mult, op1=ALU.add), rd=xrk + ["xc", "pp"], wr=["xc"])
                        P.op("act", lambda e: e.activation(out=xcb[:], in_=xc[:], func=AF.Identity), rd=["xc"], wr=["xcb"])
                        for tt in range(NT):
                            bank = 2 + tt % 2
                            mm_group(ps[bank][:], [(wabd[:, j, :], xcb[:, tsl(tt)])], rd=["wabd", "xcb"], wr=[psk(bank)])
                            P.op("act", lambda e, j=j, tt=tt, bank=bank: e.activation(
                                out=ra[:, tsl(tt)], in_=ps[bank][:], func=AF.Sigmoid, bias=pp[:, lba + j:lba + j + 1]),
                                rd=[psk(bank), "pp"], wr=[("ra", tt)])
                            bank2 = 4 + tt % 2
                            mm_group(ps[bank2][:], [(wxbd[:, j, :], xcb[:, tsl(tt)])], rd=["wxbd", "xcb"], wr=[psk(bank2)])
                            P.op("act", lambda e, j=j, tt=tt, bank2=bank2: e.activation(
                                out=ig[:, tsl(tt)], in_=ps[bank2][:], func=AF.Sigmoid, bias=pp[:, lbx + j:lbx + j + 1]),
                                rd=[psk(bank2), "pp"], wr=[("ig", tt)])
                        rak = [("ra", t_) for t_ in range(NT)]
                        igk = [("ig", t_) for t_ in range(NT)]
                        P.op("act", lambda e, j=j: e.activation(out=ra[:], in_=ra[:], func=AF.Exp, scale=cl[:, j:j + 1]),
                             rd=rak + ["misc"], wr=rak)
                        P.op("act", lambda e: e.activation(out=s2[:], in_=ra[:], func=AF.Square), rd=rak, wr=["s2"])
                        P.op("act", lambda e: e.activation(out=s2[:], in_=s2[:], func=AF.Sqrt, scale=-1.0, bias=1.0),
                             rd=["s2"], wr=["s2"])
                        P.op("dve", lambda e: e.tensor_tensor(out=s2[:], in0=s2[:], in1=ig[:], op=ALU.mult),
                             rd=["s2"] + igk, wr=["s2"])
                        P.op("dve", lambda e: e.tensor_tensor(out=s2[:], in0=s2[:], in1=xc[:], op=ALU.mult),
                             rd=["s2", "xc"], wr=["s2"])
                        P.op("dve", lambda e: e.tensor_tensor_scan(out=xc[:], data0=ra[:], data1=s2[:], initial=0.0,
                                                                    op0=ALU.mult, op1=ALU.add),
                             rd=rak + ["s2", "xc"], wr=["xc"])
                        for tt in range(NT):
                            bank = 6 + tt % 2
                            mm_group(ps[bank][:], [(wsg[:, k, :], h[:, k, tsl(tt)]) for k in range(8)],
                                     rd=[wkg] + hk_all(tt), wr=[psk(bank)])
                            P.op("act", lambda e, tt=tt, bank=bank: e.activation(
                                out=gx[:, tsl(tt)], in_=ps[bank][:], func=AF.Identity),
                                rd=[psk(bank)], wr=[("gx", tt)])
                        gxk = [("gx", t_) for t_ in range(NT)]
                        P.op("act", lambda e: e.activation(out=gel[:], in_=gx[:], func=AF.Square), rd=gxk, wr=["gel"])
                        P.op("dve", lambda e: e.tensor_scalar(out=gel[:], in0=gel[:], scalar1=0.044715, scalar2=1.0,
                                                              op0=ALU.mult, op1=ALU.add), rd=["gel"], wr=["gel"])
                        P.op("dve", lambda e: e.tensor_tensor(out=gel[:], in0=gel[:], in1=gx[:], op=ALU.mult),
                             rd=["gel"] + gxk, wr=["gel"])
                        P.op("act", lambda e: e.activation(out=gel[:], in_=gel[:], func=AF.Sigmoid, scale=1.5957691216057308),
                             rd=["gel"], wr=["gel"])
                        P.op("dve", lambda e: e.tensor_tensor(out=gel[:], in0=gel[:], in1=gx[:], op=ALU.mult),
                             rd=["gel"] + gxk, wr=["gel"])
                        P.op("dve", lambda e, j=j: e.tensor_tensor(out=ya[:, j, :], in0=xc[:], in1=gel[:], op=ALU.mult),
                             rd=["xc", "gel"], wr=[("ya", j)])
                    if s == 0:
                        dump("lxc", xc[:], [128, T], rd=["xc"])
                        dump("lra", ra[:], [128, T], rd=[("ra", t_) for t_ in range(NT)])
                        dump("lig", ig[:], [128, T], rd=[("ig", t_) for t_ in range(NT)])
                        dump("ls2", s2[:], [128, T], rd=["s2"])
                        dump("ya", ya[:].rearrange("p c t -> p (c t)"), [128, 4 * T], rd=[("ya", j) for j in range(4)])
                    P.barrier()
                    P.flush()
                if stop == "lru":
                    sty.close()
                    break
                out_proj_residual(ev_w_out, lambda k: ya[:, k, :], lambda tt: [("ya", j) for j in range(4)], l, b, 2, r0=0, nk=4)
                P.barrier()
                P.flush()
                sty.close()
                qz = [sb(st0, f"qz{i}", [128, 4, T], BF16) for i in range(2)]
                kn = sb(st0, "kn", [128, 4, T], BF16)
                vt = sb(st0, "vt", [128, 16, 512], BF16)
                P.op("dve", lambda e: e.memset(qz[0][64:128, :, :], 0.0), wr=[("qzpad", 0)])
                P.op("dve", lambda e: e.memset(qz[1][0:64, :, :], 0.0), wr=[("qzpad", 1)])
                with ExitStack() as st2:
                    sqb = [sb(st2, f"sqb{i}", [128, TS], BF16) for i in range(2)]
                    sdb = [sb(st2, f"sdb{i}", [128, TS], F32) for i in range(2)]
                    rsb = [sb(st2, f"rsb{i}", [128, TS], F32) for i in range(2)]
                    for j in range(4):
                        qk_norm_chunk(ev_w_in, 1024 + j * 128, None, ("qz", j), evq8, h, (sqb, sdb, rsb),
                                      split=(qz[0][:, j, :], qz[1][:, j, :]))
                        qk_norm_chunk(ev_w_in, 1536 + j * 128, kn[:, j, :], ("kn", j), ppv("evkg"), h, (sqb, sdb, rsb))
                    wvs = []
                    for q4 in range(4):
                        ws_, wk_ = wslot()
                        wload(ws_[:], wcols(ev_w_in, 2048 + q4 * 128, 128), wk_)
                        wvs.append((ws_, wk_))
                    for blk in range(16):
                        bank = 6 + blk % 2

                        def vproj(e, blk=blk, bank=bank):
                            ins = None
                            for q4 in range(4):
                                for k in range(8):
                                    ins = e.matmul(ps[bank][:, q4 * 128:(q4 + 1) * 128], lhsT=h[:, k, blk * 128:(blk + 1) * 128],
                                                   rhs=wvs[q4][0][:, k, :], start=(k == 0), stop=(k == 7))
                            return ins
                        P.op("pe", vproj, rd=[w_[1] for w_ in wvs] + hk_all(blk // 4), wr=[psk(bank)])
                        if blk % 2 == 0:
                            P.op("act", lambda e, blk=blk, bank=bank: e.activation(out=vt[:, blk, :], in_=ps[bank][:],
                                                                                   func=AF.Identity),
                                 rd=[psk(bank)], wr=[("vt", blk)])
                        else:
                            P.op("dve", lambda e, blk=blk, bank=bank: e.tensor_copy(out=vt[:, blk, :], in_=ps[bank][:]),
                                 rd=[psk(bank)], wr=[("vt", blk)])
                    if s == 0:
                        dump("kn", kn[:].rearrange("p c t -> p (c t)"), [128, 4 * T],
                             rd=[(("kn", j), t_) for j in range(4) for t_ in range(NT)])
                        dump("vt", vt[:].rearrange("p c t -> p (c t)"), [128, 16 * 512], rd=[("vt", k) for k in range(16)])
                    P.barrier()
                    P.flush()
            if stop in ("norm0", "lru", "sbproj"):
                break
            yb = sb(st0, "yb", [128, 4, T], BF16)
            with ExitStack() as st2:
                eb = [sb(st2, f"eb{i}", [128, TS], F32) for i in range(3)]
                LOOK = 2
                NSP, NRB = LOOK + 3, LOOK + 2
                spb = [sb(st2, f"spb{i}", [128, TS], BF16) for i in range(NSP)]
                Rb = [sb(st2, f"Rb{i}", [128, TS], BF16) for i in range(NRB)]
                wb_ = [sb(st2, f"wb{i}", [128, TS], BF16) for i in range(4)]
                mo, _ = CO["md"]
                tiles = []
                for j in range(4):
                    for tt in range(NT):
                        for hh in range(2):
                            ob = 4 + ((j * NT + tt) * 2 + hh) % 4
                            for idx, kb in enumerate(range(4 * tt + 3, -1, -1)):
                                tiles.append(dict(j=j, tt=tt, hh=hh, kb=kb, idx=idx, ob=ob, R=None))
                cnt_ = dict(z=0, e=0, sp=0, r=0, rb=0, w=0)

                def stage_a(ti):
                    t = tiles[ti]
                    j, tt, hh, kb, idx = t["j"], t["tt"], t["hh"], t["kb"], t["idx"]
                    p0 = hh * 64
                    dz = kb - 4 * tt
                    zb = cnt_["z"] % 2
                    cnt_["z"] += 1
                    mm_group(ps[zb][:], [(kn[:, j, kb * 128:(kb + 1) * 128], qz[hh][:, j, tsl(tt)])],
                             rd=[(("kn", j), kb // 4), (("qz", j), hh, tt), ("qzpad", hh)], wr=[psk(zb)])
                    ei = cnt_["e"] % 3
                    cnt_["e"] += 1
                    P.op("act", lambda e, ei=ei, zb=zb: e.activation(out=eb[ei][:], in_=ps[zb][:], func=AF.Exp),
                         rd=[psk(zb)], wr=[("eb", ei)])
                    si = cnt_["sp"] % NSP
                    cnt_["sp"] += 1
                    t["si"] = si
                    P.op("act", lambda e, ei=ei, si=si: e.activation(out=spb[si][:], in_=eb[ei][:], func=AF.Ln, bias=1.0),
                         rd=[("eb", ei)], wr=[("spb", si)])
                    if dz >= 0:
                        P.op("dve", lambda e, si=si, dz=dz: e.tensor_tensor(
                            out=spb[si][:], in0=spb[si][:], in1=con[:, mo + dz * 512:mo + (dz + 1) * 512], op=ALU.mult),
                            rd=[("spb", si), "con"], wr=[("spb", si)])
                    if kb > 0:
                        nt_ = tiles[ti + 1]
                        if idx == 0:
                            nt_["R"] = (spb[si], ("spb", si))
                        else:
                            rn = cnt_["rb"] % NRB
                            cnt_["rb"] += 1
                            rsrc, rkey = t["R"]
                            P.op("dve", lambda e, si=si, rsrc=rsrc, rn=rn: e.tensor_tensor(
                                out=Rb[rn][:], in0=rsrc[:], in1=spb[si][:], op=ALU.add),
                                rd=[("spb", si), rkey], wr=[("Rb", rn)])
                            nt_["R"] = (Rb[rn], ("Rb", rn))

                def stage_b(ti):
                    t = tiles[ti]
                    j, tt, hh, kb, idx, ob, si = t["j"], t["tt"], t["hh"], t["kb"], t["idx"], t["ob"], t["si"]
                    p0 = hh * 64
                    dz = kb - 4 * tt
                    rb = 2 + cnt_["r"] % 2
                    cnt_["r"] += 1
                    pairs = [(ntri, spb[si][:])]
                    rdk = [("spb", si), "con", (("kn", j), kb // 4), (("qz", j), hh, tt), ("qzpad", hh)]
                    if t["R"] is not None:
                        pairs.append((nones, t["R"][0][:]))
                        rdk.append(t["R"][1])
                    pairs.append((kn[:, j, kb * 128:(kb + 1) * 128], qz[hh][:, j, tsl(tt)]))
                    mm_group(ps[rb][:], pairs, rd=rdk, wr=[psk(rb)])
                    wi = cnt_["w"] % 4
                    cnt_["w"] += 1
                    t["wi"] = wi
                    P.op("act", lambda e, wi=wi, rb=rb: e.activation(out=wb_[wi][:], in_=ps[rb][:], func=AF.Exp),
                         rd=[psk(rb)], wr=[("wb", wi)])
                    if dz >= 0:
                        P.op("dve", lambda e, wi=wi, dz=dz: e.tensor_tensor(
                            out=wb_[wi][:], in0=wb_[wi][:], in1=con[:, mo + dz * 512:mo + (dz + 1) * 512], op=ALU.mult),
                            rd=[("wb", wi), "con"], wr=[("wb", wi)])

                def stage_c(ti):
                    t = tiles[ti]
                    j, tt, hh, kb, idx, ob, wi = t["j"], t["tt"], t["hh"], t["kb"], t["idx"], t["ob"], t["wi"]
                    p0 = hh * 64

                    def pv(e, ob=ob, kb=kb, j=j, wi=wi, first=(idx == 0), last=(kb == 0)):
                        return e.matmul(ps[ob][:], lhsT=vt[:, kb, j * 128:(j + 1) * 128],
                                        rhs=wb_[wi][:], start=first, stop=last)
                    P.op("pe", pv, rd=[("wb", wi), ("vt", kb)], wr=[("pso", ob)])
                    if kb == 0:
                        P.op("act", lambda e, j=j, tt=tt, ob=ob, p0=p0: e.activation(
                            out=yb[p0:p0 + 64, j, tsl(tt)], in_=ps[ob][p0:p0 + 64, :], func=AF.Identity),
                            rd=[("pso", ob)], wr=[("yb", j, tt, hh)])

                ntl = len(tiles)
                for ti in range(ntl + LOOK + 1):
                    if ti < ntl:
                        stage_a(ti)
                    if 0 <= ti - LOOK < ntl:
                        stage_b(ti - LOOK)
                    if ti - LOOK - 1 >= 0:
                        stage_c(ti - LOOK - 1)
                if s == 0:
                    dump("yb", yb[:].rearrange("p c t -> p (c t)"), [128, 4 * T],
                         rd=[("yb", j, t_, hh) for j in range(4) for t_ in range(NT) for hh in range(2)])
                P.barrier()
                P.flush()
            if stop == "sb":
                break
            out_proj_residual(ev_w_out, lambda k: yb[:, k, :],
                              lambda tt: [("yb", j, tt, hh) for j in range(4) for hh in range(2)], l, b, 2, r0=4, nk=4)
            P.barrier()
            P.flush()
        if s == 0:
            dump("x0mid", X[:].rearrange("p c t -> p (c t)"), [128, 8 * T], rd=[Xk(c, tt) for c in range(8) for tt in range(NT)])
        if stop == "mix0":
            break
        with ExitStack() as st1:
            h = sb(st1, "h", [128, 8, T], BF16)
            with ExitStack() as st2:
                do_norm(st2, h, l, b, 1)
                P.barrier()
                P.flush()
            with ExitStack() as st2:
                do_ffn(st2, h, l, b)
                P.barrier()
                P.flush()
        if s == 0:
            dump("x1", X[:].rearrange("p c t -> p (c t)"), [128, 8 * T], rd=[Xk(c, tt) for c in range(8) for tt in range(NT)])
        if stop == "l0":
            break
        l = 1
        with ExitStack() as st0:
            with ExitStack() as st1:
                h = sb(st1, "h", [128, 8, T], BF16)
                with ExitStack() as st2:
                    do_norm(st2, h, l, b, 0)
                    P.barrier()
                    P.flush()
                qn = sb(st0, "qn1", [128, 8, T], BF16)
                kd = sb(st0, "kd", [128, 4, T], BF16)
                va = sb(st0, "va", [128, 16, 4, 65], BF16)
                with ExitStack() as st2:
                    sqb = [sb(st2, f"sqb{i}", [128, TS], BF16) for i in range(2)]
                    sdb = [sb(st2, f"sdb{i}", [128, TS], F32) for i in range(2)]
                    rsb = [sb(st2, f"rsb{i}", [128, TS], F32) for i in range(2)]
                    wv = sb(st2, "wv", [128, 8, 512], BF16)
                    for c in range(8):
                        qk_norm_chunk(od_w_in, c * 128, qn[:, c, :], ("qn", c), odq8, h, (sqb, sdb, rsb))
                    for g in range(4):
                        qk_norm_chunk(od_w_in, 1024 + g * 64, kd[:, g, :], ("kd", g), ppv("odkg"), h, (sqb, sdb, rsb), dup64=True)
                    P.op("dve", lambda e: e.memset(va[:, :, :, 64:65], 1.0), wr=["vaones"])
                    wload(wv[:, :, 0:256], wcols(od_w_in, 1280, 256), "wv")
                    for blk in range(16):
                        bank = 4 + blk % 4
                        mm_group(ps[bank][:, 0:256], [(h[:, k, blk * 128:(blk + 1) * 128], wv[:, k, 0:256]) for k in range(8)],
                                 rd=["wv"] + hk_all(blk // 4), wr=[psk(bank)])
                        P.op("act" if blk % 2 == 0 else "dve",
                             (lambda e, blk=blk, bank=bank: e.activation(
                                 out=va[:, blk, :, 0:64], in_=ps[bank][:, 0:256].rearrange("p (g d) -> p g d", g=4), func=AF.Identity))
                             if blk % 2 == 0 else
                             (lambda e, blk=blk, bank=bank: e.tensor_copy(
                                 out=va[:, blk, :, 0:64], in_=ps[bank][:, 0:256].rearrange("p (g d) -> p g d", g=4))),
                             rd=[psk(bank)], wr=[("va", blk)])
                    P.barrier()
                    P.flush()
            if stop == "l1proj":
                break
            yT = sb(st0, "yT", [128, 8, T], BF16)
            with ExitStack() as st2:
                pb = [sb(st2, f"pb{i}", [128, 2, TS], BF16) for i in range(2)]
                den = [sb(st2, f"den{i}", [128, 4], F32) for i in range(2)]
                ytok = [sb(st2, f"ytok{i}", [128, D], BF16) for i in range(2)]
                units = [(qb, g) for qb in range(16) for g in range(4)]

                def swa_a(u):
                    qb, g = units[u]
                    pi = u % 2
                    kbs = [qb - 1, qb] if qb > 0 else [qb]
                    c0 = 0 if qb > 0 else 256
                    for hb in range(2):
                        sbank = hb + 2 * pi

                        def sc(e, sbank=sbank, g=g, kbs=kbs, qb=qb, hb=hb):
                            ins = None
                            for kb in kbs:
                                which = 0 if kb == qb - 1 else 1
                                ins = e.matmul(ps[sbank][:, which * 256:(which + 1) * 256],
                                               lhsT=kd[hb * 64:(hb + 1) * 64, g, kb * 128:(kb + 1) * 128],
                                               rhs=qn[hb * 64:(hb + 1) * 64, 2 * g:2 * g + 2, qb * 128:(qb + 1) * 128],
                                               start=True, stop=True)
                            return ins
                        P.op("pe", sc, rd=[(("kd", g), kb // 4) for kb in kbs] + [(("qn", 2 * g), qb // 4), (("qn", 2 * g + 1), qb // 4)],
                             wr=[psk(sbank)])
                        P.op("act", lambda e, pi=pi, hb=hb, sbank=sbank, c0=c0: e.activation(
                            out=pb[pi][:, hb, c0:512], in_=ps[sbank][:, c0:512], func=AF.Exp),
                            rd=[psk(sbank)], wr=[("pb", pi, hb)])
                        P.op("dve", lambda e, pi=pi, hb=hb, c0=c0: e.tensor_tensor(
                            out=pb[pi][:, hb, c0:512], in0=pb[pi][:, hb, c0:512], in1=maskp[:, c0:512], op=ALU.mult),
                            rd=[("pb", pi, hb), "con"], wr=[("pb", pi, hb)])

                def swa_b(u):
                    qb, g = units[u]
                    pi = u % 2
                    yi = qb % 2
                    kbs = [qb - 1, qb] if qb > 0 else [qb]
                    ybank = 4 + pi

                    def pvm(e, ybank=ybank, pi=pi, kbs=kbs, qb=qb, g=g):
                        ins = None
                        for hc in range(4):
                            hb, e_ = hc // 2, hc % 2
                            for i_, kb in enumerate(kbs):
                                which = 0 if kb == qb - 1 else 1
                                ins = e.matmul(ps[ybank][:, hc * 65:(hc + 1) * 65],
                                               lhsT=pb[pi][:, hb, which * 256 + e_ * 128:which * 256 + (e_ + 1) * 128],
                                               rhs=va[:, kb, g, :], start=(i_ == 0), stop=(i_ == len(kbs) - 1))
                        return ins
                    P.op("pe", pvm, rd=[("pb", pi, 0), ("pb", pi, 1)] + [("va", kb) for kb in kbs] + ["vaones"], wr=[psk(ybank)])
                    yv = ps[ybank][:, 0:260].rearrange("p (hb e d) -> p hb e d", hb=2, e=2)
                    P.op("dve", lambda e, pi=pi, yv=yv, g=g: e.tensor_tensor(
                        out=den[pi][:].rearrange("p (hb e) -> p hb e", hb=2),
                        in0=yv[:, :, :, 64],
                        in1=esink[:, 4 * g:4 * g + 4].rearrange("p (e hb) -> p hb e", hb=2), op=ALU.add),
                        rd=[psk(ybank), "misc"], wr=[("den", pi)])
                    P.op("dve", lambda e, pi=pi: e.reciprocal(out=den[pi][:], in_=den[pi][:]), rd=[("den", pi)], wr=[("den", pi)])
                    P.op("dve", lambda e, pi=pi, yv=yv, g=g, yi=yi: e.tensor_tensor(
                        out=ytok[yi][:, g * 256:(g + 1) * 256].rearrange("p (e hb d) -> p hb e d", e=2, hb=2),
                        in0=yv[:, :, :, 0:64],
                        in1=den[pi][:].rearrange("p (hb e) -> p hb e", hb=2).unsqueeze(3).broadcast_to([128, 2, 2, 64]),
                        op=ALU.mult),
                        rd=[psk(ybank), ("den", pi)], wr=[("ytok", yi, g)])
                    if g == 3:
                        tbank = 6 + yi
                        tp = ps[tbank][:].bitcast(BF16)

                        def trn(e, tp=tp, yi=yi):
                            ins = None
                            for c in range(8):
                                ins = e.transpose(out=tp[:, c * 128:(c + 1) * 128], in_=ytok[yi][:, c * 128:(c + 1) * 128], identity=ident)
                            return ins
                        P.op("pe", trn, rd=[("ytok", yi, g_) for g_ in range(4)] + ["con"], wr=[psk(tbank)])
                        P.op("act", lambda e, tp=tp, qb=qb: e.activation(
                            out=yT[:, :, qb * 128:(qb + 1) * 128], in_=tp.rearrange("p (c t) -> p c t", c=8), func=AF.Identity),
                            rd=[psk(tbank)], wr=[("yT", qb // 4)])

                nun = len(units)
                for u in range(nun + 1):
                    if u < nun:
                        swa_a(u)
                    if u >= 1:
                        swa_b(u - 1)
                if s == 0:
                    dump("yT", yT[:].rearrange("p c t -> p (c t)"), [128, 8 * T], rd=[("yT", t_) for t_ in range(NT)])
                P.barrier()
                P.flush()
            if stop == "swa":
                break
            out_proj_residual(od_w_out, lambda k: yT[:, k, :], lambda tt: [("yT", tt)], l, b, 2)
            P.barrier()
            P.flush()
        if s == 0:
            dump("x1mid", X[:].rearrange("p c t -> p (c t)"), [128, 8 * T], rd=[Xk(c, tt) for c in range(8) for tt in range(NT)])
        with ExitStack() as st1:
            h = sb(st1, "h", [128, 8, T], BF16)
            with ExitStack() as st2:
                do_norm(st2, h, l, b, 1)
                P.barrier()
                P.flush()
            with ExitStack() as st2:
                do_ffn(st2, h, l, b)
                P.barrier()
                P.flush()
        for c in range(8):
            P.dma("sp", out_d[s, c], X[:, c, :], rd=[Xk(c, tt) for tt in range(NT)], wr=[("out", s, c)])
        P.flush()

    P.barrier()
    P.op("sp", None)
    P.flush()
    top.close()
    return nc, P, dbg_out


_CACHE = {}


def kernel(**inputs):
    if "nc" not in _CACHE:
        _CACHE["nc"] = build()[0]
    nc = _CACHE["nc"]
    in_maps = [_host_inputs(inputs, core) for core in range(NCORES)]
    res = run_bass_kernel_spmd(nc, in_maps, core_ids=list(range(NCORES)))
    outs = []
    for core in range(NCORES):
        o = np.asarray(res.results[core]["out"], np.float32).reshape(2, D, T)
        outs.append(o.transpose(0, 2, 1))
    return np.ascontiguousarray(np.concatenate(outs, axis=0)).astype(np.float32)
```

```python
from contextlib import ExitStack

import numpy as np
import concourse.bass as bass
import concourse.mybir as mybir
from concourse.bass_utils import run_bass_kernel_spmd

F32 = mybir.dt.float32
BF16 = mybir.dt.bfloat16
AF = mybir.ActivationFunctionType
ALU = mybir.AluOpType

NCORES = 8
T = 2048
NT = 4
TS = 512
D = 1024
DFF = 2816
NFF = 22
EPS = 1e-6
FQ = [(0, 6), (6, 6), (12, 5), (17, 5)]


class Prog:
    NRING = 8

    def __init__(self, nc):
        self.nc = nc
        self.engs = {"pe": nc.tensor, "act": nc.scalar, "dve": nc.vector,
                     "pool": nc.gpsimd, "sp": nc.sync}
        self.ops = []
        self.nflushed = 0
        self.last_w = {}
        self.readers = {}
        self.last_on_eng = {}
        self.dmas_since_barrier = []
        self.barrier_deps = set()
        self.dma_hist = {}
        self.sems = {e: nc.alloc_semaphore("c_" + e) for e in self.engs}
        self.rings = {}
        self.cnt = {e: 0 for e in self.engs}
        self.done = []
        self.waited = {e: {} for e in self.engs}
        self.nwaits = 0

    limit = None

    def op(self, eng, fn, rd=(), wr=(), dma=False):
        i = len(self.ops)
        if self.limit is not None and i >= self.limit and not dma and fn is not None:
            return None
        o = dict(eng=eng, fn=fn, dma=dma)
        ops = self.ops
        d = set()
        raw = set()
        for k in rd:
            j = self.last_w.get(k)
            if j is not None:
                d.add(j)
                raw.add(j)
        for k in wr:
            j = self.last_w.get(k)
            if j is not None:
                d.add(j)
            d.update(self.readers.get(k, ()))
        keep = set()
        for j in d:
            oj = ops[j]
            if (not oj["dma"]) and (not dma) and oj["eng"] == eng and eng == "pe":
                continue
            keep.add(j)
        for j in self.barrier_deps:
            oj = ops[j]
            if (not oj["dma"]) and (not dma) and oj["eng"] == eng:
                continue
            keep.add(j)
        for k in rd:
            self.readers.setdefault(k, []).append(i)
        for k in wr:
            self.last_w[k] = i
            self.readers[k] = []
        if dma:
            hist = self.dma_hist.setdefault(eng, [])
            c = len(hist)
            if eng not in self.rings:
                self.rings[eng] = [self.nc.alloc_semaphore(f"d_{eng}{r}") for r in range(self.NRING)]
            o["ring"] = c % self.NRING
            o["rval"] = 16 * (c // self.NRING + 1)
            if c >= self.NRING:
                keep.add(hist[c - self.NRING])
            hist.append(i)
            self.dmas_since_barrier.append(i)
        else:
            self.last_on_eng[eng] = i
        o["deps"] = keep
        ops.append(o)
        self.done.append(None)
        return i

    def dma(self, eng, out, in_, rd=(), wr=()):
        return self.op(eng, lambda e: e.dma_start(out=out, in_=in_), rd, wr, dma=True)

    def barrier(self):
        self.barrier_deps = set(self.last_on_eng.values()) | set(self.dmas_since_barrier)
        self.dmas_since_barrier = []

    def flush(self):
        ops = self.ops
        n = len(ops)
        start = self.nflushed
        needs = set()
        for i in range(start, n):
            needs.update(ops[i]["deps"])
        needs.update(self.last_on_eng.values())
        needs.update(self.last_w.values())
        for r in self.readers.values():
            needs.update(r)
        needs.update(self.barrier_deps)
        for i in range(start, n):
            o = ops[i]
            e = o["eng"]
            eng = self.engs[e]
            need = {}
            for j in o["deps"]:
                s, v = self.done[j]
                k = id(s)
                if k not in need or need[k][1] < v:
                    need[k] = (s, v)
            for k, (s, v) in need.items():
                if self.waited[e].get(k, 0) >= v:
                    continue
                eng.wait_ge(s, v)
                self.nwaits += 1
                self.waited[e][k] = v
            if o["fn"] is None:
                self.done[i] = (self.sems[e], self.cnt[e])
                continue
            ins = o["fn"](eng)
            if o["dma"]:
                s = self.rings[e][o["ring"]]
                ins.then_inc(s, 16)
                self.done[i] = (s, o["rval"])
            elif i in needs:
                self.cnt[e] += 1
                ins.then_inc(self.sems[e], 1)
                self.done[i] = (self.sems[e], self.cnt[e])
            else:
                self.done[i] = (self.sems[e], self.cnt[e] + 1)
            o["fn"] = None
        self.nflushed = n


PP = {}
_off = 0
for _name, _n in [("cT", 16), ("adab", 96), ("gmix", 16), ("gffn", 16), ("lcw", 16), ("lcb", 4),
                  ("lba", 4), ("lbx", 4), ("llam", 4), ("evqg", 1), ("evkg", 1), ("odqg", 1),
                  ("odkg", 1), ("sinks", 16), ("fcw", 132), ("fcb", 44)]:
    PP[_name] = (_off, _n)
    _off += _n
NPP = _off

CO = {}
_off = 0
for _name, _n in [("ident", 128), ("ones", 128), ("bones", 128), ("tri", 128), ("ntri", 128), ("nones", 128), ("md", 2048),
                  ("maskp", 512), ("maskc", 512)]:
    CO[_name] = (_off, _n)
    _off += _n
NCON = _off


def _consts():
    c = np.zeros((128, NCON), np.float32)
    p = np.arange(128)[:, None]
    m = np.arange(128)[None, :]
    c[:, CO["ident"][0]:CO["ident"][0] + 128] = (p == m)
    c[:, CO["ones"][0]:CO["ones"][0] + 128] = 1.0
    c[:, CO["bones"][0]:CO["bones"][0] + 128] = ((p // 64) == (m // 64))
    c[:, CO["tri"][0]:CO["tri"][0] + 128] = (p >= m)
    c[:, CO["ntri"][0]:CO["ntri"][0] + 128] = -1.0 * (p >= m)
    c[:, CO["nones"][0]:CO["nones"][0] + 128] = -1.0
    t = np.arange(512)[None, :]
    for d in range(4):
        c[:, CO["md"][0] + d * 512:CO["md"][0] + (d + 1) * 512] = ((d * 128 + p) < t)
    c[:, CO["maskp"][0]:CO["maskp"][0] + 512] = np.concatenate([np.tile((p > m), (1, 2)), np.tile((p <= m), (1, 2))], 1)
    c[:, CO["maskc"][0]:CO["maskc"][0] + 512] = np.tile((p <= m), (1, 4))
    return c


def _pcol(v):
    v = np.asarray(v, np.float32)
    return np.ascontiguousarray(v.reshape(-1, 128).T)


def _host_inputs(inp, core):
    f = lambda a: np.ascontiguousarray(np.asarray(a, np.float32))
    b0 = 2 * core
    x = f(inp["x"][b0:b0 + 2])
    xT = np.ascontiguousarray(x.transpose(0, 2, 1)).reshape(2, 8, 128, T)
    pp = np.zeros((128, NPP), np.float32)

    def put(name, arr):
        o, n = PP[name]
        pp[:, o:o + n] = np.asarray(arr, np.float32).reshape(128, n)

    c = f(inp["c"][b0:b0 + 2])
    put("cT", c.reshape(2, 8, 128).transpose(2, 1, 0))
    put("adab", np.stack([_pcol(inp["ada_b"][l]) for l in range(2)], 1))
    put("gmix", np.stack([_pcol(inp["norm_mix_g"][l]) for l in range(2)], 1))
    put("gffn", np.stack([_pcol(inp["norm_ffn_g"][l]) for l in range(2)], 1))
    cw = f(inp["ev_conv_w"][0])
    put("lcw", np.stack([_pcol(cw[k]) for k in range(4)], 2))
    put("lcb", _pcol(inp["ev_conv_b"][0]))
    put("lba", _pcol(inp["ev_ba"][0]))
    put("lbx", _pcol(inp["ev_bx"][0]))
    put("llam", _pcol(inp["ev_lam"][0]))
    put("evqg", np.tile(f(inp["ev_qn_g"][0]), 2)[:, None])
    put("evkg", np.tile(f(inp["ev_kn_g"][0]), 2)[:, None])
    put("odqg", np.tile(f(inp["od_qn_g"][0]), 2)[:, None])
    put("odkg", np.tile(f(inp["od_kn_g"][0]), 2)[:, None])
    put("sinks", np.tile(f(inp["od_sinks"][0])[None, :], (128, 1)))
    fcw = f(inp["ffn_conv_w"])
    put("fcw", np.stack([np.stack([_pcol(fcw[l, k]) for k in range(3)], 2) for l in range(2)], 1))
    put("fcb", np.stack([_pcol(inp["ffn_conv_b"][l]) for l in range(2)], 1))

    def bd(w):
        w = f(w)
        o = np.zeros((128, 4, 128), np.float32)
        for j in range(4):
            o[0:64, j, 0:64] = w[2 * j]
            o[64:128, j, 64:128] = w[2 * j + 1]
        return o

    return {
        "xT": xT, "pp": pp, "consts": _consts(),
        "ada_w": f(inp["ada_w"]),
        "ev_w_in": f(inp["ev_w_in"][0]), "ev_w_out": f(inp["ev_w_out"][0]),
        "od_w_in": f(inp["od_w_in"][0]), "od_w_out": f(inp["od_w_out"][0]),
        "w_gate": f(inp["ffn_w_gate"]), "w_up": f(inp["ffn_w_up"]), "w_down": f(inp["ffn_w_down"]),
        "wabd": bd(inp["ev_wa"][0]), "wxbd": bd(inp["ev_wx"][0]),
    }


def build(nseq=2, dbg=None, stop=None):
    dbg = dbg or set()
    nc = bass.Bass("TRN2", target_bir_lowering=False)
    P = Prog(nc)

    def din(name, shape):
        return nc.dram_tensor(name, list(shape), F32, kind="ExternalInput").ap()

    xT = din("xT", [2, 8, 128, T])
    pp_d = din("pp", [128, NPP])
    con_d = din("consts", [128, NCON])
    ada_w = din("ada_w", [2, D, 6 * D])
    ev_w_in = din("ev_w_in", [D, 2560])
    ev_w_out = din("ev_w_out", [D, D])
    od_w_in = din("od_w_in", [D, 1536])
    od_w_out = din("od_w_out", [D, D])
    w_gate = din("w_gate", [2, D, DFF])
    w_up = din("w_up", [2, D, DFF])
    w_down = din("w_down", [2, DFF, D])
    wabd_d = din("wabd", [128, 4, 128])
    wxbd_d = din("wxbd", [128, 4, 128])
    out_d = nc.dram_tensor("out", [2, 8, 128, T], F32, kind="ExternalOutput").ap()
    dbg_out = {}

    def dump(name, ap_sb, shape, rd):
        if name not in dbg:
            return
        d = nc.dram_tensor("dbg_" + name, list(shape), F32, kind="ExternalOutput").ap()
        dbg_out[name] = d
        P.dma("pool", d, ap_sb, rd=rd, wr=[("dbgout", name)])

    top = ExitStack()

    uid = [0]
    sb_lo = (nc.sbuf_base + 63) // 64 * 64
    free_list = [[sb_lo, nc.sbuf_top]]
    peak = [0]

    def sb(st, name, shape, dt):
        uid[0] += 1
        nbytes = int(np.prod(shape[1:])) * (2 if dt == BF16 else 4)
        nbytes = (nbytes + 63) // 64 * 64
        for seg in free_list:
            if seg[1] - seg[0] >= nbytes:
                off = seg[0]
                seg[0] += nbytes
                break
        else:
            raise RuntimeError(f"SBUF full allocating {name} ({nbytes} B); free={free_list}")
        peak[0] = max(peak[0], off + nbytes)

        def release():
            free_list.append([off, off + nbytes])
            free_list.sort()
            merged = []
            for sg in free_list:
                if sg[0] >= sg[1]:
                    continue
                if merged and merged[-1][1] == sg[0]:
                    merged[-1][1] = sg[1]
                else:
                    merged.append(sg)
            free_list[:] = merged
        st.callback(release)
        return nc.alloc_sbuf_tensor_at(f"{name}_u{uid[0]}", list(shape), dt, offset=off)

    ps = [top.enter_context(nc.psum_tensor(f"ps{i}", [128, 512], F32)) for i in range(8)]
    X = sb(top, "X", [128, 8, T], F32)
    pp = sb(top, "pp", [128, NPP], F32)
    con = sb(top, "con", [128, NCON], BF16)
    modp = sb(top, "modp", [128, 2, 2, 6, 8], F32)
    misc = sb(top, "misc", [128, 64], F32)
    wring = [sb(top, f"wring{i}", [128, 8, 128], BF16) for i in range(8)]
    ring_n = [0]

    def cview(name):
        o, n = CO[name]
        return con[:, o:o + n]

    ident, ones_c, bones, tri = cview("ident"), cview("ones"), cview("bones"), cview("tri")
    ntri, nones = cview("ntri"), cview("nones")
    md_all = cview("md")
    maskp, maskc = cview("maskp"), cview("maskc")

    def ppv(name):
        o, n = PP[name]
        return pp[:, o:o + n]

    def wslot():
        i = ring_n[0] % len(wring)
        ring_n[0] += 1
        return wring[i], ("wring", i)

    def wload(dst, src, key):
        P.dma("pool", dst, src, wr=[key])

    def wcols(w2d, c0, n):
        return w2d[:, c0:c0 + n].rearrange("(kc p) n -> p kc n", p=128)

    def mm_group(out, pairs, rd, wr):
        def fn(e):
            ins = None
            n = len(pairs)
            for i, (l, r) in enumerate(pairs):
                ins = e.matmul(out, lhsT=l, rhs=r, start=(i == 0), stop=(i == n - 1))
            return ins
        P.op("pe", fn, rd=rd, wr=wr)

    def tsl(tt):
        return slice(tt * TS, (tt + 1) * TS)

    Xk = lambda c, tt: ("X", c, tt)
    psk = lambda b: ("ps", b)

    P.dma("sp", pp[:], pp_d, wr=["pp"])
    P.dma("pool", con[:], con_d, wr=["con"])
    with ExitStack() as st:
        cs = sb(st, "cs", [128, 8, 2], BF16)
        wbig = [sb(st, f"wbig{i}", [128, 8, 1024], BF16) for i in range(2)]
        mod = sb(st, "mod", [128, 96, 2], F32)
        tmpa = sb(st, "tmpa", [128, 16], F32)
        o, n = PP["cT"]
        P.op("act", lambda e: e.activation(out=cs[:].rearrange("p k b -> p (k b)"), in_=pp[:, o:o + n], func=AF.Silu),
             rd=["pp"], wr=["cs"])
        for l in range(2):
            for pc in range(6):
                wb = wbig[(l * 6 + pc) % 2]
                wk = ("wbig", (l * 6 + pc) % 2)
                wload(wb[:], wcols(ada_w[l], pc * 1024, 1024), wk)
                for nn in range(8):
                    g = pc * 8 + nn
                    col = (l * 48 + g) * 2
                    mm_group(ps[7][:, col:col + 2],
                             [(wb[:, k, nn * 128:(nn + 1) * 128], cs[:, k, :]) for k in range(8)],
                             rd=[wk, "cs"], wr=[psk(7)])
        P.op("dve", lambda e: e.tensor_tensor(
            out=mod[:], in0=ps[7][:, 0:192].rearrange("p (g b) -> p g b", b=2),
            in1=ppv("adab").unsqueeze(2).broadcast_to([128, 96, 2]), op=ALU.add),
            rd=[psk(7), "pp"], wr=["mod"])
        modv = mod[:].rearrange("p (l j c) b -> p l j c b", l=2, j=6)
        for l in range(2):
            for b in range(2):
                for (dst, jsc, gname) in ((0, 1, "gmix"), (3, 4, "gffn")):
                    go, _ = PP[gname]
                    P.op("dve", lambda e, l=l, b=b, dst=dst, jsc=jsc, go=go: e.scalar_tensor_tensor(
                        out=modp[:, l, b, dst, :], in0=modv[:, l, jsc, :, b], scalar=1.0,
                        in1=pp[:, go + l * 8:go + l * 8 + 8], op0=ALU.add, op1=ALU.mult),
                        rd=["mod", "pp"], wr=["modp"])
                for (dst, j) in ((1, 0), (2, 2), (4, 3), (5, 5)):
                    P.op("dve", lambda e, l=l, b=b, dst=dst, j=j: e.tensor_copy(
                        out=modp[:, l, b, dst, :], in_=modv[:, l, j, :, b]), rd=["mod"], wr=["modp"])
        P.op("act", lambda e: e.activation(out=tmpa[:, 0:4], in_=ppv("llam"), func=AF.Exp, scale=-1.0),
             rd=["pp"], wr=["tmpa"])
        P.op("act", lambda e: e.activation(out=tmpa[:, 4:8], in_=tmpa[:, 0:4], func=AF.Ln, bias=1.0),
             rd=["tmpa"], wr=["tmpa2"])
        P.op("dve", lambda e: e.tensor_scalar(out=misc[:, 0:4], in0=tmpa[:, 4:8], scalar1=-8.0, scalar2=None,
                                              op0=ALU.mult), rd=["tmpa2"], wr=["misc"])
        P.op("dve", lambda e: e.tensor_scalar(out=misc[:, 4:5], in0=ppv("evqg"), scalar1=0.125, scalar2=None,
                                              op0=ALU.mult), rd=["pp"], wr=["misc"])
        P.op("dve", lambda e: e.tensor_scalar(out=misc[:, 5:6], in0=ppv("odqg"), scalar1=0.125, scalar2=None,
                                              op0=ALU.mult), rd=["pp"], wr=["misc"])
        P.op("act", lambda e: e.activation(out=misc[:, 8:24], in_=ppv("sinks"), func=AF.Exp),
             rd=["pp"], wr=["misc"])
        dump("modp", modp[:].rearrange("p l b j c -> p (l b j c)"), [128, 192], rd=["modp"])
        P.barrier()
        P.flush()
    cl = misc[:, 0:4]
    evq8 = misc[:, 4:5]
    odq8 = misc[:, 5:6]
    esink = misc[:, 8:24]

    def do_norm(st, h, l, b, which):
        ia, ish = (0, 1) if which == 0 else (3, 4)
        sq = [sb(st, f"sq{i}", [128, 8, TS], BF16) for i in range(2)]
        sd = [sb(st, f"sd{i}", [128, TS], F32) for i in range(2)]
        rs = [sb(st, f"rs{i}", [128, TS], F32) for i in range(2)]
        tm = [sb(st, f"tm{i}", [128, TS], F32) for i in range(3)]
        ti = 0
        for tt in range(NT):
            i2 = tt % 2
            P.op("act", lambda e, tt=tt, i2=i2: e.activation(out=sq[i2][:], in_=X[:, :, tsl(tt)], func=AF.Square),
                 rd=[Xk(c, tt) for c in range(8)], wr=[("sq", i2)])
            bank = 6 + i2
            mm_group(ps[bank][:], [(ones_c, sq[i2][:, c, :]) for c in range(8)], rd=[("sq", i2), "con"], wr=[psk(bank)])
            P.op("act", lambda e, i2=i2, bank=bank: e.activation(out=sd[i2][:], in_=ps[bank][:], func=AF.Ln,
                                                                  scale=1.0 / D, bias=EPS),
                 rd=[psk(bank)], wr=[("sd", i2)])
            P.op("act", lambda e, i2=i2: e.activation(out=rs[i2][:], in_=sd[i2][:], func=AF.Exp, scale=-0.5),
                 rd=[("sd", i2)], wr=[("rs", i2)])
            for c in range(8):
                t3 = ti % 3
                ti += 1
                P.op("dve", lambda e, c=c, tt=tt, i2=i2, t3=t3: e.tensor_tensor(
                    out=tm[t3][:], in0=X[:, c, tsl(tt)], in1=rs[i2][:], op=ALU.mult),
                    rd=[Xk(c, tt), ("rs", i2)], wr=[("tm", t3)])
                P.op("act", lambda e, c=c, tt=tt, t3=t3: e.activation(
                    out=h[:, c, tsl(tt)], in_=tm[t3][:], func=AF.Identity,
                    scale=modp[:, l, b, ia, c:c + 1], bias=modp[:, l, b, ish, c:c + 1]),
                    rd=[("tm", t3), "modp"], wr=[("h", c, tt)])

    def hk_all(tt):
        return [("h", c, tt) for c in range(8)]

    def out_proj_residual(w2d, ysrc, ykeys, l, b, gidx, r0=0, nk=8):
        slots = {}

        def ld(n):
            slots[n] = wslot()
            wload(slots[n][0][:, 0:nk, :],
                  w2d[r0 * 128:(r0 + nk) * 128, n * 128:(n + 1) * 128].rearrange("(kc p) n -> p kc n", p=128), slots[n][1])
        for n in range(3):
            ld(n)
        for n in range(8):
            if n + 3 < 8:
                ld(n + 3)
            ws, wk = slots[n]
            for tt in range(NT):
                bank = (n * NT + tt) % 6
                mm_group(ps[bank][:], [(ws[:, k, :], ysrc(k)[:, tsl(tt)]) for k in range(nk)],
                         rd=[wk] + ykeys(tt), wr=[psk(bank)])
                P.op("dve", lambda e, n=n, tt=tt, bank=bank: e.scalar_tensor_tensor(
                    out=X[:, n, tsl(tt)], in0=ps[bank][:], scalar=modp[:, l, b, gidx, n:n + 1],
                    in1=X[:, n, tsl(tt)], op0=ALU.mult, op1=ALU.add),
                    rd=[psk(bank), Xk(n, tt), "modp"], wr=[Xk(n, tt)])

    def do_ffn(st, h, l, b):
        GL = 2
        act = sb(st, "act", [128, 6, T], BF16)
        wd = [sb(st, f"wd{i}", [128, 6, D], BF16) for i in range(2)]
        gb = [sb(st, f"gb{i}", [128, 2 + T], F32) for i in range(GL + 1)]
        gc = sb(st, "gc", [128, T], F32)
        sg = sb(st, "sg", [128, T], BF16)
        fo, _ = PP["fcw"]
        bo, _ = PP["fcb"]
        for i in range(GL + 1):
            P.op("dve", lambda e, i=i: e.memset(gb[i][:, 0:2], 0.0), wr=[("gbpad", i)])
        wg2, wu2 = w_gate[l], w_up[l]
        slots = {}
        qof = {}
        for qi, (c0, ncq) in enumerate(FQ):
            for ci in range(ncq):
                qof[c0 + ci] = (qi, ci, c0, ncq)

        def load_c(c):
            sg_, sk = wslot()
            wload(sg_[:], wcols(wg2, c * 128, 128), sk)
            su_, uk = wslot()
            wload(su_[:], wcols(wu2, c * 128, 128), uk)
            slots[c] = (sg_, sk, su_, uk)

        def gate_stage(c):
            if c + 1 < NFF:
                load_c(c + 1)
            sg_, sk, su_, uk = slots[c]
            gi = c % (GL + 1)
            for tt in range(NT):
                bank = tt
                mm_group(ps[bank][:], [(sg_[:, k, :], h[:, k, tsl(tt)]) for k in range(8)],
                         rd=[sk] + hk_all(tt), wr=[psk(bank)])
                P.op("act", lambda e, gi=gi, tt=tt, bank=bank: e.activation(
                    out=gb[gi][:, 2 + tt * TS:2 + (tt + 1) * TS], in_=ps[bank][:], func=AF.Identity),
                    rd=[psk(bank)], wr=[("gb", gi, tt)])

        def rest_stage(c):
            qi, ci, c0, ncq = qof[c]
            wdb = wd[qi % 2]
            wdk = ("wd", qi % 2)
            if ci == 0:
                wload(wdb[:, 0:ncq, :],
                      w_down[l][c0 * 128:(c0 + ncq) * 128, :].rearrange("(kc p) n -> p kc n", p=128), wdk)
            sg_, sk, su_, uk = slots.pop(c)
            gi = c % (GL + 1)
            wo = fo + (l * NFF + c) * 3
            P.op("dve", lambda e, gi=gi, wo=wo, c=c: e.tensor_scalar(
                out=gc[:], in0=gb[gi][:, 0:T], scalar1=pp[:, wo:wo + 1],
                scalar2=pp[:, bo + l * NFF + c:bo + l * NFF + c + 1], op0=ALU.mult, op1=ALU.add),
                rd=[("gb", gi, t_) for t_ in range(NT)] + [("gbpad", gi), "pp"], wr=["gc"])
            for k in (1, 2):
                P.op("dve", lambda e, gi=gi, wo=wo, k=k: e.scalar_tensor_tensor(
                    out=gc[:], in0=gb[gi][:, k:k + T], scalar=pp[:, wo + k:wo + k + 1], in1=gc[:],
                    op0=ALU.mult, op1=ALU.add),
                    rd=[("gb", gi, t_) for t_ in range(NT)] + ["gc", "pp"], wr=["gc"])
            P.op("act", lambda e: e.activation(out=sg[:], in_=gc[:], func=AF.Silu), rd=["gc"], wr=["sg"])
            for tt in range(NT):
                bank = 4 + tt % 2
                mm_group(ps[bank][:], [(su_[:, k, :], h[:, k, tsl(tt)]) for k in range(8)],
                         rd=[uk] + hk_all(tt), wr=[psk(bank)])
                P.op("dve", lambda e, ci=ci, tt=tt, bank=bank: e.tensor_tensor(
                    out=act[:, ci, tsl(tt)], in0=sg[:, tsl(tt)], in1=ps[bank][:], op=ALU.mult),
                    rd=["sg", psk(bank)], wr=[("act", ci, tt)])
            if ci == ncq - 1:
                for n in range(8):
                    for tt in range(NT):
                        bank = 6 + (n * NT + tt) % 2
                        mm_group(ps[bank][:], [(wdb[:, cj, n * 128:(n + 1) * 128], act[:, cj, tsl(tt)]) for cj in range(ncq)],
                                 rd=[wdk] + [("act", cj, tt) for cj in range(ncq)], wr=[psk(bank)])
                        P.op("dve", lambda e, n=n, tt=tt, bank=bank: e.scalar_tensor_tensor(
                            out=X[:, n, tsl(tt)], in0=ps[bank][:], scalar=modp[:, l, b, 5, n:n + 1],
                            in1=X[:, n, tsl(tt)], op0=ALU.mult, op1=ALU.add),
                            rd=[psk(bank), Xk(n, tt), "modp"], wr=[Xk(n, tt)])

        load_c(0)
        for i in range(NFF + GL):
            if i < NFF:
                gate_stage(i)
            if i - GL >= 0:
                rest_stage(i - GL)

    qkc = [0]

    def qk_norm_chunk(w2d, col0, dst, dstkey, gain_ap, h, tmp, dup64=False, split=None):
        ws, wk = wslot()
        if dup64:
            for hb in range(2):
                P.dma("pool", ws[:, :, hb * 64:(hb + 1) * 64], wcols(w2d, col0, 64), wr=[wk])
        else:
            wload(ws[:], wcols(w2d, col0, 128), wk)
        sqb, sdb, rsb = tmp
        for tt in range(NT):
            i2 = qkc[0] % 2
            bA = qkc[0] % 4
            bB = 4 + qkc[0] % 2
            qkc[0] += 1
            mm_group(ps[bA][:], [(ws[:, k, :], h[:, k, tsl(tt)]) for k in range(8)], rd=[wk] + hk_all(tt), wr=[psk(bA)])
            P.op("act", lambda e, i2=i2, bA=bA: e.activation(out=sqb[i2][:], in_=ps[bA][:], func=AF.Square),
                 rd=[psk(bA)], wr=[("sqb", i2)])
            mm_group(ps[bB][:], [(bones, sqb[i2][:])], rd=[("sqb", i2), "con"], wr=[psk(bB)])
            P.op("act", lambda e, i2=i2, bB=bB: e.activation(out=sdb[i2][:], in_=ps[bB][:], func=AF.Ln,
                                                              scale=1.0 / 64, bias=EPS),
                 rd=[psk(bB)], wr=[("sdb", i2)])
            P.op("act", lambda e, i2=i2: e.activation(out=rsb[i2][:], in_=sdb[i2][:], func=AF.Exp, scale=-0.5),
                 rd=[("sdb", i2)], wr=[("rsb", i2)])
            if split is None:
                P.op("dve", lambda e, i2=i2, bA=bA, tt=tt: e.scalar_tensor_tensor(
                    out=dst[:, tsl(tt)], in0=ps[bA][:], scalar=gain_ap, in1=rsb[i2][:], op0=ALU.mult, op1=ALU.mult),
                    rd=[psk(bA), ("rsb", i2), "misc", "pp"], wr=[(dstkey, tt)])
            else:
                for hh in range(2):
                    pr = slice(hh * 64, (hh + 1) * 64)
                    P.op("dve", lambda e, i2=i2, bA=bA, tt=tt, pr=pr, hh=hh: e.scalar_tensor_tensor(
                        out=split[hh][pr, tsl(tt)], in0=ps[bA][pr, :], scalar=gain_ap[pr, :], in1=rsb[i2][pr, :],
                        op0=ALU.mult, op1=ALU.mult),
                        rd=[psk(bA), ("rsb", i2), "misc", "pp"], wr=[(dstkey, hh, tt)])

    for s in range(nseq):
        b = s
        for c in range(8):
            P.dma("sp", X[:, c, :], xT[s, c], wr=[Xk(c, tt) for tt in range(NT)])
        l = 0
        with ExitStack() as st0:
            sty = ExitStack()
            ya = sb(sty, "ya", [128, 4, T], BF16)
            with ExitStack() as st1:
                h = sb(st1, "h", [128, 8, T], BF16)
                with ExitStack() as st2:
                    do_norm(st2, h, l, b, 0)
                    if s == 0:
                        dump("h0", h[:].rearrange("p c t -> p (c t)"), [128, 8 * T],
                             rd=[("h", c, tt) for c in range(8) for tt in range(NT)])
                    P.barrier()
                    P.flush()
                if stop == "norm0":
                    break
                with ExitStack() as st2:
                    xr = sb(st2, "xr", [128, 3 + T], F32)
                    xc = sb(st2, "xc", [128, T], F32)
                    xcb = sb(st2, "xcb", [128, T], BF16)
                    ra = sb(st2, "ra", [128, T], F32)
                    ig = sb(st2, "ig", [128, T], F32)
                    s2 = sb(st2, "s2", [128, T], F32)
                    gel = sb(st2, "gel", [128, T], F32)
                    gx = sb(st2, "gx", [128, T], F32)
                    wabd = sb(st2, "wabd", [128, 4, 128], BF16)
                    wxbd = sb(st2, "wxbd", [128, 4, 128], BF16)
                    P.dma("pool", wabd[:], wabd_d, wr=["wabd"])
                    P.dma("pool", wxbd[:], wxbd_d, wr=["wxbd"])
                    P.op("dve", lambda e: e.memset(xr[:, 0:3], 0.0), wr=["xrpad"])
                    lcw, _ = PP["lcw"]
                    lcb, _ = PP["lcb"]
                    lba, _ = PP["lba"]
                    lbx, _ = PP["lbx"]
                    import os as _os
                    for j in [int(v) for v in _os.environ.get("LRU_CHUNKS", "0,1,2,3").split(",")]:
                        wsx, wkx = wslot()
                        wload(wsx[:], wcols(ev_w_in, j * 128, 128), wkx)
                        wsg, wkg = wslot()
                        wload(wsg[:], wcols(ev_w_in, 512 + j * 128, 128), wkg)
                        for tt in range(NT):
                            bank = tt % 2
                            mm_group(ps[bank][:], [(wsx[:, k, :], h[:, k, tsl(tt)]) for k in range(8)],
                                     rd=[wkx] + hk_all(tt), wr=[psk(bank)])
                            P.op("act", lambda e, tt=tt, bank=bank: e.activation(
                                out=xr[:, 3 + tt * TS:3 + (tt + 1) * TS], in_=ps[bank][:], func=AF.Identity),
                                rd=[psk(bank)], wr=[("xr", tt)])
                        xrk = [("xr", t_) for t_ in range(NT)]
                        P.op("dve", lambda e, j=j: e.tensor_scalar(
                            out=xc[:], in0=xr[:, 0:T], scalar1=pp[:, lcw + j * 4:lcw + j * 4 + 1],
                            scalar2=pp[:, lcb + j:lcb + j + 1], op0=ALU.mult, op1=ALU.add),
                            rd=xrk + ["xrpad", "pp"], wr=["xc"])
                        for k in (1, 2, 3):
                            P.op("dve", lambda e, j=j, k=k: e.scalar_tensor_tensor(
                                out=xc[:], in0=xr[:, k:k + T], scalar=pp[:, lcw + j * 4 + k:lcw + j * 4 + k + 1],
                                in1=xc[:], op0=ALU.mult, op1=ALU.add), rd=xrk + ["xc", "pp"], wr=["xc"])
                        P.op("act", lambda e: e.activation(out=xcb[:], in_=xc[:], func=AF.Identity), rd=["xc"], wr=["xcb"])
                        for tt in range(NT):
                            bank = 2 + tt % 2
                            mm_group(ps[bank][:], [(wabd[:, j, :], xcb[:, tsl(tt)])], rd=["wabd", "xcb"], wr=[psk(bank)])
                            P.op("act", lambda e, j=j, tt=tt, bank=bank: e.activation(
                                out=ra[:, tsl(tt)], in_=ps[bank][:], func=AF.Sigmoid, bias=pp[:, lba + j:lba + j + 1]),
                                rd=[psk(bank), "pp"], wr=[("ra", tt)])
                            bank2 = 4 + tt % 2
                            mm_group(ps[bank2][:], [(wxbd[:, j, :], xcb[:, tsl(tt)])], rd=["wxbd", "xcb"], wr=[psk(bank2)])
                            P.op("act", lambda e, j=j, tt=tt, bank2=bank2: e.activation(
                                out=ig[:, tsl(tt)], in_=ps[bank2][:], func=AF.Sigmoid, bias=pp[:, lbx + j:lbx + j + 1]),
                                rd=[psk(bank2), "pp"], wr=[("ig", tt)])
                        rak = [("ra", t_) for t_ in range(NT)]
                        igk = [("ig", t_) for t_ in range(NT)]
                        P.op("act", lambda e, j=j: e.activation(out=ra[:], in_=ra[:], func=AF.Exp, scale=cl[:, j:j + 1]),
                             rd=rak + ["misc"], wr=rak)
                        P.op("act", lambda e: e.activation(out=s2[:], in_=ra[:], func=AF.Square), rd=rak, wr=["s2"])
                        P.op("act", lambda e: e.activation(out=s2[:], in_=s2[:], func=AF.Sqrt, scale=-1.0, bias=1.0),
                             rd=["s2"], wr=["s2"])
                        P.op("dve", lambda e: e.tensor_tensor(out=s2[:], in0=s2[:], in1=ig[:], op=ALU.mult),
                             rd=["s2"] + igk, wr=["s2"])
                        P.op("dve", lambda e: e.tensor_tensor(out=s2[:], in0=s2[:], in1=xc[:], op=ALU.mult),
                             rd=["s2", "xc"], wr=["s2"])
                        P.op("dve", lambda e: e.tensor_tensor_scan(out=xc[:], data0=ra[:], data1=s2[:], initial=0.0,
                                                                    op0=ALU.mult, op1=ALU.add),
                             rd=rak + ["s2", "xc"], wr=["xc"])
                        for tt in range(NT):
                            bank = 6 + tt % 2
                            mm_group(ps[bank][:], [(wsg[:, k, :], h[:, k, tsl(tt)]) for k in range(8)],
                                     rd=[wkg] + hk_all(tt), wr=[psk(bank)])
                            P.op("act", lambda e, tt=tt, bank=bank: e.activation(
                                out=gx[:, tsl(tt)], in_=ps[bank][:], func=AF.Identity),
                                rd=[psk(bank)], wr=[("gx", tt)])
                        gxk = [("gx", t_) for t_ in range(NT)]
                        P.op("act", lambda e: e.activation(out=gel[:], in_=gx[:], func=AF.Square), rd=gxk, wr=["gel"])
                        P.op("dve", lambda e: e.tensor_scalar(out=gel[:], in0=gel[:], scalar1=0.044715, scalar2=1.0,
                                                              op0=ALU.mult, op1=ALU.add), rd=["gel"], wr=["gel"])
                        P.op("dve", lambda e: e.tensor_tensor(out=gel[:], in0=gel[:], in1=gx[:], op=ALU.mult),
                             rd=["gel"] + gxk, wr=["gel"])
                        P.op("act", lambda e: e.activation(out=gel[:], in_=gel[:], func=AF.Sigmoid, scale=1.5957691216057308),
                             rd=["gel"], wr=["gel"])
                        P.op("dve", lambda e: e.tensor_tensor(out=gel[:], in0=gel[:], in1=gx[:], op=ALU.mult),
                             rd=["gel"] + gxk, wr=["gel"])
                        P.op("dve", lambda e, j=j: e.tensor_tensor(out=ya[:, j, :], in0=xc[:], in1=gel[:], op=ALU.mult),
                             rd=["xc", "gel"], wr=[("ya", j)])
                    if s == 0:
                        dump("lxc", xc[:], [128, T], rd=["xc"])
                        dump("lra", ra[:], [128, T], rd=[("ra", t_) for t_ in range(NT)])
                        dump("lig", ig[:], [128, T], rd=[("ig", t_) for t_ in range(NT)])
                        dump("ls2", s2[:], [128, T], rd=["s2"])
                        dump("ya", ya[:].rearrange("p c t -> p (c t)"), [128, 4 * T], rd=[("ya", j) for j in range(4)])
                    P.barrier()
                    P.flush()
                if stop == "lru":
                    sty.close()
                    break
                out_proj_residual(ev_w_out, lambda k: ya[:, k, :], lambda tt: [("ya", j) for j in range(4)], l, b, 2, r0=0, nk=4)
                P.barrier()
                P.flush()
                sty.close()
                qz = [sb(st0, f"qz{i}", [128, 4, T], BF16) for i in range(2)]
                kn = sb(st0, "kn", [128, 4, T], BF16)
                vt = sb(st0, "vt", [128, 16, 512], BF16)
                P.op("dve", lambda e: e.memset(qz[0][64:128, :, :], 0.0), wr=[("qzpad", 0)])
                P.op("dve", lambda e: e.memset(qz[1][0:64, :, :], 0.0), wr=[("qzpad", 1)])
                with ExitStack() as st2:
                    sqb = [sb(st2, f"sqb{i}", [128, TS], BF16) for i in range(2)]
                    sdb = [sb(st2, f"sdb{i}", [128, TS], F32) for i in range(2)]
                    rsb = [sb(st2, f"rsb{i}", [128, TS], F32) for i in range(2)]
                    for j in range(4):
                        qk_norm_chunk(ev_w_in, 1024 + j * 128, None, ("qz", j), evq8, h, (sqb, sdb, rsb),
                                      split=(qz[0][:, j, :], qz[1][:, j, :]))
                        qk_norm_chunk(ev_w_in, 1536 + j * 128, kn[:, j, :], ("kn", j), ppv("evkg"), h, (sqb, sdb, rsb))
                    wvs = []
                    for q4 in range(4):
                        ws_, wk_ = wslot()
                        wload(ws_[:], wcols(ev_w_in, 2048 + q4 * 128, 128), wk_)
                        wvs.append((ws_, wk_))
                    for blk in range(16):
                        bank = 6 + blk % 2

                        def vproj(e, blk=blk, bank=bank):
                            ins = None
                            for q4 in range(4):
                                for k in range(8):
                                    ins = e.matmul(ps[bank][:, q4 * 128:(q4 + 1) * 128], lhsT=h[:, k, blk * 128:(blk + 1) * 128],
                                                   rhs=wvs[q4][0][:, k, :], start=(k == 0), stop=(k == 7))
                            return ins
                        P.op("pe", vproj, rd=[w_[1] for w_ in wvs] + hk_all(blk // 4), wr=[psk(bank)])
                        if blk % 2 == 0:
                            P.op("act", lambda e, blk=blk, bank=bank: e.activation(out=vt[:, blk, :], in_=ps[bank][:],
                                                                                   func=AF.Identity),
                                 rd=[psk(bank)], wr=[("vt", blk)])
                        else:
                            P.op("dve", lambda e, blk=blk, bank=bank: e.tensor_copy(out=vt[:, blk, :], in_=ps[bank][:]),
                                 rd=[psk(bank)], wr=[("vt", blk)])
                    if s == 0:
                        dump("kn", kn[:].rearrange("p c t -> p (c t)"), [128, 4 * T],
                             rd=[(("kn", j), t_) for j in range(4) for t_ in range(NT)])
                        dump("vt", vt[:].rearrange("p c t -> p (c t)"), [128, 16 * 512], rd=[("vt", k) for k in range(16)])
                    P.barrier()
                    P.flush()
            if stop in ("norm0", "lru", "sbproj"):
                break
            yb = sb(st0, "yb", [128, 4, T], BF16)
            with ExitStack() as st2:
                eb = [sb(st2, f"eb{i}", [128, TS], F32) for i in range(3)]
                LOOK = 2
                NSP, NRB = LOOK + 3, LOOK + 2
                spb = [sb(st2, f"spb{i}", [128, TS], BF16) for i in range(NSP)]
                Rb = [sb(st2, f"Rb{i}", [128, TS], BF16) for i in range(NRB)]
                wb_ = [sb(st2, f"wb{i}", [128, TS], BF16) for i in range(4)]
                mo, _ = CO["md"]
                for i in range(NSP):
                    P.op("dve", lambda e, i=i: e.memset(spb[i][:], 0.0), wr=[("spb", i)])
                tiles = []
                for j in range(4):
                    for tt in range(NT):
                        for hh in range(2):
                            ob = 4 + ((j * NT + tt) * 2 + hh) % 4
                            for idx, kb in enumerate(range(4 * tt + 3, -1, -1)):
                                tiles.append(dict(j=j, tt=tt, hh=hh, kb=kb, idx=idx, ob=ob, R=None))
                cnt_ = dict(z=0, e=0, sp=0, r=0, rb=0, w=0)

                def stage_a(ti):
                    t = tiles[ti]
                    j, tt, hh, kb, idx = t["j"], t["tt"], t["hh"], t["kb"], t["idx"]
                    p0 = hh * 64
                    dz = kb - 4 * tt
                    zb = cnt_["z"] % 2
                    cnt_["z"] += 1
                    c0 = max(dz, 0) * 128
                    t["c0"] = c0
                    mm_group(ps[zb][:, c0:TS], [(kn[:, j, kb * 128:(kb + 1) * 128], qz[hh][:, j, tt * TS + c0:(tt + 1) * TS])],
                             rd=[(("kn", j), kb // 4), (("qz", j), hh, tt), ("qzpad", hh)], wr=[psk(zb)])
                    ei = cnt_["e"] % 3
                    cnt_["e"] += 1
                    P.op("act", lambda e, ei=ei, zb=zb, c0=c0: e.activation(out=eb[ei][:, c0:TS], in_=ps[zb][:, c0:TS], func=AF.Exp),
                         rd=[psk(zb)], wr=[("eb", ei)])
                    si = cnt_["sp"] % NSP
                    cnt_["sp"] += 1
                    t["si"] = si
                    P.op("act", lambda e, ei=ei, si=si, c0=c0: e.activation(out=spb[si][:, c0:TS], in_=eb[ei][:, c0:TS], func=AF.Ln, bias=1.0),
                         rd=[("eb", ei)], wr=[("spb", si)])
                    if dz >= 0:
                        P.op("dve", lambda e, si=si, dz=dz: e.tensor_tensor(
                            out=spb[si][:], in0=spb[si][:], in1=con[:, mo + dz * 512:mo + (dz + 1) * 512], op=ALU.mult),
                            rd=[("spb", si), "con"], wr=[("spb", si)])
                    if kb > 0:
                        nt_ = tiles[ti + 1]
                        if idx == 0:
                            nt_["R"] = (spb[si], ("spb", si))
                        else:
                            rn = cnt_["rb"] % NRB
                            cnt_["rb"] += 1
                            rsrc, rkey = t["R"]
                            P.op("dve", lambda e, si=si, rsrc=rsrc, rn=rn: e.tensor_tensor(
                                out=Rb[rn][:], in0=rsrc[:], in1=spb[si][:], op=ALU.add),
                                rd=[("spb", si), rkey], wr=[("Rb", rn)])
                            nt_["R"] = (Rb[rn], ("Rb", rn))

                def stage_b(ti):
                    t = tiles[ti]
                    j, tt, hh, kb, idx, ob, si = t["j"], t["tt"], t["hh"], t["kb"], t["idx"], t["ob"], t["si"]
                    p0 = hh * 64
                    dz = kb - 4 * tt
                    rb = 2 + cnt_["r"] % 2
                    cnt_["r"] += 1
                    c0 = t["c0"]
                    pairs = [(ntri, spb[si][:, c0:TS])]
                    rdk = [("spb", si), "con", (("kn", j), kb // 4), (("qz", j), hh, tt), ("qzpad", hh)]
                    if t["R"] is not None:
                        pairs.append((nones, t["R"][0][:, c0:TS]))
                        rdk.append(t["R"][1])
                    pairs.append((kn[:, j, kb * 128:(kb + 1) * 128], qz[hh][:, j, tt * TS + c0:(tt + 1) * TS]))
                    mm_group(ps[rb][:, c0:TS], pairs, rd=rdk, wr=[psk(rb)])
                    wi = cnt_["w"] % 4
                    cnt_["w"] += 1
                    t["wi"] = wi
                    P.op("act", lambda e, wi=wi, rb=rb, c0=c0: e.activation(out=wb_[wi][:, c0:TS], in_=ps[rb][:, c0:TS], func=AF.Exp),
                         rd=[psk(rb)], wr=[("wb", wi)])
                    if dz >= 0:
                        P.op("dve", lambda e, wi=wi, dz=dz, c0=c0: e.tensor_tensor(
                            out=wb_[wi][:, c0:TS], in0=wb_[wi][:, c0:TS], in1=con[:, mo + dz * 512 + c0:mo + (dz + 1) * 512], op=ALU.mult),
                            rd=[("wb", wi), "con"], wr=[("wb", wi)])

                def stage_c(ti):
                    t = tiles[ti]
                    j, tt, hh, kb, idx, ob, wi = t["j"], t["tt"], t["hh"], t["kb"], t["idx"], t["ob"], t["wi"]
                    p0 = hh * 64
                    c0 = t["c0"]

                    def pv(e, ob=ob, kb=kb, j=j, wi=wi, c0=c0, first=(idx == 0), last=(kb == 0)):
                        return e.matmul(ps[ob][:, c0:TS], lhsT=vt[:, kb, j * 128:(j + 1) * 128],
                                        rhs=wb_[wi][:, c0:TS], start=first, stop=last, skip_group_check=True)
                    P.op("pe", pv, rd=[("wb", wi), ("vt", kb)], wr=[("pso", ob)])
                    if kb == 0:
                        P.op("act", lambda e, j=j, tt=tt, ob=ob, p0=p0: e.activation(
                            out=yb[p0:p0 + 64, j, tsl(tt)], in_=ps[ob][p0:p0 + 64, :], func=AF.Identity),
                            rd=[("pso", ob)], wr=[("yb", j, tt, hh)])

                ntl = len(tiles)
                for ti in range(ntl + LOOK + 1):
                    if ti < ntl:
                        stage_a(ti)
                    if 0 <= ti - LOOK < ntl:
                        stage_b(ti - LOOK)
                    if ti - LOOK - 1 >= 0:
                        stage_c(ti - LOOK - 1)
                if s == 0:
                    dump("yb", yb[:].rearrange("p c t -> p (c t)"), [128, 4 * T],
                         rd=[("yb", j, t_, hh) for j in range(4) for t_ in range(NT) for hh in range(2)])
                P.barrier()
                P.flush()
            if stop == "sb":
                break
            out_proj_residual(ev_w_out, lambda k: yb[:, k, :],
                              lambda tt: [("yb", j, tt, hh) for j in range(4) for hh in range(2)], l, b, 2, r0=4, nk=4)
            P.barrier()
            P.flush()
        if s == 0:
            dump("x0mid", X[:].rearrange("p c t -> p (c t)"), [128, 8 * T], rd=[Xk(c, tt) for c in range(8) for tt in range(NT)])
        if stop == "mix0":
            break
        with ExitStack() as st1:
            h = sb(st1, "h", [128, 8, T], BF16)
            with ExitStack() as st2:
                do_norm(st2, h, l, b, 1)
                P.barrier()
                P.flush()
            with ExitStack() as st2:
                do_ffn(st2, h, l, b)
                P.barrier()
                P.flush()
        if s == 0:
            dump("x1", X[:].rearrange("p c t -> p (c t)"), [128, 8 * T], rd=[Xk(c, tt) for c in range(8) for tt in range(NT)])
        if stop == "l0":
            break
        l = 1
        with ExitStack() as st0:
            with ExitStack() as st1:
                h = sb(st1, "h", [128, 8, T], BF16)
                with ExitStack() as st2:
                    do_norm(st2, h, l, b, 0)
                    P.barrier()
                    P.flush()
                qn = sb(st0, "qn1", [128, 8, T], BF16)
                kd = sb(st0, "kd", [128, 4, T], BF16)
                va = sb(st0, "va", [128, 16, 4, 65], BF16)
                with ExitStack() as st2:
                    sqb = [sb(st2, f"sqb{i}", [128, TS], BF16) for i in range(2)]
                    sdb = [sb(st2, f"sdb{i}", [128, TS], F32) for i in range(2)]
                    rsb = [sb(st2, f"rsb{i}", [128, TS], F32) for i in range(2)]
                    wv = sb(st2, "wv", [128, 8, 512], BF16)
                    for c in range(8):
                        qk_norm_chunk(od_w_in, c * 128, qn[:, c, :], ("qn", c), odq8, h, (sqb, sdb, rsb))
                    for g in range(4):
                        qk_norm_chunk(od_w_in, 1024 + g * 64, kd[:, g, :], ("kd", g), ppv("odkg"), h, (sqb, sdb, rsb), dup64=True)
                    P.op("dve", lambda e: e.memset(va[:, :, :, 64:65], 1.0), wr=["vaones"])
                    wload(wv[:, :, 0:256], wcols(od_w_in, 1280, 256), "wv")
                    for blk in range(16):
                        bank = 4 + blk % 4
                        mm_group(ps[bank][:, 0:256], [(h[:, k, blk * 128:(blk + 1) * 128], wv[:, k, 0:256]) for k in range(8)],
                                 rd=["wv"] + hk_all(blk // 4), wr=[psk(bank)])
                        P.op("act" if blk % 2 == 0 else "dve",
                             (lambda e, blk=blk, bank=bank: e.activation(
                                 out=va[:, blk, :, 0:64], in_=ps[bank][:, 0:256].rearrange("p (g d) -> p g d", g=4), func=AF.Identity))
                             if blk % 2 == 0 else
                             (lambda e, blk=blk, bank=bank: e.tensor_copy(
                                 out=va[:, blk, :, 0:64], in_=ps[bank][:, 0:256].rearrange("p (g d) -> p g d", g=4))),
                             rd=[psk(bank)], wr=[("va", blk)])
                    P.barrier()
                    P.flush()
            if stop == "l1proj":
                break
            yT = sb(st0, "yT", [128, 8, T], BF16)
            with ExitStack() as st2:
                pb = [sb(st2, f"pb{i}", [128, 2, TS], BF16) for i in range(2)]
                den = [sb(st2, f"den{i}", [128, 4], F32) for i in range(2)]
                ytok = [sb(st2, f"ytok{i}", [128, D], BF16) for i in range(2)]
                units = [(qb, g) for qb in range(16) for g in range(4)]

                def swa_a(u):
                    qb, g = units[u]
                    pi = u % 2
                    kbs = [qb - 1, qb] if qb > 0 else [qb]
                    c0 = 0 if qb > 0 else 256
                    for hb in range(2):
                        sbank = hb + 2 * pi

                        def sc(e, sbank=sbank, g=g, kbs=kbs, qb=qb, hb=hb):
                            ins = None
                            for kb in kbs:
                                which = 0 if kb == qb - 1 else 1
                                ins = e.matmul(ps[sbank][:, which * 256:(which + 1) * 256],
                                               lhsT=kd[hb * 64:(hb + 1) * 64, g, kb * 128:(kb + 1) * 128],
                                               rhs=qn[hb * 64:(hb + 1) * 64, 2 * g:2 * g + 2, qb * 128:(qb + 1) * 128],
                                               start=True, stop=True)
                            return ins
                        P.op("pe", sc, rd=[(("kd", g), kb // 4) for kb in kbs] + [(("qn", 2 * g), qb // 4), (("qn", 2 * g + 1), qb // 4)],
                             wr=[psk(sbank)])
                        P.op("act", lambda e, pi=pi, hb=hb, sbank=sbank, c0=c0: e.activation(
                            out=pb[pi][:, hb, c0:512], in_=ps[sbank][:, c0:512], func=AF.Exp),
                            rd=[psk(sbank)], wr=[("pb", pi, hb)])
                        P.op("dve", lambda e, pi=pi, hb=hb, c0=c0: e.tensor_tensor(
                            out=pb[pi][:, hb, c0:512], in0=pb[pi][:, hb, c0:512], in1=maskp[:, c0:512], op=ALU.mult),
                            rd=[("pb", pi, hb), "con"], wr=[("pb", pi, hb)])

                def swa_b(u):
                    qb, g = units[u]
                    pi = u % 2
                    yi = qb % 2
                    kbs = [qb - 1, qb] if qb > 0 else [qb]
                    ybank = 4 + pi

                    def pvm(e, ybank=ybank, pi=pi, kbs=kbs, qb=qb, g=g):
                        ins = None
                        for hc in range(4):
                            hb, e_ = hc // 2, hc % 2
                            for i_, kb in enumerate(kbs):
                                which = 0 if kb == qb - 1 else 1
                                ins = e.matmul(ps[ybank][:, hc * 65:(hc + 1) * 65],
                                               lhsT=pb[pi][:, hb, which * 256 + e_ * 128:which * 256 + (e_ + 1) * 128],
                                               rhs=va[:, kb, g, :], start=(i_ == 0), stop=(i_ == len(kbs) - 1))
                        return ins
                    P.op("pe", pvm, rd=[("pb", pi, 0), ("pb", pi, 1)] + [("va", kb) for kb in kbs] + ["vaones"], wr=[psk(ybank)])
                    yv = ps[ybank][:, 0:260].rearrange("p (hb e d) -> p hb e d", hb=2, e=2)
                    P.op("dve", lambda e, pi=pi, yv=yv, g=g: e.tensor_tensor(
                        out=den[pi][:].rearrange("p (hb e) -> p hb e", hb=2),
                        in0=yv[:, :, :, 64],
                        in1=esink[:, 4 * g:4 * g + 4].rearrange("p (e hb) -> p hb e", hb=2), op=ALU.add),
                        rd=[psk(ybank), "misc"], wr=[("den", pi)])
                    P.op("dve", lambda e, pi=pi: e.reciprocal(out=den[pi][:], in_=den[pi][:]), rd=[("den", pi)], wr=[("den", pi)])
                    P.op("dve", lambda e, pi=pi, yv=yv, g=g, yi=yi: e.tensor_tensor(
                        out=ytok[yi][:, g * 256:(g + 1) * 256].rearrange("p (e hb d) -> p hb e d", e=2, hb=2),
                        in0=yv[:, :, :, 0:64],
                        in1=den[pi][:].rearrange("p (hb e) -> p hb e", hb=2).unsqueeze(3).broadcast_to([128, 2, 2, 64]),
                        op=ALU.mult),
                        rd=[psk(ybank), ("den", pi)], wr=[("ytok", yi, g)])
                    if g == 3:
                        tbank = 6 + yi
                        tp = ps[tbank][:].bitcast(BF16)

                        def trn(e, tp=tp, yi=yi):
                            ins = None
                            for c in range(8):
                                ins = e.transpose(out=tp[:, c * 128:(c + 1) * 128], in_=ytok[yi][:, c * 128:(c + 1) * 128], identity=ident)
                            return ins
                        P.op("pe", trn, rd=[("ytok", yi, g_) for g_ in range(4)] + ["con"], wr=[psk(tbank)])
                        P.op("act", lambda e, tp=tp, qb=qb: e.activation(
                            out=yT[:, :, qb * 128:(qb + 1) * 128], in_=tp.rearrange("p (c t) -> p c t", c=8), func=AF.Identity),
                            rd=[psk(tbank)], wr=[("yT", qb // 4)])

                nun = len(units)
                for u in range(nun + 1):
                    if u < nun:
                        swa_a(u)
                    if u >= 1:
                        swa_b(u - 1)
                if s == 0:
                    dump("yT", yT[:].rearrange("p c t -> p (c t)"), [128, 8 * T], rd=[("yT", t_) for t_ in range(NT)])
                P.barrier()
                P.flush()
            if stop == "swa":
                break
            out_proj_residual(od_w_out, lambda k: yT[:, k, :], lambda tt: [("yT", tt)], l, b, 2)
            P.barrier()
            P.flush()
        if s == 0:
            dump("x1mid", X[:].rearrange("p c t -> p (c t)"), [128, 8 * T], rd=[Xk(c, tt) for c in range(8) for tt in range(NT)])
        with ExitStack() as st1:
            h = sb(st1, "h", [128, 8, T], BF16)
            with ExitStack() as st2:
                do_norm(st2, h, l, b, 1)
                P.barrier()
                P.flush()
            with ExitStack() as st2:
                do_ffn(st2, h, l, b)
                P.barrier()
                P.flush()
        for c in range(8):
            P.dma("sp", out_d[s, c], X[:, c, :], rd=[Xk(c, tt) for tt in range(NT)], wr=[("out", s, c)])
        P.flush()

    P.barrier()
    P.op("sp", None)
    P.flush()
    top.close()
    return nc, P, dbg_out


_CACHE = {}


def kernel(**inputs):
    if "nc" not in _CACHE:
        _CACHE["nc"] = build()[0]
    nc = _CACHE["nc"]
    in_maps = [_host_inputs(inputs, core) for core in range(NCORES)]
    res = run_bass_kernel_spmd(nc, in_maps, core_ids=list(range(NCORES)))
    outs = []
    for core in range(NCORES):
        o = np.asarray(res.results[core]["out"], np.float32).reshape(2, D, T)
        outs.append(o.transpose(0, 2, 1))
    return np.ascontiguousarray(np.concatenate(outs, axis=0)).astype(np.float32)
```

```python
from contextlib import ExitStack

import numpy as np
import concourse.bass as bass
import concourse.mybir as mybir
from concourse.bass_utils import run_bass_kernel_spmd

F32 = mybir.dt.float32
BF16 = mybir.dt.bfloat16
AF = mybir.ActivationFunctionType
ALU = mybir.AluOpType

NCORES = 8
T = 2048
NT = 4
TS = 512
D = 1024
DFF = 2816
NFF = 22
EPS = 1e-6
FQ = [(0, 6), (6, 6), (12, 5), (17, 5)]


class Prog:
    NRING = 8

    def __init__(self, nc):
        self.nc = nc
        self.engs = {"pe": nc.tensor, "act": nc.scalar, "dve": nc.vector,
                     "pool": nc.gpsimd, "sp": nc.sync}
        self.ops = []
        self.nflushed = 0
        self.last_w = {}
        self.readers = {}
        self.last_on_eng = {}
        self.dmas_since_barrier = []
        self.barrier_deps = set()
        self.dma_hist = {}
        self.sems = {e: nc.alloc_semaphore("c_" + e) for e in self.engs}
        self.rings = {}
        self.cnt = {e: 0 for e in self.engs}
        self.done = []
        self.waited = {e: {} for e in self.engs}
        self.nwaits = 0

    limit = None

    def op(self, eng, fn, rd=(), wr=(), dma=False, nobarrier=False):
        i = len(self.ops)
        if self.limit is not None and i >= self.limit and not dma and fn is not None:
            return None
        o = dict(eng=eng, fn=fn, dma=dma)
        ops = self.ops
        d = set()
        raw = set()
        for k in rd:
            j = self.last_w.get(k)
            if j is not None:
                d.add(j)
                raw.add(j)
        for k in wr:
            j = self.last_w.get(k)
            if j is not None:
                d.add(j)
            d.update(self.readers.get(k, ()))
        keep = set()
        for j in d:
            oj = ops[j]
            if (not oj["dma"]) and (not dma) and oj["eng"] == eng and eng == "pe":
                continue
            keep.add(j)
        for j in (() if nobarrier else self.barrier_deps):
            oj = ops[j]
            if (not oj["dma"]) and (not dma) and oj["eng"] == eng:
                continue
            keep.add(j)
        for k in rd:
            self.readers.setdefault(k, []).append(i)
        for k in wr:
            self.last_w[k] = i
            self.readers[k] = []
        if dma:
            hist = self.dma_hist.setdefault(eng, [])
            c = len(hist)
            if eng not in self.rings:
                self.rings[eng] = [self.nc.alloc_semaphore(f"d_{eng}{r}") for r in range(self.NRING)]
            o["ring"] = c % self.NRING
            o["rval"] = 16 * (c // self.NRING + 1)
            if c >= self.NRING:
                keep.add(hist[c - self.NRING])
            hist.append(i)
            self.dmas_since_barrier.append(i)
        else:
            self.last_on_eng[eng] = i
        o["deps"] = keep
        ops.append(o)
        self.done.append(None)
        return i

    def dma(self, eng, out, in_, rd=(), wr=(), nobarrier=False):
        return self.op(eng, lambda e: e.dma_start(out=out, in_=in_), rd, wr, dma=True, nobarrier=nobarrier)

    def barrier(self):
        self.barrier_deps = set(self.last_on_eng.values()) | set(self.dmas_since_barrier)
        self.dmas_since_barrier = []

    def flush(self):
        ops = self.ops
        n = len(ops)
        start = self.nflushed
        needs = set()
        for i in range(start, n):
            needs.update(ops[i]["deps"])
        needs.update(self.last_on_eng.values())
        needs.update(self.last_w.values())
        for r in self.readers.values():
            needs.update(r)
        needs.update(self.barrier_deps)
        for i in range(start, n):
            o = ops[i]
            e = o["eng"]
            eng = self.engs[e]
            need = {}
            for j in o["deps"]:
                s, v = self.done[j]
                k = id(s)
                if k not in need or need[k][1] < v:
                    need[k] = (s, v)
            for k, (s, v) in need.items():
                if self.waited[e].get(k, 0) >= v:
                    continue
                eng.wait_ge(s, v)
                self.nwaits += 1
                self.waited[e][k] = v
            if o["fn"] is None:
                self.done[i] = (self.sems[e], self.cnt[e])
                continue
            ins = o["fn"](eng)
            if o["dma"]:
                s = self.rings[e][o["ring"]]
                ins.then_inc(s, 16)
                self.done[i] = (s, o["rval"])
            elif i in needs:
                self.cnt[e] += 1
                ins.then_inc(self.sems[e], 1)
                self.done[i] = (self.sems[e], self.cnt[e])
            else:
                self.done[i] = (self.sems[e], self.cnt[e] + 1)
            o["fn"] = None
        self.nflushed = n


PP = {}
_off = 0
for _name, _n in [("cT", 16), ("adab", 96), ("gmix", 16), ("gffn", 16), ("lcw", 16), ("lcb", 4),
                  ("lba", 4), ("lbx", 4), ("llam", 4), ("evqg", 1), ("evkg", 1), ("odqg", 1),
                  ("odkg", 1), ("sinks", 16), ("fcw", 132), ("fcb", 44)]:
    PP[_name] = (_off, _n)
    _off += _n
NPP = _off

CO = {}
_off = 0
for _name, _n in [("ident", 128), ("ones", 128), ("bones", 128), ("tri", 128), ("ntri", 128), ("nones", 128), ("md", 2048),
                  ("maskp", 512), ("maskc", 512)]:
    CO[_name] = (_off, _n)
    _off += _n
NCON = _off


def _consts():
    c = np.zeros((128, NCON), np.float32)
    p = np.arange(128)[:, None]
    m = np.arange(128)[None, :]
    c[:, CO["ident"][0]:CO["ident"][0] + 128] = (p == m)
    c[:, CO["ones"][0]:CO["ones"][0] + 128] = 1.0
    c[:, CO["bones"][0]:CO["bones"][0] + 128] = ((p // 64) == (m // 64))
    c[:, CO["tri"][0]:CO["tri"][0] + 128] = (p >= m)
    c[:, CO["ntri"][0]:CO["ntri"][0] + 128] = -1.0 * (p >= m)
    c[:, CO["nones"][0]:CO["nones"][0] + 128] = -1.0
    t = np.arange(512)[None, :]
    for d in range(4):
        c[:, CO["md"][0] + d * 512:CO["md"][0] + (d + 1) * 512] = ((d * 128 + p) < t)
    c[:, CO["maskp"][0]:CO["maskp"][0] + 512] = np.concatenate([np.tile((p > m), (1, 2)), np.tile((p <= m), (1, 2))], 1)
    c[:, CO["maskc"][0]:CO["maskc"][0] + 512] = np.tile((p <= m), (1, 4))
    return c


def _pcol(v):
    v = np.asarray(v, np.float32)
    return np.ascontiguousarray(v.reshape(-1, 128).T)


def _host_inputs(inp, core):
    f = lambda a: np.ascontiguousarray(np.asarray(a, np.float32))
    b0 = 2 * core
    x = f(inp["x"][b0:b0 + 2])
    xT = np.ascontiguousarray(x.transpose(0, 2, 1)).reshape(2, 8, 128, T)
    pp = np.zeros((128, NPP), np.float32)

    def put(name, arr):
        o, n = PP[name]
        pp[:, o:o + n] = np.asarray(arr, np.float32).reshape(128, n)

    c = f(inp["c"][b0:b0 + 2])
    put("cT", c.reshape(2, 8, 128).transpose(2, 1, 0))
    put("adab", np.stack([_pcol(inp["ada_b"][l]) for l in range(2)], 1))
    put("gmix", np.stack([_pcol(inp["norm_mix_g"][l]) for l in range(2)], 1))
    put("gffn", np.stack([_pcol(inp["norm_ffn_g"][l]) for l in range(2)], 1))
    cw = f(inp["ev_conv_w"][0])
    put("lcw", np.stack([_pcol(cw[k]) for k in range(4)], 2))
    put("lcb", _pcol(inp["ev_conv_b"][0]))
    put("lba", _pcol(inp["ev_ba"][0]))
    put("lbx", _pcol(inp["ev_bx"][0]))
    put("llam", _pcol(inp["ev_lam"][0]))
    put("evqg", np.tile(f(inp["ev_qn_g"][0]), 2)[:, None])
    put("evkg", np.tile(f(inp["ev_kn_g"][0]), 2)[:, None])
    put("odqg", np.tile(f(inp["od_qn_g"][0]), 2)[:, None])
    put("odkg", np.tile(f(inp["od_kn_g"][0]), 2)[:, None])
    put("sinks", np.tile(f(inp["od_sinks"][0])[None, :], (128, 1)))
    fcw = f(inp["ffn_conv_w"])
    put("fcw", np.stack([np.stack([_pcol(fcw[l, k]) for k in range(3)], 2) for l in range(2)], 1))
    put("fcb", np.stack([_pcol(inp["ffn_conv_b"][l]) for l in range(2)], 1))

    def bd(w):
        w = f(w)
        o = np.zeros((128, 4, 128), np.float32)
        for j in range(4):
            o[0:64, j, 0:64] = w[2 * j]
            o[64:128, j, 64:128] = w[2 * j + 1]
        return o

    return {
        "xT": xT, "pp": pp, "consts": _consts(),
        "ada_w": f(inp["ada_w"]),
        "ev_w_in": f(inp["ev_w_in"][0]), "ev_w_out": f(inp["ev_w_out"][0]),
        "od_w_in": f(inp["od_w_in"][0]), "od_w_out": f(inp["od_w_out"][0]),
        "w_gate": f(inp["ffn_w_gate"]), "w_up": f(inp["ffn_w_up"]), "w_down": f(inp["ffn_w_down"]),
        "wabd": bd(inp["ev_wa"][0]), "wxbd": bd(inp["ev_wx"][0]),
    }


def build(nseq=2, dbg=None, stop=None):
    dbg = dbg or set()
    nc = bass.Bass("TRN2", target_bir_lowering=False)
    P = Prog(nc)

    def din(name, shape):
        return nc.dram_tensor(name, list(shape), F32, kind="ExternalInput").ap()

    xT = din("xT", [2, 8, 128, T])
    pp_d = din("pp", [128, NPP])
    con_d = din("consts", [128, NCON])
    ada_w = din("ada_w", [2, D, 6 * D])
    ev_w_in = din("ev_w_in", [D, 2560])
    ev_w_out = din("ev_w_out", [D, D])
    od_w_in = din("od_w_in", [D, 1536])
    od_w_out = din("od_w_out", [D, D])
    w_gate = din("w_gate", [2, D, DFF])
    w_up = din("w_up", [2, D, DFF])
    w_down = din("w_down", [2, DFF, D])
    wabd_d = din("wabd", [128, 4, 128])
    wxbd_d = din("wxbd", [128, 4, 128])
    out_d = nc.dram_tensor("out", [2, 8, 128, T], F32, kind="ExternalOutput").ap()
    dbg_out = {}

    def dump(name, ap_sb, shape, rd):
        if name not in dbg:
            return
        d = nc.dram_tensor("dbg_" + name, list(shape), F32, kind="ExternalOutput").ap()
        dbg_out[name] = d
        P.dma("pool", d, ap_sb, rd=rd, wr=[("dbgout", name)])

    top = ExitStack()

    uid = [0]
    sb_lo = (nc.sbuf_base + 63) // 64 * 64
    free_list = [[sb_lo, nc.sbuf_top]]
    peak = [0]

    def sb(st, name, shape, dt):
        uid[0] += 1
        nbytes = int(np.prod(shape[1:])) * (2 if dt == BF16 else 4)
        nbytes = (nbytes + 63) // 64 * 64
        for seg in free_list:
            if seg[1] - seg[0] >= nbytes:
                off = seg[0]
                seg[0] += nbytes
                break
        else:
            raise RuntimeError(f"SBUF full allocating {name} ({nbytes} B); free={free_list}")
        peak[0] = max(peak[0], off + nbytes)

        def release():
            free_list.append([off, off + nbytes])
            free_list.sort()
            merged = []
            for sg in free_list:
                if sg[0] >= sg[1]:
                    continue
                if merged and merged[-1][1] == sg[0]:
                    merged[-1][1] = sg[1]
                else:
                    merged.append(sg)
            free_list[:] = merged
        st.callback(release)
        return nc.alloc_sbuf_tensor_at(f"{name}_u{uid[0]}", list(shape), dt, offset=off)

    ps = [top.enter_context(nc.psum_tensor(f"ps{i}", [128, 512], F32)) for i in range(8)]
    X = sb(top, "X", [128, 8, T], F32)
    pp = sb(top, "pp", [128, NPP], F32)
    con = sb(top, "con", [128, NCON], BF16)
    modp = sb(top, "modp", [128, 2, 2, 6, 8], F32)
    misc = sb(top, "misc", [128, 64], F32)
    wring = [sb(top, f"wring{i}", [128, 8, 128], BF16) for i in range(8)]
    ring_n = [0]

    def cview(name):
        o, n = CO[name]
        return con[:, o:o + n]

    ident, ones_c, bones, tri = cview("ident"), cview("ones"), cview("bones"), cview("tri")
    ntri, nones = cview("ntri"), cview("nones")
    md_all = cview("md")
    maskp, maskc = cview("maskp"), cview("maskc")

    def ppv(name):
        o, n = PP[name]
        return pp[:, o:o + n]

    def wslot():
        i = ring_n[0] % len(wring)
        ring_n[0] += 1
        return wring[i], ("wring", i)

    def wload(dst, src, key):
        ring = isinstance(key, tuple) and key[0] == "wring"
        P.dma("pool", dst, src, wr=[key], nobarrier=ring)

    def wcols(w2d, c0, n):
        return w2d[:, c0:c0 + n].rearrange("(kc p) n -> p kc n", p=128)

    def mm_group(out, pairs, rd, wr):
        def fn(e):
            ins = None
            n = len(pairs)
            for i, (l, r) in enumerate(pairs):
                ins = e.matmul(out, lhsT=l, rhs=r, start=(i == 0), stop=(i == n - 1))
            return ins
        P.op("pe", fn, rd=rd, wr=wr)

    def tsl(tt):
        return slice(tt * TS, (tt + 1) * TS)

    Xk = lambda c, tt: ("X", c, tt)
    psk = lambda b: ("ps", b)

    P.dma("sp", pp[:], pp_d, wr=["pp"])
    P.dma("pool", con[:], con_d, wr=["con"])
    with ExitStack() as st:
        cs = sb(st, "cs", [128, 8, 2], BF16)
        wbig = [sb(st, f"wbig{i}", [128, 8, 1024], BF16) for i in range(2)]
        wstg = sb(st, "wstg", [128, 8, 1024], F32)
        mod = sb(st, "mod", [128, 96, 2], F32)
        tmpa = sb(st, "tmpa", [128, 16], F32)
        o, n = PP["cT"]
        P.op("act", lambda e: e.activation(out=cs[:].rearrange("p k b -> p (k b)"), in_=pp[:, o:o + n], func=AF.Silu),
             rd=["pp"], wr=["cs"])
        for l in range(2):
            for pc in range(6):
                wb = wbig[(l * 6 + pc) % 2]
                wk = ("wbig", (l * 6 + pc) % 2)
                if (l * 6 + pc) % 2 == 0:
                    wload(wb[:], wcols(ada_w[l], pc * 1024, 1024), wk)
                else:
                    P.dma("sp", wstg[:], wcols(ada_w[l], pc * 1024, 1024), wr=["wstg"])
                    P.op("dve", lambda e, wb=wb: e.tensor_copy(out=wb[:, 0:4, :], in_=wstg[:, 0:4, :]), rd=["wstg"], wr=[wk])
                    P.op("act", lambda e, wb=wb: e.activation(out=wb[:, 4:8, :], in_=wstg[:, 4:8, :], func=AF.Identity),
                         rd=["wstg"], wr=[wk])
                for nn in range(8):
                    g = pc * 8 + nn
                    col = (l * 48 + g) * 2
                    mm_group(ps[7][:, col:col + 2],
                             [(wb[:, k, nn * 128:(nn + 1) * 128], cs[:, k, :]) for k in range(8)],
                             rd=[wk, "cs"], wr=[psk(7)])
        P.op("dve", lambda e: e.tensor_tensor(
            out=mod[:], in0=ps[7][:, 0:192].rearrange("p (g b) -> p g b", b=2),
            in1=ppv("adab").unsqueeze(2).broadcast_to([128, 96, 2]), op=ALU.add),
            rd=[psk(7), "pp"], wr=["mod"])
        modv = mod[:].rearrange("p (l j c) b -> p l j c b", l=2, j=6)
        for l in range(2):
            for b in range(2):
                for (dst, jsc, gname) in ((0, 1, "gmix"), (3, 4, "gffn")):
                    go, _ = PP[gname]
                    P.op("dve", lambda e, l=l, b=b, dst=dst, jsc=jsc, go=go: e.scalar_tensor_tensor(
                        out=modp[:, l, b, dst, :], in0=modv[:, l, jsc, :, b], scalar=1.0,
                        in1=pp[:, go + l * 8:go + l * 8 + 8], op0=ALU.add, op1=ALU.mult),
                        rd=["mod", "pp"], wr=["modp"])
                for (dst, j) in ((1, 0), (2, 2), (4, 3), (5, 5)):
                    P.op("dve", lambda e, l=l, b=b, dst=dst, j=j: e.tensor_copy(
                        out=modp[:, l, b, dst, :], in_=modv[:, l, j, :, b]), rd=["mod"], wr=["modp"])
        P.op("act", lambda e: e.activation(out=tmpa[:, 0:4], in_=ppv("llam"), func=AF.Exp, scale=-1.0),
             rd=["pp"], wr=["tmpa"])
        P.op("act", lambda e: e.activation(out=tmpa[:, 4:8], in_=tmpa[:, 0:4], func=AF.Ln, bias=1.0),
             rd=["tmpa"], wr=["tmpa2"])
        P.op("dve", lambda e: e.tensor_scalar(out=misc[:, 0:4], in0=tmpa[:, 4:8], scalar1=-8.0, scalar2=None,
                                              op0=ALU.mult), rd=["tmpa2"], wr=["misc"])
        P.op("dve", lambda e: e.tensor_scalar(out=misc[:, 4:5], in0=ppv("evqg"), scalar1=0.125, scalar2=None,
                                              op0=ALU.mult), rd=["pp"], wr=["misc"])
        P.op("dve", lambda e: e.tensor_scalar(out=misc[:, 5:6], in0=ppv("odqg"), scalar1=0.125, scalar2=None,
                                              op0=ALU.mult), rd=["pp"], wr=["misc"])
        P.op("act", lambda e: e.activation(out=misc[:, 8:24], in_=ppv("sinks"), func=AF.Exp),
             rd=["pp"], wr=["misc"])
        dump("modp", modp[:].rearrange("p l b j c -> p (l b j c)"), [128, 192], rd=["modp"])
        P.barrier()
        P.flush()
    cl = misc[:, 0:4]
    evq8 = misc[:, 4:5]
    odq8 = misc[:, 5:6]
    esink = misc[:, 8:24]

    def do_norm(st, h, l, b, which):
        ia, ish = (0, 1) if which == 0 else (3, 4)
        sq = [sb(st, f"sq{i}", [128, 8, TS], BF16) for i in range(2)]
        sd = [sb(st, f"sd{i}", [128, TS], F32) for i in range(2)]
        rs = [sb(st, f"rs{i}", [128, TS], F32) for i in range(2)]
        tm = [sb(st, f"tm{i}", [128, TS], F32) for i in range(3)]
        ti = 0
        for tt in range(NT):
            i2 = tt % 2
            P.op("act", lambda e, tt=tt, i2=i2: e.activation(out=sq[i2][:], in_=X[:, :, tsl(tt)], func=AF.Square),
                 rd=[Xk(c, tt) for c in range(8)], wr=[("sq", i2)])
            bank = 6 + i2
            mm_group(ps[bank][:], [(ones_c, sq[i2][:, c, :]) for c in range(8)], rd=[("sq", i2), "con"], wr=[psk(bank)])
            P.op("act", lambda e, i2=i2, bank=bank: e.activation(out=sd[i2][:], in_=ps[bank][:], func=AF.Ln,
                                                                  scale=1.0 / D, bias=EPS),
                 rd=[psk(bank)], wr=[("sd", i2)])
            P.op("act", lambda e, i2=i2: e.activation(out=rs[i2][:], in_=sd[i2][:], func=AF.Exp, scale=-0.5),
                 rd=[("sd", i2)], wr=[("rs", i2)])
            for c in range(8):
                t3 = ti % 3
                ti += 1
                P.op("dve", lambda e, c=c, tt=tt, i2=i2, t3=t3: e.tensor_tensor(
                    out=tm[t3][:], in0=X[:, c, tsl(tt)], in1=rs[i2][:], op=ALU.mult),
                    rd=[Xk(c, tt), ("rs", i2)], wr=[("tm", t3)])
                if c % 2 == 0:
                    P.op("act", lambda e, c=c, tt=tt, t3=t3: e.activation(
                        out=h[:, c, tsl(tt)], in_=tm[t3][:], func=AF.Identity,
                        scale=modp[:, l, b, ia, c:c + 1], bias=modp[:, l, b, ish, c:c + 1]),
                        rd=[("tm", t3), "modp"], wr=[("h", c, tt)])
                else:
                    P.op("dve", lambda e, c=c, tt=tt, t3=t3: e.tensor_scalar(
                        out=h[:, c, tsl(tt)], in0=tm[t3][:], scalar1=modp[:, l, b, ia, c:c + 1],
                        scalar2=modp[:, l, b, ish, c:c + 1], op0=ALU.mult, op1=ALU.add),
                        rd=[("tm", t3), "modp"], wr=[("h", c, tt)])

    def hk_all(tt):
        return [("h", c, tt) for c in range(8)]

    def out_proj_residual(w2d, ysrc, ykeys, l, b, gidx, r0=0, nk=8):
        slots = {}

        def ld(n):
            slots[n] = wslot()
            wload(slots[n][0][:, 0:nk, :],
                  w2d[r0 * 128:(r0 + nk) * 128, n * 128:(n + 1) * 128].rearrange("(kc p) n -> p kc n", p=128), slots[n][1])
        for n in range(3):
            ld(n)
        for n in range(8):
            if n + 3 < 8:
                ld(n + 3)
            ws, wk = slots[n]
            for tt in range(NT):
                bank = (n * NT + tt) % 6
                mm_group(ps[bank][:], [(ws[:, k, :], ysrc(k)[:, tsl(tt)]) for k in range(nk)],
                         rd=[wk] + ykeys(tt), wr=[psk(bank)])
                P.op("dve", lambda e, n=n, tt=tt, bank=bank: e.scalar_tensor_tensor(
                    out=X[:, n, tsl(tt)], in0=ps[bank][:], scalar=modp[:, l, b, gidx, n:n + 1],
                    in1=X[:, n, tsl(tt)], op0=ALU.mult, op1=ALU.add),
                    rd=[psk(bank), Xk(n, tt), "modp"], wr=[Xk(n, tt)])

    def do_ffn(st, h, l, b):
        GL = 2
        act = sb(st, "act", [128, 6, T], BF16)
        wd = [sb(st, f"wd{i}", [128, 6, D], BF16) for i in range(2)]
        gb = [sb(st, f"gb{i}", [128, 2 + T], F32) for i in range(GL + 1)]
        gc = sb(st, "gc", [128, T], F32)
        sg = sb(st, "sg", [128, T], BF16)
        fo, _ = PP["fcw"]
        bo, _ = PP["fcb"]
        for i in range(GL + 1):
            P.op("dve", lambda e, i=i: e.memset(gb[i][:, 0:2], 0.0), wr=[("gbpad", i)])
        wg2, wu2 = w_gate[l], w_up[l]
        slots = {}
        qof = {}
        for qi, (c0, ncq) in enumerate(FQ):
            for ci in range(ncq):
                qof[c0 + ci] = (qi, ci, c0, ncq)

        def load_c(c):
            sg_, sk = wslot()
            wload(sg_[:], wcols(wg2, c * 128, 128), sk)
            su_, uk = wslot()
            wload(su_[:], wcols(wu2, c * 128, 128), uk)
            slots[c] = (sg_, sk, su_, uk)

        def gate_stage(c):
            if c + 1 < NFF:
                load_c(c + 1)
            sg_, sk, su_, uk = slots[c]
            gi = c % (GL + 1)
            for tt in range(NT):
                bank = tt
                mm_group(ps[bank][:], [(sg_[:, k, :], h[:, k, tsl(tt)]) for k in range(8)],
                         rd=[sk] + hk_all(tt), wr=[psk(bank)])
                P.op("act", lambda e, gi=gi, tt=tt, bank=bank: e.activation(
                    out=gb[gi][:, 2 + tt * TS:2 + (tt + 1) * TS], in_=ps[bank][:], func=AF.Identity),
                    rd=[psk(bank)], wr=[("gb", gi, tt)])

        def rest_stage(c):
            qi, ci, c0, ncq = qof[c]
            wdb = wd[qi % 2]
            wdk = ("wd", qi % 2)
            if ci == 0:
                wload(wdb[:, 0:ncq, :],
                      w_down[l][c0 * 128:(c0 + ncq) * 128, :].rearrange("(kc p) n -> p kc n", p=128), wdk)
            sg_, sk, su_, uk = slots.pop(c)
            gi = c % (GL + 1)
            wo = fo + (l * NFF + c) * 3
            P.op("dve", lambda e, gi=gi, wo=wo, c=c: e.tensor_scalar(
                out=gc[:], in0=gb[gi][:, 0:T], scalar1=pp[:, wo:wo + 1],
                scalar2=pp[:, bo + l * NFF + c:bo + l * NFF + c + 1], op0=ALU.mult, op1=ALU.add),
                rd=[("gb", gi, t_) for t_ in range(NT)] + [("gbpad", gi), "pp"], wr=["gc"])
            for k in (1, 2):
                P.op("dve", lambda e, gi=gi, wo=wo, k=k: e.scalar_tensor_tensor(
                    out=gc[:], in0=gb[gi][:, k:k + T], scalar=pp[:, wo + k:wo + k + 1], in1=gc[:],
                    op0=ALU.mult, op1=ALU.add),
                    rd=[("gb", gi, t_) for t_ in range(NT)] + ["gc", "pp"], wr=["gc"])
            P.op("act", lambda e: e.activation(out=sg[:], in_=gc[:], func=AF.Silu), rd=["gc"], wr=["sg"])
            for tt in range(NT):
                bank = 4 + tt % 2
                mm_group(ps[bank][:], [(su_[:, k, :], h[:, k, tsl(tt)]) for k in range(8)],
                         rd=[uk] + hk_all(tt), wr=[psk(bank)])
                P.op("dve", lambda e, ci=ci, tt=tt, bank=bank: e.tensor_tensor(
                    out=act[:, ci, tsl(tt)], in0=sg[:, tsl(tt)], in1=ps[bank][:], op=ALU.mult),
                    rd=["sg", psk(bank)], wr=[("act", ci, tt)])
            if ci == ncq - 1:
                for n in range(8):
                    for tt in range(NT):
                        bank = 6 + (n * NT + tt) % 2
                        mm_group(ps[bank][:], [(wdb[:, cj, n * 128:(n + 1) * 128], act[:, cj, tsl(tt)]) for cj in range(ncq)],
                                 rd=[wdk] + [("act", cj, tt) for cj in range(ncq)], wr=[psk(bank)])
                        P.op("dve", lambda e, n=n, tt=tt, bank=bank: e.scalar_tensor_tensor(
                            out=X[:, n, tsl(tt)], in0=ps[bank][:], scalar=modp[:, l, b, 5, n:n + 1],
                            in1=X[:, n, tsl(tt)], op0=ALU.mult, op1=ALU.add),
                            rd=[psk(bank), Xk(n, tt), "modp"], wr=[Xk(n, tt)])

        load_c(0)
        for i in range(NFF + GL):
            if i < NFF:
                gate_stage(i)
            if i - GL >= 0:
                rest_stage(i - GL)

    qkc = [0]

    def qk_norm_chunk(w2d, col0, dst, dstkey, gain_ap, h, tmp, dup64=False, split=None):
        ws, wk = wslot()
        if dup64:
            for hb in range(2):
                P.dma("pool", ws[:, :, hb * 64:(hb + 1) * 64], wcols(w2d, col0, 64), wr=[wk])
        else:
            wload(ws[:], wcols(w2d, col0, 128), wk)
        sqb, sdb, rsb = tmp
        for tt in range(NT):
            i2 = qkc[0] % 2
            bA = qkc[0] % 4
            bB = 4 + qkc[0] % 2
            qkc[0] += 1
            mm_group(ps[bA][:], [(ws[:, k, :], h[:, k, tsl(tt)]) for k in range(8)], rd=[wk] + hk_all(tt), wr=[psk(bA)])
            P.op("act", lambda e, i2=i2, bA=bA: e.activation(out=sqb[i2][:], in_=ps[bA][:], func=AF.Square),
                 rd=[psk(bA)], wr=[("sqb", i2)])
            mm_group(ps[bB][:], [(bones, sqb[i2][:])], rd=[("sqb", i2), "con"], wr=[psk(bB)])
            P.op("act", lambda e, i2=i2, bB=bB: e.activation(out=sdb[i2][:], in_=ps[bB][:], func=AF.Ln,
                                                              scale=1.0 / 64, bias=EPS),
                 rd=[psk(bB)], wr=[("sdb", i2)])
            P.op("act", lambda e, i2=i2: e.activation(out=rsb[i2][:], in_=sdb[i2][:], func=AF.Exp, scale=-0.5),
                 rd=[("sdb", i2)], wr=[("rsb", i2)])
            if split is None:
                P.op("dve", lambda e, i2=i2, bA=bA, tt=tt: e.scalar_tensor_tensor(
                    out=dst[:, tsl(tt)], in0=ps[bA][:], scalar=gain_ap, in1=rsb[i2][:], op0=ALU.mult, op1=ALU.mult),
                    rd=[psk(bA), ("rsb", i2), "misc", "pp"], wr=[(dstkey, tt)])
            else:
                for hh in range(2):
                    pr = slice(hh * 64, (hh + 1) * 64)
                    P.op("dve", lambda e, i2=i2, bA=bA, tt=tt, pr=pr, hh=hh: e.scalar_tensor_tensor(
                        out=split[hh][pr, tsl(tt)], in0=ps[bA][pr, :], scalar=gain_ap[pr, :], in1=rsb[i2][pr, :],
                        op0=ALU.mult, op1=ALU.mult),
                        rd=[psk(bA), ("rsb", i2), "misc", "pp"], wr=[(dstkey, hh, tt)])

    for s in range(nseq):
        b = s
        for c in range(8):
            P.dma("sp", X[:, c, :], xT[s, c], wr=[Xk(c, tt) for tt in range(NT)])
        l = 0
        with ExitStack() as st0:
            sty = ExitStack()
            ya = sb(sty, "ya", [128, 4, T], BF16)
            with ExitStack() as st1:
                h = sb(st1, "h", [128, 8, T], BF16)
                with ExitStack() as st2:
                    do_norm(st2, h, l, b, 0)
                    if s == 0:
                        dump("h0", h[:].rearrange("p c t -> p (c t)"), [128, 8 * T],
                             rd=[("h", c, tt) for c in range(8) for tt in range(NT)])
                    P.barrier()
                    P.flush()
                if stop == "norm0":
                    break
                with ExitStack() as st2:
                    xr = sb(st2, "xr", [128, 3 + T], F32)
                    xc = sb(st2, "xc", [128, T], F32)
                    xcb = sb(st2, "xcb", [128, T], BF16)
                    ra = sb(st2, "ra", [128, T], F32)
                    ig = sb(st2, "ig", [128, T], F32)
                    s2 = sb(st2, "s2", [128, T], F32)
                    gel = sb(st2, "gel", [128, T], F32)
                    gx = sb(st2, "gx", [128, T], F32)
                    wabd = sb(st2, "wabd", [128, 4, 128], BF16)
                    wxbd = sb(st2, "wxbd", [128, 4, 128], BF16)
                    P.dma("pool", wabd[:], wabd_d, wr=["wabd"])
                    P.dma("pool", wxbd[:], wxbd_d, wr=["wxbd"])
                    P.op("dve", lambda e: e.memset(xr[:, 0:3], 0.0), wr=["xrpad"])
                    lcw, _ = PP["lcw"]
                    lcb, _ = PP["lcb"]
                    lba, _ = PP["lba"]
                    lbx, _ = PP["lbx"]
                    import os as _os
                    for j in [int(v) for v in _os.environ.get("LRU_CHUNKS", "0,1,2,3").split(",")]:
                        wsx, wkx = wslot()
                        wload(wsx[:], wcols(ev_w_in, j * 128, 128), wkx)
                        wsg, wkg = wslot()
                        wload(wsg[:], wcols(ev_w_in, 512 + j * 128, 128), wkg)
                        for tt in range(NT):
                            bank = tt % 2
                            mm_group(ps[bank][:], [(wsx[:, k, :], h[:, k, tsl(tt)]) for k in range(8)],
                                     rd=[wkx] + hk_all(tt), wr=[psk(bank)])
                            P.op("act", lambda e, tt=tt, bank=bank: e.activation(
                                out=xr[:, 3 + tt * TS:3 + (tt + 1) * TS], in_=ps[bank][:], func=AF.Identity),
                                rd=[psk(bank)], wr=[("xr", tt)])
                        xrk = [("xr", t_) for t_ in range(NT)]
                        P.op("dve", lambda e, j=j: e.tensor_scalar(
                            out=xc[:], in0=xr[:, 0:T], scalar1=pp[:, lcw + j * 4:lcw + j * 4 + 1],
                            scalar2=pp[:, lcb + j:lcb + j + 1], op0=ALU.mult, op1=ALU.add),
                            rd=xrk + ["xrpad", "pp"], wr=["xc"])
                        for k in (1, 2, 3):
                            P.op("dve", lambda e, j=j, k=k: e.scalar_tensor_tensor(
                                out=xc[:], in0=xr[:, k:k + T], scalar=pp[:, lcw + j * 4 + k:lcw + j * 4 + k + 1],
                                in1=xc[:], op0=ALU.mult, op1=ALU.add), rd=xrk + ["xc", "pp"], wr=["xc"])
                        P.op("act", lambda e: e.activation(out=xcb[:], in_=xc[:], func=AF.Identity), rd=["xc"], wr=["xcb"])
                        for tt in range(NT):
                            bank = 2 + tt % 2
                            mm_group(ps[bank][:], [(wabd[:, j, :], xcb[:, tsl(tt)])], rd=["wabd", "xcb"], wr=[psk(bank)])
                            P.op("act", lambda e, j=j, tt=tt, bank=bank: e.activation(
                                out=ra[:, tsl(tt)], in_=ps[bank][:], func=AF.Sigmoid, bias=pp[:, lba + j:lba + j + 1]),
                                rd=[psk(bank), "pp"], wr=[("ra", tt)])
                            bank2 = 4 + tt % 2
                            mm_group(ps[bank2][:], [(wxbd[:, j, :], xcb[:, tsl(tt)])], rd=["wxbd", "xcb"], wr=[psk(bank2)])
                            P.op("act", lambda e, j=j, tt=tt, bank2=bank2: e.activation(
                                out=ig[:, tsl(tt)], in_=ps[bank2][:], func=AF.Sigmoid, bias=pp[:, lbx + j:lbx + j + 1]),
                                rd=[psk(bank2), "pp"], wr=[("ig", tt)])
                        rak = [("ra", t_) for t_ in range(NT)]
                        igk = [("ig", t_) for t_ in range(NT)]
                        P.op("act", lambda e, j=j: e.activation(out=ra[:], in_=ra[:], func=AF.Exp, scale=cl[:, j:j + 1]),
                             rd=rak + ["misc"], wr=rak)
                        P.op("act", lambda e: e.activation(out=s2[:], in_=ra[:], func=AF.Square), rd=rak, wr=["s2"])
                        P.op("act", lambda e: e.activation(out=s2[:], in_=s2[:], func=AF.Sqrt, scale=-1.0, bias=1.0),
                             rd=["s2"], wr=["s2"])
                        P.op("dve", lambda e: e.tensor_tensor(out=s2[:], in0=s2[:], in1=ig[:], op=ALU.mult),
                             rd=["s2"] + igk, wr=["s2"])
                        P.op("dve", lambda e: e.tensor_tensor(out=s2[:], in0=s2[:], in1=xc[:], op=ALU.mult),
                             rd=["s2", "xc"], wr=["s2"])
                        P.op("dve", lambda e: e.tensor_tensor_scan(out=xc[:], data0=ra[:], data1=s2[:], initial=0.0,
                                                                    op0=ALU.mult, op1=ALU.add),
                             rd=rak + ["s2", "xc"], wr=["xc"])
                        for tt in range(NT):
                            bank = 6 + tt % 2
                            mm_group(ps[bank][:], [(wsg[:, k, :], h[:, k, tsl(tt)]) for k in range(8)],
                                     rd=[wkg] + hk_all(tt), wr=[psk(bank)])
                            P.op("act", lambda e, tt=tt, bank=bank: e.activation(
                                out=gx[:, tsl(tt)], in_=ps[bank][:], func=AF.Identity),
                                rd=[psk(bank)], wr=[("gx", tt)])
                        gxk = [("gx", t_) for t_ in range(NT)]
                        P.op("act", lambda e: e.activation(out=gel[:], in_=gx[:], func=AF.Square), rd=gxk, wr=["gel"])
                        P.op("dve", lambda e: e.tensor_scalar(out=gel[:], in0=gel[:], scalar1=0.044715, scalar2=1.0,
                                                              op0=ALU.mult, op1=ALU.add), rd=["gel"], wr=["gel"])
                        P.op("dve", lambda e: e.tensor_tensor(out=gel[:], in0=gel[:], in1=gx[:], op=ALU.mult),
                             rd=["gel"] + gxk, wr=["gel"])
                        P.op("act", lambda e: e.activation(out=gel[:], in_=gel[:], func=AF.Sigmoid, scale=1.5957691216057308),
                             rd=["gel"], wr=["gel"])
                        P.op("dve", lambda e: e.tensor_tensor(out=gel[:], in0=gel[:], in1=gx[:], op=ALU.mult),
                             rd=["gel"] + gxk, wr=["gel"])
                        P.op("dve", lambda e, j=j: e.tensor_tensor(out=ya[:, j, :], in0=xc[:], in1=gel[:], op=ALU.mult),
                             rd=["xc", "gel"], wr=[("ya", j)])
                    if s == 0:
                        dump("lxc", xc[:], [128, T], rd=["xc"])
                        dump("lra", ra[:], [128, T], rd=[("ra", t_) for t_ in range(NT)])
                        dump("lig", ig[:], [128, T], rd=[("ig", t_) for t_ in range(NT)])
                        dump("ls2", s2[:], [128, T], rd=["s2"])
                        dump("ya", ya[:].rearrange("p c t -> p (c t)"), [128, 4 * T], rd=[("ya", j) for j in range(4)])
                    P.barrier()
                    P.flush()
                if stop == "lru":
                    sty.close()
                    break
                out_proj_residual(ev_w_out, lambda k: ya[:, k, :], lambda tt: [("ya", j) for j in range(4)], l, b, 2, r0=0, nk=4)
                P.barrier()
                P.flush()
                sty.close()
                qz = [sb(st0, f"qz{i}", [128, 4, T], BF16) for i in range(2)]
                kn = sb(st0, "kn", [128, 4, T], BF16)
                vt = sb(st0, "vt", [128, 16, 512], BF16)
                P.op("dve", lambda e: e.memset(qz[0][64:128, :, :], 0.0), wr=[("qzpad", 0)])
                P.op("dve", lambda e: e.memset(qz[1][0:64, :, :], 0.0), wr=[("qzpad", 1)])
                with ExitStack() as st2:
                    sqb = [sb(st2, f"sqb{i}", [128, TS], BF16) for i in range(2)]
                    sdb = [sb(st2, f"sdb{i}", [128, TS], F32) for i in range(2)]
                    rsb = [sb(st2, f"rsb{i}", [128, TS], F32) for i in range(2)]
                    for j in range(4):
                        qk_norm_chunk(ev_w_in, 1024 + j * 128, None, ("qz", j), evq8, h, (sqb, sdb, rsb),
                                      split=(qz[0][:, j, :], qz[1][:, j, :]))
                        qk_norm_chunk(ev_w_in, 1536 + j * 128, kn[:, j, :], ("kn", j), ppv("evkg"), h, (sqb, sdb, rsb))
                    wvs = []
                    for q4 in range(4):
                        ws_, wk_ = wslot()
                        wload(ws_[:], wcols(ev_w_in, 2048 + q4 * 128, 128), wk_)
                        wvs.append((ws_, wk_))
                    for blk in range(16):
                        bank = 6 + blk % 2

                        def vproj(e, blk=blk, bank=bank):
                            ins = None
                            for q4 in range(4):
                                for k in range(8):
                                    ins = e.matmul(ps[bank][:, q4 * 128:(q4 + 1) * 128], lhsT=h[:, k, blk * 128:(blk + 1) * 128],
                                                   rhs=wvs[q4][0][:, k, :], start=(k == 0), stop=(k == 7))
                            return ins
                        P.op("pe", vproj, rd=[w_[1] for w_ in wvs] + hk_all(blk // 4), wr=[psk(bank)])
                        if blk % 2 == 0:
                            P.op("act", lambda e, blk=blk, bank=bank: e.activation(out=vt[:, blk, :], in_=ps[bank][:],
                                                                                   func=AF.Identity),
                                 rd=[psk(bank)], wr=[("vt", blk)])
                        else:
                            P.op("dve", lambda e, blk=blk, bank=bank: e.tensor_copy(out=vt[:, blk, :], in_=ps[bank][:]),
                                 rd=[psk(bank)], wr=[("vt", blk)])
                    if s == 0:
                        dump("kn", kn[:].rearrange("p c t -> p (c t)"), [128, 4 * T],
                             rd=[(("kn", j), t_) for j in range(4) for t_ in range(NT)])
                        dump("vt", vt[:].rearrange("p c t -> p (c t)"), [128, 16 * 512], rd=[("vt", k) for k in range(16)])
                    P.barrier()
                    P.flush()
            if stop in ("norm0", "lru", "sbproj"):
                break
            yb = sb(st0, "yb", [128, 4, T], BF16)
            with ExitStack() as st2:
                eb = [sb(st2, f"eb{i}", [128, TS], F32) for i in range(3)]
                LOOK = 2
                NSP, NRB = LOOK + 3, LOOK + 2
                spb = [sb(st2, f"spb{i}", [128, TS], BF16) for i in range(NSP)]
                Rb = [sb(st2, f"Rb{i}", [128, TS], BF16) for i in range(NRB)]
                wb_ = [sb(st2, f"wb{i}", [128, TS], BF16) for i in range(4)]
                mo, _ = CO["md"]
                for i in range(NSP):
                    P.op("dve", lambda e, i=i: e.memset(spb[i][:], 0.0), wr=[("spb", i)])
                tiles = []
                for j in range(4):
                    for tt in range(NT):
                        for hh in range(2):
                            ob = 4 + ((j * NT + tt) * 2 + hh) % 4
                            for idx, kb in enumerate(range(4 * tt + 3, -1, -1)):
                                tiles.append(dict(j=j, tt=tt, hh=hh, kb=kb, idx=idx, ob=ob, R=None))
                cnt_ = dict(z=0, e=0, sp=0, r=0, rb=0, w=0)

                def stage_a(ti):
                    t = tiles[ti]
                    j, tt, hh, kb, idx = t["j"], t["tt"], t["hh"], t["kb"], t["idx"]
                    p0 = hh * 64
                    dz = kb - 4 * tt
                    zb = cnt_["z"] % 2
                    cnt_["z"] += 1
                    c0 = max(dz, 0) * 128
                    t["c0"] = c0
                    mm_group(ps[zb][:, c0:TS], [(kn[:, j, kb * 128:(kb + 1) * 128], qz[hh][:, j, tt * TS + c0:(tt + 1) * TS])],
                             rd=[(("kn", j), kb // 4), (("qz", j), hh, tt), ("qzpad", hh)], wr=[psk(zb)])
                    ei = cnt_["e"] % 3
                    cnt_["e"] += 1
                    P.op("act", lambda e, ei=ei, zb=zb, c0=c0: e.activation(out=eb[ei][:, c0:TS], in_=ps[zb][:, c0:TS], func=AF.Exp),
                         rd=[psk(zb)], wr=[("eb", ei)])
                    si = cnt_["sp"] % NSP
                    cnt_["sp"] += 1
                    t["si"] = si
                    P.op("act", lambda e, ei=ei, si=si, c0=c0: e.activation(out=spb[si][:, c0:TS], in_=eb[ei][:, c0:TS], func=AF.Ln, bias=1.0),
                         rd=[("eb", ei)], wr=[("spb", si)])
                    if dz >= 0:
                        P.op("dve", lambda e, si=si, dz=dz: e.tensor_tensor(
                            out=spb[si][:], in0=spb[si][:], in1=con[:, mo + dz * 512:mo + (dz + 1) * 512], op=ALU.mult),
                            rd=[("spb", si), "con"], wr=[("spb", si)])
                    if kb > 0:
                        nt_ = tiles[ti + 1]
                        if idx == 0:
                            nt_["R"] = (spb[si], ("spb", si))
                        else:
                            rn = cnt_["rb"] % NRB
                            cnt_["rb"] += 1
                            rsrc, rkey = t["R"]
                            P.op("dve", lambda e, si=si, rsrc=rsrc, rn=rn: e.tensor_tensor(
                                out=Rb[rn][:], in0=rsrc[:], in1=spb[si][:], op=ALU.add),
                                rd=[("spb", si), rkey], wr=[("Rb", rn)])
                            nt_["R"] = (Rb[rn], ("Rb", rn))

                def stage_b(ti):
                    t = tiles[ti]
                    j, tt, hh, kb, idx, ob, si = t["j"], t["tt"], t["hh"], t["kb"], t["idx"], t["ob"], t["si"]
                    p0 = hh * 64
                    dz = kb - 4 * tt
                    rb = 2 + cnt_["r"] % 2
                    cnt_["r"] += 1
                    c0 = t["c0"]
                    pairs = [(ntri, spb[si][:, c0:TS])]
                    rdk = [("spb", si), "con", (("kn", j), kb // 4), (("qz", j), hh, tt), ("qzpad", hh)]
                    if t["R"] is not None:
                        pairs.append((nones, t["R"][0][:, c0:TS]))
                        rdk.append(t["R"][1])
                    pairs.append((kn[:, j, kb * 128:(kb + 1) * 128], qz[hh][:, j, tt * TS + c0:(tt + 1) * TS]))
                    mm_group(ps[rb][:, c0:TS], pairs, rd=rdk, wr=[psk(rb)])
                    wi = cnt_["w"] % 4
                    cnt_["w"] += 1
                    t["wi"] = wi
                    P.op("act", lambda e, wi=wi, rb=rb, c0=c0: e.activation(out=wb_[wi][:, c0:TS], in_=ps[rb][:, c0:TS], func=AF.Exp),
                         rd=[psk(rb)], wr=[("wb", wi)])
                    if dz >= 0:
                        P.op("dve", lambda e, wi=wi, dz=dz, c0=c0: e.tensor_tensor(
                            out=wb_[wi][:, c0:TS], in0=wb_[wi][:, c0:TS], in1=con[:, mo + dz * 512 + c0:mo + (dz + 1) * 512], op=ALU.mult),
                            rd=[("wb", wi), "con"], wr=[("wb", wi)])

                def stage_c(ti):
                    t = tiles[ti]
                    j, tt, hh, kb, idx, ob, wi = t["j"], t["tt"], t["hh"], t["kb"], t["idx"], t["ob"], t["wi"]
                    p0 = hh * 64
                    c0 = t["c0"]

                    def pv(e, ob=ob, kb=kb, j=j, wi=wi, c0=c0, first=(idx == 0), last=(kb == 0)):
                        return e.matmul(ps[ob][:, c0:TS], lhsT=vt[:, kb, j * 128:(j + 1) * 128],
                                        rhs=wb_[wi][:, c0:TS], start=first, stop=last, skip_group_check=True)
                    P.op("pe", pv, rd=[("wb", wi), ("vt", kb)], wr=[("pso", ob)])
                    if kb == 0:
                        P.op("act", lambda e, j=j, tt=tt, ob=ob, p0=p0: e.activation(
                            out=yb[p0:p0 + 64, j, tsl(tt)], in_=ps[ob][p0:p0 + 64, :], func=AF.Identity),
                            rd=[("pso", ob)], wr=[("yb", j, tt, hh)])

                ntl = len(tiles)
                for ti in range(ntl + LOOK + 1):
                    if ti < ntl:
                        stage_a(ti)
                    if 0 <= ti - LOOK < ntl:
                        stage_b(ti - LOOK)
                    if ti - LOOK - 1 >= 0:
                        stage_c(ti - LOOK - 1)
                if s == 0:
                    dump("yb", yb[:].rearrange("p c t -> p (c t)"), [128, 4 * T],
                         rd=[("yb", j, t_, hh) for j in range(4) for t_ in range(NT) for hh in range(2)])
                P.barrier()
                P.flush()
            if stop == "sb":
                break
            out_proj_residual(ev_w_out, lambda k: yb[:, k, :],
                              lambda tt: [("yb", j, tt, hh) for j in range(4) for hh in range(2)], l, b, 2, r0=4, nk=4)
            P.barrier()
            P.flush()
        if s == 0:
            dump("x0mid", X[:].rearrange("p c t -> p (c t)"), [128, 8 * T], rd=[Xk(c, tt) for c in range(8) for tt in range(NT)])
        if stop == "mix0":
            break
        with ExitStack() as st1:
            h = sb(st1, "h", [128, 8, T], BF16)
            with ExitStack() as st2:
                do_norm(st2, h, l, b, 1)
                P.barrier()
                P.flush()
            with ExitStack() as st2:
                do_ffn(st2, h, l, b)
                P.barrier()
                P.flush()
        if s == 0:
            dump("x1", X[:].rearrange("p c t -> p (c t)"), [128, 8 * T], rd=[Xk(c, tt) for c in range(8) for tt in range(NT)])
        if stop == "l0":
            break
        l = 1
        with ExitStack() as st0:
            with ExitStack() as st1:
                h = sb(st1, "h", [128, 8, T], BF16)
                with ExitStack() as st2:
                    do_norm(st2, h, l, b, 0)
                    P.barrier()
                    P.flush()
                qn = sb(st0, "qn1", [128, 8, T], BF16)
                kd = sb(st0, "kd", [128, 4, T], BF16)
                va = sb(st0, "va", [128, 16, 4, 65], BF16)
                with ExitStack() as st2:
                    sqb = [sb(st2, f"sqb{i}", [128, TS], BF16) for i in range(2)]
                    sdb = [sb(st2, f"sdb{i}", [128, TS], F32) for i in range(2)]
                    rsb = [sb(st2, f"rsb{i}", [128, TS], F32) for i in range(2)]
                    wv = sb(st2, "wv", [128, 8, 512], BF16)
                    for c in range(8):
                        qk_norm_chunk(od_w_in, c * 128, qn[:, c, :], ("qn", c), odq8, h, (sqb, sdb, rsb))
                    for g in range(4):
                        qk_norm_chunk(od_w_in, 1024 + g * 64, kd[:, g, :], ("kd", g), ppv("odkg"), h, (sqb, sdb, rsb), dup64=True)
                    P.op("dve", lambda e: e.memset(va[:, :, :, 64:65], 1.0), wr=["vaones"])
                    wload(wv[:, :, 0:256], wcols(od_w_in, 1280, 256), "wv")
                    for blk in range(16):
                        bank = 4 + blk % 4
                        mm_group(ps[bank][:, 0:256], [(h[:, k, blk * 128:(blk + 1) * 128], wv[:, k, 0:256]) for k in range(8)],
                                 rd=["wv"] + hk_all(blk // 4), wr=[psk(bank)])
                        P.op("act" if blk % 2 == 0 else "dve",
                             (lambda e, blk=blk, bank=bank: e.activation(
                                 out=va[:, blk, :, 0:64], in_=ps[bank][:, 0:256].rearrange("p (g d) -> p g d", g=4), func=AF.Identity))
                             if blk % 2 == 0 else
                             (lambda e, blk=blk, bank=bank: e.tensor_copy(
                                 out=va[:, blk, :, 0:64], in_=ps[bank][:, 0:256].rearrange("p (g d) -> p g d", g=4))),
                             rd=[psk(bank)], wr=[("va", blk)])
                    P.barrier()
                    P.flush()
            if stop == "l1proj":
                break
            yT = sb(st0, "yT", [128, 8, T], BF16)
            with ExitStack() as st2:
                pb = [sb(st2, f"pb{i}", [128, 2, TS], BF16) for i in range(2)]
                den = [sb(st2, f"den{i}", [128, 4], F32) for i in range(2)]
                ytok = [sb(st2, f"ytok{i}", [128, D], BF16) for i in range(2)]
                units = [(qb, g) for qb in range(16) for g in range(4)]

                def swa_a(u):
                    qb, g = units[u]
                    pi = u % 2
                    kbs = [qb - 1, qb] if qb > 0 else [qb]
                    c0 = 0 if qb > 0 else 256
                    for hb in range(2):
                        sbank = hb + 2 * pi

                        def sc(e, sbank=sbank, g=g, kbs=kbs, qb=qb, hb=hb):
                            ins = None
                            for kb in kbs:
                                which = 0 if kb == qb - 1 else 1
                                ins = e.matmul(ps[sbank][:, which * 256:(which + 1) * 256],
                                               lhsT=kd[hb * 64:(hb + 1) * 64, g, kb * 128:(kb + 1) * 128],
                                               rhs=qn[hb * 64:(hb + 1) * 64, 2 * g:2 * g + 2, qb * 128:(qb + 1) * 128],
                                               start=True, stop=True)
                            return ins
                        P.op("pe", sc, rd=[(("kd", g), kb // 4) for kb in kbs] + [(("qn", 2 * g), qb // 4), (("qn", 2 * g + 1), qb // 4)],
                             wr=[psk(sbank)])
                        P.op("act", lambda e, pi=pi, hb=hb, sbank=sbank, c0=c0: e.activation(
                            out=pb[pi][:, hb, c0:512], in_=ps[sbank][:, c0:512], func=AF.Exp),
                            rd=[psk(sbank)], wr=[("pb", pi, hb)])
                        P.op("dve", lambda e, pi=pi, hb=hb, c0=c0: e.tensor_tensor(
                            out=pb[pi][:, hb, c0:512], in0=pb[pi][:, hb, c0:512], in1=maskp[:, c0:512], op=ALU.mult),
                            rd=[("pb", pi, hb), "con"], wr=[("pb", pi, hb)])

                def swa_b(u):
                    qb, g = units[u]
                    pi = u % 2
                    yi = qb % 2
                    kbs = [qb - 1, qb] if qb > 0 else [qb]
                    ybank = 4 + pi

                    def pvm(e, ybank=ybank, pi=pi, kbs=kbs, qb=qb, g=g):
                        ins = None
                        for hc in range(4):
                            hb, e_ = hc // 2, hc % 2
                            for i_, kb in enumerate(kbs):
                                which = 0 if kb == qb - 1 else 1
                                ins = e.matmul(ps[ybank][:, hc * 65:(hc + 1) * 65],
                                               lhsT=pb[pi][:, hb, which * 256 + e_ * 128:which * 256 + (e_ + 1) * 128],
                                               rhs=va[:, kb, g, :], start=(i_ == 0), stop=(i_ == len(kbs) - 1))
                        return ins
                    P.op("pe", pvm, rd=[("pb", pi, 0), ("pb", pi, 1)] + [("va", kb) for kb in kbs] + ["vaones"], wr=[psk(ybank)])
                    yv = ps[ybank][:, 0:260].rearrange("p (hb e d) -> p hb e d", hb=2, e=2)
                    P.op("dve", lambda e, pi=pi, yv=yv, g=g: e.tensor_tensor(
                        out=den[pi][:].rearrange("p (hb e) -> p hb e", hb=2),
                        in0=yv[:, :, :, 64],
                        in1=esink[:, 4 * g:4 * g + 4].rearrange("p (e hb) -> p hb e", hb=2), op=ALU.add),
                        rd=[psk(ybank), "misc"], wr=[("den", pi)])
                    P.op("dve", lambda e, pi=pi: e.reciprocal(out=den[pi][:], in_=den[pi][:]), rd=[("den", pi)], wr=[("den", pi)])
                    P.op("dve", lambda e, pi=pi, yv=yv, g=g, yi=yi: e.tensor_tensor(
                        out=ytok[yi][:, g * 256:(g + 1) * 256].rearrange("p (e hb d) -> p hb e d", e=2, hb=2),
                        in0=yv[:, :, :, 0:64],
                        in1=den[pi][:].rearrange("p (hb e) -> p hb e", hb=2).unsqueeze(3).broadcast_to([128, 2, 2, 64]),
                        op=ALU.mult),
                        rd=[psk(ybank), ("den", pi)], wr=[("ytok", yi, g)])
                    if g == 3:
                        tbank = 6 + yi
                        tp = ps[tbank][:].bitcast(BF16)

                        def trn(e, tp=tp, yi=yi):
                            ins = None
                            for c in range(8):
                                ins = e.transpose(out=tp[:, c * 128:(c + 1) * 128], in_=ytok[yi][:, c * 128:(c + 1) * 128], identity=ident)
                            return ins
                        P.op("pe", trn, rd=[("ytok", yi, g_) for g_ in range(4)] + ["con"], wr=[psk(tbank)])
                        P.op("act", lambda e, tp=tp, qb=qb: e.activation(
                            out=yT[:, :, qb * 128:(qb + 1) * 128], in_=tp.rearrange("p (c t) -> p c t", c=8), func=AF.Identity),
                            rd=[psk(tbank)], wr=[("yT", qb // 4)])

                nun = len(units)
                for u in range(nun + 1):
                    if u < nun:
                        swa_a(u)
                    if u >= 1:
                        swa_b(u - 1)
                if s == 0:
                    dump("yT", yT[:].rearrange("p c t -> p (c t)"), [128, 8 * T], rd=[("yT", t_) for t_ in range(NT)])
                P.barrier()
                P.flush()
            if stop == "swa":
                break
            out_proj_residual(od_w_out, lambda k: yT[:, k, :], lambda tt: [("yT", tt)], l, b, 2)
            P.barrier()
            P.flush()
        if s == 0:
            dump("x1mid", X[:].rearrange("p c t -> p (c t)"), [128, 8 * T], rd=[Xk(c, tt) for c in range(8) for tt in range(NT)])
        with ExitStack() as st1:
            h = sb(st1, "h", [128, 8, T], BF16)
            with ExitStack() as st2:
                do_norm(st2, h, l, b, 1)
                P.barrier()
                P.flush()
            with ExitStack() as st2:
                do_ffn(st2, h, l, b)
                P.barrier()
                P.flush()
        for c in range(8):
            P.dma("sp", out_d[s, c], X[:, c, :], rd=[Xk(c, tt) for tt in range(NT)], wr=[("out", s, c)])
        P.flush()

    P.barrier()
    P.op("sp", None)
    P.flush()
    top.close()
    return nc, P, dbg_out


_CACHE = {}


def kernel(**inputs):
    if "nc" not in _CACHE:
        _CACHE["nc"] = build()[0]
    nc = _CACHE["nc"]
    in_maps = [_host_inputs(inputs, core) for core in range(NCORES)]
    res = run_bass_kernel_spmd(nc, in_maps, core_ids=list(range(NCORES)))
    outs = []
    for core in range(NCORES):
        o = np.asarray(res.results[core]["out"], np.float32).reshape(2, D, T)
        outs.append(o.transpose(0, 2, 1))
    return np.ascontiguousarray(np.concatenate(outs, axis=0)).astype(np.float32)
```

```python
from contextlib import ExitStack

import numpy as np
import concourse.bass as bass
import concourse.mybir as mybir
from concourse.bass_utils import run_bass_kernel_spmd

F32 = mybir.dt.float32
BF16 = mybir.dt.bfloat16
AF = mybir.ActivationFunctionType
ALU = mybir.AluOpType

NCORES = 8
T = 2048
NT = 4
TS = 512
D = 1024
DFF = 2816
NFF = 22
EPS = 1e-6
FQ = [(0, 6), (6, 6), (12, 5), (17, 5)]


class Prog:
    NRING = 8

    def __init__(self, nc):
        self.nc = nc
        self.engs = {"pe": nc.tensor, "act": nc.scalar, "dve": nc.vector,
                     "pool": nc.gpsimd, "sp": nc.sync}
        self.ops = []
        self.nflushed = 0
        self.last_w = {}
        self.readers = {}
        self.last_on_eng = {}
        self.dmas_since_barrier = []
        self.barrier_deps = set()
        self.dma_hist = {}
        self.sems = {e: nc.alloc_semaphore("c_" + e) for e in self.engs}
        self.rings = {}
        self.cnt = {e: 0 for e in self.engs}
        self.done = []
        self.waited = {e: {} for e in self.engs}
        self.nwaits = 0

    limit = None

    def op(self, eng, fn, rd=(), wr=(), dma=False, nobarrier=False):
        i = len(self.ops)
        if self.limit is not None and i >= self.limit and not dma and fn is not None:
            return None
        o = dict(eng=eng, fn=fn, dma=dma)
        ops = self.ops
        d = set()
        raw = set()
        for k in rd:
            j = self.last_w.get(k)
            if j is not None:
                d.add(j)
                raw.add(j)
        for k in wr:
            j = self.last_w.get(k)
            if j is not None:
                d.add(j)
            d.update(self.readers.get(k, ()))
        keep = set()
        for j in d:
            oj = ops[j]
            if (not oj["dma"]) and (not dma) and oj["eng"] == eng and eng == "pe":
                continue
            keep.add(j)
        for j in (() if nobarrier else self.barrier_deps):
            oj = ops[j]
            if (not oj["dma"]) and (not dma) and oj["eng"] == eng:
                continue
            keep.add(j)
        for k in rd:
            self.readers.setdefault(k, []).append(i)
        for k in wr:
            self.last_w[k] = i
            self.readers[k] = []
        if dma:
            hist = self.dma_hist.setdefault(eng, [])
            c = len(hist)
            if eng not in self.rings:
                self.rings[eng] = [self.nc.alloc_semaphore(f"d_{eng}{r}") for r in range(self.NRING)]
            o["ring"] = c % self.NRING
            o["rval"] = 16 * (c // self.NRING + 1)
            if c >= self.NRING:
                keep.add(hist[c - self.NRING])
            hist.append(i)
            self.dmas_since_barrier.append(i)
        else:
            self.last_on_eng[eng] = i
        o["deps"] = keep
        ops.append(o)
        self.done.append(None)
        return i

    def dma(self, eng, out, in_, rd=(), wr=(), nobarrier=False):
        return self.op(eng, lambda e: e.dma_start(out=out, in_=in_), rd, wr, dma=True, nobarrier=nobarrier)

    def barrier(self):
        self.barrier_deps = set(self.last_on_eng.values()) | set(self.dmas_since_barrier)
        self.dmas_since_barrier = []

    def flush(self):
        ops = self.ops
        n = len(ops)
        start = self.nflushed
        needs = set()
        for i in range(start, n):
            needs.update(ops[i]["deps"])
        needs.update(self.last_on_eng.values())
        needs.update(self.last_w.values())
        for r in self.readers.values():
            needs.update(r)
        needs.update(self.barrier_deps)
        for i in range(start, n):
            o = ops[i]
            e = o["eng"]
            eng = self.engs[e]
            need = {}
            for j in o["deps"]:
                s, v = self.done[j]
                k = id(s)
                if k not in need or need[k][1] < v:
                    need[k] = (s, v)
            for k, (s, v) in need.items():
                if self.waited[e].get(k, 0) >= v:
                    continue
                eng.wait_ge(s, v)
                self.nwaits += 1
                self.waited[e][k] = v
            if o["fn"] is None:
                self.done[i] = (self.sems[e], self.cnt[e])
                continue
            ins = o["fn"](eng)
            if o["dma"]:
                s = self.rings[e][o["ring"]]
                ins.then_inc(s, 16)
                self.done[i] = (s, o["rval"])
            elif i in needs:
                self.cnt[e] += 1
                ins.then_inc(self.sems[e], 1)
                self.done[i] = (self.sems[e], self.cnt[e])
            else:
                self.done[i] = (self.sems[e], self.cnt[e] + 1)
            o["fn"] = None
        self.nflushed = n


PP = {}
_off = 0
for _name, _n in [("cT", 16), ("adab", 96), ("gmix", 16), ("gffn", 16), ("lcw", 16), ("lcb", 4),
                  ("lba", 4), ("lbx", 4), ("llam", 4), ("evqg", 1), ("evkg", 1), ("odqg", 1),
                  ("odkg", 1), ("sinks", 16), ("fcw", 132), ("fcb", 44)]:
    PP[_name] = (_off, _n)
    _off += _n
NPP = _off

CO = {}
_off = 0
for _name, _n in [("ident", 128), ("ones", 128), ("bones", 128), ("tri", 128), ("ntri", 128), ("nones", 128), ("md", 2048),
                  ("maskp", 512), ("maskc", 512)]:
    CO[_name] = (_off, _n)
    _off += _n
NCON = _off


def _consts():
    c = np.zeros((128, NCON), np.float32)
    p = np.arange(128)[:, None]
    m = np.arange(128)[None, :]
    c[:, CO["ident"][0]:CO["ident"][0] + 128] = (p == m)
    c[:, CO["ones"][0]:CO["ones"][0] + 128] = 1.0
    c[:, CO["bones"][0]:CO["bones"][0] + 128] = ((p // 64) == (m // 64))
    c[:, CO["tri"][0]:CO["tri"][0] + 128] = (p >= m)
    c[:, CO["ntri"][0]:CO["ntri"][0] + 128] = -1.0 * (p >= m)
    c[:, CO["nones"][0]:CO["nones"][0] + 128] = -1.0
    t = np.arange(512)[None, :]
    for d in range(4):
        c[:, CO["md"][0] + d * 512:CO["md"][0] + (d + 1) * 512] = ((d * 128 + p) < t)
    c[:, CO["maskp"][0]:CO["maskp"][0] + 512] = np.concatenate([np.tile((p > m), (1, 2)), np.tile((p <= m), (1, 2))], 1)
    c[:, CO["maskc"][0]:CO["maskc"][0] + 512] = np.tile((p <= m), (1, 4))
    return c


def _pcol(v):
    v = np.asarray(v, np.float32)
    return np.ascontiguousarray(v.reshape(-1, 128).T)


def _host_inputs(inp, core):
    f = lambda a: np.ascontiguousarray(np.asarray(a, np.float32))
    b0 = 2 * core
    x = f(inp["x"][b0:b0 + 2])
    xT = np.ascontiguousarray(x.transpose(0, 2, 1)).reshape(2, 8, 128, T)
    pp = np.zeros((128, NPP), np.float32)

    def put(name, arr):
        o, n = PP[name]
        pp[:, o:o + n] = np.asarray(arr, np.float32).reshape(128, n)

    c = f(inp["c"][b0:b0 + 2])
    put("cT", c.reshape(2, 8, 128).transpose(2, 1, 0))
    put("adab", np.stack([_pcol(inp["ada_b"][l]) for l in range(2)], 1))
    put("gmix", np.stack([_pcol(inp["norm_mix_g"][l]) for l in range(2)], 1))
    put("gffn", np.stack([_pcol(inp["norm_ffn_g"][l]) for l in range(2)], 1))
    cw = f(inp["ev_conv_w"][0])
    put("lcw", np.stack([_pcol(cw[k]) for k in range(4)], 2))
    put("lcb", _pcol(inp["ev_conv_b"][0]))
    put("lba", _pcol(inp["ev_ba"][0]))
    put("lbx", _pcol(inp["ev_bx"][0]))
    put("llam", _pcol(inp["ev_lam"][0]))
    put("evqg", np.tile(f(inp["ev_qn_g"][0]), 2)[:, None])
    put("evkg", np.tile(f(inp["ev_kn_g"][0]), 2)[:, None])
    put("odqg", np.tile(f(inp["od_qn_g"][0]), 2)[:, None])
    put("odkg", np.tile(f(inp["od_kn_g"][0]), 2)[:, None])
    put("sinks", np.tile(f(inp["od_sinks"][0])[None, :], (128, 1)))
    fcw = f(inp["ffn_conv_w"])
    put("fcw", np.stack([np.stack([_pcol(fcw[l, k]) for k in range(3)], 2) for l in range(2)], 1))
    put("fcb", np.stack([_pcol(inp["ffn_conv_b"][l]) for l in range(2)], 1))

    def bd(w):
        w = f(w)
        o = np.zeros((128, 4, 128), np.float32)
        for j in range(4):
            o[0:64, j, 0:64] = w[2 * j]
            o[64:128, j, 64:128] = w[2 * j + 1]
        return o

    return {
        "xT": xT, "pp": pp, "consts": _consts(),
        "ada_w": f(inp["ada_w"]),
        "ev_w_in": f(inp["ev_w_in"][0]), "ev_w_out": f(inp["ev_w_out"][0]),
        "od_w_in": f(inp["od_w_in"][0]), "od_w_out": f(inp["od_w_out"][0]),
        "w_gate": f(inp["ffn_w_gate"]), "w_up": f(inp["ffn_w_up"]), "w_down": f(inp["ffn_w_down"]),
        "wabd": bd(inp["ev_wa"][0]), "wxbd": bd(inp["ev_wx"][0]),
    }


def build(nseq=2, dbg=None, stop=None):
    dbg = dbg or set()
    nc = bass.Bass("TRN2", target_bir_lowering=False)
    P = Prog(nc)

    def din(name, shape):
        return nc.dram_tensor(name, list(shape), F32, kind="ExternalInput").ap()

    xT = din("xT", [2, 8, 128, T])
    pp_d = din("pp", [128, NPP])
    con_d = din("consts", [128, NCON])
    ada_w = din("ada_w", [2, D, 6 * D])
    ev_w_in = din("ev_w_in", [D, 2560])
    ev_w_out = din("ev_w_out", [D, D])
    od_w_in = din("od_w_in", [D, 1536])
    od_w_out = din("od_w_out", [D, D])
    w_gate = din("w_gate", [2, D, DFF])
    w_up = din("w_up", [2, D, DFF])
    w_down = din("w_down", [2, DFF, D])
    wabd_d = din("wabd", [128, 4, 128])
    wxbd_d = din("wxbd", [128, 4, 128])
    out_d = nc.dram_tensor("out", [2, 8, 128, T], F32, kind="ExternalOutput").ap()
    dbg_out = {}

    def dump(name, ap_sb, shape, rd):
        if name not in dbg:
            return
        d = nc.dram_tensor("dbg_" + name, list(shape), F32, kind="ExternalOutput").ap()
        dbg_out[name] = d
        P.dma("pool", d, ap_sb, rd=rd, wr=[("dbgout", name)])

    top = ExitStack()

    uid = [0]
    sb_lo = (nc.sbuf_base + 63) // 64 * 64
    free_list = [[sb_lo, nc.sbuf_top]]
    peak = [0]

    def sb(st, name, shape, dt):
        uid[0] += 1
        nbytes = int(np.prod(shape[1:])) * (2 if dt == BF16 else 4)
        nbytes = (nbytes + 63) // 64 * 64
        for seg in free_list:
            if seg[1] - seg[0] >= nbytes:
                off = seg[0]
                seg[0] += nbytes
                break
        else:
            raise RuntimeError(f"SBUF full allocating {name} ({nbytes} B); free={free_list}")
        peak[0] = max(peak[0], off + nbytes)

        def release():
            free_list.append([off, off + nbytes])
            free_list.sort()
            merged = []
            for sg in free_list:
                if sg[0] >= sg[1]:
                    continue
                if merged and merged[-1][1] == sg[0]:
                    merged[-1][1] = sg[1]
                else:
                    merged.append(sg)
            free_list[:] = merged
        st.callback(release)
        return nc.alloc_sbuf_tensor_at(f"{name}_u{uid[0]}", list(shape), dt, offset=off)

    ps = [top.enter_context(nc.psum_tensor(f"ps{i}", [128, 512], F32)) for i in range(8)]
    X = sb(top, "X", [128, 8, T], F32)
    pp = sb(top, "pp", [128, NPP], F32)
    con = sb(top, "con", [128, NCON], BF16)
    modp = sb(top, "modp", [128, 2, 2, 6, 8], F32)
    misc = sb(top, "misc", [128, 64], F32)
    wring = [sb(top, f"wring{i}", [128, 8, 128], BF16) for i in range(8)]
    ring_n = [0]

    def cview(name):
        o, n = CO[name]
        return con[:, o:o + n]

    ident, ones_c, bones, tri = cview("ident"), cview("ones"), cview("bones"), cview("tri")
    ntri, nones = cview("ntri"), cview("nones")
    md_all = cview("md")
    maskp, maskc = cview("maskp"), cview("maskc")

    def ppv(name):
        o, n = PP[name]
        return pp[:, o:o + n]

    def wslot():
        i = ring_n[0] % len(wring)
        ring_n[0] += 1
        return wring[i], ("wring", i)

    def wload(dst, src, key):
        ring = isinstance(key, tuple) and key[0] == "wring"
        P.dma("pool", dst, src, wr=[key], nobarrier=ring)

    def wcols(w2d, c0, n):
        return w2d[:, c0:c0 + n].rearrange("(kc p) n -> p kc n", p=128)

    def mm_group(out, pairs, rd, wr):
        def fn(e):
            ins = None
            n = len(pairs)
            for i, (l, r) in enumerate(pairs):
                ins = e.matmul(out, lhsT=l, rhs=r, start=(i == 0), stop=(i == n - 1))
            return ins
        P.op("pe", fn, rd=rd, wr=wr)

    def tsl(tt):
        return slice(tt * TS, (tt + 1) * TS)

    Xk = lambda c, tt: ("X", c, tt)
    psk = lambda b: ("ps", b)

    P.dma("sp", pp[:], pp_d, wr=["pp"])
    P.dma("pool", con[:], con_d, wr=["con"])
    with ExitStack() as st:
        cs = sb(st, "cs", [128, 8, 2], BF16)
        wbig = [sb(st, f"wbig{i}", [128, 8, 1024], BF16) for i in range(2)]
        wstg = sb(st, "wstg", [128, 8, 1024], F32)
        mod = sb(st, "mod", [128, 96, 2], F32)
        tmpa = sb(st, "tmpa", [128, 16], F32)
        o, n = PP["cT"]
        P.op("act", lambda e: e.activation(out=cs[:].rearrange("p k b -> p (k b)"), in_=pp[:, o:o + n], func=AF.Silu),
             rd=["pp"], wr=["cs"])
        for l in range(2):
            for pc in range(6):
                wb = wbig[(l * 6 + pc) % 2]
                wk = ("wbig", (l * 6 + pc) % 2)
                if (l * 6 + pc) % 2 == 0:
                    wload(wb[:], wcols(ada_w[l], pc * 1024, 1024), wk)
                else:
                    P.dma("sp", wstg[:], wcols(ada_w[l], pc * 1024, 1024), wr=["wstg"])
                    P.op("dve", lambda e, wb=wb: e.tensor_copy(out=wb[:, 0:4, :], in_=wstg[:, 0:4, :]), rd=["wstg"], wr=[wk])
                    P.op("act", lambda e, wb=wb: e.activation(out=wb[:, 4:8, :], in_=wstg[:, 4:8, :], func=AF.Identity),
                         rd=["wstg"], wr=[wk])
                for nn in range(8):
                    g = pc * 8 + nn
                    col = (l * 48 + g) * 2
                    mm_group(ps[7][:, col:col + 2],
                             [(wb[:, k, nn * 128:(nn + 1) * 128], cs[:, k, :]) for k in range(8)],
                             rd=[wk, "cs"], wr=[psk(7)])
        P.op("dve", lambda e: e.tensor_tensor(
            out=mod[:], in0=ps[7][:, 0:192].rearrange("p (g b) -> p g b", b=2),
            in1=ppv("adab").unsqueeze(2).broadcast_to([128, 96, 2]), op=ALU.add),
            rd=[psk(7), "pp"], wr=["mod"])
        modv = mod[:].rearrange("p (l j c) b -> p l j c b", l=2, j=6)
        for l in range(2):
            for b in range(2):
                for (dst, jsc, gname) in ((0, 1, "gmix"), (3, 4, "gffn")):
                    go, _ = PP[gname]
                    P.op("dve", lambda e, l=l, b=b, dst=dst, jsc=jsc, go=go: e.scalar_tensor_tensor(
                        out=modp[:, l, b, dst, :], in0=modv[:, l, jsc, :, b], scalar=1.0,
                        in1=pp[:, go + l * 8:go + l * 8 + 8], op0=ALU.add, op1=ALU.mult),
                        rd=["mod", "pp"], wr=["modp"])
                for (dst, j) in ((1, 0), (2, 2), (4, 3), (5, 5)):
                    P.op("dve", lambda e, l=l, b=b, dst=dst, j=j: e.tensor_copy(
                        out=modp[:, l, b, dst, :], in_=modv[:, l, j, :, b]), rd=["mod"], wr=["modp"])
        P.op("act", lambda e: e.activation(out=tmpa[:, 0:4], in_=ppv("llam"), func=AF.Exp, scale=-1.0),
             rd=["pp"], wr=["tmpa"])
        P.op("act", lambda e: e.activation(out=tmpa[:, 4:8], in_=tmpa[:, 0:4], func=AF.Ln, bias=1.0),
             rd=["tmpa"], wr=["tmpa2"])
        P.op("dve", lambda e: e.tensor_scalar(out=misc[:, 0:4], in0=tmpa[:, 4:8], scalar1=-8.0, scalar2=None,
                                              op0=ALU.mult), rd=["tmpa2"], wr=["misc"])
        P.op("dve", lambda e: e.tensor_scalar(out=misc[:, 4:5], in0=ppv("evqg"), scalar1=0.125, scalar2=None,
                                              op0=ALU.mult), rd=["pp"], wr=["misc"])
        P.op("dve", lambda e: e.tensor_scalar(out=misc[:, 5:6], in0=ppv("odqg"), scalar1=0.125, scalar2=None,
                                              op0=ALU.mult), rd=["pp"], wr=["misc"])
        P.op("act", lambda e: e.activation(out=misc[:, 8:24], in_=ppv("sinks"), func=AF.Exp),
             rd=["pp"], wr=["misc"])
        dump("modp", modp[:].rearrange("p l b j c -> p (l b j c)"), [128, 192], rd=["modp"])
        P.barrier()
        P.flush()
    cl = misc[:, 0:4]
    evq8 = misc[:, 4:5]
    odq8 = misc[:, 5:6]
    esink = misc[:, 8:24]

    def do_norm(st, h, l, b, which):
        ia, ish = (0, 1) if which == 0 else (3, 4)
        sq = [sb(st, f"sq{i}", [128, 8, TS], BF16) for i in range(NT)]
        sd = [sb(st, f"sd{i}", [128, TS], F32) for i in range(NT)]
        rs = [sb(st, f"rs{i}", [128, TS], F32) for i in range(NT)]
        tm = [sb(st, f"tm{i}", [128, TS], F32) for i in range(3)]

        def stat_a(tt):
            P.op("act", lambda e, tt=tt: e.activation(out=sq[tt][:], in_=X[:, :, tsl(tt)], func=AF.Square),
                 rd=[Xk(c, tt) for c in range(8)], wr=[("sq", tt)])
            bank = 6 + tt % 2
            mm_group(ps[bank][:], [(ones_c, sq[tt][:, c, :]) for c in range(8)], rd=[("sq", tt), "con"], wr=[psk(bank)])

        def stat_b(tt):
            bank = 6 + tt % 2
            P.op("act", lambda e, tt=tt, bank=bank: e.activation(out=sd[tt][:], in_=ps[bank][:], func=AF.Ln,
                                                                  scale=1.0 / D, bias=EPS),
                 rd=[psk(bank)], wr=[("sd", tt)])
            P.op("act", lambda e, tt=tt: e.activation(out=rs[tt][:], in_=sd[tt][:], func=AF.Exp, scale=-0.5),
                 rd=[("sd", tt)], wr=[("rs", tt)])

        for tt in range(NT + 1):
            if tt < NT:
                stat_a(tt)
            if tt >= 1:
                stat_b(tt - 1)
        ti = 0
        for tt in range(NT):
            for c in range(8):
                t3 = ti % 3
                ti += 1
                P.op("dve", lambda e, c=c, tt=tt, t3=t3: e.tensor_tensor(
                    out=tm[t3][:], in0=X[:, c, tsl(tt)], in1=rs[tt][:], op=ALU.mult),
                    rd=[Xk(c, tt), ("rs", tt)], wr=[("tm", t3)])
                if c % 2 == 0:
                    P.op("act", lambda e, c=c, tt=tt, t3=t3: e.activation(
                        out=h[:, c, tsl(tt)], in_=tm[t3][:], func=AF.Identity,
                        scale=modp[:, l, b, ia, c:c + 1], bias=modp[:, l, b, ish, c:c + 1]),
                        rd=[("tm", t3), "modp"], wr=[("h", c, tt)])
                else:
                    P.op("dve", lambda e, c=c, tt=tt, t3=t3: e.tensor_scalar(
                        out=h[:, c, tsl(tt)], in0=tm[t3][:], scalar1=modp[:, l, b, ia, c:c + 1],
                        scalar2=modp[:, l, b, ish, c:c + 1], op0=ALU.mult, op1=ALU.add),
                        rd=[("tm", t3), "modp"], wr=[("h", c, tt)])

    def hk_all(tt):
        return [("h", c, tt) for c in range(8)]

    def out_proj_residual(w2d, ysrc, ykeys, l, b, gidx, r0=0, nk=8):
        slots = {}

        def ld(n):
            slots[n] = wslot()
            wload(slots[n][0][:, 0:nk, :],
                  w2d[r0 * 128:(r0 + nk) * 128, n * 128:(n + 1) * 128].rearrange("(kc p) n -> p kc n", p=128), slots[n][1])
        for n in range(3):
            ld(n)
        for n in range(8):
            if n + 3 < 8:
                ld(n + 3)
            ws, wk = slots[n]
            for tt in range(NT):
                bank = (n * NT + tt) % 6
                mm_group(ps[bank][:], [(ws[:, k, :], ysrc(k)[:, tsl(tt)]) for k in range(nk)],
                         rd=[wk] + ykeys(tt), wr=[psk(bank)])
                P.op("dve", lambda e, n=n, tt=tt, bank=bank: e.scalar_tensor_tensor(
                    out=X[:, n, tsl(tt)], in0=ps[bank][:], scalar=modp[:, l, b, gidx, n:n + 1],
                    in1=X[:, n, tsl(tt)], op0=ALU.mult, op1=ALU.add),
                    rd=[psk(bank), Xk(n, tt), "modp"], wr=[Xk(n, tt)])

    def do_ffn(st, h, l, b):
        GL = 2
        act = sb(st, "act", [128, 6, T], BF16)
        wd = [sb(st, f"wd{i}", [128, 6, D], BF16) for i in range(2)]
        gb = [sb(st, f"gb{i}", [128, 2 + T], F32) for i in range(GL + 1)]
        gc = sb(st, "gc", [128, T], F32)
        sg = sb(st, "sg", [128, T], BF16)
        fo, _ = PP["fcw"]
        bo, _ = PP["fcb"]
        for i in range(GL + 1):
            P.op("dve", lambda e, i=i: e.memset(gb[i][:, 0:2], 0.0), wr=[("gbpad", i)])
        wg2, wu2 = w_gate[l], w_up[l]
        slots = {}
        qof = {}
        for qi, (c0, ncq) in enumerate(FQ):
            for ci in range(ncq):
                qof[c0 + ci] = (qi, ci, c0, ncq)

        def load_c(c):
            sg_, sk = wslot()
            wload(sg_[:], wcols(wg2, c * 128, 128), sk)
            su_, uk = wslot()
            wload(su_[:], wcols(wu2, c * 128, 128), uk)
            slots[c] = (sg_, sk, su_, uk)

        def gate_stage(c):
            if c + 1 < NFF:
                load_c(c + 1)
            sg_, sk, su_, uk = slots[c]
            gi = c % (GL + 1)
            for tt in range(NT):
                bank = tt
                mm_group(ps[bank][:], [(sg_[:, k, :], h[:, k, tsl(tt)]) for k in range(8)],
                         rd=[sk] + hk_all(tt), wr=[psk(bank)])
                P.op("act", lambda e, gi=gi, tt=tt, bank=bank: e.activation(
                    out=gb[gi][:, 2 + tt * TS:2 + (tt + 1) * TS], in_=ps[bank][:], func=AF.Identity),
                    rd=[psk(bank)], wr=[("gb", gi, tt)])

        def rest_stage(c):
            qi, ci, c0, ncq = qof[c]
            wdb = wd[qi % 2]
            wdk = ("wd", qi % 2)
            if ci == 0:
                wload(wdb[:, 0:ncq, :],
                      w_down[l][c0 * 128:(c0 + ncq) * 128, :].rearrange("(kc p) n -> p kc n", p=128), wdk)
            sg_, sk, su_, uk = slots.pop(c)
            gi = c % (GL + 1)
            wo = fo + (l * NFF + c) * 3
            P.op("dve", lambda e, gi=gi, wo=wo, c=c: e.tensor_scalar(
                out=gc[:], in0=gb[gi][:, 0:T], scalar1=pp[:, wo:wo + 1],
                scalar2=pp[:, bo + l * NFF + c:bo + l * NFF + c + 1], op0=ALU.mult, op1=ALU.add),
                rd=[("gb", gi, t_) for t_ in range(NT)] + [("gbpad", gi), "pp"], wr=["gc"])
            for k in (1, 2):
                P.op("dve", lambda e, gi=gi, wo=wo, k=k: e.scalar_tensor_tensor(
                    out=gc[:], in0=gb[gi][:, k:k + T], scalar=pp[:, wo + k:wo + k + 1], in1=gc[:],
                    op0=ALU.mult, op1=ALU.add),
                    rd=[("gb", gi, t_) for t_ in range(NT)] + ["gc", "pp"], wr=["gc"])
            P.op("act", lambda e: e.activation(out=sg[:], in_=gc[:], func=AF.Silu), rd=["gc"], wr=["sg"])
            for tt in range(NT):
                bank = 4 + tt % 2
                mm_group(ps[bank][:], [(su_[:, k, :], h[:, k, tsl(tt)]) for k in range(8)],
                         rd=[uk] + hk_all(tt), wr=[psk(bank)])
                P.op("dve", lambda e, ci=ci, tt=tt, bank=bank: e.tensor_tensor(
                    out=act[:, ci, tsl(tt)], in0=sg[:, tsl(tt)], in1=ps[bank][:], op=ALU.mult),
                    rd=["sg", psk(bank)], wr=[("act", ci, tt)])
            if ci == ncq - 1:
                for n in range(8):
                    for tt in range(NT):
                        bank = 6 + (n * NT + tt) % 2
                        mm_group(ps[bank][:], [(wdb[:, cj, n * 128:(n + 1) * 128], act[:, cj, tsl(tt)]) for cj in range(ncq)],
                                 rd=[wdk] + [("act", cj, tt) for cj in range(ncq)], wr=[psk(bank)])
                        P.op("dve", lambda e, n=n, tt=tt, bank=bank: e.scalar_tensor_tensor(
                            out=X[:, n, tsl(tt)], in0=ps[bank][:], scalar=modp[:, l, b, 5, n:n + 1],
                            in1=X[:, n, tsl(tt)], op0=ALU.mult, op1=ALU.add),
                            rd=[psk(bank), Xk(n, tt), "modp"], wr=[Xk(n, tt)])

        load_c(0)
        for i in range(NFF + GL):
            if i < NFF:
                gate_stage(i)
            if i - GL >= 0:
                rest_stage(i - GL)

    qkc = [0]

    def qk_norm_chunk(w2d, col0, dst, dstkey, gain_ap, h, tmp, dup64=False, split=None):
        ws, wk = wslot()
        if dup64:
            for hb in range(2):
                P.dma("pool", ws[:, :, hb * 64:(hb + 1) * 64], wcols(w2d, col0, 64), wr=[wk])
        else:
            wload(ws[:], wcols(w2d, col0, 128), wk)
        sqb, sdb, rsb = tmp
        for tt in range(NT):
            i2 = qkc[0] % 2
            bA = qkc[0] % 4
            bB = 4 + qkc[0] % 2
            qkc[0] += 1
            mm_group(ps[bA][:], [(ws[:, k, :], h[:, k, tsl(tt)]) for k in range(8)], rd=[wk] + hk_all(tt), wr=[psk(bA)])
            P.op("act", lambda e, i2=i2, bA=bA: e.activation(out=sqb[i2][:], in_=ps[bA][:], func=AF.Square),
                 rd=[psk(bA)], wr=[("sqb", i2)])
            mm_group(ps[bB][:], [(bones, sqb[i2][:])], rd=[("sqb", i2), "con"], wr=[psk(bB)])
            P.op("act", lambda e, i2=i2, bB=bB: e.activation(out=sdb[i2][:], in_=ps[bB][:], func=AF.Ln,
                                                              scale=1.0 / 64, bias=EPS),
                 rd=[psk(bB)], wr=[("sdb", i2)])
            P.op("act", lambda e, i2=i2: e.activation(out=rsb[i2][:], in_=sdb[i2][:], func=AF.Exp, scale=-0.5),
                 rd=[("sdb", i2)], wr=[("rsb", i2)])
            if split is None:
                P.op("dve", lambda e, i2=i2, bA=bA, tt=tt: e.scalar_tensor_tensor(
                    out=dst[:, tsl(tt)], in0=ps[bA][:], scalar=gain_ap, in1=rsb[i2][:], op0=ALU.mult, op1=ALU.mult),
                    rd=[psk(bA), ("rsb", i2), "misc", "pp"], wr=[(dstkey, tt)])
            else:
                for hh in range(2):
                    pr = slice(hh * 64, (hh + 1) * 64)
                    P.op("dve", lambda e, i2=i2, bA=bA, tt=tt, pr=pr, hh=hh: e.scalar_tensor_tensor(
                        out=split[hh][pr, tsl(tt)], in0=ps[bA][pr, :], scalar=gain_ap[pr, :], in1=rsb[i2][pr, :],
                        op0=ALU.mult, op1=ALU.mult),
                        rd=[psk(bA), ("rsb", i2), "misc", "pp"], wr=[(dstkey, hh, tt)])

    for s in range(nseq):
        b = s
        for c in range(8):
            P.dma("sp", X[:, c, :], xT[s, c], wr=[Xk(c, tt) for tt in range(NT)], nobarrier=True)
        l = 0
        with ExitStack() as st0:
            sty = ExitStack()
            ya = sb(sty, "ya", [128, 4, T], BF16)
            with ExitStack() as st1:
                h = sb(st1, "h", [128, 8, T], BF16)
                with ExitStack() as st2:
                    do_norm(st2, h, l, b, 0)
                    if s == 0:
                        dump("h0", h[:].rearrange("p c t -> p (c t)"), [128, 8 * T],
                             rd=[("h", c, tt) for c in range(8) for tt in range(NT)])
                    P.barrier()
                    P.flush()
                if stop == "norm0":
                    break
                with ExitStack() as st2:
                    xr = sb(st2, "xr", [128, 3 + T], F32)
                    xc = sb(st2, "xc", [128, T], F32)
                    xcb = sb(st2, "xcb", [128, T], BF16)
                    ra = sb(st2, "ra", [128, T], F32)
                    ig = sb(st2, "ig", [128, T], F32)
                    s2 = sb(st2, "s2", [128, T], F32)
                    gel = sb(st2, "gel", [128, T], F32)
                    gx = sb(st2, "gx", [128, T], F32)
                    wabd = sb(st2, "wabd", [128, 4, 128], BF16)
                    wxbd = sb(st2, "wxbd", [128, 4, 128], BF16)
                    P.dma("pool", wabd[:], wabd_d, wr=["wabd"])
                    P.dma("pool", wxbd[:], wxbd_d, wr=["wxbd"])
                    P.op("dve", lambda e: e.memset(xr[:, 0:3], 0.0), wr=["xrpad"])
                    lcw, _ = PP["lcw"]
                    lcb, _ = PP["lcb"]
                    lba, _ = PP["lba"]
                    lbx, _ = PP["lbx"]
                    import os as _os
                    for j in [int(v) for v in _os.environ.get("LRU_CHUNKS", "0,1,2,3").split(",")]:
                        wsx, wkx = wslot()
                        wload(wsx[:], wcols(ev_w_in, j * 128, 128), wkx)
                        wsg, wkg = wslot()
                        wload(wsg[:], wcols(ev_w_in, 512 + j * 128, 128), wkg)
                        for tt in range(NT):
                            bank = tt % 2
                            mm_group(ps[bank][:], [(wsx[:, k, :], h[:, k, tsl(tt)]) for k in range(8)],
                                     rd=[wkx] + hk_all(tt), wr=[psk(bank)])
                            P.op("act", lambda e, tt=tt, bank=bank: e.activation(
                                out=xr[:, 3 + tt * TS:3 + (tt + 1) * TS], in_=ps[bank][:], func=AF.Identity),
                                rd=[psk(bank)], wr=[("xr", tt)])
                        xrk = [("xr", t_) for t_ in range(NT)]
                        P.op("dve", lambda e, j=j: e.tensor_scalar(
                            out=xc[:], in0=xr[:, 0:T], scalar1=pp[:, lcw + j * 4:lcw + j * 4 + 1],
                            scalar2=pp[:, lcb + j:lcb + j + 1], op0=ALU.mult, op1=ALU.add),
                            rd=xrk + ["xrpad", "pp"], wr=["xc"])
                        for k in (1, 2, 3):
                            P.op("dve", lambda e, j=j, k=k: e.scalar_tensor_tensor(
                                out=xc[:], in0=xr[:, k:k + T], scalar=pp[:, lcw + j * 4 + k:lcw + j * 4 + k + 1],
                                in1=xc[:], op0=ALU.mult, op1=ALU.add), rd=xrk + ["xc", "pp"], wr=["xc"])
                        P.op("act", lambda e: e.activation(out=xcb[:], in_=xc[:], func=AF.Identity), rd=["xc"], wr=["xcb"])
                        for tt in range(NT):
                            bank = 2 + tt % 2
                            mm_group(ps[bank][:], [(wabd[:, j, :], xcb[:, tsl(tt)])], rd=["wabd", "xcb"], wr=[psk(bank)])
                            P.op("act", lambda e, j=j, tt=tt, bank=bank: e.activation(
                                out=ra[:, tsl(tt)], in_=ps[bank][:], func=AF.Sigmoid, bias=pp[:, lba + j:lba + j + 1]),
                                rd=[psk(bank), "pp"], wr=[("ra", tt)])
                            bank2 = 4 + tt % 2
                            mm_group(ps[bank2][:], [(wxbd[:, j, :], xcb[:, tsl(tt)])], rd=["wxbd", "xcb"], wr=[psk(bank2)])
                            P.op("act", lambda e, j=j, tt=tt, bank2=bank2: e.activation(
                                out=ig[:, tsl(tt)], in_=ps[bank2][:], func=AF.Sigmoid, bias=pp[:, lbx + j:lbx + j + 1]),
                                rd=[psk(bank2), "pp"], wr=[("ig", tt)])
                        rak = [("ra", t_) for t_ in range(NT)]
                        igk = [("ig", t_) for t_ in range(NT)]
                        P.op("act", lambda e, j=j: e.activation(out=ra[:], in_=ra[:], func=AF.Exp, scale=cl[:, j:j + 1]),
                             rd=rak + ["misc"], wr=rak)
                        P.op("act", lambda e: e.activation(out=s2[:], in_=ra[:], func=AF.Square), rd=rak, wr=["s2"])
                        P.op("act", lambda e: e.activation(out=s2[:], in_=s2[:], func=AF.Sqrt, scale=-1.0, bias=1.0),
                             rd=["s2"], wr=["s2"])
                        P.op("dve", lambda e: e.tensor_tensor(out=s2[:], in0=s2[:], in1=ig[:], op=ALU.mult),
                             rd=["s2"] + igk, wr=["s2"])
                        P.op("dve", lambda e: e.tensor_tensor(out=s2[:], in0=s2[:], in1=xc[:], op=ALU.mult),
                             rd=["s2", "xc"], wr=["s2"])
                        P.op("dve", lambda e: e.tensor_tensor_scan(out=xc[:], data0=ra[:], data1=s2[:], initial=0.0,
                                                                    op0=ALU.mult, op1=ALU.add),
                             rd=rak + ["s2", "xc"], wr=["xc"])
                        for tt in range(NT):
                            bank = 6 + tt % 2
                            mm_group(ps[bank][:], [(wsg[:, k, :], h[:, k, tsl(tt)]) for k in range(8)],
                                     rd=[wkg] + hk_all(tt), wr=[psk(bank)])
                            P.op("act", lambda e, tt=tt, bank=bank: e.activation(
                                out=gx[:, tsl(tt)], in_=ps[bank][:], func=AF.Identity),
                                rd=[psk(bank)], wr=[("gx", tt)])
                        gxk = [("gx", t_) for t_ in range(NT)]
                        P.op("act", lambda e: e.activation(out=gel[:], in_=gx[:], func=AF.Square), rd=gxk, wr=["gel"])
                        P.op("dve", lambda e: e.tensor_scalar(out=gel[:], in0=gel[:], scalar1=0.044715, scalar2=1.0,
                                                              op0=ALU.mult, op1=ALU.add), rd=["gel"], wr=["gel"])
                        P.op("dve", lambda e: e.tensor_tensor(out=gel[:], in0=gel[:], in1=gx[:], op=ALU.mult),
                             rd=["gel"] + gxk, wr=["gel"])
                        P.op("act", lambda e: e.activation(out=gel[:], in_=gel[:], func=AF.Sigmoid, scale=1.5957691216057308),
                             rd=["gel"], wr=["gel"])
                        P.op("dve", lambda e: e.tensor_tensor(out=gel[:], in0=gel[:], in1=gx[:], op=ALU.mult),
                             rd=["gel"] + gxk, wr=["gel"])
                        P.op("dve", lambda e, j=j: e.tensor_tensor(out=ya[:, j, :], in0=xc[:], in1=gel[:], op=ALU.mult),
                             rd=["xc", "gel"], wr=[("ya", j)])
                    if s == 0:
                        dump("lxc", xc[:], [128, T], rd=["xc"])
                        dump("lra", ra[:], [128, T], rd=[("ra", t_) for t_ in range(NT)])
                        dump("lig", ig[:], [128, T], rd=[("ig", t_) for t_ in range(NT)])
                        dump("ls2", s2[:], [128, T], rd=["s2"])
                        dump("ya", ya[:].rearrange("p c t -> p (c t)"), [128, 4 * T], rd=[("ya", j) for j in range(4)])
                    P.barrier()
                    P.flush()
                if stop == "lru":
                    sty.close()
                    break
                out_proj_residual(ev_w_out, lambda k: ya[:, k, :], lambda tt: [("ya", j) for j in range(4)], l, b, 2, r0=0, nk=4)
                P.barrier()
                P.flush()
                sty.close()
                qz = [sb(st0, f"qz{i}", [128, 4, T], BF16) for i in range(2)]
                kn = sb(st0, "kn", [128, 4, T], BF16)
                vt = sb(st0, "vt", [128, 16, 512], BF16)
                P.op("dve", lambda e: e.memset(qz[0][64:128, :, :], 0.0), wr=[("qzpad", 0)])
                P.op("dve", lambda e: e.memset(qz[1][0:64, :, :], 0.0), wr=[("qzpad", 1)])
                with ExitStack() as st2:
                    sqb = [sb(st2, f"sqb{i}", [128, TS], BF16) for i in range(2)]
                    sdb = [sb(st2, f"sdb{i}", [128, TS], F32) for i in range(2)]
                    rsb = [sb(st2, f"rsb{i}", [128, TS], F32) for i in range(2)]
                    for j in range(4):
                        qk_norm_chunk(ev_w_in, 1024 + j * 128, None, ("qz", j), evq8, h, (sqb, sdb, rsb),
                                      split=(qz[0][:, j, :], qz[1][:, j, :]))
                        qk_norm_chunk(ev_w_in, 1536 + j * 128, kn[:, j, :], ("kn", j), ppv("evkg"), h, (sqb, sdb, rsb))
                    wvs = []
                    for q4 in range(4):
                        ws_, wk_ = wslot()
                        wload(ws_[:], wcols(ev_w_in, 2048 + q4 * 128, 128), wk_)
                        wvs.append((ws_, wk_))
                    for blk in range(16):
                        bank = 6 + blk % 2

                        def vproj(e, blk=blk, bank=bank):
                            ins = None
                            for q4 in range(4):
                                for k in range(8):
                                    ins = e.matmul(ps[bank][:, q4 * 128:(q4 + 1) * 128], lhsT=h[:, k, blk * 128:(blk + 1) * 128],
                                                   rhs=wvs[q4][0][:, k, :], start=(k == 0), stop=(k == 7))
                            return ins
                        P.op("pe", vproj, rd=[w_[1] for w_ in wvs] + hk_all(blk // 4), wr=[psk(bank)])
                        if blk % 2 == 0:
                            P.op("act", lambda e, blk=blk, bank=bank: e.activation(out=vt[:, blk, :], in_=ps[bank][:],
                                                                                   func=AF.Identity),
                                 rd=[psk(bank)], wr=[("vt", blk)])
                        else:
                            P.op("dve", lambda e, blk=blk, bank=bank: e.tensor_copy(out=vt[:, blk, :], in_=ps[bank][:]),
                                 rd=[psk(bank)], wr=[("vt", blk)])
                    if s == 0:
                        dump("kn", kn[:].rearrange("p c t -> p (c t)"), [128, 4 * T],
                             rd=[(("kn", j), t_) for j in range(4) for t_ in range(NT)])
                        dump("vt", vt[:].rearrange("p c t -> p (c t)"), [128, 16 * 512], rd=[("vt", k) for k in range(16)])
                    P.barrier()
                    P.flush()
            if stop in ("norm0", "lru", "sbproj"):
                break
            yb = sb(st0, "yb", [128, 4, T], BF16)
            with ExitStack() as st2:
                eb = [sb(st2, f"eb{i}", [128, TS], F32) for i in range(3)]
                LOOK = 2
                NSP, NRB = LOOK + 3, LOOK + 2
                spb = [sb(st2, f"spb{i}", [128, TS], BF16) for i in range(NSP)]
                Rb = [sb(st2, f"Rb{i}", [128, TS], BF16) for i in range(NRB)]
                wb_ = [sb(st2, f"wb{i}", [128, TS], BF16) for i in range(4)]
                mo, _ = CO["md"]
                for i in range(NSP):
                    P.op("dve", lambda e, i=i: e.memset(spb[i][:], 0.0), wr=[("spb", i)])
                tiles = []
                for j in range(4):
                    for tt in range(NT):
                        for hh in range(2):
                            ob = 4 + ((j * NT + tt) * 2 + hh) % 4
                            for idx, kb in enumerate(range(4 * tt + 3, -1, -1)):
                                tiles.append(dict(j=j, tt=tt, hh=hh, kb=kb, idx=idx, ob=ob, R=None))
                cnt_ = dict(z=0, e=0, sp=0, r=0, rb=0, w=0)

                def stage_a(ti):
                    t = tiles[ti]
                    j, tt, hh, kb, idx = t["j"], t["tt"], t["hh"], t["kb"], t["idx"]
                    p0 = hh * 64
                    dz = kb - 4 * tt
                    zb = cnt_["z"] % 2
                    cnt_["z"] += 1
                    c0 = max(dz, 0) * 128
                    t["c0"] = c0
                    mm_group(ps[zb][:, c0:TS], [(kn[:, j, kb * 128:(kb + 1) * 128], qz[hh][:, j, tt * TS + c0:(tt + 1) * TS])],
                             rd=[(("kn", j), kb // 4), (("qz", j), hh, tt), ("qzpad", hh)], wr=[psk(zb)])
                    ei = cnt_["e"] % 3
                    cnt_["e"] += 1
                    P.op("act", lambda e, ei=ei, zb=zb, c0=c0: e.activation(out=eb[ei][:, c0:TS], in_=ps[zb][:, c0:TS], func=AF.Exp),
                         rd=[psk(zb)], wr=[("eb", ei)])
                    si = cnt_["sp"] % NSP
                    cnt_["sp"] += 1
                    t["si"] = si
                    P.op("act", lambda e, ei=ei, si=si, c0=c0: e.activation(out=spb[si][:, c0:TS], in_=eb[ei][:, c0:TS], func=AF.Ln, bias=1.0),
                         rd=[("eb", ei)], wr=[("spb", si)])
                    if dz >= 0:
                        P.op("dve", lambda e, si=si, dz=dz: e.tensor_tensor(
                            out=spb[si][:], in0=spb[si][:], in1=con[:, mo + dz * 512:mo + (dz + 1) * 512], op=ALU.mult),
                            rd=[("spb", si), "con"], wr=[("spb", si)])
                    if kb > 0:
                        nt_ = tiles[ti + 1]
                        if idx == 0:
                            nt_["R"] = (spb[si], ("spb", si))
                        else:
                            rn = cnt_["rb"] % NRB
                            cnt_["rb"] += 1
                            rsrc, rkey = t["R"]
                            P.op("dve", lambda e, si=si, rsrc=rsrc, rn=rn: e.tensor_tensor(
                                out=Rb[rn][:], in0=rsrc[:], in1=spb[si][:], op=ALU.add),
                                rd=[("spb", si), rkey], wr=[("Rb", rn)])
                            nt_["R"] = (Rb[rn], ("Rb", rn))

                def stage_b(ti):
                    t = tiles[ti]
                    j, tt, hh, kb, idx, ob, si = t["j"], t["tt"], t["hh"], t["kb"], t["idx"], t["ob"], t["si"]
                    p0 = hh * 64
                    dz = kb - 4 * tt
                    rb = 2 + cnt_["r"] % 2
                    cnt_["r"] += 1
                    c0 = t["c0"]
                    pairs = [(ntri, spb[si][:, c0:TS])]
                    rdk = [("spb", si), "con", (("kn", j), kb // 4), (("qz", j), hh, tt), ("qzpad", hh)]
                    if t["R"] is not None:
                        pairs.append((nones, t["R"][0][:, c0:TS]))
                        rdk.append(t["R"][1])
                    pairs.append((kn[:, j, kb * 128:(kb + 1) * 128], qz[hh][:, j, tt * TS + c0:(tt + 1) * TS]))
                    mm_group(ps[rb][:, c0:TS], pairs, rd=rdk, wr=[psk(rb)])
                    wi = cnt_["w"] % 4
                    cnt_["w"] += 1
                    t["wi"] = wi
                    P.op("act", lambda e, wi=wi, rb=rb, c0=c0: e.activation(out=wb_[wi][:, c0:TS], in_=ps[rb][:, c0:TS], func=AF.Exp),
                         rd=[psk(rb)], wr=[("wb", wi)])
                    if dz >= 0:
                        P.op("dve", lambda e, wi=wi, dz=dz, c0=c0: e.tensor_tensor(
                            out=wb_[wi][:, c0:TS], in0=wb_[wi][:, c0:TS], in1=con[:, mo + dz * 512 + c0:mo + (dz + 1) * 512], op=ALU.mult),
                            rd=[("wb", wi), "con"], wr=[("wb", wi)])

                def stage_c(ti):
                    t = tiles[ti]
                    j, tt, hh, kb, idx, ob, wi = t["j"], t["tt"], t["hh"], t["kb"], t["idx"], t["ob"], t["wi"]
                    p0 = hh * 64
                    c0 = t["c0"]

                    def pv(e, ob=ob, kb=kb, j=j, wi=wi, c0=c0, first=(idx == 0), last=(kb == 0)):
                        return e.matmul(ps[ob][:, c0:TS], lhsT=vt[:, kb, j * 128:(j + 1) * 128],
                                        rhs=wb_[wi][:, c0:TS], start=first, stop=last, skip_group_check=True)
                    P.op("pe", pv, rd=[("wb", wi), ("vt", kb)], wr=[("pso", ob)])
                    if kb == 0:
                        P.op("act", lambda e, j=j, tt=tt, ob=ob, p0=p0: e.activation(
                            out=yb[p0:p0 + 64, j, tsl(tt)], in_=ps[ob][p0:p0 + 64, :], func=AF.Identity),
                            rd=[("pso", ob)], wr=[("yb", j, tt, hh)])

                ntl = len(tiles)
                for ti in range(ntl + LOOK + 1):
                    if ti < ntl:
                        stage_a(ti)
                    if 0 <= ti - LOOK < ntl:
                        stage_b(ti - LOOK)
                    if ti - LOOK - 1 >= 0:
                        stage_c(ti - LOOK - 1)
                if s == 0:
                    dump("yb", yb[:].rearrange("p c t -> p (c t)"), [128, 4 * T],
                         rd=[("yb", j, t_, hh) for j in range(4) for t_ in range(NT) for hh in range(2)])
                P.barrier()
                P.flush()
            if stop == "sb":
                break
            out_proj_residual(ev_w_out, lambda k: yb[:, k, :],
                              lambda tt: [("yb", j, tt, hh) for j in range(4) for hh in range(2)], l, b, 2, r0=4, nk=4)
            P.barrier()
            P.flush()
        if s == 0:
            dump("x0mid", X[:].rearrange("p c t -> p (c t)"), [128, 8 * T], rd=[Xk(c, tt) for c in range(8) for tt in range(NT)])
        if stop == "mix0":
            break
        with ExitStack() as st1:
            h = sb(st1, "h", [128, 8, T], BF16)
            with ExitStack() as st2:
                do_norm(st2, h, l, b, 1)
                P.barrier()
                P.flush()
            with ExitStack() as st2:
                do_ffn(st2, h, l, b)
                P.barrier()
                P.flush()
        if s == 0:
            dump("x1", X[:].rearrange("p c t -> p (c t)"), [128, 8 * T], rd=[Xk(c, tt) for c in range(8) for tt in range(NT)])
        if stop == "l0":
            break
        l = 1
        with ExitStack() as st0:
            with ExitStack() as st1:
                h = sb(st1, "h", [128, 8, T], BF16)
                with ExitStack() as st2:
                    do_norm(st2, h, l, b, 0)
                    P.barrier()
                    P.flush()
                qn = sb(st0, "qn1", [128, 8, T], BF16)
                kd = sb(st0, "kd", [128, 4, T], BF16)
                va = sb(st0, "va", [128, 16, 4, 65], BF16)
                with ExitStack() as st2:
                    sqb = [sb(st2, f"sqb{i}", [128, TS], BF16) for i in range(2)]
                    sdb = [sb(st2, f"sdb{i}", [128, TS], F32) for i in range(2)]
                    rsb = [sb(st2, f"rsb{i}", [128, TS], F32) for i in range(2)]
                    wv = sb(st2, "wv", [128, 8, 512], BF16)
                    for c in range(8):
                        qk_norm_chunk(od_w_in, c * 128, qn[:, c, :], ("qn", c), odq8, h, (sqb, sdb, rsb))
                    for g in range(4):
                        qk_norm_chunk(od_w_in, 1024 + g * 64, kd[:, g, :], ("kd", g), ppv("odkg"), h, (sqb, sdb, rsb), dup64=True)
                    P.op("dve", lambda e: e.memset(va[:, :, :, 64:65], 1.0), wr=["vaones"])
                    wload(wv[:, :, 0:256], wcols(od_w_in, 1280, 256), "wv")
                    for blk in range(16):
                        bank = 4 + blk % 4
                        mm_group(ps[bank][:, 0:256], [(h[:, k, blk * 128:(blk + 1) * 128], wv[:, k, 0:256]) for k in range(8)],
                                 rd=["wv"] + hk_all(blk // 4), wr=[psk(bank)])
                        P.op("act" if blk % 2 == 0 else "dve",
                             (lambda e, blk=blk, bank=bank: e.activation(
                                 out=va[:, blk, :, 0:64], in_=ps[bank][:, 0:256].rearrange("p (g d) -> p g d", g=4), func=AF.Identity))
                             if blk % 2 == 0 else
                             (lambda e, blk=blk, bank=bank: e.tensor_copy(
                                 out=va[:, blk, :, 0:64], in_=ps[bank][:, 0:256].rearrange("p (g d) -> p g d", g=4))),
                             rd=[psk(bank)], wr=[("va", blk)])
                    P.barrier()
                    P.flush()
            if stop == "l1proj":
                break
            yT = sb(st0, "yT", [128, 8, T], BF16)
            with ExitStack() as st2:
                pb = [sb(st2, f"pb{i}", [128, 2, TS], BF16) for i in range(2)]
                den = [sb(st2, f"den{i}", [128, 4], F32) for i in range(2)]
                ytok = [sb(st2, f"ytok{i}", [128, D], BF16) for i in range(2)]
                units = [(qb, g) for qb in range(16) for g in range(4)]

                def swa_a(u):
                    qb, g = units[u]
                    pi = u % 2
                    kbs = [qb - 1, qb] if qb > 0 else [qb]
                    c0 = 0 if qb > 0 else 256
                    for hb in range(2):
                        sbank = hb + 2 * pi

                        def sc(e, sbank=sbank, g=g, kbs=kbs, qb=qb, hb=hb):
                            ins = None
                            for kb in kbs:
                                which = 0 if kb == qb - 1 else 1
                                ins = e.matmul(ps[sbank][:, which * 256:(which + 1) * 256],
                                               lhsT=kd[hb * 64:(hb + 1) * 64, g, kb * 128:(kb + 1) * 128],
                                               rhs=qn[hb * 64:(hb + 1) * 64, 2 * g:2 * g + 2, qb * 128:(qb + 1) * 128],
                                               start=True, stop=True)
                            return ins
                        P.op("pe", sc, rd=[(("kd", g), kb // 4) for kb in kbs] + [(("qn", 2 * g), qb // 4), (("qn", 2 * g + 1), qb // 4)],
                             wr=[psk(sbank)])
                        P.op("act", lambda e, pi=pi, hb=hb, sbank=sbank, c0=c0: e.activation(
                            out=pb[pi][:, hb, c0:512], in_=ps[sbank][:, c0:512], func=AF.Exp),
                            rd=[psk(sbank)], wr=[("pb", pi, hb)])
                        P.op("dve", lambda e, pi=pi, hb=hb, c0=c0: e.tensor_tensor(
                            out=pb[pi][:, hb, c0:512], in0=pb[pi][:, hb, c0:512], in1=maskp[:, c0:512], op=ALU.mult),
                            rd=[("pb", pi, hb), "con"], wr=[("pb", pi, hb)])

                def swa_b(u):
                    qb, g = units[u]
                    pi = u % 2
                    yi = qb % 2
                    kbs = [qb - 1, qb] if qb > 0 else [qb]
                    ybank = 4 + pi

                    def pvm(e, ybank=ybank, pi=pi, kbs=kbs, qb=qb, g=g):
                        ins = None
                        for hc in range(4):
                            hb, e_ = hc // 2, hc % 2
                            for i_, kb in enumerate(kbs):
                                which = 0 if kb == qb - 1 else 1
                                ins = e.matmul(ps[ybank][:, hc * 65:(hc + 1) * 65],
                                               lhsT=pb[pi][:, hb, which * 256 + e_ * 128:which * 256 + (e_ + 1) * 128],
                                               rhs=va[:, kb, g, :], start=(i_ == 0), stop=(i_ == len(kbs) - 1))
                        return ins
                    P.op("pe", pvm, rd=[("pb", pi, 0), ("pb", pi, 1)] + [("va", kb) for kb in kbs] + ["vaones"], wr=[psk(ybank)])
                    yv = ps[ybank][:, 0:260].rearrange("p (hb e d) -> p hb e d", hb=2, e=2)
                    P.op("dve", lambda e, pi=pi, yv=yv, g=g: e.tensor_tensor(
                        out=den[pi][:].rearrange("p (hb e) -> p hb e", hb=2),
                        in0=yv[:, :, :, 64],
                        in1=esink[:, 4 * g:4 * g + 4].rearrange("p (e hb) -> p hb e", hb=2), op=ALU.add),
                        rd=[psk(ybank), "misc"], wr=[("den", pi)])
                    P.op("dve", lambda e, pi=pi: e.reciprocal(out=den[pi][:], in_=den[pi][:]), rd=[("den", pi)], wr=[("den", pi)])
                    P.op("dve", lambda e, pi=pi, yv=yv, g=g, yi=yi: e.tensor_tensor(
                        out=ytok[yi][:, g * 256:(g + 1) * 256].rearrange("p (e hb d) -> p hb e d", e=2, hb=2),
                        in0=yv[:, :, :, 0:64],
                        in1=den[pi][:].rearrange("p (hb e) -> p hb e", hb=2).unsqueeze(3).broadcast_to([128, 2, 2, 64]),
                        op=ALU.mult),
                        rd=[psk(ybank), ("den", pi)], wr=[("ytok", yi, g)])
                    if g == 3:
                        tbank = 6 + yi
                        tp = ps[tbank][:].bitcast(BF16)

                        def trn(e, tp=tp, yi=yi):
                            ins = None
                            for c in range(8):
                                ins = e.transpose(out=tp[:, c * 128:(c + 1) * 128], in_=ytok[yi][:, c * 128:(c + 1) * 128], identity=ident)
                            return ins
                        P.op("pe", trn, rd=[("ytok", yi, g_) for g_ in range(4)] + ["con"], wr=[psk(tbank)])
                        P.op("act", lambda e, tp=tp, qb=qb: e.activation(
                            out=yT[:, :, qb * 128:(qb + 1) * 128], in_=tp.rearrange("p (c t) -> p c t", c=8), func=AF.Identity),
                            rd=[psk(tbank)], wr=[("yT", qb // 4)])

                nun = len(units)
                for u in range(nun + 1):
                    if u < nun:
                        swa_a(u)
                    if u >= 1:
                        swa_b(u - 1)
                if s == 0:
                    dump("yT", yT[:].rearrange("p c t -> p (c t)"), [128, 8 * T], rd=[("yT", t_) for t_ in range(NT)])
                P.barrier()
                P.flush()
            if stop == "swa":
                break
            out_proj_residual(od_w_out, lambda k: yT[:, k, :], lambda tt: [("yT", tt)], l, b, 2)
            P.barrier()
            P.flush()
        if s == 0:
            dump("x1mid", X[:].rearrange("p c t -> p (c t)"), [128, 8 * T], rd=[Xk(c, tt) for c in range(8) for tt in range(NT)])
        with ExitStack() as st1:
            h = sb(st1, "h", [128, 8, T], BF16)
            with ExitStack() as st2:
                do_norm(st2, h, l, b, 1)
                P.barrier()
                P.flush()
            with ExitStack() as st2:
                do_ffn(st2, h, l, b)
                P.barrier()
                P.flush()
        for c in range(8):
            P.dma("sp", out_d[s, c], X[:, c, :], rd=[Xk(c, tt) for tt in range(NT)], wr=[("out", s, c)], nobarrier=True)
        P.flush()

    P.barrier()
    P.op("sp", None)
    P.flush()
    top.close()
    return nc, P, dbg_out


_CACHE = {}


def kernel(**inputs):
    if "nc" not in _CACHE:
        _CACHE["nc"] = build()[0]
    nc = _CACHE["nc"]
    in_maps = [_host_inputs(inputs, core) for core in range(NCORES)]
    res = run_bass_kernel_spmd(nc, in_maps, core_ids=list(range(NCORES)))
    outs = []
    for core in range(NCORES):
        o = np.asarray(res.results[core]["out"], np.float32).reshape(2, D, T)
        outs.append(o.transpose(0, 2, 1))
    return np.ascontiguousarray(np.concatenate(outs, axis=0)).astype(np.float32)
```
